# Optimizing a Trainium2 kernel written in Bass

```python
import math
import jax
import jax.numpy as jnp
from jax import lax
import numpy as np

D_MODEL = 1024
BATCH = 16
SEQ = 2048
DEPTH = 2

GRID_W = 64
CTX_LEN = 256
HEAD_DIM = 64
N_GROUPS = 4
D_MIX = D_MODEL
GROUP_W = D_MIX // N_GROUPS
ATT_HEADS = GROUP_W // HEAD_DIM
ATT_KV_HEADS = 2
NA_HEADS = GROUP_W // HEAD_DIM
WIN_H = 8
WIN_W = 16
FNO_HEADS = 4
FNO_HEAD_W = GROUP_W // FNO_HEADS
SSM_GROUP = 16
SSM_GROUPS = GROUP_W // SSM_GROUP
SSM_STATE = 64
Q_BLOCK = 128
ROPE_THETA = 10000.0
D_FF = ((8 * D_MODEL // 3 + 255) // 256) * 256
N_MOD = 6
EPS = 1e-6
ATT_Q_W = ATT_HEADS * HEAD_DIM
ATT_KV_W = ATT_KV_HEADS * HEAD_DIM
NA_W = NA_HEADS * HEAD_DIM
PROJ_WIDTHS = (ATT_Q_W, ATT_KV_W, ATT_KV_W, NA_W, NA_W, NA_W, GROUP_W, GROUP_W)
IN_W = sum(PROJ_WIDTHS)

kernel_name = 'hybrid_parallel_heads_diffusion_block'


def _rmsnorm(t, g):
    tf = t.astype(jnp.float32)
    y = tf * lax.rsqrt(jnp.mean(tf * tf, axis=-1, keepdims=True) + EPS)
    return (y * g.astype(jnp.float32)).astype(t.dtype)


def _split_proj(p):
    outs, start = [], 0
    for w in PROJ_WIDTHS:
        outs.append(p[..., start:start + w])
        start += w
    return outs


def _heads(t, n_heads):
    b, n, _ = t.shape
    return t.reshape(b, n, n_heads, HEAD_DIM).transpose(0, 2, 1, 3)


def _merge_heads(t):
    b, h, n, d = t.shape
    return t.transpose(0, 2, 1, 3).reshape(b, n, h * d)


def _rope_tables(n_tokens):
    t = jnp.arange(n_tokens, dtype=jnp.int32)
    rows = (t // GRID_W).astype(jnp.float32)
    cols = (t % GRID_W).astype(jnp.float32)
    axis_dim = HEAD_DIM // 2
    inv_freq = ROPE_THETA ** (-jnp.arange(0, axis_dim, 2, dtype=jnp.float32) / axis_dim)
    ang_r = rows[:, None] * inv_freq[None, :]
    ang_c = cols[:, None] * inv_freq[None, :]
    return (jnp.cos(ang_r), jnp.sin(ang_r), jnp.cos(ang_c), jnp.sin(ang_c))


def _rotate(t, cos, sin):
    t1, t2 = jnp.split(t, 2, axis=-1)
    return jnp.concatenate([t1 * cos - t2 * sin, t2 * cos + t1 * sin], axis=-1)


def _apply_rope_2d(t, rope):
    cos_r, sin_r, cos_c, sin_c = rope
    tr, tc = jnp.split(t, 2, axis=-1)
    out = jnp.concatenate([_rotate(tr, cos_r, sin_r), _rotate(tc, cos_c, sin_c)], axis=-1)
    return out.astype(t.dtype)


def _gqa_mixer(q, k, v, qc, kc, vc, gq, gk, rope, need_ctx):
    b, n, _ = q.shape
    nc = qc.shape[1]
    rep = ATT_HEADS // ATT_KV_HEADS
    scale = HEAD_DIM ** -0.5
    qh = _apply_rope_2d(_rmsnorm(_heads(q, ATT_HEADS), gq), rope)
    kh = _apply_rope_2d(_rmsnorm(_heads(k, ATT_KV_HEADS), gk), rope)
    vh = _heads(v, ATT_KV_HEADS)
    kch = _rmsnorm(_heads(kc, ATT_KV_HEADS), gk)
    vch = _heads(vc, ATT_KV_HEADS)
    keys = jnp.concatenate([kh, kch], axis=2)
    vals = jnp.concatenate([vh, vch], axis=2)
    n_blk = n // Q_BLOCK
    q_blocks = jnp.moveaxis(qh.reshape(b, ATT_KV_HEADS, rep, n_blk, Q_BLOCK, HEAD_DIM), 3, 0)

    def block(qb):
        s = jnp.einsum('bgrqd,bgkd->bgrqk', qb, keys).astype(jnp.float32) * scale
        p = jax.nn.softmax(s, axis=-1).astype(vals.dtype)
        return jnp.einsum('bgrqk,bgkd->bgrqd', p, vals)

    o = lax.map(block, q_blocks)
    o = jnp.moveaxis(o, 0, 3).reshape(b, ATT_HEADS, n, HEAD_DIM)
    out = _merge_heads(o)
    out_c = None
    if need_ctx:
        qch = _rmsnorm(_heads(qc, ATT_HEADS), gq).reshape(b, ATT_KV_HEADS, rep, nc, HEAD_DIM)
        s = jnp.einsum('bgrqd,bgkd->bgrqk', qch, kch).astype(jnp.float32) * scale
        p = jax.nn.softmax(s, axis=-1).astype(vch.dtype)
        oc = jnp.einsum('bgrqk,bgkd->bgrqd', p, vch).reshape(b, ATT_HEADS, nc, HEAD_DIM)
        out_c = _merge_heads(oc)
    return out, out_c


def _na_mixer(q, k, v, qc, kc, vc, gq, gk, rel_bias, need_ctx):
    b, n, _ = q.shape
    rows = n // GRID_W
    kh_win = min(WIN_H, rows)
    n_loc = kh_win * WIN_W
    scale = HEAD_DIM ** -0.5
    qh = _rmsnorm(_heads(q, NA_HEADS), gq)
    kh = _rmsnorm(_heads(k, NA_HEADS), gk)
    vh = _heads(v, NA_HEADS)
    qch = _rmsnorm(_heads(qc, NA_HEADS), gq)
    kch = _rmsnorm(_heads(kc, NA_HEADS), gk)
    vch = _heads(vc, NA_HEADS)
    q_rows = jnp.moveaxis(qh.reshape(b, NA_HEADS, rows, GRID_W, HEAD_DIM), 2, 0)
    k_grid = kh.reshape(b, NA_HEADS, rows, GRID_W, HEAD_DIM)
    v_grid = vh.reshape(b, NA_HEADS, rows, GRID_W, HEAD_DIM)
    cols = jnp.arange(GRID_W)
    col_start = jnp.clip(cols - WIN_W // 2, 0, GRID_W - WIN_W)
    col_idx = col_start[:, None] + jnp.arange(WIN_W)[None, :]
    dc_idx = col_idx - cols[:, None] + (WIN_W - 1)

    def row_block(args):
        r, qr = args
        rs = jnp.clip(r - WIN_H // 2, 0, rows - kh_win)
        k_win = lax.dynamic_slice_in_dim(k_grid, rs, kh_win, axis=2)[:, :, :, col_idx]
        v_win = lax.dynamic_slice_in_dim(v_grid, rs, kh_win, axis=2)[:, :, :, col_idx]
        dr_idx = rs + jnp.arange(kh_win) - r + (WIN_H - 1)
        bias = rel_bias[:, dr_idx[None, :, None], dc_idx[:, None, :]]
        s_loc = jnp.einsum('bhqd,bhiqjd->bhqij', qr, k_win).astype(jnp.float32) * scale + bias.astype(jnp.float32)
        s_ctx = jnp.einsum('bhqd,bhkd->bhqk', qr, kch).astype(jnp.float32) * scale
        s = jnp.concatenate([s_loc.reshape(b, NA_HEADS, GRID_W, n_loc), s_ctx], axis=-1)
        p = jax.nn.softmax(s, axis=-1).astype(vh.dtype)
        p_loc = p[..., :n_loc].reshape(b, NA_HEADS, GRID_W, kh_win, WIN_W)
        return (jnp.einsum('bhqij,bhiqjd->bhqd', p_loc, v_win)
                + jnp.einsum('bhqk,bhkd->bhqd', p[..., n_loc:], vch))

    o = lax.map(row_block, (jnp.arange(rows), q_rows))
    o = jnp.moveaxis(o, 0, 2).reshape(b, NA_HEADS, n, HEAD_DIM)
    out = _merge_heads(o)
    out_c = None
    if need_ctx:
        s = jnp.einsum('bhqd,bhkd->bhqk', qch, kch).astype(jnp.float32) * scale
        p = jax.nn.softmax(s, axis=-1).astype(vch.dtype)
        out_c = _merge_heads(jnp.einsum('bhqk,bhkd->bhqd', p, vch))
    return out, out_c


def _fourier_mixer(f, w):
    b, n, _ = f.shape
    fh = f.astype(jnp.float32).reshape(b, n, FNO_HEADS, FNO_HEAD_W)
    y = jnp.fft.fftn(fh, axes=(1, 3), norm='ortho').real
    return (y.reshape(b, n, GROUP_W) @ w.astype(jnp.float32)).astype(f.dtype)


def _zoh(lam_re, lam_im, log_dt, b_re, b_im):
    lam = lax.complex(lam_re.astype(jnp.float32), lam_im.astype(jnp.float32))
    dt = jnp.exp(log_dt.astype(jnp.float32))[:, None]
    a_bar = jnp.exp(lam * dt)
    b_mat = lax.complex(b_re.astype(jnp.float32), b_im.astype(jnp.float32))
    b_bar = ((a_bar - 1.0) / lam)[..., None] * b_mat
    return a_bar, b_bar


def _diag_scan(u, a_bar, b_bar, h0, reverse):
    n = u.shape[1]
    bu = jnp.einsum('gpc,bngc->bngp', b_bar, u)
    if h0 is not None:
        edge = n - 1 if reverse else 0
        bu = bu.at[:, edge].add(a_bar * h0)
    a = jnp.broadcast_to(a_bar, (1, n) + a_bar.shape)

    def combine(e1, e2):
        a1, b1 = e1
        a2, b2 = e2
        return a1 * a2, a2 * b1 + b2

    _, h = lax.associative_scan(combine, (a, bu), axis=1, reverse=reverse)
    return h


def _glu(y, w_glu, b_glu):
    g = jax.nn.gelu(y)
    return g * jax.nn.sigmoid(g @ w_glu.astype(jnp.float32) + b_glu.astype(jnp.float32))


def _s5_mixer(u, uc, lam_re, lam_im, log_dt, b_re, b_im, c_re, c_im, d_skip, w_glu, b_glu, need_ctx):
    b, n, _ = u.shape
    nc = uc.shape[1]
    uf = u.astype(jnp.float32).reshape(b, n, SSM_GROUPS, SSM_GROUP)
    ucf = uc.astype(jnp.float32).reshape(b, nc, SSM_GROUPS, SSM_GROUP)
    d_g = d_skip.astype(jnp.float32).reshape(SSM_GROUPS, SSM_GROUP)
    y = uf * d_g
    yc = ucf * d_g
    for direction in range(2):
        reverse = direction == 1
        a_bar, b_bar = _zoh(lam_re[direction], lam_im[direction], log_dt[direction], b_re[direction], b_im[direction])
        c_mat = lax.complex(c_re[direction].astype(jnp.float32), c_im[direction].astype(jnp.float32))
        h_ctx = _diag_scan(ucf, a_bar, b_bar, None, reverse)
        h0 = h_ctx[:, 0] if reverse else h_ctx[:, -1]
        h_lat = _diag_scan(uf, a_bar, b_bar, h0, reverse)
        y = y + jnp.einsum('gcp,bngp->bngc', c_mat, h_lat).real
        if need_ctx:
            yc = yc + jnp.einsum('gcp,bngp->bngc', c_mat, h_ctx).real
    out = _glu(y.reshape(b, n, GROUP_W), w_glu, b_glu).astype(u.dtype)
    out_c = None
    if need_ctx:
        out_c = _glu(yc.reshape(b, nc, GROUP_W), w_glu, b_glu).astype(uc.dtype)
    return out, out_c


def _group_norm(o, g):
    b, n, _ = o.shape
    og = _rmsnorm(o.reshape(b, n, N_GROUPS, GROUP_W), g.reshape(N_GROUPS, GROUP_W))
    return og.reshape(b, n, D_MIX)


def _swiglu(h, w1, w3, w2):
    return (jax.nn.silu(h @ w1) * (h @ w3)) @ w2


def _layer(x, xc, c, c_ctx, rope, need_ctx, w_mod, b_mod, g_norm1, w_in, att_q_gain, att_k_gain,
           na_q_gain, na_k_gain, na_rel_bias, w_fourier, ssm_lam_re, ssm_lam_im, ssm_log_dt,
           ssm_b_re, ssm_b_im, ssm_c_re, ssm_c_im, ssm_d, w_glu, b_glu, g_group, w_out,
           g_norm2, w_ff1, w_ff3, w_ff2):
    sh1, sc1, ga1, sh2, sc2, ga2 = [m[:, None, :] for m in jnp.split(jax.nn.silu(c) @ w_mod + b_mod, N_MOD, axis=-1)]
    csh1, csc1, cga1, csh2, csc2, cga2 = jnp.split(jax.nn.silu(c_ctx) @ w_mod + b_mod, N_MOD, axis=-1)
    h = _rmsnorm(x, g_norm1) * (1.0 + sc1) + sh1
    hc = _rmsnorm(xc, g_norm1) * (1.0 + csc1) + csh1
    aq, ak, av, dq, dk, dv, fin, sin_ = _split_proj(h @ w_in)
    caq, cak, cav, cdq, cdk, cdv, cfin, csin = _split_proj(hc @ w_in)
    o_a, oc_a = _gqa_mixer(aq, ak, av, caq, cak, cav, att_q_gain, att_k_gain, rope, need_ctx)
    o_d, oc_d = _na_mixer(dq, dk, dv, cdq, cdk, cdv, na_q_gain, na_k_gain, na_rel_bias, need_ctx)
    o_f = _fourier_mixer(fin, w_fourier)
    o_s, oc_s = _s5_mixer(sin_, csin, ssm_lam_re, ssm_lam_im, ssm_log_dt, ssm_b_re, ssm_b_im,
                          ssm_c_re, ssm_c_im, ssm_d, w_glu, b_glu, need_ctx)
    o = _group_norm(jnp.concatenate([o_a, o_d, o_f, o_s.astype(o_a.dtype)], axis=-1), g_group)
    x = x + ga1 * (o @ w_out)
    h2 = _rmsnorm(x, g_norm2) * (1.0 + sc2) + sh2
    x = x + ga2 * _swiglu(h2, w_ff1, w_ff3, w_ff2)
    if need_ctx:
        oc_f = _fourier_mixer(cfin, w_fourier)
        oc = _group_norm(jnp.concatenate([oc_a, oc_d, oc_f, oc_s.astype(oc_a.dtype)], axis=-1), g_group)
        xc = xc + cga1 * (oc @ w_out)
        hc2 = _rmsnorm(xc, g_norm2) * (1.0 + csc2) + csh2
        xc = xc + cga2 * _swiglu(hc2, w_ff1, w_ff3, w_ff2)
    return x, xc


def setup_inputs(seed: int = 0) -> dict:
    key = jax.random.key(seed)
    ks = jax.random.split(key, 32)
    f32 = jnp.float32
    nrm = lambda k, s, sc: jax.random.normal(k, s, f32) * sc
    gain = lambda k, s: 1.0 + 0.02 * jax.random.normal(k, s, f32)
    lam_im0 = math.pi * jnp.arange(SSM_STATE, dtype=f32)
    return {
        'x': nrm(ks[0], (BATCH, SEQ, D_MODEL), 1.0),
        'c': nrm(ks[1], (BATCH, D_MODEL), 1.0),
        'ctx': nrm(ks[2], (BATCH, CTX_LEN, D_MODEL), 1.0),
        'c_ctx': nrm(ks[3], (D_MODEL,), 1.0),
        'w_mod': nrm(ks[4], (DEPTH, D_MODEL, N_MOD * D_MODEL), 0.5 * D_MODEL ** -0.5),
        'b_mod': nrm(ks[5], (DEPTH, N_MOD * D_MODEL), 0.02),
        'g_norm1': gain(ks[6], (DEPTH, D_MODEL)),
        'w_in': nrm(ks[7], (DEPTH, D_MODEL, IN_W), D_MODEL ** -0.5),
        'att_q_gain': gain(ks[8], (DEPTH, HEAD_DIM)),
        'att_k_gain': gain(ks[9], (DEPTH, HEAD_DIM)),
        'na_q_gain': gain(ks[10], (DEPTH, HEAD_DIM)),
        'na_k_gain': gain(ks[11], (DEPTH, HEAD_DIM)),
        'na_rel_bias': nrm(ks[12], (DEPTH, NA_HEADS, 2 * WIN_H - 1, 2 * WIN_W - 1), 0.02),
        'w_fourier': nrm(ks[13], (DEPTH, GROUP_W, GROUP_W), GROUP_W ** -0.5),
        'ssm_lam_re': -0.5 + 0.01 * jax.random.normal(ks[14], (DEPTH, 2, SSM_GROUPS, SSM_STATE), f32),
        'ssm_lam_im': lam_im0 + 0.01 * jax.random.normal(ks[15], (DEPTH, 2, SSM_GROUPS, SSM_STATE), f32),
        'ssm_log_dt': jax.random.uniform(ks[16], (DEPTH, 2, SSM_GROUPS), f32, minval=math.log(1e-3), maxval=math.log(1e-1)),
        'ssm_b_re': nrm(ks[17], (DEPTH, 2, SSM_GROUPS, SSM_STATE, SSM_GROUP), (2 * SSM_GROUP) ** -0.5),
        'ssm_b_im': nrm(ks[18], (DEPTH, 2, SSM_GROUPS, SSM_STATE, SSM_GROUP), (2 * SSM_GROUP) ** -0.5),
        'ssm_c_re': nrm(ks[19], (DEPTH, 2, SSM_GROUPS, SSM_GROUP, SSM_STATE), (2 * SSM_STATE) ** -0.5),
        'ssm_c_im': nrm(ks[20], (DEPTH, 2, SSM_GROUPS, SSM_GROUP, SSM_STATE), (2 * SSM_STATE) ** -0.5),
        'ssm_d': nrm(ks[21], (DEPTH, GROUP_W), 1.0),
        'w_glu': nrm(ks[22], (DEPTH, GROUP_W, GROUP_W), GROUP_W ** -0.5),
        'b_glu': nrm(ks[23], (DEPTH, GROUP_W), 0.02),
        'g_group': gain(ks[24], (DEPTH, D_MIX)),
        'w_out': nrm(ks[25], (DEPTH, D_MIX, D_MODEL), D_MIX ** -0.5),
        'g_norm2': gain(ks[26], (DEPTH, D_MODEL)),
        'w_ff1': nrm(ks[27], (DEPTH, D_MODEL, D_FF), D_MODEL ** -0.5),
        'w_ff3': nrm(ks[28], (DEPTH, D_MODEL, D_FF), D_MODEL ** -0.5),
        'w_ff2': nrm(ks[29], (DEPTH, D_FF, D_MODEL), D_FF ** -0.5),
    }


def reference(x, c, ctx, c_ctx, w_mod, b_mod, g_norm1, w_in, att_q_gain, att_k_gain, na_q_gain,
              na_k_gain, na_rel_bias, w_fourier, ssm_lam_re, ssm_lam_im, ssm_log_dt, ssm_b_re,
              ssm_b_im, ssm_c_re, ssm_c_im, ssm_d, w_glu, b_glu, g_group, w_out, g_norm2,
              w_ff1, w_ff3, w_ff2):
    rope = _rope_tables(x.shape[1])
    xc = ctx
    for i in range(DEPTH):
        x, xc = _layer(x, xc, c, c_ctx, rope, i < DEPTH - 1, w_mod[i], b_mod[i], g_norm1[i], w_in[i],
                       att_q_gain[i], att_k_gain[i], na_q_gain[i], na_k_gain[i], na_rel_bias[i],
                       w_fourier[i], ssm_lam_re[i], ssm_lam_im[i], ssm_log_dt[i], ssm_b_re[i],
                       ssm_b_im[i], ssm_c_re[i], ssm_c_im[i], ssm_d[i], w_glu[i], b_glu[i],
                       g_group[i], w_out[i], g_norm2[i], w_ff1[i], w_ff3[i], w_ff2[i])
    return x
```

```python
import contextlib
import math
import numpy as np
import ml_dtypes
import concourse.bass as bass
import concourse.mybir as mybir
from concourse.bass_utils import run_bass_kernel_spmd

F32 = mybir.dt.float32
BF16 = mybir.dt.bfloat16
AF = mybir.ActivationFunctionType
ALU = mybir.AluOpType
AX = mybir.AxisListType

NCORES = 8
NB = 2
D = 1024
SEQ = 2048
CTXL = 256
T = SEQ + CTXL
NT = T // 128
DFF = 2816
NF = DFF // 128
EPS = 1e-6
NCH = T // 8
BIG = -30000.0
DMA_K = 8
COMPUTE = ("pe", "act", "dve", "pool")


class Buf:
    __slots__ = ("t", "name", "writers", "readers")

    def __init__(self, ap, name=""):
        self.t = ap
        self.name = name
        self.writers = {}
        self.readers = {}

    def __getitem__(self, idx):
        return View(self, self.t[idx])

    def ap(self):
        return View(self, self.t)

    def v(self, pattern=None, **kw):
        if pattern is None:
            return View(self, self.t)
        return View(self, self.t.rearrange(pattern, **kw))


class View:
    __slots__ = ("buf", "ap")

    def __init__(self, buf, ap):
        self.buf = buf
        self.ap = ap

    def __getitem__(self, idx):
        return View(self.buf, self.ap[idx])

    def rearrange(self, *a, **k):
        return View(self.buf, self.ap.rearrange(*a, **k))

    def bitcast(self, dt):
        return View(self.buf, self.ap.bitcast(dt))

    def bc(self, shape):
        return View(self.buf, self.ap.to_broadcast(list(shape)))

    def unsq(self, i):
        return View(self.buf, self.ap.unsqueeze(i))


class FW:
    def __init__(self, nc):
        self.nc = nc
        self.stack = contextlib.ExitStack()
        self.streams = {e: [] for e in ("pe", "act", "dve", "pool", "sp")}
        self.seq = {e: 0 for e in COMPUTE}
        self.sems = {}
        self.waited = {e: {} for e in self.streams}
        self.dma_count = {"sp": 0, "pool": 0, "act": 0}
        self.dma_last = {}
        self.n_inst = 0

    def sem(self, key):
        if key not in self.sems:
            name = "s_" + ("_".join(str(k) for k in key) if isinstance(key, tuple) else str(key))
            self.sems[key] = self.stack.enter_context(self.nc.semaphore(name))
        return self.sems[key]

    def sbuf(self, name, shape, dtype):
        t = self.stack.enter_context(self.nc.sbuf_tensor(name, list(shape), dtype))
        return Buf(t[:], name)

    def psum(self, name, shape, dtype=F32):
        t = self.stack.enter_context(self.nc.psum_tensor(name, list(shape), dtype))
        return Buf(t[:], name)

    def dram(self, name, shape, dtype, kind="Internal"):
        t = self.nc.dram_tensor(name, list(shape), dtype, kind=kind)
        return Buf(t.ap(), name)

    def _need(self, eng, waits, key, val):
        if self.waited[eng].get(key, 0) >= val:
            return
        if val > waits.get(key, 0):
            waits[key] = val

    def _deps(self, eng, reads, writes):
        waits = {}
        for v in reads:
            for key, val in v.buf.writers.items():
                if key == eng and eng == "pe":
                    continue
                self._need(eng, waits, key, val)
        for v in writes:
            for key, val in v.buf.writers.items():
                if key == eng:
                    continue
                self._need(eng, waits, key, val)
            for key, val in v.buf.readers.items():
                if key == eng:
                    continue
                self._need(eng, waits, key, val)
        for key, val in waits.items():
            self.waited[eng][key] = val
        return list(waits.items())

    def _mark(self, token, reads, writes):
        key, val = token
        for v in reads:
            v.buf.readers[key] = val
        for v in writes:
            b = v.buf
            b.writers = {key: val}
            b.readers = {}

    def op(self, eng, fn, reads=(), writes=()):
        reads = [r for r in reads if isinstance(r, View)]
        writes = [w for w in writes if isinstance(w, View)]
        waits = self._deps(eng, reads, writes)
        self.seq[eng] += 1
        token = (eng, self.seq[eng])
        self._mark(token, reads, writes)
        self.streams[eng].append((waits, fn, (eng, 1)))
        self.n_inst += 1
        return token

    def dma(self, out, in_, q="sp", **kw):
        i = self.dma_count[q]
        self.dma_count[q] += 1
        r, rnd = i % DMA_K, i // DMA_K
        key = ("dma", q, r)
        waits = {}
        if rnd > 0:
            self._need(q, waits, key, 16 * rnd)
        for k2, v2 in waits.items():
            self.waited[q][k2] = v2
        w2 = self._deps(q, [in_], [out])
        allw = list(waits.items()) + w2
        token = (key, 16 * (rnd + 1))
        self.dma_last[key] = 16 * (rnd + 1)
        self._mark(token, [in_], [out])
        oa, ia = out.ap, in_.ap
        self.streams[q].append((allw, lambda e, oa=oa, ia=ia, kw=kw: e.dma_start(out=oa, in_=ia, **kw), (key, 16)))
        self.n_inst += 1
        return token

    def barrier(self):
        toks = [(e, self.seq[e]) for e in COMPUTE if self.seq[e] > 0]
        toks += list(self.dma_last.items())
        for eng in self.streams:
            waits = {}
            for key, val in toks:
                if key == eng:
                    continue
                self._need(eng, waits, key, val)
            for k2, v2 in waits.items():
                self.waited[eng][k2] = v2
            if waits:
                self.streams[eng].append((list(waits.items()), None, None))

    def emit(self):
        nc = self.nc
        for e in COMPUTE:
            self.sem(e)
        for q in self.dma_count:
            for r in range(DMA_K):
                self.sem(("dma", q, r))
        engmap = {"pe": "tensor", "act": "scalar", "dve": "vector", "pool": "gpsimd", "sp": "sync"}
        with nc.Block() as block:
            for e, attr in engmap.items():
                stream = self.streams[e]

                def body(engine, stream=stream):
                    for waits, fn, inc in stream:
                        for key, val in waits:
                            engine.wait_ge(self.sems[key], val)
                        if fn is None:
                            continue
                        ins = fn(engine)
                        if inc is not None:
                            ins.then_inc(self.sems[inc[0]], inc[1])

                getattr(block, attr)(body)
        self.stack.close()

    def matmul(self, out, lhsT, rhs, start=True, stop=True, **kw):
        oa, la, ra = out.ap, lhsT.ap, rhs.ap
        return self.op("pe", lambda e: e.matmul(oa, la, ra, start=start, stop=stop, **kw),
                       reads=[lhsT, rhs], writes=[out])

    def transpose(self, out, in_, ident):
        oa, ia, da = out.ap, in_.ap, ident.ap
        return self.op("pe", lambda e: e.transpose(oa, ia, da), reads=[in_, ident], writes=[out])

    def act(self, out, in_, func, bias=None, scale=None, accum_out=None):
        oa, ia = out.ap, in_.ap
        kw = {}
        reads = [in_]
        writes = [out]
        if bias is not None:
            if isinstance(bias, View):
                kw["bias"] = bias.ap
                reads.append(bias)
            else:
                kw["bias"] = bias
        if scale is not None:
            if isinstance(scale, View):
                kw["scale"] = scale.ap
                reads.append(scale)
            else:
                kw["scale"] = scale
        if accum_out is not None:
            kw["accum_out"] = accum_out.ap
            writes.append(accum_out)
        return self.op("act", lambda e: e.activation(oa, ia, func, **kw), reads=reads, writes=writes)

    def tt(self, out, in0, in1, op, eng="dve"):
        oa, a, b = out.ap, in0.ap, in1.ap
        return self.op(eng, lambda e: e.tensor_tensor(oa, a, b, op), reads=[in0, in1], writes=[out])

    def ts(self, out, in0, s1, s2, op0, op1=None, eng="dve"):
        oa, a = out.ap, in0.ap
        reads = [in0]
        s1a = s1.ap if isinstance(s1, View) else s1
        s2a = s2.ap if isinstance(s2, View) else s2
        if isinstance(s1, View):
            reads.append(s1)
        if isinstance(s2, View):
            reads.append(s2)
        kw = {}
        if op1 is not None:
            kw["op1"] = op1
        return self.op(eng, lambda e: e.tensor_scalar(oa, a, s1a, s2a, op0, **kw), reads=reads, writes=[out])

    def stt(self, out, in0, scalar, in1, op0, op1, eng="dve"):
        oa, a, b = out.ap, in0.ap, in1.ap
        reads = [in0, in1]
        sa = scalar.ap if isinstance(scalar, View) else scalar
        if isinstance(scalar, View):
            reads.append(scalar)
        eng = "dve"
        return self.op(eng, lambda e: e.scalar_tensor_tensor(oa, a, sa, b, op0, op1), reads=reads, writes=[out])

    def copy(self, out, in_, eng="dve"):
        oa, ia = out.ap, in_.ap
        if eng == "act":
            return self.op(eng, lambda e: e.copy(oa, ia), reads=[in_], writes=[out])
        return self.op(eng, lambda e: e.tensor_copy(oa, ia), reads=[in_], writes=[out])

    def memset(self, out, val, eng="pool"):
        oa = out.ap
        return self.op(eng, lambda e: e.memset(oa, val), reads=[], writes=[out])

    def reduce(self, out, in_, op, axis=AX.X, eng="dve"):
        oa, ia = out.ap, in_.ap
        return self.op(eng, lambda e: e.tensor_reduce(oa, ia, axis, op), reads=[in_], writes=[out])

    def recip(self, out, in_):
        oa, ia = out.ap, in_.ap
        return self.op("dve", lambda e: e.reciprocal(oa, ia), reads=[in_], writes=[out])

    def scan(self, out, d0, d1, initial, op0=ALU.mult, op1=ALU.add):
        oa, a, b = out.ap, d0.ap, d1.ap
        reads = [d0, d1]
        ia = initial.ap if isinstance(initial, View) else initial
        if isinstance(initial, View):
            reads.append(initial)
        return self.op("dve", lambda e: e.tensor_tensor_scan(oa, a, b, ia, op0, op1), reads=reads, writes=[out])


class Arena:
    def __init__(self, fw, words):
        self.fw = fw
        self.words = words
        self.base = fw.sbuf("arena", [128, words], F32)
        self.off = 0
        self.n = 0

    def alloc(self, nelem, dtype=F32, name=None):
        if dtype == BF16:
            w = (nelem + 1) // 2
        else:
            w = nelem
        w = (w + 7) // 8 * 8
        assert self.off + w <= self.words, "arena overflow %d + %d > %d" % (self.off, w, self.words)
        ap = self.base.t[:, self.off:self.off + w]
        if dtype == BF16:
            ap = ap.bitcast(BF16)[:, 0:nelem]
        else:
            ap = ap[:, 0:nelem]
        self.off += w
        self.n += 1
        return Buf(ap, name or ("a%d" % self.n))

    def reset(self):
        self.fw.barrier()
        self.off = 0


def _bf16(a):
    return np.asarray(a, dtype=np.float32).astype(ml_dtypes.bfloat16)


def host_constants():
    k = {}
    k["k_ident_bf"] = _bf16(np.eye(128))
    k["k_ident_f"] = np.eye(128, dtype=np.float32)
    t = np.arange(SEQ)
    rows = (t // 64).astype(np.float32)
    cols = (t % 64).astype(np.float32)
    inv = (np.float32(10000.0) ** (-np.arange(0, 32, 2, dtype=np.float32) / np.float32(32))).astype(np.float32)
    ar = (rows[:, None] * inv[None, :]).astype(np.float32)
    ac = (cols[:, None] * inv[None, :]).astype(np.float32)
    cr, sr, cc, sc = np.cos(ar), np.sin(ar), np.cos(ac), np.sin(ac)
    cos_t = np.concatenate([cr, cr, cc, cc], axis=1).astype(np.float32)
    sin_t = np.concatenate([-sr, sr, -sc, sc], axis=1).astype(np.float32)
    k["k_rope_cos"] = cos_t.reshape(16, 128, 64)
    k["k_rope_sin"] = sin_t.reshape(16, 128, 64)
    n = np.arange(SEQ, dtype=np.int64)
    nk = (n[:, None] * n[None, :]) % SEQ
    ang = nk.astype(np.float64) * (2.0 * np.pi / SEQ)
    k["k_dft"] = np.stack([_bf16(np.cos(ang)), _bf16(np.sin(ang))])
    m = np.arange(64, dtype=np.int64)
    a64 = ((m[:, None] * m[None, :]) % 64).astype(np.float64) * (2.0 * np.pi / 64)
    cb = np.zeros((256, 256), np.float32)
    sb = np.zeros((256, 256), np.float32)
    for h in range(4):
        cb[h * 64:(h + 1) * 64, h * 64:(h + 1) * 64] = np.cos(a64)
        sb[h * 64:(h + 1) * 64, h * 64:(h + 1) * 64] = np.sin(a64)
    k["k_cblk"] = cb
    k["k_sblk"] = sb
    s_idx = np.arange(128) // 16
    k["k_s5mask"] = (s_idx[None, :] >= s_idx[:, None]).astype(np.float32)
    k["k_tau"] = np.tile(np.arange(-7, 9, dtype=np.float32)[None, :], (128, 1))
    k["k_nidx"] = np.tile(np.arange(NCH, dtype=np.float32)[None, :], (128, 1))
    qc = np.arange(64)
    cs = np.clip(qc - 8, 0, 48)
    kc = np.arange(64)
    ok = (kc[:, None] >= cs[None, :]) & (kc[:, None] < cs[None, :] + 16)
    blk = np.where(ok, 0.0, BIG).astype(np.float32)
    k["k_colmask"] = np.tile(blk, (2, 8))
    return k


def na_plan():
    pats = {}
    plan = []
    for t in range(16):
        lst = []
        for u in range(16):
            pat = []
            anyv = False
            for krl in range(2):
                for qrl in range(2):
                    kr, qr = 2 * u + krl, 2 * t + qrl
                    rs = min(max(qr - 4, 0), 24)
                    if rs <= kr < rs + 8:
                        pat.append(kr - qr + 7)
                        anyv = True
                    else:
                        pat.append(None)
            if anyv:
                pat = tuple(pat)
                if pat not in pats:
                    pats[pat] = len(pats)
                lst.append((u, pats[pat]))
        plan.append(lst)
    plist = [None] * len(pats)
    for p, i in pats.items():
        plist[i] = p
    return plan, plist


def build(stages=None, dbg=(), layers=(0, 1), batches=(0, 1)):
    nc = bass.Bass("TRN2", target_bir_lowering=False)
    fw = FW(nc)
    dbg = set(dbg)

    def ein(name, shape, dt=F32):
        return fw.dram(name, shape, dt, kind="ExternalInput")

    def scratch(name, shape, dt):
        return fw.dram(name, shape, dt, kind=("ExternalOutput" if name in dbg else "Internal"))

    x = ein("x", [NB, SEQ, D])
    ctx = ein("ctx", [NB, CTXL, D])
    c = ein("c", [NB, D])
    c_ctx = ein("c_ctx", [1, D])
    w_mod = ein("w_mod", [2, D, 6 * D])
    b_mod = ein("b_mod", [2, 6 * D])
    g_norm1 = ein("g_norm1", [2, D])
    w_in = ein("w_in", [2, D, 1792])
    att_q_gain = ein("att_q_gain", [2, 64])
    att_k_gain = ein("att_k_gain", [2, 64])
    na_q_gain = ein("na_q_gain", [2, 64])
    na_k_gain = ein("na_k_gain", [2, 64])
    na_rel_bias = ein("na_rel_bias", [2, 60, 31])
    w_fourier = ein("w_fourier", [2, 256, 256])
    ssm_lam_re = ein("ssm_lam_re", [2, 2, 16, 64])
    ssm_lam_im = ein("ssm_lam_im", [2, 2, 16, 64])
    ssm_log_dt = ein("ssm_log_dt", [2, 2, 16])
    ssm_b_re = ein("ssm_b_re", [2, 2, 16, 64, 16])
    ssm_b_im = ein("ssm_b_im", [2, 2, 16, 64, 16])
    ssm_c_re = ein("ssm_c_re", [2, 2, 16, 16, 64])
    ssm_c_im = ein("ssm_c_im", [2, 2, 16, 16, 64])
    ssm_d = ein("ssm_d", [2, 256])
    w_glu = ein("w_glu", [2, 256, 256])
    b_glu = ein("b_glu", [2, 256])
    g_group = ein("g_group", [2, D])
    w_out = ein("w_out", [2, D, D])
    g_norm2 = ein("g_norm2", [2, D])
    w_ff1 = ein("w_ff1", [2, D, DFF])
    w_ff3 = ein("w_ff3", [2, D, DFF])
    w_ff2 = ein("w_ff2", [2, DFF, D])
    k_ident_bf = ein("k_ident_bf", [128, 128], BF16)
    k_ident_f = ein("k_ident_f", [128, 128])
    k_rope_cos = ein("k_rope_cos", [16, 128, 64])
    k_rope_sin = ein("k_rope_sin", [16, 128, 64])
    k_dft = ein("k_dft", [2, SEQ, SEQ], BF16)
    k_cblk = ein("k_cblk", [256, 256])
    k_sblk = ein("k_sblk", [256, 256])
    k_s5mask = ein("k_s5mask", [128, 128])
    k_tau = ein("k_tau", [128, 16])
    k_nidx = ein("k_nidx", [128, NCH])
    k_colmask = ein("k_colmask", [128, 512])

    out = fw.dram("out", [NB, SEQ, D], F32, kind="ExternalOutput")

    XR = scratch("XR", [NB, T, D], F32)
    MODV = scratch("MODV", [2, 3, 6 * D], F32)
    WIN = scratch("WIN", [2, 128, 8, 1792], BF16)
    WOUT = scratch("WOUT", [2, 128, 8, D], BF16)
    W1 = scratch("W1", [2, NF, 128, 8, 128], BF16)
    W3 = scratch("W3", [2, NF, 128, 8, 128], BF16)
    W2 = scratch("W2", [2, DFF, D], BF16)
    QK = scratch("QK", [NB, 896, T], BF16)
    VV = scratch("VV", [NB, T, 390], BF16)
    FIN = scratch("FIN", [NB, T, 256], BF16)
    UT = scratch("UT", [NB, 256, T], BF16)
    OO = scratch("OO", [NB, T, D], BF16)
    PADT = scratch("PADT", [2, 61, 8192], F32)
    BIAS = scratch("BIAS", [2, 32, 128, 512], BF16)
    WCS = scratch("WCS", [2, 2, 256, 256], BF16)
    SCRU = scratch("SCRU", [2, 16, 8, 16, NCH], BF16)
    SCRY = scratch("SCRY", [2, 16, 8, 16, NCH], BF16)
    S5T = scratch("S5T", [2, 2, 16, 128, 128], BF16)
    S5B = scratch("S5B", [2, 2, 16, 128, 2, 64], BF16)
    S5C = scratch("S5C", [2, 2, 8, 128, 2, 128], BF16)
    S5D = scratch("S5D", [2, 2, 128, 8], F32)
    S5COS = scratch("S5COS", [2, 2, 128, 8, NCH], F32)
    S5SIN = scratch("S5SIN", [2, 2, 128, 8, NCH], F32)

    ar = Arena(fw, 47 * 1024)
    banks = [fw.psum("bank%d" % i, [128, 512], F32) for i in range(8)]

    all_stages = ["W", "M", "A", "GQA", "NAB", "NA", "FW", "F", "S5P", "S5", "C"]
    if stages is None:
        stages = all_stages
    stages = set(stages)

    def bank_bf(i):
        return View(banks[i], banks[i].t.bitcast(BF16))

    def rstd_from(ss, n, rs, tmp):
        fw.act(tmp, ss, AF.Sqrt, scale=1.0 / n, bias=EPS)
        fw.recip(rs, tmp)

    def stage_W():
        for l in range(2):
            for half in range(2):
                sl = slice(half * 896, (half + 1) * 896)
                fw.dma(WIN[l][:, :, sl], w_in[l].rearrange("(kc p) n -> p kc n", p=128)[:, :, sl], q="pool")
            fw.dma(WOUT[l], w_out[l].rearrange("(kc p) n -> p kc n", p=128), q="pool")
            for f in range(NF):
                fw.dma(W1[l][f], w_ff1[l][:, f * 128:(f + 1) * 128].rearrange("(kc p) n -> p kc n", p=128), q="pool")
                fw.dma(W3[l][f], w_ff3[l][:, f * 128:(f + 1) * 128].rearrange("(kc p) n -> p kc n", p=128), q="pool")
            for f0 in range(0, DFF, 704):
                fw.dma(W2[l][f0:f0 + 704, :], w_ff2[l][f0:f0 + 704, :], q="pool")

    def stage_M(l):
        ar.reset()
        cT = ar.alloc(24)
        cT3 = cT.v("p (k r) -> p k r", r=3)
        for r in range(2):
            fw.dma(cT3[:, :, r], c[r].rearrange("(kc p) -> p kc", p=128), allow_slow_non_contiguous=True)
        fw.dma(cT3[:, :, 2], c_ctx[0].rearrange("(kc p) -> p kc", p=128), allow_slow_non_contiguous=True)
        cS = ar.alloc(24)
        fw.act(cS.ap(), cT.ap(), AF.Silu)
        cS3 = cS.v("p (k r) -> p k r", r=3)
        bm = ar.alloc(6 * D)
        fw.dma(bm[0:3, :], b_mod[l:l + 1, :].bc([3, 6 * D]))
        g1b = ar.alloc(D)
        g2b = ar.alloc(D)
        fw.dma(g1b[0:3, :], g_norm1[l:l + 1, :].bc([3, D]))
        fw.dma(g2b[0:3, :], g_norm2[l:l + 1, :].bc([3, D]))
        mv = ar.alloc(6 * D)
        wbuf = [ar.alloc(8 * 512) for _ in range(2)]
        for nb in range(12):
            wt = wbuf[nb % 2].v("p (k n) -> p k n", k=8)
            fw.dma(wt, w_mod[l][:, nb * 512:(nb + 1) * 512].rearrange("(kc p) n -> p kc n", p=128))
            ps = banks[nb % 2]
            for kc in range(8):
                fw.matmul(ps[0:3, :], cS3[:, kc, :], wt[:, kc, :], start=(kc == 0), stop=(kc == 7))
            fw.tt(mv[0:3, nb * 512:(nb + 1) * 512], ps[0:3, :], bm[0:3, nb * 512:(nb + 1) * 512], ALU.add)
        fw.stt(mv[0:3, D:2 * D], mv[0:3, D:2 * D], 1.0, g1b[0:3, :], ALU.add, ALU.mult)
        fw.stt(mv[0:3, 4 * D:5 * D], mv[0:3, 4 * D:5 * D], 1.0, g2b[0:3, :], ALU.add, ALU.mult)
        fw.dma(MODV[l], mv[0:3, :])

    def modrow(l, r, idx):
        return MODV[l][r:r + 1, idx * D:(idx + 1) * D].bc([128, D])

    def load_consts_ident():
        idb = ar.alloc(128, BF16)
        fw.dma(idb.ap(), k_ident_bf.ap())
        return idb

    def stage_A(b, l):
        ar.reset()
        idb = load_consts_ident()
        win = ar.alloc(8 * 1792, BF16)
        win3 = win.v("p (k n) -> p k n", k=8)
        fw.dma(win3, WIN[l])
        A1 = ar.alloc(D); B1 = ar.alloc(D); A1c = ar.alloc(D); B1c = ar.alloc(D)
        fw.dma(A1.ap(), modrow(l, b, 1)); fw.dma(B1.ap(), modrow(l, b, 0))
        fw.dma(A1c.ap(), modrow(l, 2, 1)); fw.dma(B1c.ap(), modrow(l, 2, 0))
        G = ar.alloc(14 * 64)
        G3 = G.v("p (h d) -> p h d", d=64)
        gsrc = [att_q_gain] * 4 + [att_k_gain] * 2 + [na_q_gain] * 4 + [na_k_gain] * 4
        for h in range(14):
            fw.dma(G3[:, h, :], gsrc[h][l:l + 1, :].bc([128, 64]))
        COS = ar.alloc(16 * 64); SIN = ar.alloc(16 * 64)
        COS3 = COS.v("p (t d) -> p t d", d=64); SIN3 = SIN.v("p (t d) -> p t d", d=64)
        fw.dma(COS3, k_rope_cos.v("t p d -> p t d"))
        fw.dma(SIN3, k_rope_sin.v("t p d -> p t d"))
        xs = [ar.alloc(D) for _ in range(2)]
        sqj = ar.alloc(D, BF16)
        ss = ar.alloc(4); sd = ar.alloc(4); rstd = ar.alloc(4)
        tt_ = ar.alloc(D)
        hb = [ar.alloc(D, BF16) for _ in range(2)]
        hT = [ar.alloc(D, BF16) for _ in range(2)]
        sq2 = ar.alloc(896)
        ssh = ar.alloc(16); sdh = ar.alloc(16); rsh = ar.alloc(16)
        qkn = ar.alloc(896)
        vsw = ar.alloc(384); m1 = ar.alloc(384); m2 = ar.alloc(384)
        qkb = [ar.alloc(896, BF16) for _ in range(2)]
        qkT = [ar.alloc(896, BF16) for _ in range(2)]
        vaug = [ar.alloc(6 * 65, BF16) for _ in range(2)]
        finb = [ar.alloc(256, BF16) for _ in range(2)]
        uTb = [ar.alloc(256, BF16) for _ in range(2)]
        for vb in vaug:
            fw.memset(vb.v("p (h e) -> p h e", e=65)[:, :, 64:65], 1.0)
        for i in range(NT):
            j = i % 2
            isctx = i < 2
            if l == 0:
                src = ctx[b][i * 128:(i + 1) * 128, :] if isctx else x[b][(i - 2) * 128:(i - 1) * 128, :]
            else:
                src = XR[b][i * 128:(i + 1) * 128, :]
            fw.dma(xs[j].ap(), src)
            fw.act(sqj.ap(), xs[j].ap(), AF.Square, accum_out=ss[:, 0:1])
            rstd_from(ss[:, 0:1], D, rstd[:, 0:1], sd[:, 0:1])
            fw.stt(tt_.ap(), xs[j].ap(), rstd[:, 0:1], (A1c if isctx else A1).ap(), ALU.mult, ALU.mult)
            fw.tt(hb[j].ap(), tt_.ap(), (B1c if isctx else B1).ap(), ALU.add, eng="pool")
            pT = bank_bf(4)
            for k in range(8):
                fw.transpose(pT[:, k * 128:(k + 1) * 128], hb[j][:, k * 128:(k + 1) * 128], idb.ap())
            fw.copy(hT[j].ap(), pT[:, 0:D], eng="act")
            hT3 = hT[j].v("p (k n) -> p k n", k=8)
            for nb in range(3):
                for k in range(8):
                    fw.matmul(banks[nb].ap(), hT3[:, k, :], win3[:, k, nb * 512:(nb + 1) * 512], start=(k == 0), stop=(k == 7))
            for m in range(2):
                for k in range(8):
                    fw.matmul(banks[3][:, m * 128:(m + 1) * 128], win3[:, k, 1536 + m * 128:1536 + (m + 1) * 128], hT3[:, k, :],
                              start=(k == 0), stop=(k == 7))
            fw.act(sq2[:, 0:384], banks[0][:, 0:384], AF.Square)
            fw.act(sq2[:, 384:896], banks[1][:, 0:512], AF.Square)
            fw.reduce(ssh[:, 0:14], sq2.v("p (h d) -> p h d", d=64), ALU.add)
            rstd_from(ssh[:, 0:14], 64, rsh[:, 0:14], sdh[:, 0:14])
            qkn3 = qkn.v("p (h d) -> p h d", d=64)
            fw.tt(qkn3[:, 0:6, :], banks[0][:, 0:384].rearrange("p (h d) -> p h d", d=64),
                  rsh[:, 0:6].unsq(2).bc([128, 6, 64]), ALU.mult)
            fw.tt(qkn3[:, 6:14, :], banks[1][:, 0:512].rearrange("p (h d) -> p h d", d=64),
                  rsh[:, 6:14].unsq(2).bc([128, 8, 64]), ALU.mult)
            fw.tt(qkn.ap(), qkn.ap(), G.ap(), ALU.mult, eng="pool")
            if isctx:
                fw.copy(qkb[j][:, 0:384], qkn[:, 0:384], eng="pool")
            else:
                ti = i - 2
                fw.copy(vsw.v("p (a two x) -> p a two x", two=2, x=16),
                        qkn[:, 0:384].rearrange("p (a two x) -> p a two x", two=2, x=16)[:, :, ::-1, :], eng="pool")
                fw.tt(m1.v("p (h d) -> p h d", d=64), qkn3[:, 0:6, :], COS3[:, ti, :].unsq(1).bc([128, 6, 64]), ALU.mult)
                fw.tt(m2.v("p (h d) -> p h d", d=64), vsw.v("p (h d) -> p h d", d=64),
                      SIN3[:, ti, :].unsq(1).bc([128, 6, 64]), ALU.mult, eng="pool")
                fw.tt(qkb[j][:, 0:384], m1.ap(), m2.ap(), ALU.add)
            fw.copy(qkb[j][:, 384:896], qkn[:, 384:896], eng="pool")
            pq = bank_bf(5)
            for jj in range(7):
                fw.transpose(pq[:, jj * 128:(jj + 1) * 128], qkb[j][:, jj * 128:(jj + 1) * 128], idb.ap())
            fw.copy(qkT[j].ap(), pq[:, 0:896], eng="act")
            fw.dma(QK[b].rearrange("(j p) n -> p j n", p=128)[:, :, i * 128:(i + 1) * 128],
                   qkT[j].v("p (j n) -> p j n", j=7))
            va3 = vaug[j].v("p (h e) -> p h e", e=65)
            fw.copy(va3[:, 0:2, 0:64], banks[0][:, 384:512].rearrange("p (h d) -> p h d", d=64), eng="dve")
            fw.copy(va3[:, 2:6, 0:64], banks[2][:, 0:256].rearrange("p (h d) -> p h d", d=64), eng="dve")
            fw.dma(VV[b][i * 128:(i + 1) * 128, :], vaug[j].ap())
            fw.copy(finb[j].ap(), banks[2][:, 256:512], eng="act")
            fw.dma(FIN[b][i * 128:(i + 1) * 128, :], finb[j].ap())
            fw.copy(uTb[j].ap(), banks[3][:, 0:256], eng="dve")
            fw.dma(UT[b].rearrange("(m p) n -> p m n", p=128)[:, :, i * 128:(i + 1) * 128],
                   uTb[j].v("p (m n) -> p m n", m=2))

    def attn_block(QT3, KT3, V3, qhead, khead, vcol, qcol0, nq, ktiles, PT, psS, psO, st):
        nsub = nq // 128
        nk = len(ktiles)
        for ki, kt in enumerate(ktiles):
            pS = psS[st["s"] % len(psS)]
            st["s"] += 1
            pt = PT[st["p"] % len(PT)]
            st["p"] += 1
            fw.matmul(pS[:, 0:nq], KT3[0:64, khead, kt * 128:(kt + 1) * 128], QT3[0:64, qhead, qcol0:qcol0 + nq])
            fw.act(pt[:, 0:nq], pS[:, 0:nq], AF.Exp, scale=0.125)
            for sub in range(nsub):
                fw.matmul(psO[:, sub * 65:(sub + 1) * 65], pt[:, sub * 128:(sub + 1) * 128], V3[:, kt, vcol:vcol + 65],
                          start=(ki == 0 and sub == 0), stop=(ki == nk - 1 and sub == nsub - 1))

    def attn_finish(psO, nsub, rden, ob):
        po3 = psO[:, 0:nsub * 65].rearrange("p (s e) -> p s e", e=65)
        fw.recip(rden[:, 0:nsub], po3[:, :, 64:65].rearrange("p s e -> p (s e)"))
        fw.tt(ob.v("p (s d) -> p s d", d=64)[:, 0:nsub, :], po3[:, :, 0:64],
              rden[:, 0:nsub].unsq(2).bc([128, nsub, 64]), ALU.mult)

    def stage_GQA(b, l):
        ar.reset()
        QT = ar.alloc(4 * T, BF16); KT = ar.alloc(2 * T, BF16); VA = ar.alloc(NT * 130, BF16)
        QT3 = QT.v("p (h n) -> p h n", h=4); KT3 = KT.v("p (h n) -> p h n", h=2)
        VA3 = VA.v("p (t c) -> p t c", c=130)
        fw.dma(QT3[0:64], QK[b][0:256, :].rearrange("(h d) n -> d h n", d=64))
        fw.dma(KT3[0:64], QK[b][256:384, :].rearrange("(h d) n -> d h n", d=64))
        fw.dma(VA3, VV[b].rearrange("(t p) c -> p t c", p=128)[:, :, 0:130])
        PT = [ar.alloc(512, BF16) for _ in range(3)]
        rden = ar.alloc(4)
        obs = [ar.alloc(256, BF16) for _ in range(2)]
        st = {"s": 0, "p": 0}
        psS = [banks[0], banks[1]]
        cnt = 0
        O3 = OO[b].rearrange("(t p) c -> p t c", p=128)
        blocks = [(256 + qb * 512, 512, list(range(NT)), 2 + qb * 4) for qb in range(4)]
        if l == 0:
            blocks.append((0, 256, [0, 1], 0))
        for h in range(4):
            kv = h // 2
            for (qc0, nq, kts, t0) in blocks:
                psO = banks[2 + cnt % 2]
                ob = obs[cnt % 2]
                cnt += 1
                attn_block(QT3, KT3, VA3, h, kv, kv * 65, qc0, nq, kts, PT, psS, psO, st)
                nsub = nq // 128
                attn_finish(psO, nsub, rden, ob)
                fw.dma(O3[:, t0:t0 + nsub, h * 64:(h + 1) * 64], ob.v("p (s d) -> p s d", d=64)[:, 0:nsub, :])

    NA_PLAN, NA_PATS = na_plan()

    def stage_NAB(l):
        ar.reset()
        rb = ar.alloc(31)
        fw.dma(rb[0:60, :], na_rel_bias[l])
        pad = ar.alloc(128)
        fw.memset(pad.ap(), BIG)
        fw.ts(pad[0:60, 48:79], rb[0:60, ::-1], 8.0, None, ALU.mult)
        padt = PADT[l].ap.tensor
        base = PADT[l].ap.offset
        fw.dma(View(PADT, bass.AP(tensor=padt, offset=base, ap=[[8192, 61], [127, 64], [1, 127]])),
               pad[0:61, 0:127].unsq(1).bc([61, 64, 127]))
        cm = ar.alloc(512)
        fw.dma(cm.ap(), k_colmask.ap())
        stg = [ar.alloc(512) for _ in range(2)]
        bt = [ar.alloc(512, BF16) for _ in range(2)]
        for bid, pat in enumerate(NA_PATS):
            s_ = stg[bid % 2]
            for krl in range(2):
                for qrl in range(2):
                    dri = pat[krl * 2 + qrl]
                    for h in range(4):
                        row = 60 if dri is None else h * 15 + dri
                        src = View(PADT, bass.AP(tensor=padt, offset=base + row * 8192 + 63, ap=[[126, 64], [1, 64]]))
                        fw.dma(s_[krl * 64:(krl + 1) * 64, h * 128 + qrl * 64:h * 128 + (qrl + 1) * 64], src)
            fw.tt(bt[bid % 2].ap(), s_.ap(), cm.ap(), ALU.add)
            fw.dma(BIAS[l][bid], bt[bid % 2].ap())

    def stage_NA(b, l):
        ar.reset()
        idb = load_consts_ident()
        QT = ar.alloc(4 * T, BF16); KT = ar.alloc(4 * T, BF16); VD = ar.alloc(NT * 260, BF16)
        QT3 = QT.v("p (h n) -> p h n", h=4); KT3 = KT.v("p (h n) -> p h n", h=4)
        VD3 = VD.v("p (t c) -> p t c", c=260)
        fw.dma(QT3[0:64], QK[b][384:640, :].rearrange("(h d) n -> d h n", d=64))
        fw.dma(KT3[0:64], QK[b][640:896, :].rearrange("(h d) n -> d h n", d=64))
        fw.dma(VD3, VV[b].rearrange("(t p) c -> p t c", p=128)[:, :, 130:390])
        nbias = len(NA_PATS)
        BT = ar.alloc(nbias * 512, BF16)
        BT3 = BT.v("p (i n) -> p i n", n=512)
        fw.dma(BT3, BIAS[l][0:nbias].rearrange("i p n -> p i n"))
        PT = [ar.alloc(512, BF16) for _ in range(3)]
        rden = ar.alloc(4)
        obs = [ar.alloc(256, BF16) for _ in range(2)]
        O3 = OO[b].rearrange("(t p) c -> p t c", p=128)
        si = 0
        for t in range(16):
            qc0 = 256 + t * 128
            klist = [(2 + u, bid) for (u, bid) in NA_PLAN[t]] + [(0, None), (1, None)]
            psO = banks[2 + t % 2]
            ob = obs[t % 2]
            nk = len(klist)
            for ki, (kt, bid) in enumerate(klist):
                pS = banks[si % 2]
                pt = PT[si % 3]
                si += 1
                for h in range(4):
                    fw.matmul(pS[:, h * 128:(h + 1) * 128], KT3[0:64, h, kt * 128:(kt + 1) * 128], QT3[0:64, h, qc0:qc0 + 128],
                              start=(h == 0), stop=(h == 3 and bid is None))
                if bid is not None:
                    fw.matmul(pS.ap(), idb.ap(), BT3[:, bid, :], start=False, stop=True)
                fw.act(pt.ap(), pS.ap(), AF.Exp, scale=0.125)
                for h in range(4):
                    fw.matmul(psO[:, h * 65:(h + 1) * 65], pt[:, h * 128:(h + 1) * 128], VD3[:, kt, h * 65:(h + 1) * 65],
                              start=(ki == 0 and h == 0), stop=(ki == nk - 1 and h == 3))
            attn_finish(psO, 4, rden, ob)
            fw.dma(OO[b][(2 + t) * 128:(3 + t) * 128, 256:512], ob.ap())
        if l == 0:
            st = {"s": 0, "p": 0}
            for h in range(4):
                psO = banks[2 + h % 2]
                ob = obs[h % 2]
                attn_block(QT3, KT3, VD3, h, h, h * 65, 0, 256, [0, 1], PT, [banks[0], banks[1]], psO, st)
                attn_finish(psO, 2, rden, ob)
                fw.dma(O3[:, 0:2, 256 + h * 64:256 + (h + 1) * 64], ob.v("p (s d) -> p s d", d=64)[:, 0:2, :])

    def stage_FW(l):
        ar.reset()
        wf = ar.alloc(2 * 256)
        wf3 = wf.v("p (k n) -> p k n", k=2)
        fw.dma(wf3, w_fourier[l].rearrange("(k p) n -> p k n", p=128))
        for ci, ktab in enumerate((k_cblk, k_sblk)):
            cb = ar.alloc(2 * 256)
            cb3 = cb.v("p (k n) -> p k n", k=2)
            fw.dma(cb3, ktab.v("(k p) n -> p k n", p=128))
            wo = ar.alloc(2 * 256, BF16)
            wo3 = wo.v("p (k n) -> p k n", k=2)
            for fo in range(2):
                ps = banks[(ci * 2 + fo) % 4]
                for k in range(2):
                    fw.matmul(ps[:, 0:256], cb3[:, k, fo * 128:(fo + 1) * 128], wf3[:, k, :], start=(k == 0), stop=(k == 1))
                fw.act(wo3[:, fo, :], ps[:, 0:256], AF.Copy, scale=(1.0 if ci == 0 else -1.0))
            fw.dma(WCS[l][ci].rearrange("(k p) n -> p k n", p=128), wo3)

    def stage_F(b, l):
        ar.reset()
        fin = ar.alloc(NT * 256, BF16)
        fin3 = fin.v("p (t f) -> p t f", f=256)
        fw.dma(fin3, FIN[b].rearrange("(t p) f -> p t f", p=128))
        wcs = ar.alloc(2 * 2 * 256, BF16)
        wcs4 = wcs.v("p (c k n) -> p c k n", c=2, k=2)
        for ci in range(2):
            fw.dma(wcs4[:, ci], WCS[l][ci].rearrange("(k p) n -> p k n", p=128))
        tabs = [ar.alloc(512, BF16) for _ in range(6)]
        G = [ar.alloc(4 * 512, BF16) for _ in range(2)]
        ofb = [ar.alloc(256, BF16) for _ in range(2)]
        ti = 0
        oi = 0

        def second_stage(Gv, nk, tile0, scale):
            nonlocal oi
            for kt in range(nk // 128):
                ps = banks[4 + oi % 2]
                n_ = 0
                for ci in range(2):
                    for fc in range(2):
                        fw.matmul(ps[:, 0:256], Gv[:, ci, fc, kt * 128:(kt + 1) * 128], wcs4[:, ci, fc, :],
                                  start=(n_ == 0), stop=(n_ == 3))
                        n_ += 1
                o_ = ofb[oi % 2]
                oi += 1
                fw.act(o_.ap(), ps[:, 0:256], AF.Copy, scale=scale)
                fw.dma(OO[b][(tile0 + kt) * 128:(tile0 + kt + 1) * 128, 512:768], o_.ap())

        for kb in range(4):
            for ncix in range(16):
                for ci in range(2):
                    tb = tabs[ti % 6]
                    ti += 1
                    fw.dma(tb.ap(), k_dft[ci][ncix * 128:(ncix + 1) * 128, kb * 512:(kb + 1) * 512])
                    for fc in range(2):
                        fw.matmul(banks[ci * 2 + fc].ap(), fin3[:, 2 + ncix, fc * 128:(fc + 1) * 128], tb.ap(),
                                  start=(ncix == 0), stop=(ncix == 15))
            Gb = G[kb % 2]
            Gv = Gb.v("p (c f n) -> p c f n", c=2, f=2)
            for ci in range(2):
                for fc in range(2):
                    if (ci + fc) % 2 == 0:
                        fw.copy(Gv[:, ci, fc, :], banks[ci * 2 + fc].ap(), eng="act")
                    else:
                        fw.copy(Gv[:, ci, fc, :], banks[ci * 2 + fc].ap(), eng="dve")
            second_stage(Gv, 512, 2 + kb * 4, 1.0 / math.sqrt(SEQ * 64.0))
        if l == 0:
            for ncix in range(2):
                for ci in range(2):
                    tb = tabs[ti % 6]
                    ti += 1
                    fw.dma(tb[:, 0:256], k_dft[ci][ncix * 1024:(ncix + 1) * 1024:8, 0:256])
                    for fc in range(2):
                        fw.matmul(banks[ci * 2 + fc][:, 0:256], fin3[:, ncix, fc * 128:(fc + 1) * 128], tb[:, 0:256],
                                  start=(ncix == 0), stop=(ncix == 1))
            Gb = G[0]
            Gv = Gb.v("p (c f n) -> p c f n", c=2, f=2)
            for ci in range(2):
                for fc in range(2):
                    fw.copy(Gv[:, ci, fc, 0:256], banks[ci * 2 + fc][:, 0:256], eng=("act" if (ci + fc) % 2 == 0 else "dve"))
            second_stage(Gv, 256, 0, 1.0 / math.sqrt(CTXL * 64.0))

    def stage_C(b, l):
        ar.reset()
        idb = load_consts_ident()
        w2 = ar.alloc(NF * D, BF16)
        w23 = w2.v("p (f n) -> p f n", f=NF)
        fw.dma(w23, W2[l].rearrange("(f p) n -> p f n", p=128))
        wo = ar.alloc(8 * D, BF16)
        wo3 = wo.v("p (k n) -> p k n", k=8)
        fw.dma(wo3, WOUT[l])
        gg = ar.alloc(D)
        fw.dma(gg.ap(), g_group[l:l + 1, :].bc([128, D]))
        GA1 = ar.alloc(D); A2 = ar.alloc(D); B2 = ar.alloc(D); GA2 = ar.alloc(D)
        hid = ar.alloc(NF * 512, BF16)
        hid3 = hid.v("p (f n) -> p f n", f=NF)
        h2T = ar.alloc(8 * 512, BF16)
        h2T3 = h2T.v("p (k n) -> p k n", k=8)
        x1 = ar.alloc(4 * D)
        x13 = x1.v("p (s n) -> p s n", s=4)
        wt1 = [ar.alloc(8 * 128, BF16) for _ in range(3)]
        wt3 = [ar.alloc(8 * 128, BF16) for _ in range(3)]
        xs = [ar.alloc(D) for _ in range(2)]
        ob = [ar.alloc(D, BF16) for _ in range(2)]
        sq = ar.alloc(D)
        ss4 = ar.alloc(4); sd4 = ar.alloc(4); rs4 = ar.alloc(4)
        ss = ar.alloc(4); sd = ar.alloc(4); rstd = ar.alloc(4)
        on1 = ar.alloc(D)
        onb = ar.alloc(D, BF16)
        onT = ar.alloc(D, BF16)
        tmp = ar.alloc(D)
        sqj = ar.alloc(D, BF16)
        h2b = ar.alloc(D, BF16)
        sa = [ar.alloc(512) for _ in range(2)]
        xo = [ar.alloc(D) for _ in range(2)]
        groups = [[2 + g * 4 + s for s in range(4)] for g in range(4)]
        if l == 0:
            groups = [[0, 1]] + groups
        cur_row = None
        xi = 0
        wi = 0
        oi = 0
        for grp in groups:
            isctx = grp[0] < 2
            row = 2 if isctx else b
            if row != cur_row:
                fw.dma(GA1.ap(), modrow(l, row, 2)); fw.dma(A2.ap(), modrow(l, row, 4))
                fw.dma(B2.ap(), modrow(l, row, 3)); fw.dma(GA2.ap(), modrow(l, row, 5))
                cur_row = row
            ntk = len(grp) * 128
            for s, i in enumerate(grp):
                xj = xs[xi % 2]; oj = ob[xi % 2]
                xi += 1
                if l == 0:
                    src = ctx[b][i * 128:(i + 1) * 128, :] if i < 2 else x[b][(i - 2) * 128:(i - 1) * 128, :]
                else:
                    src = XR[b][i * 128:(i + 1) * 128, :]
                fw.dma(xj.ap(), src)
                fw.dma(oj.ap(), OO[b][i * 128:(i + 1) * 128, :])
                fw.act(sq.ap(), oj.ap(), AF.Square)
                fw.reduce(ss4[:, 0:4], sq.v("p (g d) -> p g d", g=4), ALU.add)
                rstd_from(ss4[:, 0:4], 256, rs4[:, 0:4], sd4[:, 0:4])
                fw.tt(on1.v("p (g d) -> p g d", g=4), oj.v("p (g d) -> p g d", g=4),
                      rs4[:, 0:4].unsq(2).bc([128, 4, 256]), ALU.mult)
                fw.tt(onb.ap(), on1.ap(), gg.ap(), ALU.mult, eng="pool")
                pT = bank_bf(4)
                for k in range(8):
                    fw.transpose(pT[:, k * 128:(k + 1) * 128], onb[:, k * 128:(k + 1) * 128], idb.ap())
                fw.copy(onT.ap(), pT[:, 0:D], eng="act")
                onT3 = onT.v("p (k n) -> p k n", k=8)
                for nb in range(2):
                    for k in range(8):
                        fw.matmul(banks[nb].ap(), onT3[:, k, :], wo3[:, k, nb * 512:(nb + 1) * 512], start=(k == 0), stop=(k == 7))
                for nb in range(2):
                    sl = slice(nb * 512, (nb + 1) * 512)
                    fw.tt(tmp[:, sl], banks[nb].ap(), GA1[:, sl], ALU.mult)
                fw.tt(x13[:, s, :], tmp.ap(), xj.ap(), ALU.add, eng="pool")
                fw.act(sqj.ap(), x13[:, s, :], AF.Square, accum_out=ss[:, 0:1])
                rstd_from(ss[:, 0:1], D, rstd[:, 0:1], sd[:, 0:1])
                fw.stt(tmp.ap(), x13[:, s, :], rstd[:, 0:1], A2.ap(), ALU.mult, ALU.mult)
                fw.tt(h2b.ap(), tmp.ap(), B2.ap(), ALU.add, eng="pool")
                pT2 = bank_bf(5)
                for k in range(8):
                    fw.transpose(pT2[:, k * 128:(k + 1) * 128], h2b[:, k * 128:(k + 1) * 128], idb.ap())
                fw.copy(h2T3[:, :, s * 128:(s + 1) * 128], pT2[:, 0:D].rearrange("p (k n) -> p k n", k=8), eng="act")
            for f in range(NF):
                a_ = wt1[wi % 3]; b_ = wt3[wi % 3]
                wi += 1
                fw.dma(a_.ap(), W1[l][f].rearrange("p k n -> p (k n)"))
                fw.dma(b_.ap(), W3[l][f].rearrange("p k n -> p (k n)"))
                a3 = a_.v("p (k n) -> p k n", k=8); b3 = b_.v("p (k n) -> p k n", k=8)
                pa = banks[2 + (f % 2) * 2]; pb = banks[3 + (f % 2) * 2]
                for k in range(8):
                    fw.matmul(pa[:, 0:ntk], a3[:, k, :], h2T3[:, k, 0:ntk], start=(k == 0), stop=(k == 7))
                for k in range(8):
                    fw.matmul(pb[:, 0:ntk], b3[:, k, :], h2T3[:, k, 0:ntk], start=(k == 0), stop=(k == 7))
                sj = sa[f % 2]
                fw.act(sj[:, 0:ntk], pa[:, 0:ntk], AF.Silu)
                fw.tt(hid3[:, f, 0:ntk], sj[:, 0:ntk], pb[:, 0:ntk], ALU.mult)
            for s, i in enumerate(grp):
                xoj = xo[oi % 2]
                oi += 1
                for nb in range(2):
                    ps = banks[6 + nb]
                    for f in range(NF):
                        fw.matmul(ps.ap(), hid3[:, f, s * 128:(s + 1) * 128], w23[:, f, nb * 512:(nb + 1) * 512],
                                  start=(f == 0), stop=(f == NF - 1))
                    sl = slice(nb * 512, (nb + 1) * 512)
                    fw.tt(tmp[:, sl], ps.ap(), GA2[:, sl], ALU.mult)
                fw.tt(xoj.ap(), tmp.ap(), x13[:, s, :], ALU.add, eng="pool")
                if l == 0:
                    fw.dma(XR[b][i * 128:(i + 1) * 128, :], xoj.ap())
                else:
                    fw.dma(out[b][(i - 2) * 128:(i - 1) * 128, :], xoj.ap())

    TWO_PI = 2.0 * math.pi
    CW1 = 6.28125
    CW2 = TWO_PI - 6.28125
    MAGIC = 12582912.0
    PI_SAFE = 3.141592

    def range_reduce(out, in_, kbuf, shift=0.0, eng="dve"):
        if shift != 0.0:
            fw.ts(out, in_, shift, None, ALU.add, eng=eng)
            src = out
        else:
            src = in_
        fw.ts(kbuf, src, 1.0 / TWO_PI, MAGIC, ALU.mult, ALU.add, eng=eng)
        fw.ts(kbuf, kbuf, -MAGIC, None, ALU.add, eng=eng)
        fw.stt(out, kbuf, -CW1, src, ALU.mult, ALU.add, eng=eng)
        fw.stt(out, kbuf, -CW2, out, ALU.mult, ALU.add, eng=eng)
        fw.ts(out, out, -PI_SAFE, PI_SAFE, ALU.max, ALU.min, eng=eng)

    def stage_S5P(l):
        ar.reset()
        idf = ar.alloc(128)
        fw.dma(idf.ap(), k_ident_f.ap())
        tau = ar.alloc(16)
        fw.dma(tau.ap(), k_tau.ap())
        nidx = ar.alloc(NCH)
        fw.dma(nidx.ap(), k_nidx.ap())
        msk = ar.alloc(128)
        fw.dma(msk.ap(), k_s5mask.ap())
        mark = ar.off
        for d_ in range(2):
            ar.off = mark
            if d_ == 1:
                fw.barrier()
            lamre = ar.alloc(8); lamim = ar.alloc(8); dt = ar.alloc(8)
            for gi in range(2):
                sl = slice(64 * gi, 64 * gi + 64)
                fw.dma(lamre[sl, :], ssm_lam_re[l][d_].rearrange("(q two) p -> two p q", two=2)[gi], allow_slow_non_contiguous=True)
                fw.dma(lamim[sl, :], ssm_lam_im[l][d_].rearrange("(q two) p -> two p q", two=2)[gi], allow_slow_non_contiguous=True)
                fw.dma(dt[sl, :], ssm_log_dt[l][d_:d_ + 1, :].rearrange("o (q two) -> two o q", two=2)[gi].bc([64, 8]),
                       allow_slow_non_contiguous=True)
            fw.act(dt.ap(), dt.ap(), AF.Exp)
            zr = ar.alloc(8); zi = ar.alloc(8)
            fw.tt(zr.ap(), lamre.ap(), dt.ap(), ALU.mult)
            fw.tt(zi.ap(), lamim.ap(), dt.ap(), ALU.mult)
            PZ = ar.alloc(128); PEx = ar.alloc(128); KB = ar.alloc(128); RS = ar.alloc(128); RC = ar.alloc(128)
            Are = ar.alloc(128); Aim = ar.alloc(128)
            v3 = lambda bf: bf.v("p (t q) -> p t q", q=8)
            fw.tt(v3(PZ), tau.ap().unsq(2).bc([128, 16, 8]), zr.ap().unsq(1).bc([128, 16, 8]), ALU.mult)
            fw.act(PEx.ap(), PZ.ap(), AF.Exp)
            fw.tt(v3(PZ), tau.ap().unsq(2).bc([128, 16, 8]), zi.ap().unsq(1).bc([128, 16, 8]), ALU.mult)
            range_reduce(RS.ap(), PZ.ap(), KB.ap())
            range_reduce(RC.ap(), PZ.ap(), KB.ap(), shift=0.5 * math.pi)
            fw.act(RS.ap(), RS.ap(), AF.Sin)
            fw.act(RC.ap(), RC.ap(), AF.Sin)
            fw.tt(Are.ap(), PEx.ap(), RC.ap(), ALU.mult)
            fw.tt(Aim.ap(), PEx.ap(), RS.ap(), ALU.mult)
            Are3 = v3(Are); Aim3 = v3(Aim)
            fw.dma(S5D[l][d_], v3(PEx)[:, 15, :])
            phi = ar.alloc(8); kb8 = ar.alloc(8)
            fw.ts(phi.ap(), zi.ap(), 8.0, None, ALU.mult)
            range_reduce(phi.ap(), phi.ap(), kb8.ap())
            ANG = ar.alloc(8 * NCH); KB2 = ar.alloc(8 * NCH); R2 = ar.alloc(8 * NCH)
            a3 = lambda bf: bf.v("p (q n) -> p q n", q=8)
            fw.tt(a3(ANG), nidx.ap().unsq(1).bc([128, 8, NCH]), phi.ap().unsq(2).bc([128, 8, NCH]), ALU.mult)
            range_reduce(R2.ap(), ANG.ap(), KB2.ap(), eng="pool")
            fw.act(R2.ap(), R2.ap(), AF.Sin)
            fw.dma(S5SIN[l][d_], a3(R2))
            R3 = ar.alloc(8 * NCH)
            range_reduce(R3.ap(), ANG.ap(), KB2.ap(), shift=0.5 * math.pi, eng="pool")
            fw.act(R3.ap(), R3.ap(), AF.Sin)
            fw.dma(S5COS[l][d_], a3(R3))
            nr = ar.alloc(8); den = ar.alloc(8); t8a = ar.alloc(8); t8b = ar.alloc(8); cr = ar.alloc(8); ci = ar.alloc(8)
            fw.ts(nr.ap(), Are3[:, 8, :], -1.0, None, ALU.add)
            fw.tt(den.ap(), lamre.ap(), lamre.ap(), ALU.mult)
            fw.tt(t8a.ap(), lamim.ap(), lamim.ap(), ALU.mult)
            fw.tt(den.ap(), den.ap(), t8a.ap(), ALU.add)
            fw.recip(den.ap(), den.ap())
            fw.tt(t8a.ap(), nr.ap(), lamre.ap(), ALU.mult)
            fw.tt(t8b.ap(), Aim3[:, 8, :], lamim.ap(), ALU.mult)
            fw.tt(t8a.ap(), t8a.ap(), t8b.ap(), ALU.add)
            fw.tt(cr.ap(), t8a.ap(), den.ap(), ALU.mult)
            fw.tt(t8a.ap(), Aim3[:, 8, :], lamre.ap(), ALU.mult)
            fw.tt(t8b.ap(), nr.ap(), lamim.ap(), ALU.mult)
            fw.tt(t8a.ap(), t8a.ap(), t8b.ap(), ALU.subtract)
            fw.tt(ci.ap(), t8a.ap(), den.ap(), ALU.mult)
            bre = ar.alloc(128); bim = ar.alloc(128); Bre = ar.alloc(128); Bim = ar.alloc(128); tb1 = ar.alloc(128); tb2 = ar.alloc(128)
            b3 = lambda bf: bf.v("p (q c) -> p q c", q=8)
            for gi in range(2):
                sl = slice(64 * gi, 64 * gi + 64)
                fw.dma(b3(bre)[sl], ssm_b_re[l][d_].rearrange("(q two) p c -> two p q c", two=2)[gi])
                fw.dma(b3(bim)[sl], ssm_b_im[l][d_].rearrange("(q two) p c -> two p q c", two=2)[gi])
            crb = cr.ap().unsq(2).bc([128, 8, 16]); cib = ci.ap().unsq(2).bc([128, 8, 16])
            fw.tt(b3(tb1), b3(bre), crb, ALU.mult); fw.tt(b3(tb2), b3(bim), cib, ALU.mult)
            fw.tt(Bre.ap(), tb1.ap(), tb2.ap(), ALU.subtract)
            fw.tt(b3(tb1), b3(bim), crb, ALU.mult); fw.tt(b3(tb2), b3(bre), cib, ALU.mult)
            fw.tt(Bim.ap(), tb1.ap(), tb2.ap(), ALU.add)
            Cre = ar.alloc(128); Cim = ar.alloc(128)
            for (Cdst, csrc, bk) in ((Cre, ssm_c_re, 4), (Cim, ssm_c_im, 5)):
                X = ar.alloc(128)
                for q in range(8):
                    for gi in range(2):
                        fw.dma(X[16 * q:16 * q + 16, 64 * gi:64 * gi + 64], csrc[l][d_][2 * q + gi])
                fw.transpose(banks[bk][:, 0:128], X.ap(), idf.ap())
                fw.copy(Cdst.ap(), banks[bk][:, 0:128], eng="act")
            BcR = ar.alloc(8 * 128); BcI = ar.alloc(8 * 128)
            CpR = ar.alloc(8 * 128); CpI = ar.alloc(8 * 128)
            CcR = ar.alloc(8 * 128, BF16); CcI = ar.alloc(8 * 128, BF16)
            u1 = ar.alloc(128); u2 = ar.alloc(128)
            m3 = lambda bf, q: bf.v("p (q s c) -> p q s c", q=8, s=8)[:, q]
            w3 = lambda bf: bf.v("p (s c) -> p s c", s=8)
            for q in range(8):
                ArB = Are3[:, 14:6:-1, q].unsq(2).bc([128, 8, 16]); AiB = Aim3[:, 14:6:-1, q].unsq(2).bc([128, 8, 16])
                Br = b3(Bre)[:, q, :].unsq(1).bc([128, 8, 16]); Bi = b3(Bim)[:, q, :].unsq(1).bc([128, 8, 16])
                fw.tt(w3(u1), ArB, Br, ALU.mult); fw.tt(w3(u2), AiB, Bi, ALU.mult, eng="pool")
                fw.tt(m3(BcR, q), w3(u1), w3(u2), ALU.subtract)
                fw.tt(w3(u1), ArB, Bi, ALU.mult); fw.tt(w3(u2), AiB, Br, ALU.mult, eng="pool")
                fw.tt(m3(BcI, q), w3(u1), w3(u2), ALU.add)
                Cr = b3(Cre)[:, q, :].unsq(1).bc([128, 8, 16]); Ci = b3(Cim)[:, q, :].unsq(1).bc([128, 8, 16])
                A0r = Are3[:, 0:8, q].unsq(2).bc([128, 8, 16]); A0i = Aim3[:, 0:8, q].unsq(2).bc([128, 8, 16])
                fw.tt(w3(u1), A0r, Cr, ALU.mult); fw.tt(w3(u2), A0i, Ci, ALU.mult, eng="pool")
                fw.tt(m3(CpR, q), w3(u1), w3(u2), ALU.subtract)
                fw.tt(w3(u1), A0r, Ci, ALU.mult); fw.tt(w3(u2), A0i, Cr, ALU.mult, eng="pool")
                fw.stt(m3(CpI, q), w3(u1), -1.0, w3(u2), ALU.mult, ALU.subtract)
                A1r = Are3[:, 8:16, q].unsq(2).bc([128, 8, 16]); A1i = Aim3[:, 8:16, q].unsq(2).bc([128, 8, 16])
                fw.tt(w3(u1), A1r, Cr, ALU.mult); fw.tt(w3(u2), A1i, Ci, ALU.mult, eng="pool")
                fw.tt(m3(CcR, q), w3(u1), w3(u2), ALU.subtract)
                fw.tt(w3(u1), A1r, Ci, ALU.mult); fw.tt(w3(u2), A1i, Cr, ALU.mult, eng="pool")
                fw.stt(m3(CcI, q), w3(u1), -1.0, w3(u2), ALU.mult, ALU.subtract)
            fw.dma(S5C[l][d_].rearrange("q p r n -> p q r n")[:, :, 0, :], CcR.v("p (q n) -> p q n", q=8))
            fw.dma(S5C[l][d_].rearrange("q p r n -> p q r n")[:, :, 1, :], CcI.v("p (q n) -> p q n", q=8))
            Tb = [ar.alloc(128, BF16) for _ in range(2)]
            BT = [ar.alloc(256, BF16) for _ in range(2)]
            qv = lambda bf, q: bf.v("p (q n) -> p q n", q=8)[:, q, :]
            for q in range(8):
                for gi in range(2):
                    g = 2 * q + gi
                    sl = slice(64 * gi, 64 * gi + 64)
                    ps = banks[gi]
                    fw.matmul(ps[:, 0:128], qv(BcR, q)[sl], qv(CpR, q)[sl], start=True, stop=False)
                    fw.matmul(ps[:, 0:128], qv(BcI, q)[sl], qv(CpI, q)[sl], start=False, stop=True)
                    tb_ = Tb[g % 2]
                    fw.tt(tb_.ap(), ps[:, 0:128], msk.ap(), ALU.mult)
                    fw.dma(S5T[l][d_][g], tb_.ap())
                bt_ = BT[q % 2]
                for ri, src in enumerate((BcR, BcI)):
                    ps = banks[2 + ri]
                    fw.transpose(ps[:, 0:128], qv(src, q), idf.ap())
                    fw.copy(bt_.v("p (g r n) -> p g r n", g=2, r=2)[:, :, ri, :], ps[:, 0:128].rearrange("p (g n) -> p g n", g=2), eng="act")
                fw.dma(S5B[l][d_][2 * q:2 * q + 2].rearrange("g sc r n -> sc g (r n)"), bt_.v("p (g x) -> p g x", g=2))

    def stage_S5(b, l):
        ar.reset()
        idb = load_consts_ident()
        uT = ar.alloc(2 * T, BF16)
        uT3 = uT.v("p (m n) -> p m n", m=2)
        fw.dma(uT3, UT[b].rearrange("(m p) n -> p m n", p=128))
        Dcol = ar.alloc(2)
        fw.dma(Dcol.ap(), ssm_d[l].rearrange("(m p) -> p m", p=128), allow_slow_non_contiguous=True)
        bgl = ar.alloc(2)
        fw.dma(bgl.ap(), b_glu[l].rearrange("(m p) -> p m", p=128), allow_slow_non_contiguous=True)
        wgf = ar.alloc(512)
        fw.dma(wgf.v("p (k n) -> p k n", k=2), w_glu[l].rearrange("(k p) n -> p k n", p=128))
        wgb = ar.alloc(512, BF16)
        fw.copy(wgb.ap(), wgf.ap())
        wgb3 = wgb.v("p (k n) -> p k n", k=2)
        Tm = ar.alloc(32 * 128, BF16); T3 = Tm.v("p (g n) -> p g n", g=32)
        fw.dma(T3, S5T[l].rearrange("d g sc n -> sc (d g) n"))
        Bm = ar.alloc(32 * 128, BF16); B4 = Bm.v("p (g r n) -> p g r n", g=32, r=2)
        fw.dma(Bm.v("p (g x) -> p g x", g=32), S5B[l].rearrange("d g sc r n -> sc (d g) (r n)"))
        Cm = ar.alloc(16 * 256, BF16); C4 = Cm.v("p (g r n) -> p g r n", g=16, r=2)
        fw.dma(Cm.v("p (g x) -> p g x", g=16), S5C[l].rearrange("d q p r n -> p (d q) (r n)"))
        DEC = ar.alloc(16); DEC3 = DEC.v("p (d q) -> p d q", d=2)
        fw.dma(DEC3, S5D[l].rearrange("d p q -> p d q"))
        stg = [ar.alloc(2 * 8 * NCH, BF16) for _ in range(2)]
        s4 = lambda bf: bf.v("p (m s j) -> p m s j", m=2, s=8)
        for m in range(2):
            fw.copy(s4(stg[0])[:, m], uT3[:, m, :].rearrange("p (j s) -> p s j", s=8), eng=("dve" if m == 0 else "pool"))
            fw.copy(s4(stg[1])[:, m, :, 0:32], uT3[:, m, 0:256][:, ::-1].rearrange("p (j s) -> p s j", s=8), eng="dve")
            fw.copy(s4(stg[1])[:, m, :, 32:NCH], uT3[:, m, 256:T][:, ::-1].rearrange("p (j s) -> p s j", s=8), eng="pool")
        U = ar.alloc(32 * NCH, BF16); U3 = U.v("p (g j) -> p g j", g=32)
        for d_ in range(2):
            for g in range(16):
                m, gl = g // 8, g % 8
                fw.dma(SCRU[d_][g].rearrange("s c j -> c s j"), s4(stg[d_])[16 * gl:16 * gl + 16, m])
        for d_ in range(2):
            for g in range(16):
                fw.dma(U3[:, d_ * 16 + g, :], SCRU[d_][g].rearrange("s c j -> (s c) j"))
        cosb = [ar.alloc(NCH) for _ in range(2)]; sinb = [ar.alloc(NCH) for _ in range(2)]
        Sre = ar.alloc(NCH); Sim = ar.alloc(NCH)
        t1 = ar.alloc(NCH); t2 = ar.alloc(NCH); t3 = ar.alloc(NCH); t4 = ar.alloc(NCH)
        wr_in = ar.alloc(NCH); wi_in = ar.alloc(NCH); wr = ar.alloc(NCH); wi = ar.alloc(NCH)
        Hre = [ar.alloc(NCH, BF16) for _ in range(2)]; Him = [ar.alloc(NCH, BF16) for _ in range(2)]
        Yg = [ar.alloc(NCH, BF16) for _ in range(2)]
        it = 0
        for d_ in range(2):
            for q in range(8):
                cb = cosb[it % 2]; sb = sinb[it % 2]
                hr = Hre[it % 2]; hi = Him[it % 2]
                it += 1
                fw.dma(cb.ap(), S5COS[l][d_][:, q, :])
                fw.dma(sb.ap(), S5SIN[l][d_][:, q, :])
                for gi in range(2):
                    g = d_ * 16 + 2 * q + gi
                    sl = slice(64 * gi, 64 * gi + 64)
                    fw.matmul(banks[0][sl, 0:NCH], B4[:, g, 0, :], U3[:, g, :])
                    fw.matmul(banks[1][sl, 0:NCH], B4[:, g, 1, :], U3[:, g, :])
                fw.copy(Sre.ap(), banks[0][:, 0:NCH], eng="act")
                fw.copy(Sim.ap(), banks[1][:, 0:NCH], eng="act")
                fw.tt(t1.ap(), Sre.ap(), cb.ap(), ALU.mult)
                fw.tt(t2.ap(), Sim.ap(), sb.ap(), ALU.mult, eng="pool")
                fw.tt(wr_in.ap(), t1.ap(), t2.ap(), ALU.add)
                fw.tt(t3.ap(), Sim.ap(), cb.ap(), ALU.mult, eng="pool")
                fw.tt(t4.ap(), Sre.ap(), sb.ap(), ALU.mult)
                fw.tt(wi_in.ap(), t3.ap(), t4.ap(), ALU.subtract, eng="pool")
                dec = DEC3[:, d_, q:q + 1].bc([128, NCH])
                fw.scan(wr.ap(), dec, wr_in.ap(), 0.0)
                fw.scan(wi.ap(), dec, wi_in.ap(), 0.0)
                fw.tt(t1.ap(), wr.ap(), cb.ap(), ALU.mult)
                fw.tt(t2.ap(), wi.ap(), sb.ap(), ALU.mult, eng="pool")
                fw.tt(hr.ap(), t1.ap(), t2.ap(), ALU.subtract)
                fw.tt(t3.ap(), wr.ap(), sb.ap(), ALU.mult, eng="pool")
                fw.tt(t4.ap(), wi.ap(), cb.ap(), ALU.mult)
                fw.tt(hi.ap(), t3.ap(), t4.ap(), ALU.add, eng="pool")
                for gi in range(2):
                    g16 = 2 * q + gi
                    g = d_ * 16 + g16
                    sl = slice(64 * gi, 64 * gi + 64)
                    psY = banks[2 + gi]
                    fw.matmul(psY[:, 0:NCH], T3[:, g, :], U3[:, g, :], start=True, stop=False)
                    fw.matmul(psY[:, 1:NCH], C4[sl, d_ * 8 + q, 0, :], hr[sl, 0:NCH - 1], start=False, stop=False)
                    fw.matmul(psY[:, 1:NCH], C4[sl, d_ * 8 + q, 1, :], hi[sl, 0:NCH - 1], start=False, stop=True)
                    yg = Yg[gi]
                    fw.copy(yg.ap(), psY[:, 0:NCH], eng="act")
                    fw.dma(SCRY[d_][g16].rearrange("t c j -> (t c) j"), yg.ap())
        ys = stg
        for d_ in range(2):
            for g in range(16):
                m, gl = g // 8, g % 8
                fw.dma(s4(ys[d_])[16 * gl:16 * gl + 16, m], SCRY[d_][g].rearrange("t c j -> c t j"))
        y = ar.alloc(2 * T); y3 = y.v("p (m n) -> p m n", m=2)
        gb = ar.alloc(2 * T, BF16); gb3 = gb.v("p (m n) -> p m n", m=2)
        for m in range(2):
            f3 = s4(ys[0])[:, m]
            r3 = s4(ys[1])[:, m]
            eng = "dve" if m == 0 else "pool"
            fw.tt(y3[:, m, 0:256].rearrange("p (j t) -> p j t", t=8), f3[:, :, 0:32].rearrange("p t j -> p j t"),
                  r3[:, :, 0:32].rearrange("p s n -> p n s")[:, ::-1, ::-1], ALU.add, eng=eng)
            fw.tt(y3[:, m, 256:T].rearrange("p (j t) -> p j t", t=8), f3[:, :, 32:NCH].rearrange("p t j -> p j t"),
                  r3[:, :, 32:NCH].rearrange("p s n -> p n s")[:, ::-1, ::-1], ALU.add, eng=eng)
            fw.stt(y3[:, m, :], uT3[:, m, :], Dcol[:, m:m + 1], y3[:, m, :], ALU.mult, ALU.add, eng=eng)
        fw.act(y.ap(), y.ap(), AF.Gelu_apprx_tanh)
        fw.copy(gb.ap(), y.ap(), eng="pool")
        osT = ar.alloc(2 * T, BF16); osT3 = osT.v("p (m n) -> p m n", m=2)
        gate = [ar.alloc(512) for _ in range(2)]
        bi = 0
        for mo in range(2):
            for (c0, nn) in [(0, 512), (512, 512), (1024, 512), (1536, 512), (2048, 256)]:
                ps = banks[4 + bi % 2]
                gt = gate[bi % 2]
                bi += 1
                for m in range(2):
                    fw.matmul(ps[:, 0:nn], wgb3[:, m, mo * 128:(mo + 1) * 128], gb3[:, m, c0:c0 + nn], start=(m == 0), stop=(m == 1))
                fw.act(gt[:, 0:nn], ps[:, 0:nn], AF.Sigmoid, bias=bgl[:, mo:mo + 1])
                fw.tt(osT3[:, mo, c0:c0 + nn], y3[:, mo, c0:c0 + nn], gt[:, 0:nn], ALU.mult, eng=("dve" if bi % 2 == 0 else "pool"))
        otb = [ar.alloc(256, BF16) for _ in range(2)]
        for i in range(NT):
            if l == 1 and i < 2:
                continue
            pT = bank_bf(6 + i % 2)
            for mo in range(2):
                fw.transpose(pT[:, mo * 128:(mo + 1) * 128], osT3[:, mo, i * 128:(i + 1) * 128], idb.ap())
            ot = otb[i % 2]
            fw.copy(ot.ap(), pT[:, 0:256], eng="act")
            fw.dma(OO[b][i * 128:(i + 1) * 128, 768:1024], ot.ap())

    if "W" in stages:
        stage_W()
    for l in layers:
        if "M" in stages:
            stage_M(l)
        if "NAB" in stages:
            stage_NAB(l)
        if "FW" in stages:
            stage_FW(l)
        if "S5P" in stages:
            stage_S5P(l)
        for b in batches:
            if "A" in stages:
                stage_A(b, l)
            if "GQA" in stages:
                stage_GQA(b, l)
            if "NA" in stages:
                stage_NA(b, l)
            if "F" in stages:
                stage_F(b, l)
            if "S5" in stages:
                stage_S5(b, l)
            if "C" in stages:
                stage_C(b, l)
    fw.barrier()
    fw.emit()
    return nc


_RESHAPE = {
    "c_ctx": (1, D),
    "na_rel_bias": (2, 60, 31),
}


def make_in_maps(inputs, ncores=NCORES):
    consts = host_constants()
    shared = {}
    for name, arr in inputs.items():
        if name in ("x", "ctx", "c"):
            continue
        a = np.ascontiguousarray(arr)
        if name in _RESHAPE:
            a = a.reshape(_RESHAPE[name])
        shared[name] = a
    shared.update(consts)
    maps = []
    for i in range(ncores):
        m = dict(shared)
        m["x"] = np.ascontiguousarray(inputs["x"][i * NB:(i + 1) * NB])
        m["ctx"] = np.ascontiguousarray(inputs["ctx"][i * NB:(i + 1) * NB])
        m["c"] = np.ascontiguousarray(inputs["c"][i * NB:(i + 1) * NB])
        maps.append(m)
    return maps


def kernel(**inputs):
    nc = build()
    maps = make_in_maps(inputs)
    res = run_bass_kernel_spmd(nc, maps, core_ids=list(range(NCORES)))
    outs = [np.asarray(r["out"]) for r in res.results]
    return np.concatenate(outs, axis=0).astype(np.float32)
```

```python
import contextlib
import math
import numpy as np
import ml_dtypes
import concourse.bass as bass
import concourse.mybir as mybir
from concourse.bass_utils import run_bass_kernel_spmd

F32 = mybir.dt.float32
BF16 = mybir.dt.bfloat16
AF = mybir.ActivationFunctionType
ALU = mybir.AluOpType
AX = mybir.AxisListType

NCORES = 8
NB = 2
D = 1024
SEQ = 2048
CTXL = 256
T = SEQ + CTXL
NT = T // 128
DFF = 2816
NF = DFF // 128
EPS = 1e-6
NCH = T // 8
BIG = -30000.0
DMA_K = 8
COMPUTE = ("pe", "act", "dve", "pool")
SAME_ENGINE_SYNC = True


class Buf:
    __slots__ = ("t", "name", "writers", "readers")

    def __init__(self, ap, name=""):
        self.t = ap
        self.name = name
        self.writers = {}
        self.readers = {}

    def __getitem__(self, idx):
        return View(self, self.t[idx])

    def ap(self):
        return View(self, self.t)

    def v(self, pattern=None, **kw):
        if pattern is None:
            return View(self, self.t)
        return View(self, self.t.rearrange(pattern, **kw))


class View:
    __slots__ = ("buf", "ap")

    def __init__(self, buf, ap):
        self.buf = buf
        self.ap = ap

    def __getitem__(self, idx):
        return View(self.buf, self.ap[idx])

    def rearrange(self, *a, **k):
        return View(self.buf, self.ap.rearrange(*a, **k))

    def bitcast(self, dt):
        return View(self.buf, self.ap.bitcast(dt))

    def bc(self, shape):
        return View(self.buf, self.ap.to_broadcast(list(shape)))

    def unsq(self, i):
        return View(self.buf, self.ap.unsqueeze(i))


class FW:
    def __init__(self, nc):
        self.nc = nc
        self.stack = contextlib.ExitStack()
        self.streams = {e: [] for e in ("pe", "act", "dve", "pool", "sp")}
        self.seq = {e: 0 for e in COMPUTE}
        self.sems = {}
        self.waited = {e: {} for e in self.streams}
        self.dma_count = {"sp": 0, "pool": 0, "act": 0}
        self.dma_last = {}
        self.n_inst = 0

    def sem(self, key):
        if key not in self.sems:
            name = "s_" + ("_".join(str(k) for k in key) if isinstance(key, tuple) else str(key))
            self.sems[key] = self.stack.enter_context(self.nc.semaphore(name))
        return self.sems[key]

    def sbuf(self, name, shape, dtype):
        t = self.stack.enter_context(self.nc.sbuf_tensor(name, list(shape), dtype))
        return Buf(t[:], name)

    def psum(self, name, shape, dtype=F32):
        t = self.stack.enter_context(self.nc.psum_tensor(name, list(shape), dtype))
        return Buf(t[:], name)

    def dram(self, name, shape, dtype, kind="Internal"):
        t = self.nc.dram_tensor(name, list(shape), dtype, kind=kind)
        return Buf(t.ap(), name)

    def _need(self, eng, waits, key, val):
        if self.waited[eng].get(key, 0) >= val:
            return
        if val > waits.get(key, 0):
            waits[key] = val

    def _deps(self, eng, reads, writes):
        waits = {}
        for v in reads:
            for key, val in v.buf.writers.items():
                if key == eng and eng == "pe":
                    continue
                self._need(eng, waits, key, val)
        for v in writes:
            for key, val in v.buf.writers.items():
                if key == eng and (eng == "pe" or not SAME_ENGINE_SYNC):
                    continue
                self._need(eng, waits, key, val)
            for key, val in v.buf.readers.items():
                if key == eng and (eng == "pe" or not SAME_ENGINE_SYNC):
                    continue
                self._need(eng, waits, key, val)
        for key, val in waits.items():
            self.waited[eng][key] = val
        return list(waits.items())

    def _mark(self, token, reads, writes):
        key, val = token
        for v in reads:
            v.buf.readers[key] = val
        for v in writes:
            b = v.buf
            b.writers = {key: val}
            b.readers = {}

    def op(self, eng, fn, reads=(), writes=()):
        reads = [r for r in reads if isinstance(r, View)]
        writes = [w for w in writes if isinstance(w, View)]
        waits = self._deps(eng, reads, writes)
        self.seq[eng] += 1
        token = (eng, self.seq[eng])
        self._mark(token, reads, writes)
        self.streams[eng].append((waits, fn, (eng, 1)))
        self.n_inst += 1
        return token

    def dma(self, out, in_, q="sp", **kw):
        i = self.dma_count[q]
        self.dma_count[q] += 1
        r, rnd = i % DMA_K, i // DMA_K
        key = ("dma", q, r)
        waits = {}
        if rnd > 0:
            self._need(q, waits, key, 16 * rnd)
        for k2, v2 in waits.items():
            self.waited[q][k2] = v2
        w2 = self._deps(q, [in_], [out])
        allw = list(waits.items()) + w2
        token = (key, 16 * (rnd + 1))
        self.dma_last[key] = 16 * (rnd + 1)
        self._mark(token, [in_], [out])
        oa, ia = out.ap, in_.ap
        self.streams[q].append((allw, lambda e, oa=oa, ia=ia, kw=kw: e.dma_start(out=oa, in_=ia, **kw), (key, 16)))
        self.n_inst += 1
        return token

    def barrier(self):
        toks = [(e, self.seq[e]) for e in COMPUTE if self.seq[e] > 0]
        toks += list(self.dma_last.items())
        for eng in self.streams:
            waits = {}
            for key, val in toks:
                if key == eng:
                    continue
                self._need(eng, waits, key, val)
            for k2, v2 in waits.items():
                self.waited[eng][k2] = v2
            if waits:
                self.streams[eng].append((list(waits.items()), None, None))

    def emit(self):
        nc = self.nc
        for e in COMPUTE:
            self.sem(e)
        for q in self.dma_count:
            for r in range(DMA_K):
                self.sem(("dma", q, r))
        engmap = {"pe": "tensor", "act": "scalar", "dve": "vector", "pool": "gpsimd", "sp": "sync"}
        with nc.Block() as block:
            for e, attr in engmap.items():
                stream = self.streams[e]

                def body(engine, stream=stream):
                    for waits, fn, inc in stream:
                        for key, val in waits:
                            engine.wait_ge(self.sems[key], val)
                        if fn is None:
                            continue
                        ins = fn(engine)
                        if inc is not None:
                            ins.then_inc(self.sems[inc[0]], inc[1])

                getattr(block, attr)(body)
        self.stack.close()

    def matmul(self, out, lhsT, rhs, start=True, stop=True, **kw):
        oa, la, ra = out.ap, lhsT.ap, rhs.ap
        return self.op("pe", lambda e: e.matmul(oa, la, ra, start=start, stop=stop, **kw),
                       reads=[lhsT, rhs], writes=[out])

    def transpose(self, out, in_, ident):
        oa, ia, da = out.ap, in_.ap, ident.ap
        return self.op("pe", lambda e: e.transpose(oa, ia, da), reads=[in_, ident], writes=[out])

    def act(self, out, in_, func, bias=None, scale=None, accum_out=None):
        oa, ia = out.ap, in_.ap
        kw = {}
        reads = [in_]
        writes = [out]
        if bias is not None:
            if isinstance(bias, View):
                kw["bias"] = bias.ap
                reads.append(bias)
            else:
                kw["bias"] = bias
        if scale is not None:
            if isinstance(scale, View):
                kw["scale"] = scale.ap
                reads.append(scale)
            else:
                kw["scale"] = scale
        if accum_out is not None:
            kw["accum_out"] = accum_out.ap
            writes.append(accum_out)
        return self.op("act", lambda e: e.activation(oa, ia, func, **kw), reads=reads, writes=writes)

    def tt(self, out, in0, in1, op, eng="dve"):
        oa, a, b = out.ap, in0.ap, in1.ap
        return self.op(eng, lambda e: e.tensor_tensor(oa, a, b, op), reads=[in0, in1], writes=[out])

    def ts(self, out, in0, s1, s2, op0, op1=None, eng="dve"):
        oa, a = out.ap, in0.ap
        reads = [in0]
        s1a = s1.ap if isinstance(s1, View) else s1
        s2a = s2.ap if isinstance(s2, View) else s2
        if isinstance(s1, View):
            reads.append(s1)
        if isinstance(s2, View):
            reads.append(s2)
        kw = {}
        if op1 is not None:
            kw["op1"] = op1
        return self.op(eng, lambda e: e.tensor_scalar(oa, a, s1a, s2a, op0, **kw), reads=reads, writes=[out])

    def stt(self, out, in0, scalar, in1, op0, op1, eng="dve"):
        oa, a, b = out.ap, in0.ap, in1.ap
        reads = [in0, in1]
        sa = scalar.ap if isinstance(scalar, View) else scalar
        if isinstance(scalar, View):
            reads.append(scalar)
        eng = "dve"
        return self.op(eng, lambda e: e.scalar_tensor_tensor(oa, a, sa, b, op0, op1), reads=reads, writes=[out])

    def copy(self, out, in_, eng="dve"):
        oa, ia = out.ap, in_.ap
        if eng == "act":
            return self.op(eng, lambda e: e.copy(oa, ia), reads=[in_], writes=[out])
        return self.op(eng, lambda e: e.tensor_copy(oa, ia), reads=[in_], writes=[out])

    def memset(self, out, val, eng="pool"):
        oa = out.ap
        return self.op(eng, lambda e: e.memset(oa, val), reads=[], writes=[out])

    def reduce(self, out, in_, op, axis=AX.X, eng="dve"):
        oa, ia = out.ap, in_.ap
        return self.op(eng, lambda e: e.tensor_reduce(oa, ia, axis, op), reads=[in_], writes=[out])

    def recip(self, out, in_):
        oa, ia = out.ap, in_.ap
        return self.op("dve", lambda e: e.reciprocal(oa, ia), reads=[in_], writes=[out])

    def scan(self, out, d0, d1, initial, op0=ALU.mult, op1=ALU.add):
        oa, a, b = out.ap, d0.ap, d1.ap
        reads = [d0, d1]
        ia = initial.ap if isinstance(initial, View) else initial
        if isinstance(initial, View):
            reads.append(initial)
        return self.op("dve", lambda e: e.tensor_tensor_scan(oa, a, b, ia, op0, op1), reads=reads, writes=[out])


class Arena:
    def __init__(self, fw, words):
        self.fw = fw
        self.words = words
        self.base = fw.sbuf("arena", [128, words], F32)
        self.off = 0
        self.n = 0

    def alloc(self, nelem, dtype=F32, name=None):
        if dtype == BF16:
            w = (nelem + 1) // 2
        else:
            w = nelem
        w = (w + 7) // 8 * 8
        assert self.off + w <= self.words, "arena overflow %d + %d > %d" % (self.off, w, self.words)
        ap = self.base.t[:, self.off:self.off + w]
        if dtype == BF16:
            ap = ap.bitcast(BF16)[:, 0:nelem]
        else:
            ap = ap[:, 0:nelem]
        self.off += w
        self.n += 1
        return Buf(ap, name or ("a%d" % self.n))

    def reset(self):
        self.fw.barrier()
        self.off = 0


def _bf16(a):
    return np.asarray(a, dtype=np.float32).astype(ml_dtypes.bfloat16)


def host_constants():
    k = {}
    k["k_ident_bf"] = _bf16(np.eye(128))
    k["k_ident_f"] = np.eye(128, dtype=np.float32)
    t = np.arange(SEQ)
    rows = (t // 64).astype(np.float32)
    cols = (t % 64).astype(np.float32)
    inv = (np.float32(10000.0) ** (-np.arange(0, 32, 2, dtype=np.float32) / np.float32(32))).astype(np.float32)
    ar = (rows[:, None] * inv[None, :]).astype(np.float32)
    ac = (cols[:, None] * inv[None, :]).astype(np.float32)
    cr, sr, cc, sc = np.cos(ar), np.sin(ar), np.cos(ac), np.sin(ac)
    cos_t = np.concatenate([cr, cr, cc, cc], axis=1).astype(np.float32)
    sin_t = np.concatenate([-sr, sr, -sc, sc], axis=1).astype(np.float32)
    k["k_rope_cos"] = cos_t.reshape(16, 128, 64)
    k["k_rope_sin"] = sin_t.reshape(16, 128, 64)
    n = np.arange(SEQ, dtype=np.int64)
    nk = (n[:, None] * n[None, :]) % SEQ
    ang = nk.astype(np.float64) * (2.0 * np.pi / SEQ)
    k["k_dft"] = np.stack([_bf16(np.cos(ang)), _bf16(np.sin(ang))])
    m = np.arange(64, dtype=np.int64)
    a64 = ((m[:, None] * m[None, :]) % 64).astype(np.float64) * (2.0 * np.pi / 64)
    cb = np.zeros((256, 256), np.float32)
    sb = np.zeros((256, 256), np.float32)
    for h in range(4):
        cb[h * 64:(h + 1) * 64, h * 64:(h + 1) * 64] = np.cos(a64)
        sb[h * 64:(h + 1) * 64, h * 64:(h + 1) * 64] = np.sin(a64)
    k["k_cblk"] = cb
    k["k_sblk"] = sb
    s_idx = np.arange(128) // 16
    k["k_s5mask"] = (s_idx[None, :] >= s_idx[:, None]).astype(np.float32)
    k["k_tau"] = np.tile(np.arange(-7, 9, dtype=np.float32)[None, :], (128, 1))
    k["k_nidx"] = np.tile(np.arange(NCH, dtype=np.float32)[None, :], (128, 1))
    qc = np.arange(64)
    cs = np.clip(qc - 8, 0, 48)
    kc = np.arange(64)
    ok = (kc[:, None] >= cs[None, :]) & (kc[:, None] < cs[None, :] + 16)
    blk = np.where(ok, 0.0, BIG).astype(np.float32)
    k["k_colmask"] = np.tile(blk, (2, 8))
    return k


def na_plan():
    pats = {}
    plan = []
    for t in range(16):
        lst = []
        for u in range(16):
            pat = []
            anyv = False
            for krl in range(2):
                for qrl in range(2):
                    kr, qr = 2 * u + krl, 2 * t + qrl
                    rs = min(max(qr - 4, 0), 24)
                    if rs <= kr < rs + 8:
                        pat.append(kr - qr + 7)
                        anyv = True
                    else:
                        pat.append(None)
            if anyv:
                pat = tuple(pat)
                if pat not in pats:
                    pats[pat] = len(pats)
                lst.append((u, pats[pat]))
        plan.append(lst)
    plist = [None] * len(pats)
    for p, i in pats.items():
        plist[i] = p
    return plan, plist


def build(stages=None, dbg=(), layers=(0, 1), batches=(0, 1)):
    nc = bass.Bass("TRN2", target_bir_lowering=False)
    fw = FW(nc)
    dbg = set(dbg)

    def ein(name, shape, dt=F32):
        return fw.dram(name, shape, dt, kind="ExternalInput")

    def scratch(name, shape, dt):
        return fw.dram(name, shape, dt, kind=("ExternalOutput" if name in dbg else "Internal"))

    x = ein("x", [NB, SEQ, D])
    ctx = ein("ctx", [NB, CTXL, D])
    c = ein("c", [NB, D])
    c_ctx = ein("c_ctx", [1, D])
    w_mod = ein("w_mod", [2, D, 6 * D])
    b_mod = ein("b_mod", [2, 6 * D])
    g_norm1 = ein("g_norm1", [2, D])
    w_in = ein("w_in", [2, D, 1792])
    att_q_gain = ein("att_q_gain", [2, 64])
    att_k_gain = ein("att_k_gain", [2, 64])
    na_q_gain = ein("na_q_gain", [2, 64])
    na_k_gain = ein("na_k_gain", [2, 64])
    na_rel_bias = ein("na_rel_bias", [2, 60, 31])
    w_fourier = ein("w_fourier", [2, 256, 256])
    ssm_lam_re = ein("ssm_lam_re", [2, 2, 16, 64])
    ssm_lam_im = ein("ssm_lam_im", [2, 2, 16, 64])
    ssm_log_dt = ein("ssm_log_dt", [2, 2, 16])
    ssm_b_re = ein("ssm_b_re", [2, 2, 16, 64, 16])
    ssm_b_im = ein("ssm_b_im", [2, 2, 16, 64, 16])
    ssm_c_re = ein("ssm_c_re", [2, 2, 16, 16, 64])
    ssm_c_im = ein("ssm_c_im", [2, 2, 16, 16, 64])
    ssm_d = ein("ssm_d", [2, 256])
    w_glu = ein("w_glu", [2, 256, 256])
    b_glu = ein("b_glu", [2, 256])
    g_group = ein("g_group", [2, D])
    w_out = ein("w_out", [2, D, D])
    g_norm2 = ein("g_norm2", [2, D])
    w_ff1 = ein("w_ff1", [2, D, DFF])
    w_ff3 = ein("w_ff3", [2, D, DFF])
    w_ff2 = ein("w_ff2", [2, DFF, D])
    k_ident_bf = ein("k_ident_bf", [128, 128], BF16)
    k_ident_f = ein("k_ident_f", [128, 128])
    k_rope_cos = ein("k_rope_cos", [16, 128, 64])
    k_rope_sin = ein("k_rope_sin", [16, 128, 64])
    k_dft = ein("k_dft", [2, SEQ, SEQ], BF16)
    k_cblk = ein("k_cblk", [256, 256])
    k_sblk = ein("k_sblk", [256, 256])
    k_s5mask = ein("k_s5mask", [128, 128])
    k_tau = ein("k_tau", [128, 16])
    k_nidx = ein("k_nidx", [128, NCH])
    k_colmask = ein("k_colmask", [128, 512])

    out = fw.dram("out", [NB, SEQ, D], F32, kind="ExternalOutput")

    XR = scratch("XR", [NB, T, D], F32)
    MODV = scratch("MODV", [2, 3, 6 * D], F32)
    WIN = scratch("WIN", [2, 128, 8, 1792], BF16)
    WOUT = scratch("WOUT", [2, 128, 8, D], BF16)
    W1 = scratch("W1", [2, NF, 128, 8, 128], BF16)
    W3 = scratch("W3", [2, NF, 128, 8, 128], BF16)
    W2 = scratch("W2", [2, DFF, D], BF16)
    QK = scratch("QK", [NB, 896, T], BF16)
    VV = scratch("VV", [NB, T, 390], BF16)
    FIN = scratch("FIN", [NB, T, 256], BF16)
    UT = scratch("UT", [NB, 256, T], BF16)
    OO = scratch("OO", [NB, T, D], BF16)
    PADT = scratch("PADT", [2, 61, 8192], F32)
    BIAS = scratch("BIAS", [2, 32, 128, 512], BF16)
    WCS = scratch("WCS", [2, 2, 256, 256], BF16)
    SCRU = scratch("SCRU", [2, 16, 8, 16, NCH], BF16)
    SCRY = scratch("SCRY", [2, 16, 8, 16, NCH], BF16)
    S5T = scratch("S5T", [2, 2, 16, 128, 128], BF16)
    S5B = scratch("S5B", [2, 2, 16, 128, 2, 64], BF16)
    S5C = scratch("S5C", [2, 2, 8, 128, 2, 128], BF16)
    S5D = scratch("S5D", [2, 2, 128, 8], F32)
    S5COS = scratch("S5COS", [2, 2, 128, 8, NCH], F32)
    S5SIN = scratch("S5SIN", [2, 2, 128, 8, NCH], F32)

    ar = Arena(fw, 47 * 1024)
    banks = [fw.psum("bank%d" % i, [128, 512], F32) for i in range(8)]

    all_stages = ["W", "M", "A", "GQA", "NAB", "NA", "FW", "F", "S5P", "S5", "C"]
    if stages is None:
        stages = all_stages
    stages = set(stages)

    def bank_bf(i):
        return View(banks[i], banks[i].t.bitcast(BF16))

    def rstd_from(ss, n, rs, tmp):
        fw.act(tmp, ss, AF.Sqrt, scale=1.0 / n, bias=EPS)
        fw.recip(rs, tmp)

    def stage_W():
        for l in range(2):
            for half in range(2):
                sl = slice(half * 896, (half + 1) * 896)
                fw.dma(WIN[l][:, :, sl], w_in[l].rearrange("(kc p) n -> p kc n", p=128)[:, :, sl], q="pool")
            fw.dma(WOUT[l], w_out[l].rearrange("(kc p) n -> p kc n", p=128), q="pool")
            for f in range(NF):
                fw.dma(W1[l][f], w_ff1[l][:, f * 128:(f + 1) * 128].rearrange("(kc p) n -> p kc n", p=128), q="pool")
                fw.dma(W3[l][f], w_ff3[l][:, f * 128:(f + 1) * 128].rearrange("(kc p) n -> p kc n", p=128), q="pool")
            for f0 in range(0, DFF, 704):
                fw.dma(W2[l][f0:f0 + 704, :], w_ff2[l][f0:f0 + 704, :], q="pool")

    def stage_M(l):
        ar.reset()
        cT = ar.alloc(24)
        cT3 = cT.v("p (k r) -> p k r", r=3)
        for r in range(2):
            fw.dma(cT3[:, :, r], c[r].rearrange("(kc p) -> p kc", p=128), allow_slow_non_contiguous=True)
        fw.dma(cT3[:, :, 2], c_ctx[0].rearrange("(kc p) -> p kc", p=128), allow_slow_non_contiguous=True)
        cS = ar.alloc(24)
        fw.act(cS.ap(), cT.ap(), AF.Silu)
        cS3 = cS.v("p (k r) -> p k r", r=3)
        bm = ar.alloc(6 * D)
        fw.dma(bm[0:3, :], b_mod[l:l + 1, :].bc([3, 6 * D]))
        g1b = ar.alloc(D)
        g2b = ar.alloc(D)
        fw.dma(g1b[0:3, :], g_norm1[l:l + 1, :].bc([3, D]))
        fw.dma(g2b[0:3, :], g_norm2[l:l + 1, :].bc([3, D]))
        mv = ar.alloc(6 * D)
        wbuf = [ar.alloc(8 * 512) for _ in range(2)]
        for nb in range(12):
            wt = wbuf[nb % 2].v("p (k n) -> p k n", k=8)
            fw.dma(wt, w_mod[l][:, nb * 512:(nb + 1) * 512].rearrange("(kc p) n -> p kc n", p=128))
            ps = banks[nb % 2]
            for kc in range(8):
                fw.matmul(ps[0:3, :], cS3[:, kc, :], wt[:, kc, :], start=(kc == 0), stop=(kc == 7))
            fw.tt(mv[0:3, nb * 512:(nb + 1) * 512], ps[0:3, :], bm[0:3, nb * 512:(nb + 1) * 512], ALU.add)
        fw.stt(mv[0:3, D:2 * D], mv[0:3, D:2 * D], 1.0, g1b[0:3, :], ALU.add, ALU.mult)
        fw.stt(mv[0:3, 4 * D:5 * D], mv[0:3, 4 * D:5 * D], 1.0, g2b[0:3, :], ALU.add, ALU.mult)
        fw.dma(MODV[l], mv[0:3, :])

    def modrow(l, r, idx):
        return MODV[l][r:r + 1, idx * D:(idx + 1) * D].bc([128, D])

    def load_consts_ident():
        idb = ar.alloc(128, BF16)
        fw.dma(idb.ap(), k_ident_bf.ap())
        return idb

    def stage_A(b, l):
        ar.reset()
        idb = load_consts_ident()
        win = ar.alloc(8 * 1792, BF16)
        win3 = win.v("p (k n) -> p k n", k=8)
        fw.dma(win3, WIN[l])
        A1 = ar.alloc(D); B1 = ar.alloc(D); A1c = ar.alloc(D); B1c = ar.alloc(D)
        fw.dma(A1.ap(), modrow(l, b, 1)); fw.dma(B1.ap(), modrow(l, b, 0))
        fw.dma(A1c.ap(), modrow(l, 2, 1)); fw.dma(B1c.ap(), modrow(l, 2, 0))
        G = ar.alloc(14 * 64)
        G3 = G.v("p (h d) -> p h d", d=64)
        gsrc = [att_q_gain] * 4 + [att_k_gain] * 2 + [na_q_gain] * 4 + [na_k_gain] * 4
        for h in range(14):
            fw.dma(G3[:, h, :], gsrc[h][l:l + 1, :].bc([128, 64]))
        COS = ar.alloc(16 * 64); SIN = ar.alloc(16 * 64)
        COS3 = COS.v("p (t d) -> p t d", d=64); SIN3 = SIN.v("p (t d) -> p t d", d=64)
        fw.dma(COS3, k_rope_cos.v("t p d -> p t d"))
        fw.dma(SIN3, k_rope_sin.v("t p d -> p t d"))
        two = lambda n, dt=F32: [ar.alloc(n, dt) for _ in range(2)]
        xs = two(D); sqj = ar.alloc(D, BF16)
        ss = two(4); sd = two(4); rstd = two(4)
        tt_ = two(D); hb = two(D, BF16); hT = two(D, BF16)
        qks = two(896); sq2 = two(896); ssh = two(16); sdh = two(16); rsh = two(16)
        qkn = two(896); vsw = two(384); m1 = two(384); m2 = two(384)
        qkb = two(896, BF16); qkT = two(896, BF16)
        vaug = two(6 * 65, BF16); finb = two(256, BF16); uTb = two(256, BF16)
        for vb in vaug:
            fw.memset(vb.v("p (h e) -> p h e", e=65)[:, :, 64:65], 1.0)

        def load(i):
            if i >= NT:
                return
            if l == 0:
                src = ctx[b][i * 128:(i + 1) * 128, :] if i < 2 else x[b][(i - 2) * 128:(i - 1) * 128, :]
            else:
                src = XR[b][i * 128:(i + 1) * 128, :]
            fw.dma(xs[i % 2].ap(), src)

        def front(i):
            j = i % 2
            isctx = i < 2
            fw.act(sqj.ap(), xs[j].ap(), AF.Square, accum_out=ss[j][:, 0:1])
            rstd_from(ss[j][:, 0:1], D, rstd[j][:, 0:1], sd[j][:, 0:1])
            fw.stt(tt_[j].ap(), xs[j].ap(), rstd[j][:, 0:1], (A1c if isctx else A1).ap(), ALU.mult, ALU.mult)
            fw.tt(hb[j].ap(), tt_[j].ap(), (B1c if isctx else B1).ap(), ALU.add, eng="pool")
            pT = bank_bf(4 + 2 * j)
            for k in range(8):
                fw.transpose(pT[:, k * 128:(k + 1) * 128], hb[j][:, k * 128:(k + 1) * 128], idb.ap())
            fw.copy(hT[j].ap(), pT[:, 0:D], eng="act")
            hT3 = hT[j].v("p (k n) -> p k n", k=8)
            for nb in range(3):
                for k in range(8):
                    fw.matmul(banks[nb].ap(), hT3[:, k, :], win3[:, k, nb * 512:(nb + 1) * 512], start=(k == 0), stop=(k == 7))
            for m in range(2):
                for k in range(8):
                    fw.matmul(banks[3][:, m * 128:(m + 1) * 128], win3[:, k, 1536 + m * 128:1536 + (m + 1) * 128], hT3[:, k, :],
                              start=(k == 0), stop=(k == 7))

        def mid(i):
            j = i % 2
            va3 = vaug[j].v("p (h e) -> p h e", e=65)
            fw.copy(qks[j][:, 0:384], banks[0][:, 0:384], eng="act")
            fw.copy(va3[:, 0:2, 0:64], banks[0][:, 384:512].rearrange("p (h d) -> p h d", d=64), eng="act")
            fw.copy(qks[j][:, 384:896], banks[1][:, 0:512], eng="dve")
            fw.copy(va3[:, 2:6, 0:64], banks[2][:, 0:256].rearrange("p (h d) -> p h d", d=64), eng="act")
            fw.copy(finb[j].ap(), banks[2][:, 256:512], eng="act")
            fw.copy(uTb[j].ap(), banks[3][:, 0:256], eng="dve")
            fw.dma(VV[b][i * 128:(i + 1) * 128, :], vaug[j].ap())
            fw.dma(FIN[b][i * 128:(i + 1) * 128, :], finb[j].ap())
            fw.dma(UT[b].rearrange("(m p) n -> p m n", p=128)[:, :, i * 128:(i + 1) * 128],
                   uTb[j].v("p (m n) -> p m n", m=2))

        def back(i):
            j = i % 2
            isctx = i < 2
            fw.act(sq2[j].ap(), qks[j].ap(), AF.Square)
            fw.reduce(ssh[j][:, 0:14], sq2[j].v("p (h d) -> p h d", d=64), ALU.add)
            rstd_from(ssh[j][:, 0:14], 64, rsh[j][:, 0:14], sdh[j][:, 0:14])
            qkn3 = qkn[j].v("p (h d) -> p h d", d=64)
            fw.tt(qkn3, qks[j].v("p (h d) -> p h d", d=64), rsh[j][:, 0:14].unsq(2).bc([128, 14, 64]), ALU.mult)
            fw.tt(qkn[j].ap(), qkn[j].ap(), G.ap(), ALU.mult, eng="pool")
            if isctx:
                fw.copy(qkb[j][:, 0:384], qkn[j][:, 0:384], eng="pool")
            else:
                ti = i - 2
                fw.copy(vsw[j].v("p (a two x) -> p a two x", two=2, x=16),
                        qkn[j][:, 0:384].rearrange("p (a two x) -> p a two x", two=2, x=16)[:, :, ::-1, :], eng="pool")
                fw.tt(m1[j].v("p (h d) -> p h d", d=64), qkn3[:, 0:6, :], COS3[:, ti, :].unsq(1).bc([128, 6, 64]), ALU.mult)
                fw.tt(m2[j].v("p (h d) -> p h d", d=64), vsw[j].v("p (h d) -> p h d", d=64),
                      SIN3[:, ti, :].unsq(1).bc([128, 6, 64]), ALU.mult, eng="pool")
                fw.tt(qkb[j][:, 0:384], m1[j].ap(), m2[j].ap(), ALU.add)
            fw.copy(qkb[j][:, 384:896], qkn[j][:, 384:896], eng="pool")
            pq = bank_bf(5 + 2 * j)
            for jj in range(7):
                fw.transpose(pq[:, jj * 128:(jj + 1) * 128], qkb[j][:, jj * 128:(jj + 1) * 128], idb.ap())
            fw.copy(qkT[j].ap(), pq[:, 0:896], eng="act")
            fw.dma(QK[b].rearrange("(j p) n -> p j n", p=128)[:, :, i * 128:(i + 1) * 128],
                   qkT[j].v("p (j n) -> p j n", j=7))

        load(0)
        load(1)
        for i in range(NT):
            front(i)
            load(i + 2)
            mid(i)
            if i >= 1:
                back(i - 1)
        back(NT - 1)

    def attn_block(QT3, KT3, V3, qhead, khead, vcol, qcol0, nq, ktiles, PT, psS, psO, st):
        nsub = nq // 128
        nk = len(ktiles)

        def pv(ki, kt, pt):
            for sub in range(nsub):
                fw.matmul(psO[:, sub * 65:(sub + 1) * 65], pt[:, sub * 128:(sub + 1) * 128], V3[:, kt, vcol:vcol + 65],
                          start=(ki == 0 and sub == 0), stop=(ki == nk - 1 and sub == nsub - 1))

        pend = None
        for ki, kt in enumerate(ktiles):
            pS = psS[st["s"] % len(psS)]
            st["s"] += 1
            pt = PT[st["p"] % len(PT)]
            st["p"] += 1
            fw.matmul(pS[:, 0:nq], KT3[0:64, khead, kt * 128:(kt + 1) * 128], QT3[0:64, qhead, qcol0:qcol0 + nq])
            fw.act(pt[:, 0:nq], pS[:, 0:nq], AF.Exp, scale=0.125)
            if pend is not None:
                pv(*pend)
            pend = (ki, kt, pt)
        pv(*pend)

    def attn_finish(psO, nsub, rden, ob):
        po3 = psO[:, 0:nsub * 65].rearrange("p (s e) -> p s e", e=65)
        fw.recip(rden[:, 0:nsub], po3[:, :, 64:65].rearrange("p s e -> p (s e)"))
        fw.tt(ob.v("p (s d) -> p s d", d=64)[:, 0:nsub, :], po3[:, :, 0:64],
              rden[:, 0:nsub].unsq(2).bc([128, nsub, 64]), ALU.mult)

    def stage_GQA(b, l):
        ar.reset()
        QT = ar.alloc(4 * T, BF16); KT = ar.alloc(2 * T, BF16); VA = ar.alloc(NT * 130, BF16)
        QT3 = QT.v("p (h n) -> p h n", h=4); KT3 = KT.v("p (h n) -> p h n", h=2)
        VA3 = VA.v("p (t c) -> p t c", c=130)
        fw.dma(QT3[0:64], QK[b][0:256, :].rearrange("(h d) n -> d h n", d=64))
        fw.dma(KT3[0:64], QK[b][256:384, :].rearrange("(h d) n -> d h n", d=64))
        fw.dma(VA3, VV[b].rearrange("(t p) c -> p t c", p=128)[:, :, 0:130])
        PT = [ar.alloc(512, BF16) for _ in range(3)]
        rden = ar.alloc(4)
        obs = [ar.alloc(256, BF16) for _ in range(2)]
        st = {"s": 0, "p": 0}
        psS = [banks[0], banks[1]]
        cnt = 0
        O3 = OO[b].rearrange("(t p) c -> p t c", p=128)
        blocks = [(256 + qb * 512, 512, list(range(NT)), 2 + qb * 4) for qb in range(4)]
        if l == 0:
            blocks.append((0, 256, [0, 1], 0))
        for h in range(4):
            kv = h // 2
            for (qc0, nq, kts, t0) in blocks:
                psO = banks[2 + cnt % 2]
                ob = obs[cnt % 2]
                cnt += 1
                attn_block(QT3, KT3, VA3, h, kv, kv * 65, qc0, nq, kts, PT, psS, psO, st)
                nsub = nq // 128
                attn_finish(psO, nsub, rden, ob)
                fw.dma(O3[:, t0:t0 + nsub, h * 64:(h + 1) * 64], ob.v("p (s d) -> p s d", d=64)[:, 0:nsub, :])

    NA_PLAN, NA_PATS = na_plan()

    def stage_NAB(l):
        ar.reset()
        rb = ar.alloc(31)
        fw.dma(rb[0:60, :], na_rel_bias[l])
        pad = ar.alloc(128)
        fw.memset(pad.ap(), BIG)
        fw.ts(pad[0:60, 48:79], rb[0:60, ::-1], 8.0, None, ALU.mult)
        padt = PADT[l].ap.tensor
        base = PADT[l].ap.offset
        fw.dma(View(PADT, bass.AP(tensor=padt, offset=base, ap=[[8192, 61], [127, 64], [1, 127]])),
               pad[0:61, 0:127].unsq(1).bc([61, 64, 127]))
        cm = ar.alloc(512)
        fw.dma(cm.ap(), k_colmask.ap())
        stg = [ar.alloc(512) for _ in range(2)]
        bt = [ar.alloc(512, BF16) for _ in range(2)]
        for bid, pat in enumerate(NA_PATS):
            s_ = stg[bid % 2]
            for krl in range(2):
                for qrl in range(2):
                    dri = pat[krl * 2 + qrl]
                    for h in range(4):
                        row = 60 if dri is None else h * 15 + dri
                        src = View(PADT, bass.AP(tensor=padt, offset=base + row * 8192 + 63, ap=[[126, 64], [1, 64]]))
                        fw.dma(s_[krl * 64:(krl + 1) * 64, h * 128 + qrl * 64:h * 128 + (qrl + 1) * 64], src)
            fw.tt(bt[bid % 2].ap(), s_.ap(), cm.ap(), ALU.add)
            fw.dma(BIAS[l][bid], bt[bid % 2].ap())

    def stage_NA(b, l):
        ar.reset()
        idb = load_consts_ident()
        QT = ar.alloc(4 * T, BF16); KT = ar.alloc(4 * T, BF16); VD = ar.alloc(NT * 260, BF16)
        QT3 = QT.v("p (h n) -> p h n", h=4); KT3 = KT.v("p (h n) -> p h n", h=4)
        VD3 = VD.v("p (t c) -> p t c", c=260)
        fw.dma(QT3[0:64], QK[b][384:640, :].rearrange("(h d) n -> d h n", d=64))
        fw.dma(KT3[0:64], QK[b][640:896, :].rearrange("(h d) n -> d h n", d=64))
        fw.dma(VD3, VV[b].rearrange("(t p) c -> p t c", p=128)[:, :, 130:390])
        nbias = len(NA_PATS)
        BT = ar.alloc(nbias * 512, BF16)
        BT3 = BT.v("p (i n) -> p i n", n=512)
        fw.dma(BT3, BIAS[l][0:nbias].rearrange("i p n -> p i n"))
        PT = [ar.alloc(512, BF16) for _ in range(3)]
        rden = ar.alloc(4)
        obs = [ar.alloc(256, BF16) for _ in range(2)]
        O3 = OO[b].rearrange("(t p) c -> p t c", p=128)
        si = 0
        for t in range(16):
            qc0 = 256 + t * 128
            klist = [(2 + u, bid) for (u, bid) in NA_PLAN[t]] + [(0, None), (1, None)]
            psO = banks[2 + t % 2]
            ob = obs[t % 2]
            nk = len(klist)
            def pv(ki, kt, pt, psO=psO, nk=nk):
                for h in range(4):
                    fw.matmul(psO[:, h * 65:(h + 1) * 65], pt[:, h * 128:(h + 1) * 128], VD3[:, kt, h * 65:(h + 1) * 65],
                              start=(ki == 0 and h == 0), stop=(ki == nk - 1 and h == 3))

            pend = None
            for ki, (kt, bid) in enumerate(klist):
                pS = banks[si % 2]
                pt = PT[si % 3]
                si += 1
                for h in range(4):
                    fw.matmul(pS[:, h * 128:(h + 1) * 128], KT3[0:64, h, kt * 128:(kt + 1) * 128], QT3[0:64, h, qc0:qc0 + 128],
                              start=(h == 0), stop=(h == 3 and bid is None))
                if bid is not None:
                    fw.matmul(pS.ap(), idb.ap(), BT3[:, bid, :], start=False, stop=True)
                fw.act(pt.ap(), pS.ap(), AF.Exp, scale=0.125)
                if pend is not None:
                    pv(*pend)
                pend = (ki, kt, pt)
            pv(*pend)
            attn_finish(psO, 4, rden, ob)
            fw.dma(OO[b][(2 + t) * 128:(3 + t) * 128, 256:512], ob.ap())
        if l == 0:
            st = {"s": 0, "p": 0}
            for h in range(4):
                psO = banks[2 + h % 2]
                ob = obs[h % 2]
                attn_block(QT3, KT3, VD3, h, h, h * 65, 0, 256, [0, 1], PT, [banks[0], banks[1]], psO, st)
                attn_finish(psO, 2, rden, ob)
                fw.dma(O3[:, 0:2, 256 + h * 64:256 + (h + 1) * 64], ob.v("p (s d) -> p s d", d=64)[:, 0:2, :])

    def stage_FW(l):
        ar.reset()
        wf = ar.alloc(2 * 256)
        wf3 = wf.v("p (k n) -> p k n", k=2)
        fw.dma(wf3, w_fourier[l].rearrange("(k p) n -> p k n", p=128))
        for ci, ktab in enumerate((k_cblk, k_sblk)):
            cb = ar.alloc(2 * 256)
            cb3 = cb.v("p (k n) -> p k n", k=2)
            fw.dma(cb3, ktab.v("(k p) n -> p k n", p=128))
            wo = ar.alloc(2 * 256, BF16)
            wo3 = wo.v("p (k n) -> p k n", k=2)
            for fo in range(2):
                ps = banks[(ci * 2 + fo) % 4]
                for k in range(2):
                    fw.matmul(ps[:, 0:256], cb3[:, k, fo * 128:(fo + 1) * 128], wf3[:, k, :], start=(k == 0), stop=(k == 1))
                fw.act(wo3[:, fo, :], ps[:, 0:256], AF.Copy, scale=(1.0 if ci == 0 else -1.0))
            fw.dma(WCS[l][ci].rearrange("(k p) n -> p k n", p=128), wo3)

    def stage_F(b, l):
        ar.reset()
        fin = ar.alloc(NT * 256, BF16)
        fin3 = fin.v("p (t f) -> p t f", f=256)
        fw.dma(fin3, FIN[b].rearrange("(t p) f -> p t f", p=128))
        wcs = ar.alloc(2 * 2 * 256, BF16)
        wcs4 = wcs.v("p (c k n) -> p c k n", c=2, k=2)
        for ci in range(2):
            fw.dma(wcs4[:, ci], WCS[l][ci].rearrange("(k p) n -> p k n", p=128))
        tabs = [ar.alloc(512, BF16) for _ in range(6)]
        G = [ar.alloc(4 * 512, BF16) for _ in range(2)]
        ofb = [ar.alloc(256, BF16) for _ in range(2)]
        ti = 0
        oi = 0

        def second_stage(Gv, nk, tile0, scale):
            nonlocal oi
            for kt in range(nk // 128):
                ps = banks[4 + oi % 2]
                n_ = 0
                for ci in range(2):
                    for fc in range(2):
                        fw.matmul(ps[:, 0:256], Gv[:, ci, fc, kt * 128:(kt + 1) * 128], wcs4[:, ci, fc, :],
                                  start=(n_ == 0), stop=(n_ == 3))
                        n_ += 1
                o_ = ofb[oi % 2]
                oi += 1
                fw.act(o_.ap(), ps[:, 0:256], AF.Copy, scale=scale)
                fw.dma(OO[b][(tile0 + kt) * 128:(tile0 + kt + 1) * 128, 512:768], o_.ap())

        for kb in range(4):
            for ncix in range(16):
                for ci in range(2):
                    tb = tabs[ti % 6]
                    ti += 1
                    fw.dma(tb.ap(), k_dft[ci][ncix * 128:(ncix + 1) * 128, kb * 512:(kb + 1) * 512])
                    for fc in range(2):
                        fw.matmul(banks[ci * 2 + fc].ap(), fin3[:, 2 + ncix, fc * 128:(fc + 1) * 128], tb.ap(),
                                  start=(ncix == 0), stop=(ncix == 15))
            Gb = G[kb % 2]
            Gv = Gb.v("p (c f n) -> p c f n", c=2, f=2)
            for ci in range(2):
                for fc in range(2):
                    if (ci + fc) % 2 == 0:
                        fw.copy(Gv[:, ci, fc, :], banks[ci * 2 + fc].ap(), eng="act")
                    else:
                        fw.copy(Gv[:, ci, fc, :], banks[ci * 2 + fc].ap(), eng="dve")
            second_stage(Gv, 512, 2 + kb * 4, 1.0 / math.sqrt(SEQ * 64.0))
        if l == 0:
            for ncix in range(2):
                for ci in range(2):
                    tb = tabs[ti % 6]
                    ti += 1
                    fw.dma(tb[:, 0:256], k_dft[ci][ncix * 1024:(ncix + 1) * 1024:8, 0:256])
                    for fc in range(2):
                        fw.matmul(banks[ci * 2 + fc][:, 0:256], fin3[:, ncix, fc * 128:(fc + 1) * 128], tb[:, 0:256],
                                  start=(ncix == 0), stop=(ncix == 1))
            Gb = G[0]
            Gv = Gb.v("p (c f n) -> p c f n", c=2, f=2)
            for ci in range(2):
                for fc in range(2):
                    fw.copy(Gv[:, ci, fc, 0:256], banks[ci * 2 + fc][:, 0:256], eng=("act" if (ci + fc) % 2 == 0 else "dve"))
            second_stage(Gv, 256, 0, 1.0 / math.sqrt(CTXL * 64.0))

    def stage_C(b, l):
        ar.reset()
        idb = load_consts_ident()
        wo = ar.alloc(8 * D, BF16)
        wo3 = wo.v("p (k n) -> p k n", k=8)
        fw.dma(wo3, WOUT[l])
        gg = ar.alloc(D)
        fw.dma(gg.ap(), g_group[l:l + 1, :].bc([128, D]))
        GA1 = ar.alloc(D); A2 = ar.alloc(D); B2 = ar.alloc(D); GA2 = ar.alloc(D)
        hid = ar.alloc(NF * 512, BF16)
        hid3 = hid.v("p (f n) -> p f n", f=NF)
        h2T = ar.alloc(8 * 512, BF16)
        h2T3 = h2T.v("p (k n) -> p k n", k=8)
        x1 = ar.alloc(4 * D)
        x13 = x1.v("p (s n) -> p s n", s=4)
        wt1 = [ar.alloc(8 * 128, BF16) for _ in range(3)]
        wt3 = [ar.alloc(8 * 128, BF16) for _ in range(3)]
        w2t = [ar.alloc(D, BF16) for _ in range(3)]
        two = lambda n, dt=F32: [ar.alloc(n, dt) for _ in range(2)]
        xs = two(D); ob = two(D, BF16); sq = two(D)
        ss4 = two(4); sd4 = two(4); rs4 = two(4)
        ss = two(4); sd = two(4); rstd = two(4)
        on1 = two(D); onb = two(D, BF16); onT = two(D, BF16)
        tmpA = two(D); tmpB = two(D); h2b = two(D, BF16)
        sqj = ar.alloc(D, BF16)
        tmpC = ar.alloc(D)
        sa = two(512)
        xo = two(D)
        groups = [[2 + g * 4 + s_ for s_ in range(4)] for g in range(4)]
        if l == 0:
            groups = [[0, 1]] + groups
        cur_row = None
        wi = 0
        w2i = 0
        oi = 0

        def P1(i, j):
            if l == 0:
                src = ctx[b][i * 128:(i + 1) * 128, :] if i < 2 else x[b][(i - 2) * 128:(i - 1) * 128, :]
            else:
                src = XR[b][i * 128:(i + 1) * 128, :]
            fw.dma(xs[j].ap(), src)
            fw.dma(ob[j].ap(), OO[b][i * 128:(i + 1) * 128, :])
            fw.act(sq[j].ap(), ob[j].ap(), AF.Square)
            fw.reduce(ss4[j][:, 0:4], sq[j].v("p (g d) -> p g d", g=4), ALU.add)
            rstd_from(ss4[j][:, 0:4], 256, rs4[j][:, 0:4], sd4[j][:, 0:4])
            fw.tt(on1[j].v("p (g d) -> p g d", g=4), ob[j].v("p (g d) -> p g d", g=4),
                  rs4[j][:, 0:4].unsq(2).bc([128, 4, 256]), ALU.mult)
            fw.tt(onb[j].ap(), on1[j].ap(), gg.ap(), ALU.mult, eng="pool")
            pT = bank_bf(4 + 2 * j)
            for k in range(8):
                fw.transpose(pT[:, k * 128:(k + 1) * 128], onb[j][:, k * 128:(k + 1) * 128], idb.ap())
            fw.copy(onT[j].ap(), pT[:, 0:D], eng="act")
            onT3 = onT[j].v("p (k n) -> p k n", k=8)
            for nb in range(2):
                for k in range(8):
                    fw.matmul(banks[2 * j + nb].ap(), onT3[:, k, :], wo3[:, k, nb * 512:(nb + 1) * 512], start=(k == 0), stop=(k == 7))

        def P2(i, j, s_):
            for nb in range(2):
                sl = slice(nb * 512, (nb + 1) * 512)
                fw.tt(tmpA[j][:, sl], banks[2 * j + nb].ap(), GA1[:, sl], ALU.mult)
            fw.tt(x13[:, s_, :], tmpA[j].ap(), xs[j].ap(), ALU.add, eng="pool")
            fw.act(sqj.ap(), x13[:, s_, :], AF.Square, accum_out=ss[j][:, 0:1])
            rstd_from(ss[j][:, 0:1], D, rstd[j][:, 0:1], sd[j][:, 0:1])
            fw.stt(tmpB[j].ap(), x13[:, s_, :], rstd[j][:, 0:1], A2.ap(), ALU.mult, ALU.mult)
            fw.tt(h2b[j].ap(), tmpB[j].ap(), B2.ap(), ALU.add, eng="pool")
            pT2 = bank_bf(5 + 2 * j)
            for k in range(8):
                fw.transpose(pT2[:, k * 128:(k + 1) * 128], h2b[j][:, k * 128:(k + 1) * 128], idb.ap())
            fw.copy(h2T3[:, :, s_ * 128:(s_ + 1) * 128], pT2[:, 0:D].rearrange("p (k n) -> p k n", k=8), eng="act")

        for grp in groups:
            isctx = grp[0] < 2
            row = 2 if isctx else b
            if row != cur_row:
                fw.dma(GA1.ap(), modrow(l, row, 2)); fw.dma(A2.ap(), modrow(l, row, 4))
                fw.dma(B2.ap(), modrow(l, row, 3)); fw.dma(GA2.ap(), modrow(l, row, 5))
                cur_row = row
            nt_ = len(grp)
            ntk = nt_ * 128
            for s_, i in enumerate(grp):
                P1(i, s_ % 2)
                if s_ >= 1:
                    P2(grp[s_ - 1], (s_ - 1) % 2, s_ - 1)
            P2(grp[-1], (nt_ - 1) % 2, nt_ - 1)
            for f in range(NF):
                a_ = wt1[wi % 3]; b_ = wt3[wi % 3]
                wi += 1
                fw.dma(a_.ap(), W1[l][f].rearrange("p k n -> p (k n)"))
                fw.dma(b_.ap(), W3[l][f].rearrange("p k n -> p (k n)"))
                a3 = a_.v("p (k n) -> p k n", k=8); b3 = b_.v("p (k n) -> p k n", k=8)
                pa = banks[(f % 2) * 2]; pb = banks[1 + (f % 2) * 2]
                for k in range(8):
                    fw.matmul(pa[:, 0:ntk], a3[:, k, :], h2T3[:, k, 0:ntk], start=(k == 0), stop=(k == 7))
                for k in range(8):
                    fw.matmul(pb[:, 0:ntk], b3[:, k, :], h2T3[:, k, 0:ntk], start=(k == 0), stop=(k == 7))
                sj = sa[f % 2]
                fw.act(sj[:, 0:ntk], pa[:, 0:ntk], AF.Silu)
                fw.tt(hid3[:, f, 0:ntk], sj[:, 0:ntk], pb[:, 0:ntk], ALU.mult)
            for f in range(NF):
                w_ = w2t[w2i % 3]
                w2i += 1
                fw.dma(w_.ap(), W2[l][f * 128:(f + 1) * 128, :])
                for s_ in range(nt_):
                    for nb in range(2):
                        fw.matmul(banks[s_ * 2 + nb].ap(), hid3[:, f, s_ * 128:(s_ + 1) * 128], w_[:, nb * 512:(nb + 1) * 512],
                                  start=(f == 0), stop=(f == NF - 1))
            for s_, i in enumerate(grp):
                xoj = xo[oi % 2]
                oi += 1
                for nb in range(2):
                    sl = slice(nb * 512, (nb + 1) * 512)
                    fw.tt(tmpC[:, sl], banks[s_ * 2 + nb].ap(), GA2[:, sl], ALU.mult)
                fw.tt(xoj.ap(), tmpC.ap(), x13[:, s_, :], ALU.add, eng="pool")
                if l == 0:
                    fw.dma(XR[b][i * 128:(i + 1) * 128, :], xoj.ap())
                else:
                    fw.dma(out[b][(i - 2) * 128:(i - 1) * 128, :], xoj.ap())

    TWO_PI = 2.0 * math.pi
    CW1 = 6.28125
    CW2 = TWO_PI - 6.28125
    MAGIC = 12582912.0
    PI_SAFE = 3.141592

    def range_reduce(out, in_, kbuf, shift=0.0, eng="dve"):
        if shift != 0.0:
            fw.ts(out, in_, shift, None, ALU.add, eng=eng)
            src = out
        else:
            src = in_
        fw.ts(kbuf, src, 1.0 / TWO_PI, MAGIC, ALU.mult, ALU.add, eng=eng)
        fw.ts(kbuf, kbuf, -MAGIC, None, ALU.add, eng=eng)
        fw.stt(out, kbuf, -CW1, src, ALU.mult, ALU.add, eng=eng)
        fw.stt(out, kbuf, -CW2, out, ALU.mult, ALU.add, eng=eng)
        fw.ts(out, out, -PI_SAFE, PI_SAFE, ALU.max, ALU.min, eng=eng)

    def stage_S5P(l):
        ar.reset()
        idf = ar.alloc(128)
        fw.dma(idf.ap(), k_ident_f.ap())
        tau = ar.alloc(16)
        fw.dma(tau.ap(), k_tau.ap())
        nidx = ar.alloc(NCH)
        fw.dma(nidx.ap(), k_nidx.ap())
        msk = ar.alloc(128)
        fw.dma(msk.ap(), k_s5mask.ap())
        mark = ar.off
        for d_ in range(2):
            ar.off = mark
            if d_ == 1:
                fw.barrier()
            lamre = ar.alloc(8); lamim = ar.alloc(8); dt = ar.alloc(8)
            for gi in range(2):
                sl = slice(64 * gi, 64 * gi + 64)
                fw.dma(lamre[sl, :], ssm_lam_re[l][d_].rearrange("(q two) p -> two p q", two=2)[gi], allow_slow_non_contiguous=True)
                fw.dma(lamim[sl, :], ssm_lam_im[l][d_].rearrange("(q two) p -> two p q", two=2)[gi], allow_slow_non_contiguous=True)
                fw.dma(dt[sl, :], ssm_log_dt[l][d_:d_ + 1, :].rearrange("o (q two) -> two o q", two=2)[gi].bc([64, 8]),
                       allow_slow_non_contiguous=True)
            fw.act(dt.ap(), dt.ap(), AF.Exp)
            zr = ar.alloc(8); zi = ar.alloc(8)
            fw.tt(zr.ap(), lamre.ap(), dt.ap(), ALU.mult)
            fw.tt(zi.ap(), lamim.ap(), dt.ap(), ALU.mult)
            PZ = ar.alloc(128); PEx = ar.alloc(128); KB = ar.alloc(128); RS = ar.alloc(128); RC = ar.alloc(128)
            Are = ar.alloc(128); Aim = ar.alloc(128)
            v3 = lambda bf: bf.v("p (t q) -> p t q", q=8)
            fw.tt(v3(PZ), tau.ap().unsq(2).bc([128, 16, 8]), zr.ap().unsq(1).bc([128, 16, 8]), ALU.mult)
            fw.act(PEx.ap(), PZ.ap(), AF.Exp)
            fw.tt(v3(PZ), tau.ap().unsq(2).bc([128, 16, 8]), zi.ap().unsq(1).bc([128, 16, 8]), ALU.mult)
            range_reduce(RS.ap(), PZ.ap(), KB.ap())
            range_reduce(RC.ap(), PZ.ap(), KB.ap(), shift=0.5 * math.pi)
            fw.act(RS.ap(), RS.ap(), AF.Sin)
            fw.act(RC.ap(), RC.ap(), AF.Sin)
            fw.tt(Are.ap(), PEx.ap(), RC.ap(), ALU.mult)
            fw.tt(Aim.ap(), PEx.ap(), RS.ap(), ALU.mult)
            Are3 = v3(Are); Aim3 = v3(Aim)
            fw.dma(S5D[l][d_], v3(PEx)[:, 15, :])
            phi = ar.alloc(8); kb8 = ar.alloc(8)
            fw.ts(phi.ap(), zi.ap(), 8.0, None, ALU.mult)
            range_reduce(phi.ap(), phi.ap(), kb8.ap())
            ANG = ar.alloc(8 * NCH); KB2 = ar.alloc(8 * NCH); R2 = ar.alloc(8 * NCH)
            a3 = lambda bf: bf.v("p (q n) -> p q n", q=8)
            fw.tt(a3(ANG), nidx.ap().unsq(1).bc([128, 8, NCH]), phi.ap().unsq(2).bc([128, 8, NCH]), ALU.mult)
            range_reduce(R2.ap(), ANG.ap(), KB2.ap(), eng="pool")
            fw.act(R2.ap(), R2.ap(), AF.Sin)
            fw.dma(S5SIN[l][d_], a3(R2))
            R3 = ar.alloc(8 * NCH)
            range_reduce(R3.ap(), ANG.ap(), KB2.ap(), shift=0.5 * math.pi, eng="pool")
            fw.act(R3.ap(), R3.ap(), AF.Sin)
            fw.dma(S5COS[l][d_], a3(R3))
            nr = ar.alloc(8); den = ar.alloc(8); t8a = ar.alloc(8); t8b = ar.alloc(8); cr = ar.alloc(8); ci = ar.alloc(8)
            fw.ts(nr.ap(), Are3[:, 8, :], -1.0, None, ALU.add)
            fw.tt(den.ap(), lamre.ap(), lamre.ap(), ALU.mult)
            fw.tt(t8a.ap(), lamim.ap(), lamim.ap(), ALU.mult)
            fw.tt(den.ap(), den.ap(), t8a.ap(), ALU.add)
            fw.recip(den.ap(), den.ap())
            fw.tt(t8a.ap(), nr.ap(), lamre.ap(), ALU.mult)
            fw.tt(t8b.ap(), Aim3[:, 8, :], lamim.ap(), ALU.mult)
            fw.tt(t8a.ap(), t8a.ap(), t8b.ap(), ALU.add)
            fw.tt(cr.ap(), t8a.ap(), den.ap(), ALU.mult)
            fw.tt(t8a.ap(), Aim3[:, 8, :], lamre.ap(), ALU.mult)
            fw.tt(t8b.ap(), nr.ap(), lamim.ap(), ALU.mult)
            fw.tt(t8a.ap(), t8a.ap(), t8b.ap(), ALU.subtract)
            fw.tt(ci.ap(), t8a.ap(), den.ap(), ALU.mult)
            bre = ar.alloc(128); bim = ar.alloc(128); Bre = ar.alloc(128); Bim = ar.alloc(128); tb1 = ar.alloc(128); tb2 = ar.alloc(128)
            b3 = lambda bf: bf.v("p (q c) -> p q c", q=8)
            for gi in range(2):
                sl = slice(64 * gi, 64 * gi + 64)
                fw.dma(b3(bre)[sl], ssm_b_re[l][d_].rearrange("(q two) p c -> two p q c", two=2)[gi])
                fw.dma(b3(bim)[sl], ssm_b_im[l][d_].rearrange("(q two) p c -> two p q c", two=2)[gi])
            crb = cr.ap().unsq(2).bc([128, 8, 16]); cib = ci.ap().unsq(2).bc([128, 8, 16])
            fw.tt(b3(tb1), b3(bre), crb, ALU.mult); fw.tt(b3(tb2), b3(bim), cib, ALU.mult)
            fw.tt(Bre.ap(), tb1.ap(), tb2.ap(), ALU.subtract)
            fw.tt(b3(tb1), b3(bim), crb, ALU.mult); fw.tt(b3(tb2), b3(bre), cib, ALU.mult)
            fw.tt(Bim.ap(), tb1.ap(), tb2.ap(), ALU.add)
            Cre = ar.alloc(128); Cim = ar.alloc(128)
            for (Cdst, csrc, bk) in ((Cre, ssm_c_re, 4), (Cim, ssm_c_im, 5)):
                X = ar.alloc(128)
                for q in range(8):
                    for gi in range(2):
                        fw.dma(X[16 * q:16 * q + 16, 64 * gi:64 * gi + 64], csrc[l][d_][2 * q + gi])
                fw.transpose(banks[bk][:, 0:128], X.ap(), idf.ap())
                fw.copy(Cdst.ap(), banks[bk][:, 0:128], eng="act")
            BcR = ar.alloc(8 * 128); BcI = ar.alloc(8 * 128)
            CpR = ar.alloc(8 * 128); CpI = ar.alloc(8 * 128)
            CcR = ar.alloc(8 * 128, BF16); CcI = ar.alloc(8 * 128, BF16)
            u1 = ar.alloc(128); u2 = ar.alloc(128)
            m3 = lambda bf, q: bf.v("p (q s c) -> p q s c", q=8, s=8)[:, q]
            w3 = lambda bf: bf.v("p (s c) -> p s c", s=8)
            for q in range(8):
                ArB = Are3[:, 14:6:-1, q].unsq(2).bc([128, 8, 16]); AiB = Aim3[:, 14:6:-1, q].unsq(2).bc([128, 8, 16])
                Br = b3(Bre)[:, q, :].unsq(1).bc([128, 8, 16]); Bi = b3(Bim)[:, q, :].unsq(1).bc([128, 8, 16])
                fw.tt(w3(u1), ArB, Br, ALU.mult); fw.tt(w3(u2), AiB, Bi, ALU.mult, eng="pool")
                fw.tt(m3(BcR, q), w3(u1), w3(u2), ALU.subtract)
                fw.tt(w3(u1), ArB, Bi, ALU.mult); fw.tt(w3(u2), AiB, Br, ALU.mult, eng="pool")
                fw.tt(m3(BcI, q), w3(u1), w3(u2), ALU.add)
                Cr = b3(Cre)[:, q, :].unsq(1).bc([128, 8, 16]); Ci = b3(Cim)[:, q, :].unsq(1).bc([128, 8, 16])
                A0r = Are3[:, 0:8, q].unsq(2).bc([128, 8, 16]); A0i = Aim3[:, 0:8, q].unsq(2).bc([128, 8, 16])
                fw.tt(w3(u1), A0r, Cr, ALU.mult); fw.tt(w3(u2), A0i, Ci, ALU.mult, eng="pool")
                fw.tt(m3(CpR, q), w3(u1), w3(u2), ALU.subtract)
                fw.tt(w3(u1), A0r, Ci, ALU.mult); fw.tt(w3(u2), A0i, Cr, ALU.mult, eng="pool")
                fw.stt(m3(CpI, q), w3(u1), -1.0, w3(u2), ALU.mult, ALU.subtract)
                A1r = Are3[:, 8:16, q].unsq(2).bc([128, 8, 16]); A1i = Aim3[:, 8:16, q].unsq(2).bc([128, 8, 16])
                fw.tt(w3(u1), A1r, Cr, ALU.mult); fw.tt(w3(u2), A1i, Ci, ALU.mult, eng="pool")
                fw.tt(m3(CcR, q), w3(u1), w3(u2), ALU.subtract)
                fw.tt(w3(u1), A1r, Ci, ALU.mult); fw.tt(w3(u2), A1i, Cr, ALU.mult, eng="pool")
                fw.stt(m3(CcI, q), w3(u1), -1.0, w3(u2), ALU.mult, ALU.subtract)
            fw.dma(S5C[l][d_].rearrange("q p r n -> p q r n")[:, :, 0, :], CcR.v("p (q n) -> p q n", q=8))
            fw.dma(S5C[l][d_].rearrange("q p r n -> p q r n")[:, :, 1, :], CcI.v("p (q n) -> p q n", q=8))
            Tb = [ar.alloc(128, BF16) for _ in range(2)]
            BT = [ar.alloc(256, BF16) for _ in range(2)]
            qv = lambda bf, q: bf.v("p (q n) -> p q n", q=8)[:, q, :]
            for q in range(8):
                for gi in range(2):
                    g = 2 * q + gi
                    sl = slice(64 * gi, 64 * gi + 64)
                    ps = banks[gi]
                    fw.matmul(ps[:, 0:128], qv(BcR, q)[sl], qv(CpR, q)[sl], start=True, stop=False)
                    fw.matmul(ps[:, 0:128], qv(BcI, q)[sl], qv(CpI, q)[sl], start=False, stop=True)
                    tb_ = Tb[g % 2]
                    fw.tt(tb_.ap(), ps[:, 0:128], msk.ap(), ALU.mult)
                    fw.dma(S5T[l][d_][g], tb_.ap())
                bt_ = BT[q % 2]
                for ri, src in enumerate((BcR, BcI)):
                    ps = banks[2 + ri]
                    fw.transpose(ps[:, 0:128], qv(src, q), idf.ap())
                    fw.copy(bt_.v("p (g r n) -> p g r n", g=2, r=2)[:, :, ri, :], ps[:, 0:128].rearrange("p (g n) -> p g n", g=2), eng="act")
                fw.dma(S5B[l][d_][2 * q:2 * q + 2].rearrange("g sc r n -> sc g (r n)"), bt_.v("p (g x) -> p g x", g=2))

    def stage_S5(b, l):
        ar.reset()
        idb = load_consts_ident()
        uT = ar.alloc(2 * T, BF16)
        uT3 = uT.v("p (m n) -> p m n", m=2)
        fw.dma(uT3, UT[b].rearrange("(m p) n -> p m n", p=128))
        Dcol = ar.alloc(2)
        fw.dma(Dcol.ap(), ssm_d[l].rearrange("(m p) -> p m", p=128), allow_slow_non_contiguous=True)
        bgl = ar.alloc(2)
        fw.dma(bgl.ap(), b_glu[l].rearrange("(m p) -> p m", p=128), allow_slow_non_contiguous=True)
        wgf = ar.alloc(512)
        fw.dma(wgf.v("p (k n) -> p k n", k=2), w_glu[l].rearrange("(k p) n -> p k n", p=128))
        wgb = ar.alloc(512, BF16)
        fw.copy(wgb.ap(), wgf.ap())
        wgb3 = wgb.v("p (k n) -> p k n", k=2)
        Tm = ar.alloc(32 * 128, BF16); T3 = Tm.v("p (g n) -> p g n", g=32)
        fw.dma(T3, S5T[l].rearrange("d g sc n -> sc (d g) n"))
        Bm = ar.alloc(32 * 128, BF16); B4 = Bm.v("p (g r n) -> p g r n", g=32, r=2)
        fw.dma(Bm.v("p (g x) -> p g x", g=32), S5B[l].rearrange("d g sc r n -> sc (d g) (r n)"))
        Cm = ar.alloc(16 * 256, BF16); C4 = Cm.v("p (g r n) -> p g r n", g=16, r=2)
        fw.dma(Cm.v("p (g x) -> p g x", g=16), S5C[l].rearrange("d q p r n -> p (d q) (r n)"))
        DEC = ar.alloc(16); DEC3 = DEC.v("p (d q) -> p d q", d=2)
        fw.dma(DEC3, S5D[l].rearrange("d p q -> p d q"))
        stg = [ar.alloc(2 * 8 * NCH, BF16) for _ in range(2)]
        s4 = lambda bf: bf.v("p (m s j) -> p m s j", m=2, s=8)
        for m in range(2):
            fw.copy(s4(stg[0])[:, m], uT3[:, m, :].rearrange("p (j s) -> p s j", s=8), eng=("dve" if m == 0 else "pool"))
            fw.copy(s4(stg[1])[:, m, :, 0:32], uT3[:, m, 0:256][:, ::-1].rearrange("p (j s) -> p s j", s=8), eng="dve")
            fw.copy(s4(stg[1])[:, m, :, 32:NCH], uT3[:, m, 256:T][:, ::-1].rearrange("p (j s) -> p s j", s=8), eng="pool")
        U = ar.alloc(32 * NCH, BF16); U3 = U.v("p (g j) -> p g j", g=32)
        for d_ in range(2):
            for g in range(16):
                m, gl = g // 8, g % 8
                fw.dma(SCRU[d_][g].rearrange("s c j -> c s j"), s4(stg[d_])[16 * gl:16 * gl + 16, m])
        for d_ in range(2):
            for g in range(16):
                fw.dma(U3[:, d_ * 16 + g, :], SCRU[d_][g].rearrange("s c j -> (s c) j"))
        two = lambda n, dt=F32: [ar.alloc(n, dt) for _ in range(2)]
        cosb = two(NCH); sinb = two(NCH)
        Sre = two(NCH); Sim = two(NCH)
        t1 = two(NCH); t2 = two(NCH); t3 = two(NCH); t4 = two(NCH)
        wr_in = two(NCH); wi_in = two(NCH); wr = two(NCH); wi = two(NCH)
        Hre = two(NCH, BF16); Him = two(NCH, BF16)
        Yg = [ar.alloc(NCH, BF16) for _ in range(4)]

        def s5_front(it, d_, q):
            p = it % 2
            cb = cosb[p]; sb = sinb[p]; hr = Hre[p]; hi = Him[p]
            bS = (0, 1) if p == 0 else (4, 5)
            fw.dma(cb.ap(), S5COS[l][d_][:, q, :])
            fw.dma(sb.ap(), S5SIN[l][d_][:, q, :])
            for gi in range(2):
                g = d_ * 16 + 2 * q + gi
                sl = slice(64 * gi, 64 * gi + 64)
                fw.matmul(banks[bS[0]][sl, 0:NCH], B4[:, g, 0, :], U3[:, g, :])
                fw.matmul(banks[bS[1]][sl, 0:NCH], B4[:, g, 1, :], U3[:, g, :])
            fw.copy(Sre[p].ap(), banks[bS[0]][:, 0:NCH], eng="act")
            fw.copy(Sim[p].ap(), banks[bS[1]][:, 0:NCH], eng="act")
            fw.tt(t1[p].ap(), Sre[p].ap(), cb.ap(), ALU.mult)
            fw.tt(t2[p].ap(), Sim[p].ap(), sb.ap(), ALU.mult, eng="pool")
            fw.tt(wr_in[p].ap(), t1[p].ap(), t2[p].ap(), ALU.add)
            fw.tt(t3[p].ap(), Sim[p].ap(), cb.ap(), ALU.mult, eng="pool")
            fw.tt(t4[p].ap(), Sre[p].ap(), sb.ap(), ALU.mult)
            fw.tt(wi_in[p].ap(), t3[p].ap(), t4[p].ap(), ALU.subtract, eng="pool")
            dec = DEC3[:, d_, q:q + 1].bc([128, NCH])
            fw.scan(wr[p].ap(), dec, wr_in[p].ap(), 0.0)
            fw.scan(wi[p].ap(), dec, wi_in[p].ap(), 0.0)
            fw.tt(t1[p].ap(), wr[p].ap(), cb.ap(), ALU.mult)
            fw.tt(t2[p].ap(), wi[p].ap(), sb.ap(), ALU.mult, eng="pool")
            fw.tt(hr.ap(), t1[p].ap(), t2[p].ap(), ALU.subtract)
            fw.tt(t3[p].ap(), wr[p].ap(), sb.ap(), ALU.mult, eng="pool")
            fw.tt(t4[p].ap(), wi[p].ap(), cb.ap(), ALU.mult)
            fw.tt(hi.ap(), t3[p].ap(), t4[p].ap(), ALU.add, eng="pool")

        def s5_back(it, d_, q):
            p = it % 2
            hr = Hre[p]; hi = Him[p]
            bY = (2, 3) if p == 0 else (6, 7)
            for gi in range(2):
                g16 = 2 * q + gi
                g = d_ * 16 + g16
                sl = slice(64 * gi, 64 * gi + 64)
                psY = banks[bY[gi]]
                fw.matmul(psY[:, 0:NCH], T3[:, g, :], U3[:, g, :], start=True, stop=False)
                fw.matmul(psY[:, 1:NCH], C4[sl, d_ * 8 + q, 0, :], hr[sl, 0:NCH - 1], start=False, stop=False)
                fw.matmul(psY[:, 1:NCH], C4[sl, d_ * 8 + q, 1, :], hi[sl, 0:NCH - 1], start=False, stop=True)
                yg = Yg[p * 2 + gi]
                fw.copy(yg.ap(), psY[:, 0:NCH], eng="act")
                fw.dma(SCRY[d_][g16].rearrange("t c j -> (t c) j"), yg.ap())

        its = [(d_, q) for d_ in range(2) for q in range(8)]
        for it, (d_, q) in enumerate(its):
            s5_front(it, d_, q)
            if it >= 1:
                s5_back(it - 1, *its[it - 1])
        s5_back(len(its) - 1, *its[-1])
        ys = stg
        for d_ in range(2):
            for g in range(16):
                m, gl = g // 8, g % 8
                fw.dma(s4(ys[d_])[16 * gl:16 * gl + 16, m], SCRY[d_][g].rearrange("t c j -> c t j"))
        y = ar.alloc(2 * T); y3 = y.v("p (m n) -> p m n", m=2)
        gb = ar.alloc(2 * T, BF16); gb3 = gb.v("p (m n) -> p m n", m=2)
        for m in range(2):
            f3 = s4(ys[0])[:, m]
            r3 = s4(ys[1])[:, m]
            eng = "dve" if m == 0 else "pool"
            fw.tt(y3[:, m, 0:256].rearrange("p (j t) -> p j t", t=8), f3[:, :, 0:32].rearrange("p t j -> p j t"),
                  r3[:, :, 0:32].rearrange("p s n -> p n s")[:, ::-1, ::-1], ALU.add, eng=eng)
            fw.tt(y3[:, m, 256:T].rearrange("p (j t) -> p j t", t=8), f3[:, :, 32:NCH].rearrange("p t j -> p j t"),
                  r3[:, :, 32:NCH].rearrange("p s n -> p n s")[:, ::-1, ::-1], ALU.add, eng=eng)
            fw.stt(y3[:, m, :], uT3[:, m, :], Dcol[:, m:m + 1], y3[:, m, :], ALU.mult, ALU.add, eng=eng)
        fw.act(y.ap(), y.ap(), AF.Gelu_apprx_tanh)
        fw.copy(gb.ap(), y.ap(), eng="pool")
        osT = ar.alloc(2 * T, BF16); osT3 = osT.v("p (m n) -> p m n", m=2)
        gate = [ar.alloc(512) for _ in range(2)]
        bi = 0
        for mo in range(2):
            for (c0, nn) in [(0, 512), (512, 512), (1024, 512), (1536, 512), (2048, 256)]:
                ps = banks[4 + bi % 2]
                gt = gate[bi % 2]
                bi += 1
                for m in range(2):
                    fw.matmul(ps[:, 0:nn], wgb3[:, m, mo * 128:(mo + 1) * 128], gb3[:, m, c0:c0 + nn], start=(m == 0), stop=(m == 1))
                fw.act(gt[:, 0:nn], ps[:, 0:nn], AF.Sigmoid, bias=bgl[:, mo:mo + 1])
                fw.tt(osT3[:, mo, c0:c0 + nn], y3[:, mo, c0:c0 + nn], gt[:, 0:nn], ALU.mult, eng=("dve" if bi % 2 == 0 else "pool"))
        otb = [ar.alloc(256, BF16) for _ in range(2)]
        for i in range(NT):
            if l == 1 and i < 2:
                continue
            pT = bank_bf(6 + i % 2)
            for mo in range(2):
                fw.transpose(pT[:, mo * 128:(mo + 1) * 128], osT3[:, mo, i * 128:(i + 1) * 128], idb.ap())
            ot = otb[i % 2]
            fw.copy(ot.ap(), pT[:, 0:256], eng="act")
            fw.dma(OO[b][i * 128:(i + 1) * 128, 768:1024], ot.ap())

    if "W" in stages:
        stage_W()
    for l in layers:
        if "M" in stages:
            stage_M(l)
        if "NAB" in stages:
            stage_NAB(l)
        if "FW" in stages:
            stage_FW(l)
        if "S5P" in stages:
            stage_S5P(l)
        for b in batches:
            if "A" in stages:
                stage_A(b, l)
            if "GQA" in stages:
                stage_GQA(b, l)
            if "NA" in stages:
                stage_NA(b, l)
            if "F" in stages:
                stage_F(b, l)
            if "S5" in stages:
                stage_S5(b, l)
            if "C" in stages:
                stage_C(b, l)
    fw.barrier()
    fw.emit()
    return nc


_RESHAPE = {
    "c_ctx": (1, D),
    "na_rel_bias": (2, 60, 31),
}


def make_in_maps(inputs, ncores=NCORES):
    consts = host_constants()
    shared = {}
    for name, arr in inputs.items():
        if name in ("x", "ctx", "c"):
            continue
        a = np.ascontiguousarray(arr)
        if name in _RESHAPE:
            a = a.reshape(_RESHAPE[name])
        shared[name] = a
    shared.update(consts)
    maps = []
    for i in range(ncores):
        m = dict(shared)
        m["x"] = np.ascontiguousarray(inputs["x"][i * NB:(i + 1) * NB])
        m["ctx"] = np.ascontiguousarray(inputs["ctx"][i * NB:(i + 1) * NB])
        m["c"] = np.ascontiguousarray(inputs["c"][i * NB:(i + 1) * NB])
        maps.append(m)
    return maps


def kernel(**inputs):
    nc = build()
    maps = make_in_maps(inputs)
    res = run_bass_kernel_spmd(nc, maps, core_ids=list(range(NCORES)))
    outs = [np.asarray(r["out"]) for r in res.results]
    return np.concatenate(outs, axis=0).astype(np.float32)
```

```python
import contextlib
import math
import numpy as np
import ml_dtypes
import concourse.bass as bass
import concourse.mybir as mybir
from concourse.bass_utils import run_bass_kernel_spmd

F32 = mybir.dt.float32
BF16 = mybir.dt.bfloat16
AF = mybir.ActivationFunctionType
ALU = mybir.AluOpType
AX = mybir.AxisListType

NCORES = 8
NB = 2
D = 1024
SEQ = 2048
CTXL = 256
T = SEQ + CTXL
NT = T // 128
DFF = 2816
NF = DFF // 128
EPS = 1e-6
NCH = T // 8
BIG = -30000.0
DMA_K = 8
COMPUTE = ("pe", "act", "dve", "pool")
SAME_ENGINE_SYNC = True


class Buf:
    __slots__ = ("t", "name", "writers", "readers")

    def __init__(self, ap, name=""):
        self.t = ap
        self.name = name
        self.writers = {}
        self.readers = {}

    def __getitem__(self, idx):
        return View(self, self.t[idx])

    def ap(self):
        return View(self, self.t)

    def v(self, pattern=None, **kw):
        if pattern is None:
            return View(self, self.t)
        return View(self, self.t.rearrange(pattern, **kw))


class View:
    __slots__ = ("buf", "ap")

    def __init__(self, buf, ap):
        self.buf = buf
        self.ap = ap

    def __getitem__(self, idx):
        return View(self.buf, self.ap[idx])

    def rearrange(self, *a, **k):
        return View(self.buf, self.ap.rearrange(*a, **k))

    def bitcast(self, dt):
        return View(self.buf, self.ap.bitcast(dt))

    def bc(self, shape):
        return View(self.buf, self.ap.to_broadcast(list(shape)))

    def unsq(self, i):
        return View(self.buf, self.ap.unsqueeze(i))


class FW:
    def __init__(self, nc):
        self.nc = nc
        self.stack = contextlib.ExitStack()
        self.streams = {e: [] for e in ("pe", "act", "dve", "pool", "sp")}
        self.seq = {e: 0 for e in COMPUTE}
        self.sems = {}
        self.waited = {e: {} for e in self.streams}
        self.dma_count = {"sp": 0, "pool": 0, "act": 0}
        self.dma_last = {}
        self.n_inst = 0

    def sem(self, key):
        if key not in self.sems:
            name = "s_" + ("_".join(str(k) for k in key) if isinstance(key, tuple) else str(key))
            self.sems[key] = self.stack.enter_context(self.nc.semaphore(name))
        return self.sems[key]

    def sbuf(self, name, shape, dtype):
        t = self.stack.enter_context(self.nc.sbuf_tensor(name, list(shape), dtype))
        return Buf(t[:], name)

    def psum(self, name, shape, dtype=F32):
        t = self.stack.enter_context(self.nc.psum_tensor(name, list(shape), dtype))
        return Buf(t[:], name)

    def dram(self, name, shape, dtype, kind="Internal"):
        t = self.nc.dram_tensor(name, list(shape), dtype, kind=kind)
        return Buf(t.ap(), name)

    def _need(self, eng, waits, key, val):
        if self.waited[eng].get(key, 0) >= val:
            return
        if val > waits.get(key, 0):
            waits[key] = val

    def _deps(self, eng, reads, writes):
        waits = {}
        for v in reads:
            for key, val in v.buf.writers.items():
                if key == eng and eng == "pe":
                    continue
                self._need(eng, waits, key, val)
        for v in writes:
            for key, val in v.buf.writers.items():
                if key == eng and (eng == "pe" or not SAME_ENGINE_SYNC):
                    continue
                self._need(eng, waits, key, val)
            for key, val in v.buf.readers.items():
                if key == eng and (eng == "pe" or not SAME_ENGINE_SYNC):
                    continue
                self._need(eng, waits, key, val)
        for key, val in waits.items():
            self.waited[eng][key] = val
        return list(waits.items())

    def _mark(self, token, reads, writes):
        key, val = token
        for v in reads:
            v.buf.readers[key] = val
        for v in writes:
            b = v.buf
            b.writers = {key: val}
            b.readers = {}

    def op(self, eng, fn, reads=(), writes=()):
        reads = [r for r in reads if isinstance(r, View)]
        writes = [w for w in writes if isinstance(w, View)]
        waits = self._deps(eng, reads, writes)
        self.seq[eng] += 1
        token = (eng, self.seq[eng])
        self._mark(token, reads, writes)
        self.streams[eng].append((waits, fn, (eng, 1)))
        self.n_inst += 1
        return token

    def dma(self, out, in_, q="sp", **kw):
        i = self.dma_count[q]
        self.dma_count[q] += 1
        r, rnd = i % DMA_K, i // DMA_K
        key = ("dma", q, r)
        waits = {}
        if rnd > 0:
            self._need(q, waits, key, 16 * rnd)
        for k2, v2 in waits.items():
            self.waited[q][k2] = v2
        w2 = self._deps(q, [in_], [out])
        allw = list(waits.items()) + w2
        token = (key, 16 * (rnd + 1))
        self.dma_last[key] = 16 * (rnd + 1)
        self._mark(token, [in_], [out])
        oa, ia = out.ap, in_.ap
        self.streams[q].append((allw, lambda e, oa=oa, ia=ia, kw=kw: e.dma_start(out=oa, in_=ia, **kw), (key, 16)))
        self.n_inst += 1
        return token

    def barrier(self):
        toks = [(e, self.seq[e]) for e in COMPUTE if self.seq[e] > 0]
        toks += list(self.dma_last.items())
        for eng in self.streams:
            waits = {}
            for key, val in toks:
                if key == eng:
                    continue
                self._need(eng, waits, key, val)
            for k2, v2 in waits.items():
                self.waited[eng][k2] = v2
            if waits:
                self.streams[eng].append((list(waits.items()), None, None))

    def emit(self):
        nc = self.nc
        for e in COMPUTE:
            self.sem(e)
        for q in self.dma_count:
            for r in range(DMA_K):
                self.sem(("dma", q, r))
        engmap = {"pe": "tensor", "act": "scalar", "dve": "vector", "pool": "gpsimd", "sp": "sync"}
        with nc.Block() as block:
            for e, attr in engmap.items():
                stream = self.streams[e]

                def body(engine, stream=stream):
                    for waits, fn, inc in stream:
                        for key, val in waits:
                            engine.wait_ge(self.sems[key], val)
                        if fn is None:
                            continue
                        ins = fn(engine)
                        if inc is not None:
                            ins.then_inc(self.sems[inc[0]], inc[1])

                getattr(block, attr)(body)
        self.stack.close()

    def matmul(self, out, lhsT, rhs, start=True, stop=True, **kw):
        oa, la, ra = out.ap, lhsT.ap, rhs.ap
        return self.op("pe", lambda e: e.matmul(oa, la, ra, start=start, stop=stop, **kw),
                       reads=[lhsT, rhs], writes=[out])

    def transpose(self, out, in_, ident):
        oa, ia, da = out.ap, in_.ap, ident.ap
        return self.op("pe", lambda e: e.transpose(oa, ia, da), reads=[in_, ident], writes=[out])

    def act(self, out, in_, func, bias=None, scale=None, accum_out=None):
        oa, ia = out.ap, in_.ap
        kw = {}
        reads = [in_]
        writes = [out]
        if bias is not None:
            if isinstance(bias, View):
                kw["bias"] = bias.ap
                reads.append(bias)
            else:
                kw["bias"] = bias
        if scale is not None:
            if isinstance(scale, View):
                kw["scale"] = scale.ap
                reads.append(scale)
            else:
                kw["scale"] = scale
        if accum_out is not None:
            kw["accum_out"] = accum_out.ap
            writes.append(accum_out)
        return self.op("act", lambda e: e.activation(oa, ia, func, **kw), reads=reads, writes=writes)

    def tt(self, out, in0, in1, op, eng="dve"):
        oa, a, b = out.ap, in0.ap, in1.ap
        return self.op(eng, lambda e: e.tensor_tensor(oa, a, b, op), reads=[in0, in1], writes=[out])

    def ts(self, out, in0, s1, s2, op0, op1=None, eng="dve"):
        oa, a = out.ap, in0.ap
        reads = [in0]
        s1a = s1.ap if isinstance(s1, View) else s1
        s2a = s2.ap if isinstance(s2, View) else s2
        if isinstance(s1, View):
            reads.append(s1)
        if isinstance(s2, View):
            reads.append(s2)
        kw = {}
        if op1 is not None:
            kw["op1"] = op1
        return self.op(eng, lambda e: e.tensor_scalar(oa, a, s1a, s2a, op0, **kw), reads=reads, writes=[out])

    def stt(self, out, in0, scalar, in1, op0, op1, eng="dve"):
        oa, a, b = out.ap, in0.ap, in1.ap
        reads = [in0, in1]
        sa = scalar.ap if isinstance(scalar, View) else scalar
        if isinstance(scalar, View):
            reads.append(scalar)
        eng = "dve"
        return self.op(eng, lambda e: e.scalar_tensor_tensor(oa, a, sa, b, op0, op1), reads=reads, writes=[out])

    def copy(self, out, in_, eng="dve"):
        oa, ia = out.ap, in_.ap
        if eng == "act":
            return self.op(eng, lambda e: e.copy(oa, ia), reads=[in_], writes=[out])
        return self.op(eng, lambda e: e.tensor_copy(oa, ia), reads=[in_], writes=[out])

    def memset(self, out, val, eng="pool"):
        oa = out.ap
        return self.op(eng, lambda e: e.memset(oa, val), reads=[], writes=[out])

    def reduce(self, out, in_, op, axis=AX.X, eng="dve"):
        oa, ia = out.ap, in_.ap
        return self.op(eng, lambda e: e.tensor_reduce(oa, ia, axis, op), reads=[in_], writes=[out])

    def recip(self, out, in_):
        oa, ia = out.ap, in_.ap
        return self.op("dve", lambda e: e.reciprocal(oa, ia), reads=[in_], writes=[out])

    def scan(self, out, d0, d1, initial, op0=ALU.mult, op1=ALU.add):
        oa, a, b = out.ap, d0.ap, d1.ap
        reads = [d0, d1]
        ia = initial.ap if isinstance(initial, View) else initial
        if isinstance(initial, View):
            reads.append(initial)
        return self.op("dve", lambda e: e.tensor_tensor_scan(oa, a, b, ia, op0, op1), reads=reads, writes=[out])


class Arena:
    def __init__(self, fw, words):
        self.fw = fw
        self.words = words
        self.base = fw.sbuf("arena", [128, words], F32)
        self.off = 0
        self.n = 0

    def alloc(self, nelem, dtype=F32, name=None):
        if dtype == BF16:
            w = (nelem + 1) // 2
        else:
            w = nelem
        w = (w + 7) // 8 * 8
        assert self.off + w <= self.words, "arena overflow %d + %d > %d" % (self.off, w, self.words)
        ap = self.base.t[:, self.off:self.off + w]
        if dtype == BF16:
            ap = ap.bitcast(BF16)[:, 0:nelem]
        else:
            ap = ap[:, 0:nelem]
        self.off += w
        self.n += 1
        return Buf(ap, name or ("a%d" % self.n))

    def reset(self):
        self.fw.barrier()
        self.off = 0


def _bf16(a):
    return np.asarray(a, dtype=np.float32).astype(ml_dtypes.bfloat16)


def host_constants():
    k = {}
    k["k_ident_bf"] = _bf16(np.eye(128))
    k["k_ident_f"] = np.eye(128, dtype=np.float32)
    t = np.arange(SEQ)
    rows = (t // 64).astype(np.float32)
    cols = (t % 64).astype(np.float32)
    inv = (np.float32(10000.0) ** (-np.arange(0, 32, 2, dtype=np.float32) / np.float32(32))).astype(np.float32)
    ar = (rows[:, None] * inv[None, :]).astype(np.float32)
    ac = (cols[:, None] * inv[None, :]).astype(np.float32)
    cr, sr, cc, sc = np.cos(ar), np.sin(ar), np.cos(ac), np.sin(ac)
    cos_t = np.concatenate([cr, cr, cc, cc], axis=1).astype(np.float32)
    sin_t = np.concatenate([-sr, sr, -sc, sc], axis=1).astype(np.float32)
    k["k_rope_cos"] = cos_t.reshape(16, 128, 64)
    k["k_rope_sin"] = sin_t.reshape(16, 128, 64)
    n = np.arange(SEQ, dtype=np.int64)
    nk = (n[:, None] * n[None, :]) % SEQ
    ang = nk.astype(np.float64) * (2.0 * np.pi / SEQ)
    k["k_dft"] = np.stack([_bf16(np.cos(ang)), _bf16(np.sin(ang))])
    m = np.arange(64, dtype=np.int64)
    a64 = ((m[:, None] * m[None, :]) % 64).astype(np.float64) * (2.0 * np.pi / 64)
    cb = np.zeros((256, 256), np.float32)
    sb = np.zeros((256, 256), np.float32)
    for h in range(4):
        cb[h * 64:(h + 1) * 64, h * 64:(h + 1) * 64] = np.cos(a64)
        sb[h * 64:(h + 1) * 64, h * 64:(h + 1) * 64] = np.sin(a64)
    k["k_cblk"] = cb
    k["k_sblk"] = sb
    s_idx = np.arange(128) // 16
    k["k_s5mask"] = (s_idx[None, :] >= s_idx[:, None]).astype(np.float32)
    k["k_tau"] = np.tile(np.arange(-7, 9, dtype=np.float32)[None, :], (128, 1))
    k["k_nidx"] = np.tile(np.arange(NCH, dtype=np.float32)[None, :], (128, 1))
    qc = np.arange(64)
    cs = np.clip(qc - 8, 0, 48)
    kc = np.arange(64)
    ok = (kc[:, None] >= cs[None, :]) & (kc[:, None] < cs[None, :] + 16)
    blk = np.where(ok, 0.0, BIG).astype(np.float32)
    k["k_colmask"] = np.tile(blk, (2, 8))
    return k


def na_plan():
    pats = {}
    plan = []
    for t in range(16):
        lst = []
        for u in range(16):
            pat = []
            anyv = False
            for krl in range(2):
                for qrl in range(2):
                    kr, qr = 2 * u + krl, 2 * t + qrl
                    rs = min(max(qr - 4, 0), 24)
                    if rs <= kr < rs + 8:
                        pat.append(kr - qr + 7)
                        anyv = True
                    else:
                        pat.append(None)
            if anyv:
                pat = tuple(pat)
                if pat not in pats:
                    pats[pat] = len(pats)
                lst.append((u, pats[pat]))
        plan.append(lst)
    plist = [None] * len(pats)
    for p, i in pats.items():
        plist[i] = p
    return plan, plist


def build(stages=None, dbg=(), layers=(0, 1), batches=(0, 1)):
    nc = bass.Bass("TRN2", target_bir_lowering=False)
    fw = FW(nc)
    dbg = set(dbg)

    def ein(name, shape, dt=F32):
        return fw.dram(name, shape, dt, kind="ExternalInput")

    def scratch(name, shape, dt):
        return fw.dram(name, shape, dt, kind=("ExternalOutput" if name in dbg else "Internal"))

    x = ein("x", [NB, SEQ, D])
    ctx = ein("ctx", [NB, CTXL, D])
    c = ein("c", [NB, D])
    c_ctx = ein("c_ctx", [1, D])
    w_mod = ein("w_mod", [2, D, 6 * D])
    b_mod = ein("b_mod", [2, 6 * D])
    g_norm1 = ein("g_norm1", [2, D])
    w_in = ein("w_in", [2, D, 1792])
    att_q_gain = ein("att_q_gain", [2, 64])
    att_k_gain = ein("att_k_gain", [2, 64])
    na_q_gain = ein("na_q_gain", [2, 64])
    na_k_gain = ein("na_k_gain", [2, 64])
    na_rel_bias = ein("na_rel_bias", [2, 60, 31])
    w_fourier = ein("w_fourier", [2, 256, 256])
    ssm_lam_re = ein("ssm_lam_re", [2, 2, 16, 64])
    ssm_lam_im = ein("ssm_lam_im", [2, 2, 16, 64])
    ssm_log_dt = ein("ssm_log_dt", [2, 2, 16])
    ssm_b_re = ein("ssm_b_re", [2, 2, 16, 64, 16])
    ssm_b_im = ein("ssm_b_im", [2, 2, 16, 64, 16])
    ssm_c_re = ein("ssm_c_re", [2, 2, 16, 16, 64])
    ssm_c_im = ein("ssm_c_im", [2, 2, 16, 16, 64])
    ssm_d = ein("ssm_d", [2, 256])
    w_glu = ein("w_glu", [2, 256, 256])
    b_glu = ein("b_glu", [2, 256])
    g_group = ein("g_group", [2, D])
    w_out = ein("w_out", [2, D, D])
    g_norm2 = ein("g_norm2", [2, D])
    w_ff1 = ein("w_ff1", [2, D, DFF])
    w_ff3 = ein("w_ff3", [2, D, DFF])
    w_ff2 = ein("w_ff2", [2, DFF, D])
    k_ident_bf = ein("k_ident_bf", [128, 128], BF16)
    k_ident_f = ein("k_ident_f", [128, 128])
    k_rope_cos = ein("k_rope_cos", [16, 128, 64])
    k_rope_sin = ein("k_rope_sin", [16, 128, 64])
    k_dft = ein("k_dft", [2, SEQ, SEQ], BF16)
    k_cblk = ein("k_cblk", [256, 256])
    k_sblk = ein("k_sblk", [256, 256])
    k_s5mask = ein("k_s5mask", [128, 128])
    k_tau = ein("k_tau", [128, 16])
    k_nidx = ein("k_nidx", [128, NCH])
    k_colmask = ein("k_colmask", [128, 512])

    out = fw.dram("out", [NB, SEQ, D], F32, kind="ExternalOutput")

    XR = scratch("XR", [NB, T, D], F32)
    MODV = scratch("MODV", [2, 3, 6 * D], F32)
    WIN = scratch("WIN", [2, 128, 8, 1792], BF16)
    WOUT = scratch("WOUT", [2, 128, 8, D], BF16)
    W1 = scratch("W1", [2, NF, 128, 8, 128], BF16)
    W3 = scratch("W3", [2, NF, 128, 8, 128], BF16)
    W2 = scratch("W2", [2, DFF, D], BF16)
    QK = scratch("QK", [NB, 896, T], BF16)
    VV = scratch("VV", [NB, T, 390], BF16)
    FIN = scratch("FIN", [NB, T, 256], BF16)
    UT = scratch("UT", [NB, 256, T], BF16)
    OO = scratch("OO", [NB, T, D], BF16)
    PADT = scratch("PADT", [2, 61, 8192], F32)
    BIAS = scratch("BIAS", [2, 32, 128, 512], BF16)
    WCS = scratch("WCS", [2, 2, 256, 256], BF16)
    SCRU = scratch("SCRU", [2, 16, 8, 16, NCH], BF16)
    SCRY = scratch("SCRY", [2, 16, 8, 16, NCH], BF16)
    S5T = scratch("S5T", [2, 2, 16, 128, 128], BF16)
    S5B = scratch("S5B", [2, 2, 16, 128, 2, 64], BF16)
    S5C = scratch("S5C", [2, 2, 8, 128, 2, 128], BF16)
    S5D = scratch("S5D", [2, 2, 128, 8], F32)
    S5COS = scratch("S5COS", [2, 2, 128, 8, NCH], F32)
    S5SIN = scratch("S5SIN", [2, 2, 128, 8, NCH], F32)

    ar = Arena(fw, 47 * 1024)
    banks = [fw.psum("bank%d" % i, [128, 512], F32) for i in range(8)]

    all_stages = ["W", "M", "A", "GQA", "NAB", "NA", "FW", "F", "S5P", "S5", "C"]
    if stages is None:
        stages = all_stages
    stages = set(stages)

    def bank_bf(i):
        return View(banks[i], banks[i].t.bitcast(BF16))

    def rstd_from(ss, n, rs, tmp):
        fw.act(tmp, ss, AF.Sqrt, scale=1.0 / n, bias=EPS)
        fw.recip(rs, tmp)

    def stage_W():
        for l in range(2):
            for half in range(2):
                sl = slice(half * 896, (half + 1) * 896)
                fw.dma(WIN[l][:, :, sl], w_in[l].rearrange("(kc p) n -> p kc n", p=128)[:, :, sl], q="pool")
            fw.dma(WOUT[l], w_out[l].rearrange("(kc p) n -> p kc n", p=128), q="pool")
            for f in range(NF):
                fw.dma(W1[l][f], w_ff1[l][:, f * 128:(f + 1) * 128].rearrange("(kc p) n -> p kc n", p=128), q="pool")
                fw.dma(W3[l][f], w_ff3[l][:, f * 128:(f + 1) * 128].rearrange("(kc p) n -> p kc n", p=128), q="pool")
            for f0 in range(0, DFF, 704):
                fw.dma(W2[l][f0:f0 + 704, :], w_ff2[l][f0:f0 + 704, :], q="pool")

    def stage_M(l):
        ar.reset()
        cT = ar.alloc(24)
        cT3 = cT.v("p (k r) -> p k r", r=3)
        for r in range(2):
            fw.dma(cT3[:, :, r], c[r].rearrange("(kc p) -> p kc", p=128), allow_slow_non_contiguous=True)
        fw.dma(cT3[:, :, 2], c_ctx[0].rearrange("(kc p) -> p kc", p=128), allow_slow_non_contiguous=True)
        cS = ar.alloc(24)
        fw.act(cS.ap(), cT.ap(), AF.Silu)
        cS3 = cS.v("p (k r) -> p k r", r=3)
        bm = ar.alloc(6 * D)
        fw.dma(bm[0:3, :], b_mod[l:l + 1, :].bc([3, 6 * D]))
        g1b = ar.alloc(D)
        g2b = ar.alloc(D)
        fw.dma(g1b[0:3, :], g_norm1[l:l + 1, :].bc([3, D]))
        fw.dma(g2b[0:3, :], g_norm2[l:l + 1, :].bc([3, D]))
        mv = ar.alloc(6 * D)
        wbuf = [ar.alloc(8 * 512) for _ in range(2)]
        for nb in range(12):
            wt = wbuf[nb % 2].v("p (k n) -> p k n", k=8)
            fw.dma(wt, w_mod[l][:, nb * 512:(nb + 1) * 512].rearrange("(kc p) n -> p kc n", p=128))
            ps = banks[nb % 2]
            for kc in range(8):
                fw.matmul(ps[0:3, :], cS3[:, kc, :], wt[:, kc, :], start=(kc == 0), stop=(kc == 7))
            fw.tt(mv[0:3, nb * 512:(nb + 1) * 512], ps[0:3, :], bm[0:3, nb * 512:(nb + 1) * 512], ALU.add)
        fw.stt(mv[0:3, D:2 * D], mv[0:3, D:2 * D], 1.0, g1b[0:3, :], ALU.add, ALU.mult)
        fw.stt(mv[0:3, 4 * D:5 * D], mv[0:3, 4 * D:5 * D], 1.0, g2b[0:3, :], ALU.add, ALU.mult)
        fw.dma(MODV[l], mv[0:3, :])

    def modrow(l, r, idx):
        return MODV[l][r:r + 1, idx * D:(idx + 1) * D].bc([128, D])

    def load_consts_ident():
        idb = ar.alloc(128, BF16)
        fw.dma(idb.ap(), k_ident_bf.ap())
        return idb

    def stage_A(b, l):
        ar.reset()
        idb = load_consts_ident()
        win = ar.alloc(8 * 1792, BF16)
        win3 = win.v("p (k n) -> p k n", k=8)
        fw.dma(win3, WIN[l])
        A1 = ar.alloc(D); B1 = ar.alloc(D); A1c = ar.alloc(D); B1c = ar.alloc(D)
        fw.dma(A1.ap(), modrow(l, b, 1)); fw.dma(B1.ap(), modrow(l, b, 0))
        fw.dma(A1c.ap(), modrow(l, 2, 1)); fw.dma(B1c.ap(), modrow(l, 2, 0))
        G = ar.alloc(14 * 64)
        G3 = G.v("p (h d) -> p h d", d=64)
        gsrc = [att_q_gain] * 4 + [att_k_gain] * 2 + [na_q_gain] * 4 + [na_k_gain] * 4
        for h in range(14):
            fw.dma(G3[:, h, :], gsrc[h][l:l + 1, :].bc([128, 64]))
        COS = ar.alloc(16 * 64); SIN = ar.alloc(16 * 64)
        COS3 = COS.v("p (t d) -> p t d", d=64); SIN3 = SIN.v("p (t d) -> p t d", d=64)
        fw.dma(COS3, k_rope_cos.v("t p d -> p t d"))
        fw.dma(SIN3, k_rope_sin.v("t p d -> p t d"))
        two = lambda n, dt=F32: [ar.alloc(n, dt) for _ in range(2)]
        xs = two(D); sqj = ar.alloc(D, BF16)
        ss = two(4); sd = two(4); rstd = two(4)
        tt_ = two(D); hb = two(D, BF16); hT = two(D, BF16)
        qks = two(896); sq2 = two(896); ssh = two(16); sdh = two(16); rsh = two(16)
        qkn = two(896); vsw = two(384); m1 = two(384); m2 = two(384)
        qkb = two(896, BF16); qkT = two(896, BF16)
        vaug = two(6 * 65, BF16); finb = two(256, BF16); uTb = two(256, BF16)
        for vb in vaug:
            fw.memset(vb.v("p (h e) -> p h e", e=65)[:, :, 64:65], 1.0)

        def load(i):
            if i >= NT:
                return
            if l == 0:
                src = ctx[b][i * 128:(i + 1) * 128, :] if i < 2 else x[b][(i - 2) * 128:(i - 1) * 128, :]
            else:
                src = XR[b][i * 128:(i + 1) * 128, :]
            fw.dma(xs[i % 2].ap(), src)

        def frontA(i):
            j = i % 2
            isctx = i < 2
            fw.act(sqj.ap(), xs[j].ap(), AF.Square, accum_out=ss[j][:, 0:1])
            rstd_from(ss[j][:, 0:1], D, rstd[j][:, 0:1], sd[j][:, 0:1])
            fw.stt(tt_[j].ap(), xs[j].ap(), rstd[j][:, 0:1], (A1c if isctx else A1).ap(), ALU.mult, ALU.mult)
            fw.tt(hb[j].ap(), tt_[j].ap(), (B1c if isctx else B1).ap(), ALU.add, eng="pool")

        def frontB(i):
            j = i % 2
            pT = bank_bf(4 + 2 * j)
            for k in range(8):
                fw.transpose(pT[:, k * 128:(k + 1) * 128], hb[j][:, k * 128:(k + 1) * 128], idb.ap())
            fw.copy(hT[j].ap(), pT[:, 0:D], eng="act")
            hT3 = hT[j].v("p (k n) -> p k n", k=8)
            for nb in range(3):
                for k in range(8):
                    fw.matmul(banks[nb].ap(), hT3[:, k, :], win3[:, k, nb * 512:(nb + 1) * 512], start=(k == 0), stop=(k == 7))
            for m in range(2):
                for k in range(8):
                    fw.matmul(banks[3][:, m * 128:(m + 1) * 128], win3[:, k, 1536 + m * 128:1536 + (m + 1) * 128], hT3[:, k, :],
                              start=(k == 0), stop=(k == 7))

        def mid(i):
            j = i % 2
            va3 = vaug[j].v("p (h e) -> p h e", e=65)
            fw.copy(qks[j][:, 0:384], banks[0][:, 0:384], eng="act")
            fw.copy(va3[:, 0:2, 0:64], banks[0][:, 384:512].rearrange("p (h d) -> p h d", d=64), eng="act")
            fw.copy(qks[j][:, 384:896], banks[1][:, 0:512], eng="dve")
            fw.copy(va3[:, 2:6, 0:64], banks[2][:, 0:256].rearrange("p (h d) -> p h d", d=64), eng="act")
            fw.copy(finb[j].ap(), banks[2][:, 256:512], eng="act")
            fw.copy(uTb[j].ap(), banks[3][:, 0:256], eng="dve")
            fw.dma(VV[b][i * 128:(i + 1) * 128, :], vaug[j].ap())
            fw.dma(FIN[b][i * 128:(i + 1) * 128, :], finb[j].ap())
            fw.dma(UT[b].rearrange("(m p) n -> p m n", p=128)[:, :, i * 128:(i + 1) * 128],
                   uTb[j].v("p (m n) -> p m n", m=2))

        def backE(i):
            j = i % 2
            isctx = i < 2
            fw.act(sq2[j].ap(), qks[j].ap(), AF.Square)
            fw.reduce(ssh[j][:, 0:14], sq2[j].v("p (h d) -> p h d", d=64), ALU.add)
            rstd_from(ssh[j][:, 0:14], 64, rsh[j][:, 0:14], sdh[j][:, 0:14])
            qkn3 = qkn[j].v("p (h d) -> p h d", d=64)
            fw.tt(qkn3, qks[j].v("p (h d) -> p h d", d=64), rsh[j][:, 0:14].unsq(2).bc([128, 14, 64]), ALU.mult)
            fw.tt(qkn[j].ap(), qkn[j].ap(), G.ap(), ALU.mult, eng="pool")
            if isctx:
                fw.copy(qkb[j][:, 0:384], qkn[j][:, 0:384], eng="pool")
            else:
                ti = i - 2
                fw.copy(vsw[j].v("p (a two x) -> p a two x", two=2, x=16),
                        qkn[j][:, 0:384].rearrange("p (a two x) -> p a two x", two=2, x=16)[:, :, ::-1, :], eng="pool")
                fw.tt(m1[j].v("p (h d) -> p h d", d=64), qkn3[:, 0:6, :], COS3[:, ti, :].unsq(1).bc([128, 6, 64]), ALU.mult)
                fw.tt(m2[j].v("p (h d) -> p h d", d=64), vsw[j].v("p (h d) -> p h d", d=64),
                      SIN3[:, ti, :].unsq(1).bc([128, 6, 64]), ALU.mult, eng="pool")
                fw.tt(qkb[j][:, 0:384], m1[j].ap(), m2[j].ap(), ALU.add)
            fw.copy(qkb[j][:, 384:896], qkn[j][:, 384:896], eng="pool")

        def backP(i):
            j = i % 2
            pq = bank_bf(5 + 2 * j)
            for jj in range(7):
                fw.transpose(pq[:, jj * 128:(jj + 1) * 128], qkb[j][:, jj * 128:(jj + 1) * 128], idb.ap())
            fw.copy(qkT[j].ap(), pq[:, 0:896], eng="act")
            fw.dma(QK[b].rearrange("(j p) n -> p j n", p=128)[:, :, i * 128:(i + 1) * 128],
                   qkT[j].v("p (j n) -> p j n", j=7))

        load(0)
        load(1)
        for i in range(NT):
            frontA(i)
            if i >= 1:
                backE(i - 1)
            frontB(i)
            load(i + 2)
            if i >= 1:
                backP(i - 1)
            mid(i)
        backE(NT - 1)
        backP(NT - 1)

    def attn_block(QT3, KT3, V3, qhead, khead, vcol, qcol0, nq, ktiles, PT, psS, psO, st):
        nsub = nq // 128
        nk = len(ktiles)

        def pv(ki, kt, pt):
            for sub in range(nsub):
                fw.matmul(psO[:, sub * 65:(sub + 1) * 65], pt[:, sub * 128:(sub + 1) * 128], V3[:, kt, vcol:vcol + 65],
                          start=(ki == 0 and sub == 0), stop=(ki == nk - 1 and sub == nsub - 1))

        pend = None
        for ki, kt in enumerate(ktiles):
            pS = psS[st["s"] % len(psS)]
            st["s"] += 1
            pt = PT[st["p"] % len(PT)]
            st["p"] += 1
            fw.matmul(pS[:, 0:nq], KT3[0:64, khead, kt * 128:(kt + 1) * 128], QT3[0:64, qhead, qcol0:qcol0 + nq])
            fw.act(pt[:, 0:nq], pS[:, 0:nq], AF.Exp, scale=0.125)
            if pend is not None:
                pv(*pend)
            pend = (ki, kt, pt)
        pv(*pend)

    def attn_finish(psO, nsub, rden, ob):
        po3 = psO[:, 0:nsub * 65].rearrange("p (s e) -> p s e", e=65)
        fw.recip(rden[:, 0:nsub], po3[:, :, 64:65].rearrange("p s e -> p (s e)"))
        fw.tt(ob.v("p (s d) -> p s d", d=64)[:, 0:nsub, :], po3[:, :, 0:64],
              rden[:, 0:nsub].unsq(2).bc([128, nsub, 64]), ALU.mult)

    def stage_GQA(b, l):
        ar.reset()
        QT = ar.alloc(4 * T, BF16); KT = ar.alloc(2 * T, BF16); VA = ar.alloc(NT * 130, BF16)
        QT3 = QT.v("p (h n) -> p h n", h=4); KT3 = KT.v("p (h n) -> p h n", h=2)
        VA3 = VA.v("p (t c) -> p t c", c=130)
        fw.dma(QT3[0:64], QK[b][0:256, :].rearrange("(h d) n -> d h n", d=64))
        fw.dma(KT3[0:64], QK[b][256:384, :].rearrange("(h d) n -> d h n", d=64))
        fw.dma(VA3, VV[b].rearrange("(t p) c -> p t c", p=128)[:, :, 0:130])
        PT = [ar.alloc(512, BF16) for _ in range(3)]
        rden = ar.alloc(4)
        obs = [ar.alloc(256, BF16) for _ in range(2)]
        st = {"s": 0, "p": 0}
        psS = [banks[0], banks[1]]
        cnt = 0
        O3 = OO[b].rearrange("(t p) c -> p t c", p=128)
        blocks = [(256 + qb * 512, 512, list(range(NT)), 2 + qb * 4) for qb in range(4)]
        if l == 0:
            blocks.append((0, 256, [0, 1], 0))
        for h in range(4):
            kv = h // 2
            for (qc0, nq, kts, t0) in blocks:
                psO = banks[2 + cnt % 2]
                ob = obs[cnt % 2]
                cnt += 1
                attn_block(QT3, KT3, VA3, h, kv, kv * 65, qc0, nq, kts, PT, psS, psO, st)
                nsub = nq // 128
                attn_finish(psO, nsub, rden, ob)
                fw.dma(O3[:, t0:t0 + nsub, h * 64:(h + 1) * 64], ob.v("p (s d) -> p s d", d=64)[:, 0:nsub, :])

    NA_PLAN, NA_PATS = na_plan()

    def stage_NAB(l):
        ar.reset()
        rb = ar.alloc(31)
        fw.dma(rb[0:60, :], na_rel_bias[l])
        pad = ar.alloc(128)
        fw.memset(pad.ap(), BIG)
        fw.ts(pad[0:60, 48:79], rb[0:60, ::-1], 8.0, None, ALU.mult)
        padt = PADT[l].ap.tensor
        base = PADT[l].ap.offset
        fw.dma(View(PADT, bass.AP(tensor=padt, offset=base, ap=[[8192, 61], [127, 64], [1, 127]])),
               pad[0:61, 0:127].unsq(1).bc([61, 64, 127]))
        cm = ar.alloc(512)
        fw.dma(cm.ap(), k_colmask.ap())
        stg = [ar.alloc(512) for _ in range(2)]
        bt = [ar.alloc(512, BF16) for _ in range(2)]
        for bid, pat in enumerate(NA_PATS):
            s_ = stg[bid % 2]
            for krl in range(2):
                for qrl in range(2):
                    dri = pat[krl * 2 + qrl]
                    for h in range(4):
                        row = 60 if dri is None else h * 15 + dri
                        src = View(PADT, bass.AP(tensor=padt, offset=base + row * 8192 + 63, ap=[[126, 64], [1, 64]]))
                        fw.dma(s_[krl * 64:(krl + 1) * 64, h * 128 + qrl * 64:h * 128 + (qrl + 1) * 64], src)
            fw.tt(bt[bid % 2].ap(), s_.ap(), cm.ap(), ALU.add)
            fw.dma(BIAS[l][bid], bt[bid % 2].ap())

    def stage_NA(b, l):
        ar.reset()
        idb = load_consts_ident()
        QT = ar.alloc(4 * T, BF16); KT = ar.alloc(4 * T, BF16); VD = ar.alloc(NT * 260, BF16)
        QT3 = QT.v("p (h n) -> p h n", h=4); KT3 = KT.v("p (h n) -> p h n", h=4)
        VD3 = VD.v("p (t c) -> p t c", c=260)
        fw.dma(QT3[0:64], QK[b][384:640, :].rearrange("(h d) n -> d h n", d=64))
        fw.dma(KT3[0:64], QK[b][640:896, :].rearrange("(h d) n -> d h n", d=64))
        fw.dma(VD3, VV[b].rearrange("(t p) c -> p t c", p=128)[:, :, 130:390])
        nbias = len(NA_PATS)
        BT = ar.alloc(nbias * 512, BF16)
        BT3 = BT.v("p (i n) -> p i n", n=512)
        fw.dma(BT3, BIAS[l][0:nbias].rearrange("i p n -> p i n"))
        PT = [ar.alloc(512, BF16) for _ in range(3)]
        rden = ar.alloc(4)
        obs = [ar.alloc(256, BF16) for _ in range(2)]
        O3 = OO[b].rearrange("(t p) c -> p t c", p=128)
        si = 0
        for t in range(16):
            qc0 = 256 + t * 128
            klist = [(2 + u, bid) for (u, bid) in NA_PLAN[t]] + [(0, None), (1, None)]
            psO = banks[2 + t % 2]
            ob = obs[t % 2]
            nk = len(klist)
            def pv(ki, kt, pt, psO=psO, nk=nk):
                for h in range(4):
                    fw.matmul(psO[:, h * 65:(h + 1) * 65], pt[:, h * 128:(h + 1) * 128], VD3[:, kt, h * 65:(h + 1) * 65],
                              start=(ki == 0 and h == 0), stop=(ki == nk - 1 and h == 3))

            pend = None
            for ki, (kt, bid) in enumerate(klist):
                pS = banks[si % 2]
                pt = PT[si % 3]
                si += 1
                for h in range(4):
                    fw.matmul(pS[:, h * 128:(h + 1) * 128], KT3[0:64, h, kt * 128:(kt + 1) * 128], QT3[0:64, h, qc0:qc0 + 128],
                              start=(h == 0), stop=(h == 3 and bid is None))
                if bid is not None:
                    fw.matmul(pS.ap(), idb.ap(), BT3[:, bid, :], start=False, stop=True)
                fw.act(pt.ap(), pS.ap(), AF.Exp, scale=0.125)
                if pend is not None:
                    pv(*pend)
                pend = (ki, kt, pt)
            pv(*pend)
            attn_finish(psO, 4, rden, ob)
            fw.dma(OO[b][(2 + t) * 128:(3 + t) * 128, 256:512], ob.ap())
        if l == 0:
            st = {"s": 0, "p": 0}
            for h in range(4):
                psO = banks[2 + h % 2]
                ob = obs[h % 2]
                attn_block(QT3, KT3, VD3, h, h, h * 65, 0, 256, [0, 1], PT, [banks[0], banks[1]], psO, st)
                attn_finish(psO, 2, rden, ob)
                fw.dma(O3[:, 0:2, 256 + h * 64:256 + (h + 1) * 64], ob.v("p (s d) -> p s d", d=64)[:, 0:2, :])

    def stage_FW(l):
        ar.reset()
        wf = ar.alloc(2 * 256)
        wf3 = wf.v("p (k n) -> p k n", k=2)
        fw.dma(wf3, w_fourier[l].rearrange("(k p) n -> p k n", p=128))
        for ci, ktab in enumerate((k_cblk, k_sblk)):
            cb = ar.alloc(2 * 256)
            cb3 = cb.v("p (k n) -> p k n", k=2)
            fw.dma(cb3, ktab.v("(k p) n -> p k n", p=128))
            wo = ar.alloc(2 * 256, BF16)
            wo3 = wo.v("p (k n) -> p k n", k=2)
            for fo in range(2):
                ps = banks[(ci * 2 + fo) % 4]
                for k in range(2):
                    fw.matmul(ps[:, 0:256], cb3[:, k, fo * 128:(fo + 1) * 128], wf3[:, k, :], start=(k == 0), stop=(k == 1))
                fw.act(wo3[:, fo, :], ps[:, 0:256], AF.Copy, scale=(1.0 if ci == 0 else -1.0))
            fw.dma(WCS[l][ci].rearrange("(k p) n -> p k n", p=128), wo3)

    def stage_F(b, l):
        ar.reset()
        fin = ar.alloc(NT * 256, BF16)
        fin3 = fin.v("p (t f) -> p t f", f=256)
        fw.dma(fin3, FIN[b].rearrange("(t p) f -> p t f", p=128))
        wcs = ar.alloc(2 * 2 * 256, BF16)
        wcs4 = wcs.v("p (c k n) -> p c k n", c=2, k=2)
        for ci in range(2):
            fw.dma(wcs4[:, ci], WCS[l][ci].rearrange("(k p) n -> p k n", p=128))
        tabs = [ar.alloc(512, BF16) for _ in range(6)]
        G = [ar.alloc(4 * 512, BF16) for _ in range(2)]
        ofb = [ar.alloc(256, BF16) for _ in range(2)]
        ti = 0
        oi = 0

        def second_stage(Gv, nk, tile0, scale):
            nonlocal oi
            for kt in range(nk // 128):
                ps = banks[4 + oi % 2]
                n_ = 0
                for ci in range(2):
                    for fc in range(2):
                        fw.matmul(ps[:, 0:256], Gv[:, ci, fc, kt * 128:(kt + 1) * 128], wcs4[:, ci, fc, :],
                                  start=(n_ == 0), stop=(n_ == 3))
                        n_ += 1
                o_ = ofb[oi % 2]
                oi += 1
                fw.act(o_.ap(), ps[:, 0:256], AF.Copy, scale=scale)
                fw.dma(OO[b][(tile0 + kt) * 128:(tile0 + kt + 1) * 128, 512:768], o_.ap())

        for kb in range(4):
            for ncix in range(16):
                for ci in range(2):
                    tb = tabs[ti % 6]
                    ti += 1
                    fw.dma(tb.ap(), k_dft[ci][ncix * 128:(ncix + 1) * 128, kb * 512:(kb + 1) * 512])
                    for fc in range(2):
                        fw.matmul(banks[ci * 2 + fc].ap(), fin3[:, 2 + ncix, fc * 128:(fc + 1) * 128], tb.ap(),
                                  start=(ncix == 0), stop=(ncix == 15))
            Gb = G[kb % 2]
            Gv = Gb.v("p (c f n) -> p c f n", c=2, f=2)
            for ci in range(2):
                for fc in range(2):
                    if (ci + fc) % 2 == 0:
                        fw.copy(Gv[:, ci, fc, :], banks[ci * 2 + fc].ap(), eng="act")
                    else:
                        fw.copy(Gv[:, ci, fc, :], banks[ci * 2 + fc].ap(), eng="dve")
            second_stage(Gv, 512, 2 + kb * 4, 1.0 / math.sqrt(SEQ * 64.0))
        if l == 0:
            for ncix in range(2):
                for ci in range(2):
                    tb = tabs[ti % 6]
                    ti += 1
                    fw.dma(tb[:, 0:256], k_dft[ci][ncix * 1024:(ncix + 1) * 1024:8, 0:256])
                    for fc in range(2):
                        fw.matmul(banks[ci * 2 + fc][:, 0:256], fin3[:, ncix, fc * 128:(fc + 1) * 128], tb[:, 0:256],
                                  start=(ncix == 0), stop=(ncix == 1))
            Gb = G[0]
            Gv = Gb.v("p (c f n) -> p c f n", c=2, f=2)
            for ci in range(2):
                for fc in range(2):
                    fw.copy(Gv[:, ci, fc, 0:256], banks[ci * 2 + fc][:, 0:256], eng=("act" if (ci + fc) % 2 == 0 else "dve"))
            second_stage(Gv, 256, 0, 1.0 / math.sqrt(CTXL * 64.0))

    def stage_C(b, l):
        ar.reset()
        idb = load_consts_ident()
        wo = ar.alloc(8 * D, BF16)
        wo3 = wo.v("p (k n) -> p k n", k=8)
        fw.dma(wo3, WOUT[l])
        gg = ar.alloc(D)
        fw.dma(gg.ap(), g_group[l:l + 1, :].bc([128, D]))
        GA1 = ar.alloc(D); A2 = ar.alloc(D); B2 = ar.alloc(D); GA2 = ar.alloc(D)
        hid = ar.alloc(NF * 512, BF16)
        hid3 = hid.v("p (f n) -> p f n", f=NF)
        two = lambda n, dt=F32: [ar.alloc(n, dt) for _ in range(2)]
        h2T = two(8 * 512, BF16)
        x1 = two(4 * D)
        wt1 = [ar.alloc(8 * 128, BF16) for _ in range(3)]
        wt3 = [ar.alloc(8 * 128, BF16) for _ in range(3)]
        w2t = [ar.alloc(D, BF16) for _ in range(3)]
        xs = two(D); ob = two(D, BF16)
        sq = ar.alloc(D)
        ss4 = two(4); sd4 = two(4); rs4 = two(4)
        ss = two(4); sd = two(4); rstd = two(4)
        onb = two(D, BF16); onT = two(D, BF16)
        tmpA = ar.alloc(D); tmpB = ar.alloc(D); h2b = two(D, BF16)
        sqj = ar.alloc(D, BF16)
        tmpC = ar.alloc(D)
        sa = two(512)
        xo = two(D)
        groups = [[2 + g * 4 + s_ for s_ in range(4)] for g in range(4)]
        if l == 0:
            groups = [[0, 1]] + groups
        rows = [(2 if grp[0] < 2 else b) for grp in groups]
        st = {"wi": 0, "w2i": 0, "oi": 0, "pre": 0}

        def h2T3(gp):
            return h2T[gp].v("p (k n) -> p k n", k=8)

        def x13(gp):
            return x1[gp].v("p (s n) -> p s n", s=4)

        def P1a(i, j):
            if l == 0:
                src = ctx[b][i * 128:(i + 1) * 128, :] if i < 2 else x[b][(i - 2) * 128:(i - 1) * 128, :]
            else:
                src = XR[b][i * 128:(i + 1) * 128, :]
            fw.dma(xs[j].ap(), src)
            fw.dma(ob[j].ap(), OO[b][i * 128:(i + 1) * 128, :])
            fw.act(sq.ap(), ob[j].ap(), AF.Square)
            fw.reduce(ss4[j][:, 0:4], sq.v("p (g d) -> p g d", g=4), ALU.add)
            rstd_from(ss4[j][:, 0:4], 256, rs4[j][:, 0:4], sd4[j][:, 0:4])
            for g_ in range(4):
                sl = slice(g_ * 256, (g_ + 1) * 256)
                fw.stt(onb[j][:, sl], ob[j][:, sl], rs4[j][:, g_:g_ + 1], gg[:, sl], ALU.mult, ALU.mult)

        def P1b(i, j):
            pT = bank_bf(6)
            for k in range(8):
                fw.transpose(pT[:, k * 128:(k + 1) * 128], onb[j][:, k * 128:(k + 1) * 128], idb.ap())
            fw.copy(onT[j].ap(), pT[:, 0:D], eng="act")
            onT3 = onT[j].v("p (k n) -> p k n", k=8)
            for nb in range(2):
                for k in range(8):
                    fw.matmul(banks[4 + nb].ap(), onT3[:, k, :], wo3[:, k, nb * 512:(nb + 1) * 512], start=(k == 0), stop=(k == 7))

        def P2a(i, j, gp, s_):
            for nb in range(2):
                sl = slice(nb * 512, (nb + 1) * 512)
                fw.tt(tmpA[:, sl], banks[4 + nb].ap(), GA1[:, sl], ALU.mult)
            fw.tt(x13(gp)[:, s_, :], tmpA.ap(), xs[j].ap(), ALU.add, eng="pool")
            fw.act(sqj.ap(), x13(gp)[:, s_, :], AF.Square, accum_out=ss[j][:, 0:1])
            rstd_from(ss[j][:, 0:1], D, rstd[j][:, 0:1], sd[j][:, 0:1])
            fw.stt(tmpB.ap(), x13(gp)[:, s_, :], rstd[j][:, 0:1], A2.ap(), ALU.mult, ALU.mult)
            fw.tt(h2b[j].ap(), tmpB.ap(), B2.ap(), ALU.add, eng="pool")

        def P2b(i, j, gp, s_):
            pT2 = bank_bf(7)
            for k in range(8):
                fw.transpose(pT2[:, k * 128:(k + 1) * 128], h2b[j][:, k * 128:(k + 1) * 128], idb.ap())
            fw.copy(h2T3(gp)[:, :, s_ * 128:(s_ + 1) * 128], pT2[:, 0:D].rearrange("p (k n) -> p k n", k=8), eng="act")

        def pre_slots(grp, gp):
            n = len(grp)
            slots = []
            for k in range(n + 2):
                def slot(k=k):
                    if 0 <= k < n:
                        P1a(grp[k], k % 2)
                    if 0 <= k - 2 < n:
                        P2b(grp[k - 2], (k - 2) % 2, gp, k - 2)
                    if 0 <= k - 1 < n:
                        P1b(grp[k - 1], (k - 1) % 2)
                        P2a(grp[k - 1], (k - 1) % 2, gp, k - 1)
                slots.append(slot)
            return slots

        def load_mods(row):
            fw.dma(GA1.ap(), modrow(l, row, 2)); fw.dma(A2.ap(), modrow(l, row, 4))
            fw.dma(B2.ap(), modrow(l, row, 3)); fw.dma(GA2.ap(), modrow(l, row, 5))

        def issue_w13(f):
            a_ = wt1[st["wi"] % 3]; b_ = wt3[st["wi"] % 3]
            st["wi"] += 1
            fw.dma(a_.ap(), W1[l][f].rearrange("p k n -> p (k n)"))
            fw.dma(b_.ap(), W3[l][f].rearrange("p k n -> p (k n)"))
            return a_, b_

        pre_done = False
        preissued = []
        for gi_, grp in enumerate(groups):
            gp = gi_ % 2
            nt_ = len(grp)
            ntk = nt_ * 128
            if not pre_done:
                load_mods(rows[gi_])
                for slot in pre_slots(grp, gp):
                    slot()
            nxt = gi_ + 1
            can_pipe = nxt < len(groups) and rows[nxt] == rows[gi_]
            nslots = pre_slots(groups[nxt], nxt % 2) if can_pipe else []
            for f in range(NF):
                if preissued:
                    a_, b_ = preissued.pop(0)
                else:
                    a_, b_ = issue_w13(f)
                a3 = a_.v("p (k n) -> p k n", k=8); b3_ = b_.v("p (k n) -> p k n", k=8)
                pa = banks[(f % 2) * 2]; pb = banks[1 + (f % 2) * 2]
                for k in range(8):
                    fw.matmul(pa[:, 0:ntk], a3[:, k, :], h2T3(gp)[:, k, 0:ntk], start=(k == 0), stop=(k == 7))
                for k in range(8):
                    fw.matmul(pb[:, 0:ntk], b3_[:, k, :], h2T3(gp)[:, k, 0:ntk], start=(k == 0), stop=(k == 7))
                sj = sa[f % 2]
                fw.act(sj[:, 0:ntk], pa[:, 0:ntk], AF.Silu)
                fw.tt(hid3[:, f, 0:ntk], sj[:, 0:ntk], pb[:, 0:ntk], ALU.mult)
                if nslots and f % 3 == 1:
                    nslots.pop(0)()
            while nslots:
                nslots.pop(0)()
            pre_done = can_pipe
            for f in range(NF):
                w_ = w2t[st["w2i"] % 3]
                st["w2i"] += 1
                fw.dma(w_.ap(), W2[l][f * 128:(f + 1) * 128, :])
                for s_ in range(nt_):
                    for nb in range(2):
                        fw.matmul(banks[s_ * 2 + nb].ap(), hid3[:, f, s_ * 128:(s_ + 1) * 128], w_[:, nb * 512:(nb + 1) * 512],
                                  start=(f == 0), stop=(f == NF - 1))
            if nxt < len(groups):
                preissued = [issue_w13(f) for f in range(3)]
            for s_, i in enumerate(grp):
                xoj = xo[st["oi"] % 2]
                st["oi"] += 1
                for nb in range(2):
                    sl = slice(nb * 512, (nb + 1) * 512)
                    fw.tt(tmpC[:, sl], banks[s_ * 2 + nb].ap(), GA2[:, sl], ALU.mult)
                fw.tt(xoj.ap(), tmpC.ap(), x13(gp)[:, s_, :], ALU.add, eng="pool")
                if l == 0:
                    fw.dma(XR[b][i * 128:(i + 1) * 128, :], xoj.ap())
                else:
                    fw.dma(out[b][(i - 2) * 128:(i - 1) * 128, :], xoj.ap())

    TWO_PI = 2.0 * math.pi
    CW1 = 6.28125
    CW2 = TWO_PI - 6.28125
    MAGIC = 12582912.0
    PI_SAFE = 3.141592

    def range_reduce(out, in_, kbuf, shift=0.0, eng="dve"):
        if shift != 0.0:
            fw.ts(out, in_, shift, None, ALU.add, eng=eng)
            src = out
        else:
            src = in_
        fw.ts(kbuf, src, 1.0 / TWO_PI, MAGIC, ALU.mult, ALU.add, eng=eng)
        fw.ts(kbuf, kbuf, -MAGIC, None, ALU.add, eng=eng)
        fw.stt(out, kbuf, -CW1, src, ALU.mult, ALU.add, eng=eng)
        fw.stt(out, kbuf, -CW2, out, ALU.mult, ALU.add, eng=eng)
        fw.ts(out, out, -PI_SAFE, PI_SAFE, ALU.max, ALU.min, eng=eng)

    def stage_S5P(l):
        ar.reset()
        idf = ar.alloc(128)
        fw.dma(idf.ap(), k_ident_f.ap())
        tau = ar.alloc(16)
        fw.dma(tau.ap(), k_tau.ap())
        nidx = ar.alloc(NCH)
        fw.dma(nidx.ap(), k_nidx.ap())
        msk = ar.alloc(128)
        fw.dma(msk.ap(), k_s5mask.ap())
        mark = ar.off
        for d_ in range(2):
            ar.off = mark
            if d_ == 1:
                fw.barrier()
            lamre = ar.alloc(8); lamim = ar.alloc(8); dt = ar.alloc(8)
            for gi in range(2):
                sl = slice(64 * gi, 64 * gi + 64)
                fw.dma(lamre[sl, :], ssm_lam_re[l][d_].rearrange("(q two) p -> two p q", two=2)[gi], allow_slow_non_contiguous=True)
                fw.dma(lamim[sl, :], ssm_lam_im[l][d_].rearrange("(q two) p -> two p q", two=2)[gi], allow_slow_non_contiguous=True)
                fw.dma(dt[sl, :], ssm_log_dt[l][d_:d_ + 1, :].rearrange("o (q two) -> two o q", two=2)[gi].bc([64, 8]),
                       allow_slow_non_contiguous=True)
            fw.act(dt.ap(), dt.ap(), AF.Exp)
            zr = ar.alloc(8); zi = ar.alloc(8)
            fw.tt(zr.ap(), lamre.ap(), dt.ap(), ALU.mult)
            fw.tt(zi.ap(), lamim.ap(), dt.ap(), ALU.mult)
            PZ = ar.alloc(128); PEx = ar.alloc(128); KB = ar.alloc(128); RS = ar.alloc(128); RC = ar.alloc(128)
            Are = ar.alloc(128); Aim = ar.alloc(128)
            v3 = lambda bf: bf.v("p (t q) -> p t q", q=8)
            fw.tt(v3(PZ), tau.ap().unsq(2).bc([128, 16, 8]), zr.ap().unsq(1).bc([128, 16, 8]), ALU.mult)
            fw.act(PEx.ap(), PZ.ap(), AF.Exp)
            fw.tt(v3(PZ), tau.ap().unsq(2).bc([128, 16, 8]), zi.ap().unsq(1).bc([128, 16, 8]), ALU.mult)
            range_reduce(RS.ap(), PZ.ap(), KB.ap())
            range_reduce(RC.ap(), PZ.ap(), KB.ap(), shift=0.5 * math.pi)
            fw.act(RS.ap(), RS.ap(), AF.Sin)
            fw.act(RC.ap(), RC.ap(), AF.Sin)
            fw.tt(Are.ap(), PEx.ap(), RC.ap(), ALU.mult)
            fw.tt(Aim.ap(), PEx.ap(), RS.ap(), ALU.mult)
            Are3 = v3(Are); Aim3 = v3(Aim)
            fw.dma(S5D[l][d_], v3(PEx)[:, 15, :])
            phi = ar.alloc(8); kb8 = ar.alloc(8)
            fw.ts(phi.ap(), zi.ap(), 8.0, None, ALU.mult)
            range_reduce(phi.ap(), phi.ap(), kb8.ap())
            ANG = ar.alloc(8 * NCH); KB2 = ar.alloc(8 * NCH); R2 = ar.alloc(8 * NCH)
            a3 = lambda bf: bf.v("p (q n) -> p q n", q=8)
            fw.tt(a3(ANG), nidx.ap().unsq(1).bc([128, 8, NCH]), phi.ap().unsq(2).bc([128, 8, NCH]), ALU.mult)
            range_reduce(R2.ap(), ANG.ap(), KB2.ap())
            fw.act(R2.ap(), R2.ap(), AF.Sin)
            fw.dma(S5SIN[l][d_], a3(R2))
            R3 = ar.alloc(8 * NCH)
            range_reduce(R3.ap(), ANG.ap(), KB2.ap(), shift=0.5 * math.pi)
            fw.act(R3.ap(), R3.ap(), AF.Sin)
            fw.dma(S5COS[l][d_], a3(R3))
            nr = ar.alloc(8); den = ar.alloc(8); t8a = ar.alloc(8); t8b = ar.alloc(8); cr = ar.alloc(8); ci = ar.alloc(8)
            fw.ts(nr.ap(), Are3[:, 8, :], -1.0, None, ALU.add)
            fw.tt(den.ap(), lamre.ap(), lamre.ap(), ALU.mult)
            fw.tt(t8a.ap(), lamim.ap(), lamim.ap(), ALU.mult)
            fw.tt(den.ap(), den.ap(), t8a.ap(), ALU.add)
            fw.recip(den.ap(), den.ap())
            fw.tt(t8a.ap(), nr.ap(), lamre.ap(), ALU.mult)
            fw.tt(t8b.ap(), Aim3[:, 8, :], lamim.ap(), ALU.mult)
            fw.tt(t8a.ap(), t8a.ap(), t8b.ap(), ALU.add)
            fw.tt(cr.ap(), t8a.ap(), den.ap(), ALU.mult)
            fw.tt(t8a.ap(), Aim3[:, 8, :], lamre.ap(), ALU.mult)
            fw.tt(t8b.ap(), nr.ap(), lamim.ap(), ALU.mult)
            fw.tt(t8a.ap(), t8a.ap(), t8b.ap(), ALU.subtract)
            fw.tt(ci.ap(), t8a.ap(), den.ap(), ALU.mult)
            bre = ar.alloc(128); bim = ar.alloc(128); Bre = ar.alloc(128); Bim = ar.alloc(128); tb1 = ar.alloc(128); tb2 = ar.alloc(128)
            b3 = lambda bf: bf.v("p (q c) -> p q c", q=8)
            for gi in range(2):
                sl = slice(64 * gi, 64 * gi + 64)
                fw.dma(b3(bre)[sl], ssm_b_re[l][d_].rearrange("(q two) p c -> two p q c", two=2)[gi])
                fw.dma(b3(bim)[sl], ssm_b_im[l][d_].rearrange("(q two) p c -> two p q c", two=2)[gi])
            crb = cr.ap().unsq(2).bc([128, 8, 16]); cib = ci.ap().unsq(2).bc([128, 8, 16])
            fw.tt(b3(tb1), b3(bre), crb, ALU.mult); fw.tt(b3(tb2), b3(bim), cib, ALU.mult)
            fw.tt(Bre.ap(), tb1.ap(), tb2.ap(), ALU.subtract)
            fw.tt(b3(tb1), b3(bim), crb, ALU.mult); fw.tt(b3(tb2), b3(bre), cib, ALU.mult)
            fw.tt(Bim.ap(), tb1.ap(), tb2.ap(), ALU.add)
            Cre = ar.alloc(128); Cim = ar.alloc(128)
            for (Cdst, csrc, bk) in ((Cre, ssm_c_re, 4), (Cim, ssm_c_im, 5)):
                X = ar.alloc(128)
                for q in range(8):
                    for gi in range(2):
                        fw.dma(X[16 * q:16 * q + 16, 64 * gi:64 * gi + 64], csrc[l][d_][2 * q + gi])
                fw.transpose(banks[bk][:, 0:128], X.ap(), idf.ap())
                fw.copy(Cdst.ap(), banks[bk][:, 0:128], eng="act")
            BcR = ar.alloc(8 * 128); BcI = ar.alloc(8 * 128)
            CpR = ar.alloc(8 * 128); CpI = ar.alloc(8 * 128)
            CcR = ar.alloc(8 * 128, BF16); CcI = ar.alloc(8 * 128, BF16)
            u1 = ar.alloc(1024); u2 = ar.alloc(1024)
            f4 = lambda bf: bf.v("p (q s c) -> p q s c", q=8, s=8)
            pw = lambda A3, sl: A3[:, sl, :].rearrange("p s q -> p q s").unsq(3).bc([128, 8, 8, 16])
            qc = lambda bf: b3(bf).unsq(2).bc([128, 8, 8, 16])

            def cmul(dst_re, dst_im, Ar, Ai, Xr, Xi, neg_im=False):
                fw.tt(f4(u1), Ar, Xr, ALU.mult); fw.tt(f4(u2), Ai, Xi, ALU.mult, eng="pool")
                fw.tt(f4(dst_re), f4(u1), f4(u2), ALU.subtract)
                fw.tt(f4(u1), Ar, Xi, ALU.mult); fw.tt(f4(u2), Ai, Xr, ALU.mult, eng="pool")
                if neg_im:
                    fw.stt(f4(dst_im), f4(u1), -1.0, f4(u2), ALU.mult, ALU.subtract)
                else:
                    fw.tt(f4(dst_im), f4(u1), f4(u2), ALU.add)

            cmul(BcR, BcI, pw(Are3, slice(14, 6, -1)), pw(Aim3, slice(14, 6, -1)), qc(Bre), qc(Bim))
            cmul(CpR, CpI, pw(Are3, slice(0, 8)), pw(Aim3, slice(0, 8)), qc(Cre), qc(Cim), neg_im=True)
            cmul(CcR, CcI, pw(Are3, slice(8, 16)), pw(Aim3, slice(8, 16)), qc(Cre), qc(Cim), neg_im=True)
            fw.dma(S5C[l][d_].rearrange("q p r n -> p q r n")[:, :, 0, :], CcR.v("p (q n) -> p q n", q=8))
            fw.dma(S5C[l][d_].rearrange("q p r n -> p q r n")[:, :, 1, :], CcI.v("p (q n) -> p q n", q=8))
            Tb = [ar.alloc(128, BF16) for _ in range(2)]
            BT = [ar.alloc(256, BF16) for _ in range(2)]
            qv = lambda bf, q: bf.v("p (q n) -> p q n", q=8)[:, q, :]
            for q in range(8):
                for gi in range(2):
                    g = 2 * q + gi
                    sl = slice(64 * gi, 64 * gi + 64)
                    ps = banks[gi]
                    fw.matmul(ps[:, 0:128], qv(BcR, q)[sl], qv(CpR, q)[sl], start=True, stop=False)
                    fw.matmul(ps[:, 0:128], qv(BcI, q)[sl], qv(CpI, q)[sl], start=False, stop=True)
                    tb_ = Tb[g % 2]
                    fw.tt(tb_.ap(), ps[:, 0:128], msk.ap(), ALU.mult)
                    fw.dma(S5T[l][d_][g], tb_.ap())
                bt_ = BT[q % 2]
                for ri, src in enumerate((BcR, BcI)):
                    ps = banks[2 + ri]
                    fw.transpose(ps[:, 0:128], qv(src, q), idf.ap())
                    fw.copy(bt_.v("p (g r n) -> p g r n", g=2, r=2)[:, :, ri, :], ps[:, 0:128].rearrange("p (g n) -> p g n", g=2), eng="act")
                fw.dma(S5B[l][d_][2 * q:2 * q + 2].rearrange("g sc r n -> sc g (r n)"), bt_.v("p (g x) -> p g x", g=2))

    def stage_S5(b, l):
        ar.reset()
        idb = load_consts_ident()
        uT = ar.alloc(2 * T, BF16)
        uT3 = uT.v("p (m n) -> p m n", m=2)
        fw.dma(uT3, UT[b].rearrange("(m p) n -> p m n", p=128))
        Dcol = ar.alloc(2)
        fw.dma(Dcol.ap(), ssm_d[l].rearrange("(m p) -> p m", p=128), allow_slow_non_contiguous=True)
        bgl = ar.alloc(2)
        fw.dma(bgl.ap(), b_glu[l].rearrange("(m p) -> p m", p=128), allow_slow_non_contiguous=True)
        wgf = ar.alloc(512)
        fw.dma(wgf.v("p (k n) -> p k n", k=2), w_glu[l].rearrange("(k p) n -> p k n", p=128))
        wgb = ar.alloc(512, BF16)
        fw.copy(wgb.ap(), wgf.ap())
        wgb3 = wgb.v("p (k n) -> p k n", k=2)
        Tm = ar.alloc(32 * 128, BF16); T3 = Tm.v("p (g n) -> p g n", g=32)
        fw.dma(T3, S5T[l].rearrange("d g sc n -> sc (d g) n"))
        Bm = ar.alloc(32 * 128, BF16); B4 = Bm.v("p (g r n) -> p g r n", g=32, r=2)
        fw.dma(Bm.v("p (g x) -> p g x", g=32), S5B[l].rearrange("d g sc r n -> sc (d g) (r n)"))
        Cm = ar.alloc(16 * 256, BF16); C4 = Cm.v("p (g r n) -> p g r n", g=16, r=2)
        fw.dma(Cm.v("p (g x) -> p g x", g=16), S5C[l].rearrange("d q p r n -> p (d q) (r n)"))
        DEC = ar.alloc(16); DEC3 = DEC.v("p (d q) -> p d q", d=2)
        fw.dma(DEC3, S5D[l].rearrange("d p q -> p d q"))
        stg = [ar.alloc(2 * 8 * NCH, BF16) for _ in range(2)]
        s4 = lambda bf: bf.v("p (m s j) -> p m s j", m=2, s=8)
        for m in range(2):
            fw.copy(s4(stg[0])[:, m], uT3[:, m, :].rearrange("p (j s) -> p s j", s=8), eng=("dve" if m == 0 else "pool"))
            fw.copy(s4(stg[1])[:, m, :, 0:32], uT3[:, m, 0:256][:, ::-1].rearrange("p (j s) -> p s j", s=8), eng="dve")
            fw.copy(s4(stg[1])[:, m, :, 32:NCH], uT3[:, m, 256:T][:, ::-1].rearrange("p (j s) -> p s j", s=8), eng="pool")
        U = ar.alloc(32 * NCH, BF16); U3 = U.v("p (g j) -> p g j", g=32)
        for d_ in range(2):
            for g in range(16):
                m, gl = g // 8, g % 8
                fw.dma(SCRU[d_][g].rearrange("s c j -> c s j"), s4(stg[d_])[16 * gl:16 * gl + 16, m])
        for d_ in range(2):
            for g in range(16):
                fw.dma(U3[:, d_ * 16 + g, :], SCRU[d_][g].rearrange("s c j -> (s c) j"))
        two = lambda n, dt=F32: [ar.alloc(n, dt) for _ in range(2)]
        cosb = two(NCH); sinb = two(NCH)
        Sre = two(NCH); Sim = two(NCH)
        t1 = two(NCH); t2 = two(NCH); t3 = two(NCH); t4 = two(NCH)
        wr_in = two(NCH); wi_in = two(NCH); wr = two(NCH); wi = two(NCH)
        Hre = two(NCH, BF16); Him = two(NCH, BF16)
        Yg = [ar.alloc(NCH, BF16) for _ in range(4)]

        def s5_front(it, d_, q):
            p = it % 2
            cb = cosb[p]; sb = sinb[p]; hr = Hre[p]; hi = Him[p]
            bS = (0, 1) if p == 0 else (4, 5)
            fw.dma(cb.ap(), S5COS[l][d_][:, q, :])
            fw.dma(sb.ap(), S5SIN[l][d_][:, q, :])
            for gi in range(2):
                g = d_ * 16 + 2 * q + gi
                sl = slice(64 * gi, 64 * gi + 64)
                fw.matmul(banks[bS[0]][sl, 0:NCH], B4[:, g, 0, :], U3[:, g, :])
                fw.matmul(banks[bS[1]][sl, 0:NCH], B4[:, g, 1, :], U3[:, g, :])
            fw.copy(Sre[p].ap(), banks[bS[0]][:, 0:NCH], eng="act")
            fw.copy(Sim[p].ap(), banks[bS[1]][:, 0:NCH], eng="act")
            fw.tt(t1[p].ap(), Sre[p].ap(), cb.ap(), ALU.mult)
            fw.tt(t2[p].ap(), Sim[p].ap(), sb.ap(), ALU.mult, eng="pool")
            fw.tt(wr_in[p].ap(), t1[p].ap(), t2[p].ap(), ALU.add)
            fw.tt(t3[p].ap(), Sim[p].ap(), cb.ap(), ALU.mult, eng="pool")
            fw.tt(t4[p].ap(), Sre[p].ap(), sb.ap(), ALU.mult)
            fw.tt(wi_in[p].ap(), t3[p].ap(), t4[p].ap(), ALU.subtract, eng="pool")
            dec = DEC3[:, d_, q:q + 1].bc([128, NCH])
            fw.scan(wr[p].ap(), dec, wr_in[p].ap(), 0.0)
            fw.scan(wi[p].ap(), dec, wi_in[p].ap(), 0.0)
            fw.tt(t1[p].ap(), wr[p].ap(), cb.ap(), ALU.mult)
            fw.tt(t2[p].ap(), wi[p].ap(), sb.ap(), ALU.mult, eng="pool")
            fw.tt(hr.ap(), t1[p].ap(), t2[p].ap(), ALU.subtract)
            fw.tt(t3[p].ap(), wr[p].ap(), sb.ap(), ALU.mult, eng="pool")
            fw.tt(t4[p].ap(), wi[p].ap(), cb.ap(), ALU.mult)
            fw.tt(hi.ap(), t3[p].ap(), t4[p].ap(), ALU.add, eng="pool")

        def s5_back(it, d_, q):
            p = it % 2
            hr = Hre[p]; hi = Him[p]
            bY = (2, 3) if p == 0 else (6, 7)
            for gi in range(2):
                g16 = 2 * q + gi
                g = d_ * 16 + g16
                sl = slice(64 * gi, 64 * gi + 64)
                psY = banks[bY[gi]]
                fw.matmul(psY[:, 0:NCH], T3[:, g, :], U3[:, g, :], start=True, stop=False)
                fw.matmul(psY[:, 1:NCH], C4[sl, d_ * 8 + q, 0, :], hr[sl, 0:NCH - 1], start=False, stop=False)
                fw.matmul(psY[:, 1:NCH], C4[sl, d_ * 8 + q, 1, :], hi[sl, 0:NCH - 1], start=False, stop=True)
                yg = Yg[p * 2 + gi]
                fw.copy(yg.ap(), psY[:, 0:NCH], eng="act")
                fw.dma(SCRY[d_][g16].rearrange("t c j -> (t c) j"), yg.ap())

        its = [(d_, q) for d_ in range(2) for q in range(8)]
        for it, (d_, q) in enumerate(its):
            s5_front(it, d_, q)
            if it >= 1:
                s5_back(it - 1, *its[it - 1])
        s5_back(len(its) - 1, *its[-1])
        ys = stg
        for d_ in range(2):
            for g in range(16):
                m, gl = g // 8, g % 8
                fw.dma(s4(ys[d_])[16 * gl:16 * gl + 16, m], SCRY[d_][g].rearrange("t c j -> c t j"))
        y = ar.alloc(2 * T); y3 = y.v("p (m n) -> p m n", m=2)
        gb = ar.alloc(2 * T, BF16); gb3 = gb.v("p (m n) -> p m n", m=2)
        for m in range(2):
            f3 = s4(ys[0])[:, m]
            r3 = s4(ys[1])[:, m]
            eng = "dve" if m == 0 else "pool"
            fw.tt(y3[:, m, 0:256].rearrange("p (j t) -> p j t", t=8), f3[:, :, 0:32].rearrange("p t j -> p j t"),
                  r3[:, :, 0:32].rearrange("p s n -> p n s")[:, ::-1, ::-1], ALU.add, eng=eng)
            fw.tt(y3[:, m, 256:T].rearrange("p (j t) -> p j t", t=8), f3[:, :, 32:NCH].rearrange("p t j -> p j t"),
                  r3[:, :, 32:NCH].rearrange("p s n -> p n s")[:, ::-1, ::-1], ALU.add, eng=eng)
            fw.stt(y3[:, m, :], uT3[:, m, :], Dcol[:, m:m + 1], y3[:, m, :], ALU.mult, ALU.add, eng=eng)
        fw.act(y.ap(), y.ap(), AF.Gelu_apprx_tanh)
        fw.copy(gb.ap(), y.ap(), eng="pool")
        osT = ar.alloc(2 * T, BF16); osT3 = osT.v("p (m n) -> p m n", m=2)
        gate = [ar.alloc(512) for _ in range(2)]
        bi = 0
        for mo in range(2):
            for (c0, nn) in [(0, 512), (512, 512), (1024, 512), (1536, 512), (2048, 256)]:
                ps = banks[4 + bi % 2]
                gt = gate[bi % 2]
                bi += 1
                for m in range(2):
                    fw.matmul(ps[:, 0:nn], wgb3[:, m, mo * 128:(mo + 1) * 128], gb3[:, m, c0:c0 + nn], start=(m == 0), stop=(m == 1))
                fw.act(gt[:, 0:nn], ps[:, 0:nn], AF.Sigmoid, bias=bgl[:, mo:mo + 1])
                fw.tt(osT3[:, mo, c0:c0 + nn], y3[:, mo, c0:c0 + nn], gt[:, 0:nn], ALU.mult, eng=("dve" if bi % 2 == 0 else "pool"))
        otb = [ar.alloc(256, BF16) for _ in range(2)]
        for i in range(NT):
            if l == 1 and i < 2:
                continue
            pT = bank_bf(6 + i % 2)
            for mo in range(2):
                fw.transpose(pT[:, mo * 128:(mo + 1) * 128], osT3[:, mo, i * 128:(i + 1) * 128], idb.ap())
            ot = otb[i % 2]
            fw.copy(ot.ap(), pT[:, 0:256], eng="act")
            fw.dma(OO[b][i * 128:(i + 1) * 128, 768:1024], ot.ap())

    if "W" in stages:
        stage_W()
    for l in layers:
        if "M" in stages:
            stage_M(l)
        if "NAB" in stages:
            stage_NAB(l)
        if "FW" in stages:
            stage_FW(l)
        if "S5P" in stages:
            stage_S5P(l)
        for b in batches:
            if "A" in stages:
                stage_A(b, l)
            if "GQA" in stages:
                stage_GQA(b, l)
            if "NA" in stages:
                stage_NA(b, l)
            if "F" in stages:
                stage_F(b, l)
            if "S5" in stages:
                stage_S5(b, l)
            if "C" in stages:
                stage_C(b, l)
    fw.barrier()
    fw.emit()
    return nc


_RESHAPE = {
    "c_ctx": (1, D),
    "na_rel_bias": (2, 60, 31),
}


def make_in_maps(inputs, ncores=NCORES):
    consts = host_constants()
    shared = {}
    for name, arr in inputs.items():
        if name in ("x", "ctx", "c"):
            continue
        a = np.ascontiguousarray(arr)
        if name in _RESHAPE:
            a = a.reshape(_RESHAPE[name])
        shared[name] = a
    shared.update(consts)
    maps = []
    for i in range(ncores):
        m = dict(shared)
        m["x"] = np.ascontiguousarray(inputs["x"][i * NB:(i + 1) * NB])
        m["ctx"] = np.ascontiguousarray(inputs["ctx"][i * NB:(i + 1) * NB])
        m["c"] = np.ascontiguousarray(inputs["c"][i * NB:(i + 1) * NB])
        maps.append(m)
    return maps


def kernel(**inputs):
    nc = build()
    maps = make_in_maps(inputs)
    res = run_bass_kernel_spmd(nc, maps, core_ids=list(range(NCORES)))
    outs = [np.asarray(r["out"]) for r in res.results]
    return np.concatenate(outs, axis=0).astype(np.float32)
```

```python
import contextlib
import math
import numpy as np
import ml_dtypes
import concourse.bass as bass
import concourse.mybir as mybir
from concourse.bass_utils import run_bass_kernel_spmd

F32 = mybir.dt.float32
BF16 = mybir.dt.bfloat16
AF = mybir.ActivationFunctionType
ALU = mybir.AluOpType
AX = mybir.AxisListType

NCORES = 8
NB = 2
D = 1024
SEQ = 2048
CTXL = 256
T = SEQ + CTXL
NT = T // 128
DFF = 2816
NF = DFF // 128
EPS = 1e-6
NCH = T // 8
BIG = -30000.0
DMA_K = 16
COMPUTE = ("pe", "act", "dve", "pool")
SAME_ENGINE_SYNC = True


class Buf:
    __slots__ = ("t", "name", "writers", "readers")

    def __init__(self, ap, name=""):
        self.t = ap
        self.name = name
        self.writers = {}
        self.readers = {}

    def __getitem__(self, idx):
        return View(self, self.t[idx])

    def ap(self):
        return View(self, self.t)

    def v(self, pattern=None, **kw):
        if pattern is None:
            return View(self, self.t)
        return View(self, self.t.rearrange(pattern, **kw))


class View:
    __slots__ = ("buf", "ap")

    def __init__(self, buf, ap):
        self.buf = buf
        self.ap = ap

    def __getitem__(self, idx):
        return View(self.buf, self.ap[idx])

    def rearrange(self, *a, **k):
        return View(self.buf, self.ap.rearrange(*a, **k))

    def bitcast(self, dt):
        return View(self.buf, self.ap.bitcast(dt))

    def bc(self, shape):
        return View(self.buf, self.ap.to_broadcast(list(shape)))

    def unsq(self, i):
        return View(self.buf, self.ap.unsqueeze(i))


class FW:
    def __init__(self, nc):
        self.nc = nc
        self.stack = contextlib.ExitStack()
        self.streams = {e: [] for e in ("pe", "act", "dve", "pool", "sp")}
        self.seq = {e: 0 for e in COMPUTE}
        self.sems = {}
        self.waited = {e: {} for e in self.streams}
        self.dma_count = {"sp": 0, "pool": 0, "act": 0}
        self.dma_last = {}
        self.n_inst = 0

    def sem(self, key):
        if key not in self.sems:
            name = "s_" + ("_".join(str(k) for k in key) if isinstance(key, tuple) else str(key))
            self.sems[key] = self.stack.enter_context(self.nc.semaphore(name))
        return self.sems[key]

    def sbuf(self, name, shape, dtype):
        t = self.stack.enter_context(self.nc.sbuf_tensor(name, list(shape), dtype))
        return Buf(t[:], name)

    def psum(self, name, shape, dtype=F32):
        t = self.stack.enter_context(self.nc.psum_tensor(name, list(shape), dtype))
        return Buf(t[:], name)

    def dram(self, name, shape, dtype, kind="Internal"):
        t = self.nc.dram_tensor(name, list(shape), dtype, kind=kind)
        return Buf(t.ap(), name)

    def _need(self, eng, waits, key, val):
        if self.waited[eng].get(key, 0) >= val:
            return
        if val > waits.get(key, 0):
            waits[key] = val

    def _deps(self, eng, reads, writes):
        waits = {}
        for v in reads:
            for key, val in v.buf.writers.items():
                if key == eng and eng == "pe":
                    continue
                self._need(eng, waits, key, val)
        for v in writes:
            for key, val in v.buf.writers.items():
                if key == eng and (eng == "pe" or not SAME_ENGINE_SYNC):
                    continue
                self._need(eng, waits, key, val)
            for key, val in v.buf.readers.items():
                if key == eng and (eng == "pe" or not SAME_ENGINE_SYNC):
                    continue
                self._need(eng, waits, key, val)
        for key, val in waits.items():
            self.waited[eng][key] = val
        return list(waits.items())

    def _mark(self, token, reads, writes):
        key, val = token
        for v in reads:
            v.buf.readers[key] = val
        for v in writes:
            b = v.buf
            b.writers = {key: val}
            b.readers = {}

    def op(self, eng, fn, reads=(), writes=()):
        reads = [r for r in reads if isinstance(r, View)]
        writes = [w for w in writes if isinstance(w, View)]
        waits = self._deps(eng, reads, writes)
        self.seq[eng] += 1
        token = (eng, self.seq[eng])
        self._mark(token, reads, writes)
        self.streams[eng].append((waits, fn, (eng, 1)))
        self.n_inst += 1
        return token

    def dma(self, out, in_, q="sp", **kw):
        i = self.dma_count[q]
        self.dma_count[q] += 1
        r, rnd = i % DMA_K, i // DMA_K
        key = ("dma", q, r)
        waits = {}
        if rnd > 0:
            self._need(q, waits, key, 16 * rnd)
        for k2, v2 in waits.items():
            self.waited[q][k2] = v2
        w2 = self._deps(q, [in_], [out])
        allw = list(waits.items()) + w2
        token = (key, 16 * (rnd + 1))
        self.dma_last[key] = 16 * (rnd + 1)
        self._mark(token, [in_], [out])
        oa, ia = out.ap, in_.ap
        self.streams[q].append((allw, lambda e, oa=oa, ia=ia, kw=kw: e.dma_start(out=oa, in_=ia, **kw), (key, 16)))
        self.n_inst += 1
        return token

    def barrier(self):
        toks = [(e, self.seq[e]) for e in COMPUTE if self.seq[e] > 0]
        toks += list(self.dma_last.items())
        for eng in self.streams:
            waits = {}
            for key, val in toks:
                if key == eng:
                    continue
                self._need(eng, waits, key, val)
            for k2, v2 in waits.items():
                self.waited[eng][k2] = v2
            if waits:
                self.streams[eng].append((list(waits.items()), None, None))

    def emit(self):
        nc = self.nc
        for e in COMPUTE:
            self.sem(e)
        for q in self.dma_count:
            for r in range(DMA_K):
                self.sem(("dma", q, r))
        engmap = {"pe": "tensor", "act": "scalar", "dve": "vector", "pool": "gpsimd", "sp": "sync"}
        with nc.Block() as block:
            for e, attr in engmap.items():
                stream = self.streams[e]

                def body(engine, stream=stream):
                    for waits, fn, inc in stream:
                        for key, val in waits:
                            engine.wait_ge(self.sems[key], val)
                        if fn is None:
                            continue
                        ins = fn(engine)
                        if inc is not None:
                            ins.then_inc(self.sems[inc[0]], inc[1])

                getattr(block, attr)(body)
        self.stack.close()

    def matmul(self, out, lhsT, rhs, start=True, stop=True, **kw):
        oa, la, ra = out.ap, lhsT.ap, rhs.ap
        return self.op("pe", lambda e: e.matmul(oa, la, ra, start=start, stop=stop, **kw),
                       reads=[lhsT, rhs], writes=[out])

    def transpose(self, out, in_, ident):
        oa, ia, da = out.ap, in_.ap, ident.ap
        return self.op("pe", lambda e: e.transpose(oa, ia, da), reads=[in_, ident], writes=[out])

    def act(self, out, in_, func, bias=None, scale=None, accum_out=None):
        oa, ia = out.ap, in_.ap
        kw = {}
        reads = [in_]
        writes = [out]
        if bias is not None:
            if isinstance(bias, View):
                kw["bias"] = bias.ap
                reads.append(bias)
            else:
                kw["bias"] = bias
        if scale is not None:
            if isinstance(scale, View):
                kw["scale"] = scale.ap
                reads.append(scale)
            else:
                kw["scale"] = scale
        if accum_out is not None:
            kw["accum_out"] = accum_out.ap
            writes.append(accum_out)
        return self.op("act", lambda e: e.activation(oa, ia, func, **kw), reads=reads, writes=writes)

    def tt(self, out, in0, in1, op, eng="dve"):
        oa, a, b = out.ap, in0.ap, in1.ap
        return self.op(eng, lambda e: e.tensor_tensor(oa, a, b, op), reads=[in0, in1], writes=[out])

    def ts(self, out, in0, s1, s2, op0, op1=None, eng="dve"):
        oa, a = out.ap, in0.ap
        reads = [in0]
        s1a = s1.ap if isinstance(s1, View) else s1
        s2a = s2.ap if isinstance(s2, View) else s2
        if isinstance(s1, View):
            reads.append(s1)
        if isinstance(s2, View):
            reads.append(s2)
        kw = {}
        if op1 is not None:
            kw["op1"] = op1
        return self.op(eng, lambda e: e.tensor_scalar(oa, a, s1a, s2a, op0, **kw), reads=reads, writes=[out])

    def stt(self, out, in0, scalar, in1, op0, op1, eng="dve"):
        oa, a, b = out.ap, in0.ap, in1.ap
        reads = [in0, in1]
        sa = scalar.ap if isinstance(scalar, View) else scalar
        if isinstance(scalar, View):
            reads.append(scalar)
        eng = "dve"
        return self.op(eng, lambda e: e.scalar_tensor_tensor(oa, a, sa, b, op0, op1), reads=reads, writes=[out])

    def copy(self, out, in_, eng="dve"):
        oa, ia = out.ap, in_.ap
        if eng == "act":
            return self.op(eng, lambda e: e.copy(oa, ia), reads=[in_], writes=[out])
        return self.op(eng, lambda e: e.tensor_copy(oa, ia), reads=[in_], writes=[out])

    def memset(self, out, val, eng="pool"):
        oa = out.ap
        return self.op(eng, lambda e: e.memset(oa, val), reads=[], writes=[out])

    def reduce(self, out, in_, op, axis=AX.X, eng="dve"):
        oa, ia = out.ap, in_.ap
        return self.op(eng, lambda e: e.tensor_reduce(oa, ia, axis, op), reads=[in_], writes=[out])

    def recip(self, out, in_):
        oa, ia = out.ap, in_.ap
        return self.op("dve", lambda e: e.reciprocal(oa, ia), reads=[in_], writes=[out])

    def scan(self, out, d0, d1, initial, op0=ALU.mult, op1=ALU.add):
        oa, a, b = out.ap, d0.ap, d1.ap
        reads = [d0, d1]
        ia = initial.ap if isinstance(initial, View) else initial
        if isinstance(initial, View):
            reads.append(initial)
        return self.op("dve", lambda e: e.tensor_tensor_scan(oa, a, b, ia, op0, op1), reads=reads, writes=[out])


class Arena:
    def __init__(self, fw, words):
        self.fw = fw
        self.words = words
        self.base = fw.sbuf("arena", [128, words], F32)
        self.off = 0
        self.n = 0

    def alloc(self, nelem, dtype=F32, name=None):
        if dtype == BF16:
            w = (nelem + 1) // 2
        else:
            w = nelem
        w = (w + 7) // 8 * 8
        assert self.off + w <= self.words, "arena overflow %d + %d > %d" % (self.off, w, self.words)
        ap = self.base.t[:, self.off:self.off + w]
        if dtype == BF16:
            ap = ap.bitcast(BF16)[:, 0:nelem]
        else:
            ap = ap[:, 0:nelem]
        self.off += w
        self.n += 1
        return Buf(ap, name or ("a%d" % self.n))

    def reset(self):
        self.fw.barrier()
        self.off = 0

    def sub(self, start, size):
        return SubArena(self, start, size)


class SubArena:
    def __init__(self, parent, start, size):
        self.parent = parent
        self.start = start
        self.limit = start + size
        self.off = start
        assert self.limit <= parent.words

    def alloc(self, nelem, dtype=F32, name=None):
        w = (nelem + 1) // 2 if dtype == BF16 else nelem
        w = (w + 7) // 8 * 8
        assert self.off + w <= self.limit, "sub-arena overflow %d + %d > %d" % (self.off, w, self.limit)
        ap = self.parent.base.t[:, self.off:self.off + w]
        if dtype == BF16:
            ap = ap.bitcast(BF16)[:, 0:nelem]
        else:
            ap = ap[:, 0:nelem]
        self.off += w
        return Buf(ap, name or ("s%d" % self.off))

    def reset(self):
        self.parent.fw.barrier()
        self.off = self.start


def _bf16(a):
    return np.asarray(a, dtype=np.float32).astype(ml_dtypes.bfloat16)


def host_constants():
    k = {}
    k["k_ident_bf"] = _bf16(np.eye(128))
    k["k_ident_f"] = np.eye(128, dtype=np.float32)
    t = np.arange(SEQ)
    rows = (t // 64).astype(np.float32)
    cols = (t % 64).astype(np.float32)
    inv = (np.float32(10000.0) ** (-np.arange(0, 32, 2, dtype=np.float32) / np.float32(32))).astype(np.float32)
    ar = (rows[:, None] * inv[None, :]).astype(np.float32)
    ac = (cols[:, None] * inv[None, :]).astype(np.float32)
    cr, sr, cc, sc = np.cos(ar), np.sin(ar), np.cos(ac), np.sin(ac)
    cos_t = np.concatenate([cr, cr, cc, cc], axis=1).astype(np.float32)
    sin_t = np.concatenate([-sr, sr, -sc, sc], axis=1).astype(np.float32)
    k["k_rope_cos"] = cos_t.reshape(16, 128, 64)
    k["k_rope_sin"] = sin_t.reshape(16, 128, 64)
    n = np.arange(SEQ, dtype=np.int64)
    nk = (n[:, None] * n[None, :]) % SEQ
    ang = nk.astype(np.float64) * (2.0 * np.pi / SEQ)
    k["k_dft"] = np.stack([_bf16(np.cos(ang)), _bf16(np.sin(ang))])
    m = np.arange(64, dtype=np.int64)
    a64 = ((m[:, None] * m[None, :]) % 64).astype(np.float64) * (2.0 * np.pi / 64)
    cb = np.zeros((256, 256), np.float32)
    sb = np.zeros((256, 256), np.float32)
    for h in range(4):
        cb[h * 64:(h + 1) * 64, h * 64:(h + 1) * 64] = np.cos(a64)
        sb[h * 64:(h + 1) * 64, h * 64:(h + 1) * 64] = np.sin(a64)
    k["k_cblk"] = cb
    k["k_sblk"] = sb
    s_idx = np.arange(128) // 16
    k["k_s5mask"] = (s_idx[None, :] >= s_idx[:, None]).astype(np.float32)
    k["k_tau"] = np.tile(np.arange(-7, 9, dtype=np.float32)[None, :], (128, 1))
    k["k_nidx"] = np.tile(np.arange(NCH, dtype=np.float32)[None, :], (128, 1))
    qc = np.arange(64)
    cs = np.clip(qc - 8, 0, 48)
    kc = np.arange(64)
    ok = (kc[:, None] >= cs[None, :]) & (kc[:, None] < cs[None, :] + 16)
    blk = np.where(ok, 0.0, BIG).astype(np.float32)
    k["k_colmask"] = np.tile(blk, (2, 8))
    return k


def na_plan():
    pats = {}
    plan = []
    for t in range(16):
        lst = []
        for u in range(16):
            pat = []
            anyv = False
            for krl in range(2):
                for qrl in range(2):
                    kr, qr = 2 * u + krl, 2 * t + qrl
                    rs = min(max(qr - 4, 0), 24)
                    if rs <= kr < rs + 8:
                        pat.append(kr - qr + 7)
                        anyv = True
                    else:
                        pat.append(None)
            if anyv:
                pat = tuple(pat)
                if pat not in pats:
                    pats[pat] = len(pats)
                lst.append((u, pats[pat]))
        plan.append(lst)
    plist = [None] * len(pats)
    for p, i in pats.items():
        plist[i] = p
    return plan, plist


def build(stages=None, dbg=(), layers=(0, 1), batches=(0, 1)):
    nc = bass.Bass("TRN2", target_bir_lowering=False)
    fw = FW(nc)
    dbg = set(dbg)

    def ein(name, shape, dt=F32):
        return fw.dram(name, shape, dt, kind="ExternalInput")

    def scratch(name, shape, dt):
        return fw.dram(name, shape, dt, kind=("ExternalOutput" if name in dbg else "Internal"))

    x = ein("x", [NB, SEQ, D])
    ctx = ein("ctx", [NB, CTXL, D])
    c = ein("c", [NB, D])
    c_ctx = ein("c_ctx", [1, D])
    w_mod = ein("w_mod", [2, D, 6 * D])
    b_mod = ein("b_mod", [2, 6 * D])
    g_norm1 = ein("g_norm1", [2, D])
    w_in = ein("w_in", [2, D, 1792])
    att_q_gain = ein("att_q_gain", [2, 64])
    att_k_gain = ein("att_k_gain", [2, 64])
    na_q_gain = ein("na_q_gain", [2, 64])
    na_k_gain = ein("na_k_gain", [2, 64])
    na_rel_bias = ein("na_rel_bias", [2, 60, 31])
    w_fourier = ein("w_fourier", [2, 256, 256])
    ssm_lam_re = ein("ssm_lam_re", [2, 2, 16, 64])
    ssm_lam_im = ein("ssm_lam_im", [2, 2, 16, 64])
    ssm_log_dt = ein("ssm_log_dt", [2, 2, 16])
    ssm_b_re = ein("ssm_b_re", [2, 2, 16, 64, 16])
    ssm_b_im = ein("ssm_b_im", [2, 2, 16, 64, 16])
    ssm_c_re = ein("ssm_c_re", [2, 2, 16, 16, 64])
    ssm_c_im = ein("ssm_c_im", [2, 2, 16, 16, 64])
    ssm_d = ein("ssm_d", [2, 256])
    w_glu = ein("w_glu", [2, 256, 256])
    b_glu = ein("b_glu", [2, 256])
    g_group = ein("g_group", [2, D])
    w_out = ein("w_out", [2, D, D])
    g_norm2 = ein("g_norm2", [2, D])
    w_ff1 = ein("w_ff1", [2, D, DFF])
    w_ff3 = ein("w_ff3", [2, D, DFF])
    w_ff2 = ein("w_ff2", [2, DFF, D])
    k_ident_bf = ein("k_ident_bf", [128, 128], BF16)
    k_ident_f = ein("k_ident_f", [128, 128])
    k_rope_cos = ein("k_rope_cos", [16, 128, 64])
    k_rope_sin = ein("k_rope_sin", [16, 128, 64])
    k_dft = ein("k_dft", [2, SEQ, SEQ], BF16)
    k_cblk = ein("k_cblk", [256, 256])
    k_sblk = ein("k_sblk", [256, 256])
    k_s5mask = ein("k_s5mask", [128, 128])
    k_tau = ein("k_tau", [128, 16])
    k_nidx = ein("k_nidx", [128, NCH])
    k_colmask = ein("k_colmask", [128, 512])

    out = fw.dram("out", [NB, SEQ, D], F32, kind="ExternalOutput")

    XR = scratch("XR", [NB, T, D], F32)
    MODV = scratch("MODV", [2, 3, 6 * D], F32)
    WIN = scratch("WIN", [2, 128, 8, 1792], BF16)
    WOUT = scratch("WOUT", [2, 128, 8, D], BF16)
    W1 = scratch("W1", [2, NF, 128, 8, 128], BF16)
    W3 = scratch("W3", [2, NF, 128, 8, 128], BF16)
    W2 = scratch("W2", [2, DFF, D], BF16)
    QK = scratch("QK", [NB, 896, T], BF16)
    VV = scratch("VV", [NB, T, 390], BF16)
    FIN = scratch("FIN", [NB, T, 256], BF16)
    UT = scratch("UT", [NB, 256, T], BF16)
    OO = scratch("OO", [NB, T, D], BF16)
    PADT = scratch("PADT", [2, 61, 8192], F32)
    BIAS = scratch("BIAS", [2, 32, 128, 512], BF16)
    WCS = scratch("WCS", [2, 2, 256, 256], BF16)
    SCRU = scratch("SCRU", [2, 16, 8, 16, NCH], BF16)
    SCRY = scratch("SCRY", [2, 16, 8, 16, NCH], BF16)
    S5T = scratch("S5T", [2, 2, 16, 128, 128], BF16)
    S5B = scratch("S5B", [2, 2, 16, 128, 2, 64], BF16)
    S5C = scratch("S5C", [2, 2, 8, 128, 2, 128], BF16)
    S5D = scratch("S5D", [2, 2, 128, 8], F32)
    S5COS = scratch("S5COS", [2, 2, 128, 8, NCH], F32)
    S5SIN = scratch("S5SIN", [2, 2, 128, 8, NCH], F32)

    ar = Arena(fw, 47 * 1024)
    banks = [fw.psum("bank%d" % i, [128, 512], F32) for i in range(8)]

    all_stages = ["W", "M", "A", "GQA", "NAB", "NA", "FW", "F", "S5P", "S5", "C"]
    if stages is None:
        stages = all_stages
    stages = set(stages)

    def bank_bf(i):
        return View(banks[i], banks[i].t.bitcast(BF16))

    def rstd_from(ss, n, rs, tmp):
        fw.act(tmp, ss, AF.Sqrt, scale=1.0 / n, bias=EPS)
        fw.recip(rs, tmp)

    def stage_W():
        for l in range(2):
            for half in range(2):
                sl = slice(half * 896, (half + 1) * 896)
                fw.dma(WIN[l][:, :, sl], w_in[l].rearrange("(kc p) n -> p kc n", p=128)[:, :, sl], q="pool")
                yield 4.0
            fw.dma(WOUT[l], w_out[l].rearrange("(kc p) n -> p kc n", p=128), q="pool")
            yield 4.0
            for f in range(NF):
                fw.dma(W1[l][f], w_ff1[l][:, f * 128:(f + 1) * 128].rearrange("(kc p) n -> p kc n", p=128), q="pool")
                yield 4.0
                fw.dma(W3[l][f], w_ff3[l][:, f * 128:(f + 1) * 128].rearrange("(kc p) n -> p kc n", p=128), q="pool")
                yield 4.0
            for f0 in range(0, DFF, 704):
                fw.dma(W2[l][f0:f0 + 704, :], w_ff2[l][f0:f0 + 704, :], q="pool")
                yield 4.0

    def stage_M(l, A_=None, bk=(0, 1, 2, 3)):
        if A_ is None:
            ar.reset()
            A_ = ar
        Bk = [banks[i_] for i_ in bk]
        cT = A_.alloc(24)
        cT3 = cT.v("p (k r) -> p k r", r=3)
        for r in range(2):
            fw.dma(cT3[:, :, r], c[r].rearrange("(kc p) -> p kc", p=128), allow_slow_non_contiguous=True)
        fw.dma(cT3[:, :, 2], c_ctx[0].rearrange("(kc p) -> p kc", p=128), allow_slow_non_contiguous=True)
        cS = A_.alloc(24)
        fw.act(cS.ap(), cT.ap(), AF.Silu)
        cS3 = cS.v("p (k r) -> p k r", r=3)
        bmb = [A_.alloc(256) for _ in range(2)]
        g1b = A_.alloc(D)
        g2b = A_.alloc(D)
        fw.dma(g1b[0:3, :], g_norm1[l:l + 1, :].bc([3, D]))
        fw.dma(g2b[0:3, :], g_norm2[l:l + 1, :].bc([3, D]))
        mv = A_.alloc(6 * D)
        wbuf = [A_.alloc(8 * 256) for _ in range(2)]
        for nb in range(24):
            sl = slice(nb * 256, (nb + 1) * 256)
            wt = wbuf[nb % 2].v("p (k n) -> p k n", k=8)
            bm = bmb[nb % 2]
            fw.dma(wt, w_mod[l][:, sl].rearrange("(kc p) n -> p kc n", p=128))
            fw.dma(bm[0:3, :], b_mod[l:l + 1, sl].bc([3, 256]))
            ps = Bk[nb % 2]
            for kc in range(8):
                fw.matmul(ps[0:3, 0:256], cS3[:, kc, :], wt[:, kc, :], start=(kc == 0), stop=(kc == 7))
            fw.tt(mv[0:3, sl], ps[0:3, 0:256], bm[0:3, :], ALU.add)
            yield 5.0
        fw.stt(mv[0:3, D:2 * D], mv[0:3, D:2 * D], 1.0, g1b[0:3, :], ALU.add, ALU.mult)
        fw.stt(mv[0:3, 4 * D:5 * D], mv[0:3, 4 * D:5 * D], 1.0, g2b[0:3, :], ALU.add, ALU.mult)
        fw.dma(MODV[l], mv[0:3, :])

    def modrow(l, r, idx):
        return MODV[l][r:r + 1, idx * D:(idx + 1) * D].bc([128, D])

    def load_consts_ident(A_=None):
        idb = (A_ or ar).alloc(128, BF16)
        fw.dma(idb.ap(), k_ident_bf.ap())
        return idb

    def stage_A(b, l):
        ar.reset()
        idb = load_consts_ident()
        win = ar.alloc(8 * 1792, BF16)
        win3 = win.v("p (k n) -> p k n", k=8)
        fw.dma(win3, WIN[l])
        A1 = ar.alloc(D); B1 = ar.alloc(D); A1c = ar.alloc(D); B1c = ar.alloc(D)
        fw.dma(A1.ap(), modrow(l, b, 1)); fw.dma(B1.ap(), modrow(l, b, 0))
        fw.dma(A1c.ap(), modrow(l, 2, 1)); fw.dma(B1c.ap(), modrow(l, 2, 0))
        G = ar.alloc(14 * 64)
        G3 = G.v("p (h d) -> p h d", d=64)
        gsrc = [att_q_gain] * 4 + [att_k_gain] * 2 + [na_q_gain] * 4 + [na_k_gain] * 4
        for h in range(14):
            fw.dma(G3[:, h, :], gsrc[h][l:l + 1, :].bc([128, 64]))
        COS = ar.alloc(16 * 64); SIN = ar.alloc(16 * 64)
        COS3 = COS.v("p (t d) -> p t d", d=64); SIN3 = SIN.v("p (t d) -> p t d", d=64)
        fw.dma(COS3, k_rope_cos.v("t p d -> p t d"))
        fw.dma(SIN3, k_rope_sin.v("t p d -> p t d"))
        two = lambda n, dt=F32: [ar.alloc(n, dt) for _ in range(2)]
        xs = two(D); sqj = ar.alloc(D, BF16)
        ss = two(4); sd = two(4); rstd = two(4)
        tt_ = two(D); hb = two(D, BF16); hT = two(D, BF16)
        qks = two(896); sq2 = two(896); ssh = two(16); sdh = two(16); rsh = two(16)
        qkn = two(896); vsw = two(384); m1 = two(384); m2 = two(384)
        qkb = two(896, BF16); qkT = two(896, BF16)
        vaug = two(6 * 65, BF16); finb = two(256, BF16); uTb = two(256, BF16)
        for vb in vaug:
            fw.memset(vb.v("p (h e) -> p h e", e=65)[:, :, 64:65], 1.0)

        def load(i):
            if i >= NT:
                return
            if l == 0:
                src = ctx[b][i * 128:(i + 1) * 128, :] if i < 2 else x[b][(i - 2) * 128:(i - 1) * 128, :]
            else:
                src = XR[b][i * 128:(i + 1) * 128, :]
            fw.dma(xs[i % 2].ap(), src)

        def frontA(i):
            j = i % 2
            isctx = i < 2
            fw.act(sqj.ap(), xs[j].ap(), AF.Square, accum_out=ss[j][:, 0:1])
            rstd_from(ss[j][:, 0:1], D, rstd[j][:, 0:1], sd[j][:, 0:1])
            fw.stt(tt_[j].ap(), xs[j].ap(), rstd[j][:, 0:1], (A1c if isctx else A1).ap(), ALU.mult, ALU.mult)
            fw.tt(hb[j].ap(), tt_[j].ap(), (B1c if isctx else B1).ap(), ALU.add, eng="pool")

        def frontB(i):
            j = i % 2
            pT = bank_bf(4 + 2 * j)
            for k in range(8):
                fw.transpose(pT[:, k * 128:(k + 1) * 128], hb[j][:, k * 128:(k + 1) * 128], idb.ap())
            fw.copy(hT[j].ap(), pT[:, 0:D], eng="act")
            hT3 = hT[j].v("p (k n) -> p k n", k=8)
            for nb in range(3):
                for k in range(8):
                    fw.matmul(banks[nb].ap(), hT3[:, k, :], win3[:, k, nb * 512:(nb + 1) * 512], start=(k == 0), stop=(k == 7))
            for m in range(2):
                for k in range(8):
                    fw.matmul(banks[3][:, m * 128:(m + 1) * 128], win3[:, k, 1536 + m * 128:1536 + (m + 1) * 128], hT3[:, k, :],
                              start=(k == 0), stop=(k == 7))

        def mid(i):
            j = i % 2
            va3 = vaug[j].v("p (h e) -> p h e", e=65)
            fw.copy(qks[j][:, 0:384], banks[0][:, 0:384], eng="act")
            fw.copy(va3[:, 0:2, 0:64], banks[0][:, 384:512].rearrange("p (h d) -> p h d", d=64), eng="act")
            fw.copy(qks[j][:, 384:896], banks[1][:, 0:512], eng="dve")
            fw.copy(va3[:, 2:6, 0:64], banks[2][:, 0:256].rearrange("p (h d) -> p h d", d=64), eng="act")
            fw.copy(finb[j].ap(), banks[2][:, 256:512], eng="act")
            fw.copy(uTb[j].ap(), banks[3][:, 0:256], eng="dve")
            fw.dma(VV[b][i * 128:(i + 1) * 128, :], vaug[j].ap())
            fw.dma(FIN[b][i * 128:(i + 1) * 128, :], finb[j].ap())
            fw.dma(UT[b].rearrange("(m p) n -> p m n", p=128)[:, :, i * 128:(i + 1) * 128],
                   uTb[j].v("p (m n) -> p m n", m=2))

        def backE(i):
            j = i % 2
            isctx = i < 2
            fw.act(sq2[j].ap(), qks[j].ap(), AF.Square)
            fw.reduce(ssh[j][:, 0:14], sq2[j].v("p (h d) -> p h d", d=64), ALU.add)
            rstd_from(ssh[j][:, 0:14], 64, rsh[j][:, 0:14], sdh[j][:, 0:14])
            qkn3 = qkn[j].v("p (h d) -> p h d", d=64)
            fw.tt(qkn3, qks[j].v("p (h d) -> p h d", d=64), rsh[j][:, 0:14].unsq(2).bc([128, 14, 64]), ALU.mult)
            fw.tt(qkn[j].ap(), qkn[j].ap(), G.ap(), ALU.mult, eng="pool")
            if isctx:
                fw.copy(qkb[j][:, 0:384], qkn[j][:, 0:384], eng="pool")
            else:
                ti = i - 2
                fw.copy(vsw[j].v("p (a two x) -> p a two x", two=2, x=16),
                        qkn[j][:, 0:384].rearrange("p (a two x) -> p a two x", two=2, x=16)[:, :, ::-1, :], eng="pool")
                fw.tt(m1[j].v("p (h d) -> p h d", d=64), qkn3[:, 0:6, :], COS3[:, ti, :].unsq(1).bc([128, 6, 64]), ALU.mult)
                fw.tt(m2[j].v("p (h d) -> p h d", d=64), vsw[j].v("p (h d) -> p h d", d=64),
                      SIN3[:, ti, :].unsq(1).bc([128, 6, 64]), ALU.mult, eng="pool")
                fw.tt(qkb[j][:, 0:384], m1[j].ap(), m2[j].ap(), ALU.add)
            fw.copy(qkb[j][:, 384:896], qkn[j][:, 384:896], eng="pool")

        def backP(i):
            j = i % 2
            pq = bank_bf(5 + 2 * j)
            for jj in range(7):
                fw.transpose(pq[:, jj * 128:(jj + 1) * 128], qkb[j][:, jj * 128:(jj + 1) * 128], idb.ap())
            fw.copy(qkT[j].ap(), pq[:, 0:896], eng="act")
            fw.dma(QK[b].rearrange("(j p) n -> p j n", p=128)[:, :, i * 128:(i + 1) * 128],
                   qkT[j].v("p (j n) -> p j n", j=7))

        load(0)
        load(1)
        for i in range(NT):
            frontA(i)
            if i >= 1:
                backE(i - 1)
            frontB(i)
            load(i + 2)
            if i >= 1:
                backP(i - 1)
            mid(i)
        backE(NT - 1)
        backP(NT - 1)

    def attn_block(QT3, KT3, V3, qhead, khead, vcol, qcol0, nq, ktiles, PT, psS, psO, st):
        nsub = nq // 128
        nk = len(ktiles)

        def pv(ki, kt, pt):
            for sub in range(nsub):
                fw.matmul(psO[:, sub * 65:(sub + 1) * 65], pt[:, sub * 128:(sub + 1) * 128], V3[:, kt, vcol:vcol + 65],
                          start=(ki == 0 and sub == 0), stop=(ki == nk - 1 and sub == nsub - 1))

        pend = None
        for ki, kt in enumerate(ktiles):
            pS = psS[st["s"] % len(psS)]
            st["s"] += 1
            pt = PT[st["p"] % len(PT)]
            st["p"] += 1
            fw.matmul(pS[:, 0:nq], KT3[0:64, khead, kt * 128:(kt + 1) * 128], QT3[0:64, qhead, qcol0:qcol0 + nq])
            fw.act(pt[:, 0:nq], pS[:, 0:nq], AF.Exp, scale=0.125)
            if pend is not None:
                pv(*pend)
            pend = (ki, kt, pt)
            yield 1.0
        pv(*pend)

    def attn_finish(psO, nsub, rden, ob):
        po3 = psO[:, 0:nsub * 65].rearrange("p (s e) -> p s e", e=65)
        fw.recip(rden[:, 0:nsub], po3[:, :, 64:65].rearrange("p s e -> p (s e)"))
        fw.tt(ob.v("p (s d) -> p s d", d=64)[:, 0:nsub, :], po3[:, :, 0:64],
              rden[:, 0:nsub].unsq(2).bc([128, nsub, 64]), ALU.mult)

    def stage_GQA(b, l, A_=None, bk=(0, 1, 2, 3)):
        if A_ is None:
            ar.reset()
            A_ = ar
        Bk = [banks[i_] for i_ in bk]
        QT = A_.alloc(4 * T, BF16); KT = A_.alloc(2 * T, BF16); VA = A_.alloc(NT * 130, BF16)
        QT3 = QT.v("p (h n) -> p h n", h=4); KT3 = KT.v("p (h n) -> p h n", h=2)
        VA3 = VA.v("p (t c) -> p t c", c=130)
        fw.dma(QT3[0:64], QK[b][0:256, :].rearrange("(h d) n -> d h n", d=64))
        fw.dma(KT3[0:64], QK[b][256:384, :].rearrange("(h d) n -> d h n", d=64))
        fw.dma(VA3, VV[b].rearrange("(t p) c -> p t c", p=128)[:, :, 0:130])
        PT = [A_.alloc(512, BF16) for _ in range(3)]
        rden = A_.alloc(4)
        obs = [A_.alloc(256, BF16) for _ in range(2)]
        st = {"s": 0, "p": 0}
        psS = [Bk[0], Bk[1]]
        cnt = 0
        O3 = OO[b].rearrange("(t p) c -> p t c", p=128)
        blocks = [(256 + qb * 512, 512, list(range(NT)), 2 + qb * 4) for qb in range(4)]
        if l == 0:
            blocks.append((0, 256, [0, 1], 0))
        for h in range(4):
            kv = h // 2
            for (qc0, nq, kts, t0) in blocks:
                psO = Bk[2 + cnt % 2]
                ob = obs[cnt % 2]
                cnt += 1
                yield from attn_block(QT3, KT3, VA3, h, kv, kv * 65, qc0, nq, kts, PT, psS, psO, st)
                nsub = nq // 128
                attn_finish(psO, nsub, rden, ob)
                fw.dma(O3[:, t0:t0 + nsub, h * 64:(h + 1) * 64], ob.v("p (s d) -> p s d", d=64)[:, 0:nsub, :])

    NA_PLAN, NA_PATS = na_plan()

    def stage_NAB(l, A_=None, bk=(0, 1, 2, 3)):
        if A_ is None:
            ar.reset()
            A_ = ar
        Bk = [banks[i_] for i_ in bk]
        rb = A_.alloc(31)
        fw.dma(rb[0:60, :], na_rel_bias[l])
        pad = A_.alloc(128)
        fw.memset(pad.ap(), BIG)
        fw.ts(pad[0:60, 48:79], rb[0:60, ::-1], 8.0, None, ALU.mult)
        padt = PADT[l].ap.tensor
        base = PADT[l].ap.offset
        fw.dma(View(PADT, bass.AP(tensor=padt, offset=base, ap=[[8192, 61], [127, 64], [1, 127]])),
               pad[0:61, 0:127].unsq(1).bc([61, 64, 127]))
        cm = A_.alloc(512)
        fw.dma(cm.ap(), k_colmask.ap())
        stg = [A_.alloc(512) for _ in range(2)]
        bt = [A_.alloc(512, BF16) for _ in range(2)]
        for bid, pat in enumerate(NA_PATS):
            s_ = stg[bid % 2]
            for krl in range(2):
                for qrl in range(2):
                    dri = pat[krl * 2 + qrl]
                    for h in range(4):
                        row = 60 if dri is None else h * 15 + dri
                        src = View(PADT, bass.AP(tensor=padt, offset=base + row * 8192 + 63, ap=[[126, 64], [1, 64]]))
                        fw.dma(s_[krl * 64:(krl + 1) * 64, h * 128 + qrl * 64:h * 128 + (qrl + 1) * 64], src)
            fw.tt(bt[bid % 2].ap(), s_.ap(), cm.ap(), ALU.add)
            fw.dma(BIAS[l][bid], bt[bid % 2].ap())
            yield 9.0

    def stage_NA(b, l, A_=None, bk=(0, 1, 2, 3)):
        if A_ is None:
            ar.reset()
            A_ = ar
        Bk = [banks[i_] for i_ in bk]
        idb = load_consts_ident(A_)
        QT = A_.alloc(4 * T, BF16); KT = A_.alloc(4 * T, BF16); VD = A_.alloc(NT * 260, BF16)
        QT3 = QT.v("p (h n) -> p h n", h=4); KT3 = KT.v("p (h n) -> p h n", h=4)
        VD3 = VD.v("p (t c) -> p t c", c=260)
        fw.dma(QT3[0:64], QK[b][384:640, :].rearrange("(h d) n -> d h n", d=64))
        fw.dma(KT3[0:64], QK[b][640:896, :].rearrange("(h d) n -> d h n", d=64))
        fw.dma(VD3, VV[b].rearrange("(t p) c -> p t c", p=128)[:, :, 130:390])
        nbias = len(NA_PATS)
        BT = A_.alloc(nbias * 512, BF16)
        BT3 = BT.v("p (i n) -> p i n", n=512)
        fw.dma(BT3, BIAS[l][0:nbias].rearrange("i p n -> p i n"))
        PT = [A_.alloc(512, BF16) for _ in range(3)]
        rden = A_.alloc(4)
        obs = [A_.alloc(256, BF16) for _ in range(2)]
        O3 = OO[b].rearrange("(t p) c -> p t c", p=128)
        si = 0
        for t in range(16):
            qc0 = 256 + t * 128
            klist = [(2 + u, bid) for (u, bid) in NA_PLAN[t]] + [(0, None), (1, None)]
            psO = Bk[2 + t % 2]
            ob = obs[t % 2]
            nk = len(klist)
            def pv(ki, kt, pt, psO=psO, nk=nk):
                for h in range(4):
                    fw.matmul(psO[:, h * 65:(h + 1) * 65], pt[:, h * 128:(h + 1) * 128], VD3[:, kt, h * 65:(h + 1) * 65],
                              start=(ki == 0 and h == 0), stop=(ki == nk - 1 and h == 3))

            pend = None
            for ki, (kt, bid) in enumerate(klist):
                pS = Bk[si % 2]
                pt = PT[si % 3]
                si += 1
                for h in range(4):
                    fw.matmul(pS[:, h * 128:(h + 1) * 128], KT3[0:64, h, kt * 128:(kt + 1) * 128], QT3[0:64, h, qc0:qc0 + 128],
                              start=(h == 0), stop=(h == 3 and bid is None))
                if bid is not None:
                    fw.matmul(pS.ap(), idb.ap(), BT3[:, bid, :], start=False, stop=True)
                fw.act(pt.ap(), pS.ap(), AF.Exp, scale=0.125)
                if pend is not None:
                    pv(*pend)
                pend = (ki, kt, pt)
                yield 3.0
            pv(*pend)
            attn_finish(psO, 4, rden, ob)
            fw.dma(OO[b][(2 + t) * 128:(3 + t) * 128, 256:512], ob.ap())
        if l == 0:
            st = {"s": 0, "p": 0}
            for h in range(4):
                psO = Bk[2 + h % 2]
                ob = obs[h % 2]
                yield from attn_block(QT3, KT3, VD3, h, h, h * 65, 0, 256, [0, 1], PT, [Bk[0], Bk[1]], psO, st)
                attn_finish(psO, 2, rden, ob)
                fw.dma(O3[:, 0:2, 256 + h * 64:256 + (h + 1) * 64], ob.v("p (s d) -> p s d", d=64)[:, 0:2, :])

    def stage_FW(l, A_=None, bk=(0, 1, 2, 3)):
        if A_ is None:
            ar.reset()
            A_ = ar
        Bk = [banks[i_] for i_ in bk]
        wf = A_.alloc(2 * 256)
        wf3 = wf.v("p (k n) -> p k n", k=2)
        fw.dma(wf3, w_fourier[l].rearrange("(k p) n -> p k n", p=128))
        for ci, ktab in enumerate((k_cblk, k_sblk)):
            cb = A_.alloc(2 * 256)
            cb3 = cb.v("p (k n) -> p k n", k=2)
            fw.dma(cb3, ktab.v("(k p) n -> p k n", p=128))
            wo = A_.alloc(2 * 256, BF16)
            wo3 = wo.v("p (k n) -> p k n", k=2)
            for fo in range(2):
                ps = Bk[(ci * 2 + fo) % 4]
                for k in range(2):
                    fw.matmul(ps[:, 0:256], cb3[:, k, fo * 128:(fo + 1) * 128], wf3[:, k, :], start=(k == 0), stop=(k == 1))
                fw.act(wo3[:, fo, :], ps[:, 0:256], AF.Copy, scale=(1.0 if ci == 0 else -1.0))
            fw.dma(WCS[l][ci].rearrange("(k p) n -> p k n", p=128), wo3)
            yield 15.0

    def stage_F(b, l, A_=None, bk=(0, 1, 2, 3)):
        if A_ is None:
            ar.reset()
            A_ = ar
        Bk = [banks[i_] for i_ in bk]
        fin = A_.alloc(NT * 256, BF16)
        fin3 = fin.v("p (t f) -> p t f", f=256)
        fw.dma(fin3, FIN[b].rearrange("(t p) f -> p t f", p=128))
        wcs = A_.alloc(2 * 2 * 256, BF16)
        wcs4 = wcs.v("p (c k n) -> p c k n", c=2, k=2)
        for ci in range(2):
            fw.dma(wcs4[:, ci], WCS[l][ci].rearrange("(k p) n -> p k n", p=128))
        tabs = [A_.alloc(512, BF16) for _ in range(6)]
        G = [A_.alloc(4 * 512, BF16) for _ in range(2)]
        ofb = [A_.alloc(256, BF16) for _ in range(2)]
        ti = 0
        oi = 0

        def second_stage(Gv, nk, tile0, scale):
            nonlocal oi
            for kt in range(nk // 128):
                ps = Bk[2 + oi % 2]
                n_ = 0
                for ci in range(2):
                    for fc in range(2):
                        fw.matmul(ps[:, 0:256], Gv[:, ci, fc, kt * 128:(kt + 1) * 128], wcs4[:, ci, fc, :],
                                  start=(n_ == 0), stop=(n_ == 3))
                        n_ += 1
                o_ = ofb[oi % 2]
                oi += 1
                fw.act(o_.ap(), ps[:, 0:256], AF.Copy, scale=scale)
                fw.dma(OO[b][(tile0 + kt) * 128:(tile0 + kt + 1) * 128, 512:768], o_.ap())

        for kb in range(4):
            for ncix in range(16):
                for ci in range(2):
                    tb = tabs[ti % 6]
                    ti += 1
                    fw.dma(tb.ap(), k_dft[ci][ncix * 128:(ncix + 1) * 128, kb * 512:(kb + 1) * 512])
                    for fc in range(2):
                        fw.matmul(Bk[ci * 2 + fc].ap(), fin3[:, 2 + ncix, fc * 128:(fc + 1) * 128], tb.ap(),
                                  start=(ncix == 0), stop=(ncix == 15))
                yield 2.0
            Gb = G[kb % 2]
            Gv = Gb.v("p (c f n) -> p c f n", c=2, f=2)
            for ci in range(2):
                for fc in range(2):
                    if (ci + fc) % 2 == 0:
                        fw.copy(Gv[:, ci, fc, :], Bk[ci * 2 + fc].ap(), eng="act")
                    else:
                        fw.copy(Gv[:, ci, fc, :], Bk[ci * 2 + fc].ap(), eng="dve")
            second_stage(Gv, 512, 2 + kb * 4, 1.0 / math.sqrt(SEQ * 64.0))
        if l == 0:
            for ncix in range(2):
                for ci in range(2):
                    tb = tabs[ti % 6]
                    ti += 1
                    fw.dma(tb[:, 0:256], k_dft[ci][ncix * 1024:(ncix + 1) * 1024:8, 0:256])
                    for fc in range(2):
                        fw.matmul(Bk[ci * 2 + fc][:, 0:256], fin3[:, ncix, fc * 128:(fc + 1) * 128], tb[:, 0:256],
                                  start=(ncix == 0), stop=(ncix == 1))
            Gb = G[0]
            Gv = Gb.v("p (c f n) -> p c f n", c=2, f=2)
            for ci in range(2):
                for fc in range(2):
                    fw.copy(Gv[:, ci, fc, 0:256], Bk[ci * 2 + fc][:, 0:256], eng=("act" if (ci + fc) % 2 == 0 else "dve"))
            second_stage(Gv, 256, 0, 1.0 / math.sqrt(CTXL * 64.0))

    def stage_C(b, l):
        ar.reset()
        idb = load_consts_ident()
        wo = ar.alloc(8 * D, BF16)
        wo3 = wo.v("p (k n) -> p k n", k=8)
        fw.dma(wo3, WOUT[l])
        gg = ar.alloc(D)
        fw.dma(gg.ap(), g_group[l:l + 1, :].bc([128, D]))
        GA1 = ar.alloc(D); A2 = ar.alloc(D); B2 = ar.alloc(D); GA2 = ar.alloc(D)
        hid = ar.alloc(NF * 512, BF16)
        hid3 = hid.v("p (f n) -> p f n", f=NF)
        two = lambda n, dt=F32: [ar.alloc(n, dt) for _ in range(2)]
        h2T = two(8 * 512, BF16)
        x1 = two(4 * D)
        wt1 = [ar.alloc(8 * 128, BF16) for _ in range(3)]
        wt3 = [ar.alloc(8 * 128, BF16) for _ in range(3)]
        w2t = [ar.alloc(D, BF16) for _ in range(3)]
        xs = two(D); ob = two(D, BF16)
        sq = ar.alloc(D)
        ss4 = two(4); sd4 = two(4); rs4 = two(4)
        ss = two(4); sd = two(4); rstd = two(4)
        onb = two(D, BF16); onT = two(D, BF16)
        tmpA = ar.alloc(D); tmpB = ar.alloc(D); h2b = two(D, BF16)
        sqj = ar.alloc(D, BF16)
        tmpC = ar.alloc(D)
        sa = two(512)
        xo = two(D)
        groups = [[2 + g * 4 + s_ for s_ in range(4)] for g in range(4)]
        if l == 0:
            groups = [[0, 1]] + groups
        rows = [(2 if grp[0] < 2 else b) for grp in groups]
        st = {"wi": 0, "w2i": 0, "oi": 0, "pre": 0}

        def h2T3(gp):
            return h2T[gp].v("p (k n) -> p k n", k=8)

        def x13(gp):
            return x1[gp].v("p (s n) -> p s n", s=4)

        def P1a(i, j):
            if l == 0:
                src = ctx[b][i * 128:(i + 1) * 128, :] if i < 2 else x[b][(i - 2) * 128:(i - 1) * 128, :]
            else:
                src = XR[b][i * 128:(i + 1) * 128, :]
            fw.dma(xs[j].ap(), src)
            fw.dma(ob[j].ap(), OO[b][i * 128:(i + 1) * 128, :])
            fw.act(sq.ap(), ob[j].ap(), AF.Square)
            fw.reduce(ss4[j][:, 0:4], sq.v("p (g d) -> p g d", g=4), ALU.add)
            rstd_from(ss4[j][:, 0:4], 256, rs4[j][:, 0:4], sd4[j][:, 0:4])
            for g_ in range(4):
                sl = slice(g_ * 256, (g_ + 1) * 256)
                fw.stt(onb[j][:, sl], ob[j][:, sl], rs4[j][:, g_:g_ + 1], gg[:, sl], ALU.mult, ALU.mult)

        def P1b(i, j):
            pT = bank_bf(6)
            for k in range(8):
                fw.transpose(pT[:, k * 128:(k + 1) * 128], onb[j][:, k * 128:(k + 1) * 128], idb.ap())
            fw.copy(onT[j].ap(), pT[:, 0:D], eng="act")
            onT3 = onT[j].v("p (k n) -> p k n", k=8)
            for nb in range(2):
                for k in range(8):
                    fw.matmul(banks[4 + nb].ap(), onT3[:, k, :], wo3[:, k, nb * 512:(nb + 1) * 512], start=(k == 0), stop=(k == 7))

        def P2a(i, j, gp, s_):
            for nb in range(2):
                sl = slice(nb * 512, (nb + 1) * 512)
                fw.tt(tmpA[:, sl], banks[4 + nb].ap(), GA1[:, sl], ALU.mult)
            fw.tt(x13(gp)[:, s_, :], tmpA.ap(), xs[j].ap(), ALU.add, eng="pool")
            fw.act(sqj.ap(), x13(gp)[:, s_, :], AF.Square, accum_out=ss[j][:, 0:1])
            rstd_from(ss[j][:, 0:1], D, rstd[j][:, 0:1], sd[j][:, 0:1])
            fw.stt(tmpB.ap(), x13(gp)[:, s_, :], rstd[j][:, 0:1], A2.ap(), ALU.mult, ALU.mult)
            fw.tt(h2b[j].ap(), tmpB.ap(), B2.ap(), ALU.add, eng="pool")

        def P2b(i, j, gp, s_):
            pT2 = bank_bf(7)
            for k in range(8):
                fw.transpose(pT2[:, k * 128:(k + 1) * 128], h2b[j][:, k * 128:(k + 1) * 128], idb.ap())
            fw.copy(h2T3(gp)[:, :, s_ * 128:(s_ + 1) * 128], pT2[:, 0:D].rearrange("p (k n) -> p k n", k=8), eng="act")

        def pre_slots(grp, gp):
            n = len(grp)
            slots = []
            for k in range(n + 2):
                def slot(k=k):
                    if 0 <= k < n:
                        P1a(grp[k], k % 2)
                    if 0 <= k - 2 < n:
                        P2b(grp[k - 2], (k - 2) % 2, gp, k - 2)
                    if 0 <= k - 1 < n:
                        P1b(grp[k - 1], (k - 1) % 2)
                        P2a(grp[k - 1], (k - 1) % 2, gp, k - 1)
                slots.append(slot)
            return slots

        def load_mods(row):
            fw.dma(GA1.ap(), modrow(l, row, 2)); fw.dma(A2.ap(), modrow(l, row, 4))
            fw.dma(B2.ap(), modrow(l, row, 3)); fw.dma(GA2.ap(), modrow(l, row, 5))

        def issue_w13(f):
            a_ = wt1[st["wi"] % 3]; b_ = wt3[st["wi"] % 3]
            st["wi"] += 1
            fw.dma(a_.ap(), W1[l][f].rearrange("p k n -> p (k n)"))
            fw.dma(b_.ap(), W3[l][f].rearrange("p k n -> p (k n)"))
            return a_, b_

        pre_done = False
        preissued = []
        for gi_, grp in enumerate(groups):
            gp = gi_ % 2
            nt_ = len(grp)
            ntk = nt_ * 128
            if not pre_done:
                load_mods(rows[gi_])
                for slot in pre_slots(grp, gp):
                    slot()
            nxt = gi_ + 1
            can_pipe = nxt < len(groups) and rows[nxt] == rows[gi_]
            nslots = pre_slots(groups[nxt], nxt % 2) if can_pipe else []
            for f in range(NF):
                if preissued:
                    a_, b_ = preissued.pop(0)
                else:
                    a_, b_ = issue_w13(f)
                a3 = a_.v("p (k n) -> p k n", k=8); b3_ = b_.v("p (k n) -> p k n", k=8)
                pa = banks[(f % 2) * 2]; pb = banks[1 + (f % 2) * 2]
                for k in range(8):
                    fw.matmul(pa[:, 0:ntk], a3[:, k, :], h2T3(gp)[:, k, 0:ntk], start=(k == 0), stop=(k == 7))
                for k in range(8):
                    fw.matmul(pb[:, 0:ntk], b3_[:, k, :], h2T3(gp)[:, k, 0:ntk], start=(k == 0), stop=(k == 7))
                sj = sa[f % 2]
                fw.act(sj[:, 0:ntk], pa[:, 0:ntk], AF.Silu)
                fw.tt(hid3[:, f, 0:ntk], sj[:, 0:ntk], pb[:, 0:ntk], ALU.mult)
                if nslots and f % 3 == 1:
                    nslots.pop(0)()
            while nslots:
                nslots.pop(0)()
            pre_done = can_pipe
            for f in range(NF):
                w_ = w2t[st["w2i"] % 3]
                st["w2i"] += 1
                fw.dma(w_.ap(), W2[l][f * 128:(f + 1) * 128, :])
                for s_ in range(nt_):
                    for nb in range(2):
                        fw.matmul(banks[s_ * 2 + nb].ap(), hid3[:, f, s_ * 128:(s_ + 1) * 128], w_[:, nb * 512:(nb + 1) * 512],
                                  start=(f == 0), stop=(f == NF - 1))
            if nxt < len(groups):
                preissued = [issue_w13(f) for f in range(3)]
            for s_, i in enumerate(grp):
                xoj = xo[st["oi"] % 2]
                st["oi"] += 1
                for nb in range(2):
                    sl = slice(nb * 512, (nb + 1) * 512)
                    fw.tt(tmpC[:, sl], banks[s_ * 2 + nb].ap(), GA2[:, sl], ALU.mult)
                fw.tt(xoj.ap(), tmpC.ap(), x13(gp)[:, s_, :], ALU.add, eng="pool")
                if l == 0:
                    fw.dma(XR[b][i * 128:(i + 1) * 128, :], xoj.ap())
                else:
                    fw.dma(out[b][(i - 2) * 128:(i - 1) * 128, :], xoj.ap())

    TWO_PI = 2.0 * math.pi
    CW1 = 6.28125
    CW2 = TWO_PI - 6.28125
    MAGIC = 12582912.0
    PI_SAFE = 3.141592

    def range_reduce(out, in_, kbuf, shift=0.0, eng="dve"):
        if shift != 0.0:
            fw.ts(out, in_, shift, None, ALU.add, eng=eng)
            src = out
        else:
            src = in_
        fw.ts(kbuf, src, 1.0 / TWO_PI, MAGIC, ALU.mult, ALU.add, eng=eng)
        fw.ts(kbuf, kbuf, -MAGIC, None, ALU.add, eng=eng)
        fw.stt(out, kbuf, -CW1, src, ALU.mult, ALU.add, eng=eng)
        fw.stt(out, kbuf, -CW2, out, ALU.mult, ALU.add, eng=eng)
        fw.ts(out, out, -PI_SAFE, PI_SAFE, ALU.max, ALU.min, eng=eng)

    def stage_S5P(l, A_=None, bk=(0, 1, 2, 3)):
        if A_ is None:
            ar.reset()
            A_ = ar
        Bk = [banks[i_] for i_ in bk]
        idf = A_.alloc(128)
        fw.dma(idf.ap(), k_ident_f.ap())
        tau = A_.alloc(16)
        fw.dma(tau.ap(), k_tau.ap())
        nidx = A_.alloc(NCH)
        fw.dma(nidx.ap(), k_nidx.ap())
        msk = A_.alloc(128)
        fw.dma(msk.ap(), k_s5mask.ap())
        mark = A_.off
        for d_ in range(2):
            A_.off = mark
            if d_ == 1:
                fw.barrier()
            lamre = A_.alloc(8); lamim = A_.alloc(8); dt = A_.alloc(8)
            for gi in range(2):
                sl = slice(64 * gi, 64 * gi + 64)
                fw.dma(lamre[sl, :], ssm_lam_re[l][d_].rearrange("(q two) p -> two p q", two=2)[gi], allow_slow_non_contiguous=True)
                fw.dma(lamim[sl, :], ssm_lam_im[l][d_].rearrange("(q two) p -> two p q", two=2)[gi], allow_slow_non_contiguous=True)
                fw.dma(dt[sl, :], ssm_log_dt[l][d_:d_ + 1, :].rearrange("o (q two) -> two o q", two=2)[gi].bc([64, 8]),
                       allow_slow_non_contiguous=True)
            fw.act(dt.ap(), dt.ap(), AF.Exp)
            zr = A_.alloc(8); zi = A_.alloc(8)
            fw.tt(zr.ap(), lamre.ap(), dt.ap(), ALU.mult)
            fw.tt(zi.ap(), lamim.ap(), dt.ap(), ALU.mult)
            PZ = A_.alloc(128); PEx = A_.alloc(128); KB = A_.alloc(128); RS = A_.alloc(128); RC = A_.alloc(128)
            Are = A_.alloc(128); Aim = A_.alloc(128)
            v3 = lambda bf: bf.v("p (t q) -> p t q", q=8)
            fw.tt(v3(PZ), tau.ap().unsq(2).bc([128, 16, 8]), zr.ap().unsq(1).bc([128, 16, 8]), ALU.mult)
            fw.act(PEx.ap(), PZ.ap(), AF.Exp)
            fw.tt(v3(PZ), tau.ap().unsq(2).bc([128, 16, 8]), zi.ap().unsq(1).bc([128, 16, 8]), ALU.mult)
            range_reduce(RS.ap(), PZ.ap(), KB.ap())
            range_reduce(RC.ap(), PZ.ap(), KB.ap(), shift=0.5 * math.pi)
            fw.act(RS.ap(), RS.ap(), AF.Sin)
            fw.act(RC.ap(), RC.ap(), AF.Sin)
            fw.tt(Are.ap(), PEx.ap(), RC.ap(), ALU.mult)
            fw.tt(Aim.ap(), PEx.ap(), RS.ap(), ALU.mult)
            Are3 = v3(Are); Aim3 = v3(Aim)
            fw.dma(S5D[l][d_], v3(PEx)[:, 15, :])
            phi = A_.alloc(8); kb8 = A_.alloc(8)
            fw.ts(phi.ap(), zi.ap(), 8.0, None, ALU.mult)
            range_reduce(phi.ap(), phi.ap(), kb8.ap())
            ANG = A_.alloc(8 * NCH); KB2 = A_.alloc(8 * NCH); R2 = A_.alloc(8 * NCH)
            a3 = lambda bf: bf.v("p (q n) -> p q n", q=8)
            fw.tt(a3(ANG), nidx.ap().unsq(1).bc([128, 8, NCH]), phi.ap().unsq(2).bc([128, 8, NCH]), ALU.mult)
            range_reduce(R2.ap(), ANG.ap(), KB2.ap())
            fw.act(R2.ap(), R2.ap(), AF.Sin)
            fw.dma(S5SIN[l][d_], a3(R2))
            R3 = R2
            range_reduce(R3.ap(), ANG.ap(), KB2.ap(), shift=0.5 * math.pi)
            fw.act(R3.ap(), R3.ap(), AF.Sin)
            fw.dma(S5COS[l][d_], a3(R3))
            yield 40.0
            nr = A_.alloc(8); den = A_.alloc(8); t8a = A_.alloc(8); t8b = A_.alloc(8); cr = A_.alloc(8); ci = A_.alloc(8)
            fw.ts(nr.ap(), Are3[:, 8, :], -1.0, None, ALU.add)
            fw.tt(den.ap(), lamre.ap(), lamre.ap(), ALU.mult)
            fw.tt(t8a.ap(), lamim.ap(), lamim.ap(), ALU.mult)
            fw.tt(den.ap(), den.ap(), t8a.ap(), ALU.add)
            fw.recip(den.ap(), den.ap())
            fw.tt(t8a.ap(), nr.ap(), lamre.ap(), ALU.mult)
            fw.tt(t8b.ap(), Aim3[:, 8, :], lamim.ap(), ALU.mult)
            fw.tt(t8a.ap(), t8a.ap(), t8b.ap(), ALU.add)
            fw.tt(cr.ap(), t8a.ap(), den.ap(), ALU.mult)
            fw.tt(t8a.ap(), Aim3[:, 8, :], lamre.ap(), ALU.mult)
            fw.tt(t8b.ap(), nr.ap(), lamim.ap(), ALU.mult)
            fw.tt(t8a.ap(), t8a.ap(), t8b.ap(), ALU.subtract)
            fw.tt(ci.ap(), t8a.ap(), den.ap(), ALU.mult)
            bre = A_.alloc(128); bim = A_.alloc(128); Bre = A_.alloc(128); Bim = A_.alloc(128); tb1 = A_.alloc(128); tb2 = A_.alloc(128)
            b3 = lambda bf: bf.v("p (q c) -> p q c", q=8)
            for gi in range(2):
                sl = slice(64 * gi, 64 * gi + 64)
                fw.dma(b3(bre)[sl], ssm_b_re[l][d_].rearrange("(q two) p c -> two p q c", two=2)[gi])
                fw.dma(b3(bim)[sl], ssm_b_im[l][d_].rearrange("(q two) p c -> two p q c", two=2)[gi])
            crb = cr.ap().unsq(2).bc([128, 8, 16]); cib = ci.ap().unsq(2).bc([128, 8, 16])
            fw.tt(b3(tb1), b3(bre), crb, ALU.mult); fw.tt(b3(tb2), b3(bim), cib, ALU.mult)
            fw.tt(Bre.ap(), tb1.ap(), tb2.ap(), ALU.subtract)
            fw.tt(b3(tb1), b3(bim), crb, ALU.mult); fw.tt(b3(tb2), b3(bre), cib, ALU.mult)
            fw.tt(Bim.ap(), tb1.ap(), tb2.ap(), ALU.add)
            yield 25.0
            Cre = A_.alloc(128); Cim = A_.alloc(128)
            for (Cdst, csrc, bkx) in ((Cre, ssm_c_re, 0), (Cim, ssm_c_im, 1)):
                X = A_.alloc(128)
                for q in range(8):
                    for gi in range(2):
                        fw.dma(X[16 * q:16 * q + 16, 64 * gi:64 * gi + 64], csrc[l][d_][2 * q + gi])
                fw.transpose(Bk[bkx][:, 0:128], X.ap(), idf.ap())
                fw.copy(Cdst.ap(), Bk[bkx][:, 0:128], eng="act")
            BcR = A_.alloc(8 * 128); BcI = A_.alloc(8 * 128)
            CpR = A_.alloc(8 * 128); CpI = A_.alloc(8 * 128)
            CcR = A_.alloc(8 * 128, BF16); CcI = A_.alloc(8 * 128, BF16)
            u1 = A_.alloc(1024); u2 = A_.alloc(1024)
            f4 = lambda bf: bf.v("p (q s c) -> p q s c", q=8, s=8)
            pw = lambda A3, sl: A3[:, sl, :].rearrange("p s q -> p q s").unsq(3).bc([128, 8, 8, 16])
            qc = lambda bf: b3(bf).unsq(2).bc([128, 8, 8, 16])

            def cmul(dst_re, dst_im, Ar, Ai, Xr, Xi, neg_im=False):
                fw.tt(f4(u1), Ar, Xr, ALU.mult); fw.tt(f4(u2), Ai, Xi, ALU.mult, eng="pool")
                fw.tt(f4(dst_re), f4(u1), f4(u2), ALU.subtract)
                fw.tt(f4(u1), Ar, Xi, ALU.mult); fw.tt(f4(u2), Ai, Xr, ALU.mult, eng="pool")
                if neg_im:
                    fw.stt(f4(dst_im), f4(u1), -1.0, f4(u2), ALU.mult, ALU.subtract)
                else:
                    fw.tt(f4(dst_im), f4(u1), f4(u2), ALU.add)

            cmul(BcR, BcI, pw(Are3, slice(14, 6, -1)), pw(Aim3, slice(14, 6, -1)), qc(Bre), qc(Bim))
            cmul(CpR, CpI, pw(Are3, slice(0, 8)), pw(Aim3, slice(0, 8)), qc(Cre), qc(Cim), neg_im=True)
            cmul(CcR, CcI, pw(Are3, slice(8, 16)), pw(Aim3, slice(8, 16)), qc(Cre), qc(Cim), neg_im=True)
            fw.dma(S5C[l][d_].rearrange("q p r n -> p q r n")[:, :, 0, :], CcR.v("p (q n) -> p q n", q=8))
            fw.dma(S5C[l][d_].rearrange("q p r n -> p q r n")[:, :, 1, :], CcI.v("p (q n) -> p q n", q=8))
            Tb = [A_.alloc(128, BF16) for _ in range(2)]
            BT = [A_.alloc(256, BF16) for _ in range(2)]
            qv = lambda bf, q: bf.v("p (q n) -> p q n", q=8)[:, q, :]
            for q in range(8):
                for gi in range(2):
                    g = 2 * q + gi
                    sl = slice(64 * gi, 64 * gi + 64)
                    ps = Bk[gi]
                    fw.matmul(ps[:, 0:128], qv(BcR, q)[sl], qv(CpR, q)[sl], start=True, stop=False)
                    fw.matmul(ps[:, 0:128], qv(BcI, q)[sl], qv(CpI, q)[sl], start=False, stop=True)
                    tb_ = Tb[g % 2]
                    fw.tt(tb_.ap(), ps[:, 0:128], msk.ap(), ALU.mult)
                    fw.dma(S5T[l][d_][g], tb_.ap())
                bt_ = BT[q % 2]
                for ri, src in enumerate((BcR, BcI)):
                    ps = Bk[2 + ri]
                    fw.transpose(ps[:, 0:128], qv(src, q), idf.ap())
                    fw.copy(bt_.v("p (g r n) -> p g r n", g=2, r=2)[:, :, ri, :], ps[:, 0:128].rearrange("p (g n) -> p g n", g=2), eng="act")
                fw.dma(S5B[l][d_][2 * q:2 * q + 2].rearrange("g sc r n -> sc g (r n)"), bt_.v("p (g x) -> p g x", g=2))
                yield 8.0

    def stage_S5(b, l, A_=None, bk=(0, 1, 2, 3)):
        if A_ is None:
            ar.reset()
            A_ = ar
        Bk = [banks[i_] for i_ in bk]
        idb = load_consts_ident(A_)
        uT = A_.alloc(2 * T, BF16)
        uT3 = uT.v("p (m n) -> p m n", m=2)
        fw.dma(uT3, UT[b].rearrange("(m p) n -> p m n", p=128))
        Dcol = A_.alloc(2)
        fw.dma(Dcol.ap(), ssm_d[l].rearrange("(m p) -> p m", p=128), allow_slow_non_contiguous=True)
        bgl = A_.alloc(2)
        fw.dma(bgl.ap(), b_glu[l].rearrange("(m p) -> p m", p=128), allow_slow_non_contiguous=True)
        wgf = A_.alloc(512)
        fw.dma(wgf.v("p (k n) -> p k n", k=2), w_glu[l].rearrange("(k p) n -> p k n", p=128))
        wgb = A_.alloc(512, BF16)
        fw.copy(wgb.ap(), wgf.ap())
        wgb3 = wgb.v("p (k n) -> p k n", k=2)
        Tm = A_.alloc(32 * 128, BF16); T3 = Tm.v("p (g n) -> p g n", g=32)
        fw.dma(T3, S5T[l].rearrange("d g sc n -> sc (d g) n"))
        Bm = A_.alloc(32 * 128, BF16); B4 = Bm.v("p (g r n) -> p g r n", g=32, r=2)
        fw.dma(Bm.v("p (g x) -> p g x", g=32), S5B[l].rearrange("d g sc r n -> sc (d g) (r n)"))
        Cm = A_.alloc(16 * 256, BF16); C4 = Cm.v("p (g r n) -> p g r n", g=16, r=2)
        fw.dma(Cm.v("p (g x) -> p g x", g=16), S5C[l].rearrange("d q p r n -> p (d q) (r n)"))
        DEC = A_.alloc(16); DEC3 = DEC.v("p (d q) -> p d q", d=2)
        fw.dma(DEC3, S5D[l].rearrange("d p q -> p d q"))
        stg = [A_.alloc(2 * 8 * NCH, BF16) for _ in range(2)]
        s4 = lambda bf: bf.v("p (m s j) -> p m s j", m=2, s=8)
        yield 10.0
        for m in range(2):
            fw.copy(s4(stg[0])[:, m], uT3[:, m, :].rearrange("p (j s) -> p s j", s=8), eng=("dve" if m == 0 else "pool"))
            fw.copy(s4(stg[1])[:, m, :, 0:32], uT3[:, m, 0:256][:, ::-1].rearrange("p (j s) -> p s j", s=8), eng="dve")
            fw.copy(s4(stg[1])[:, m, :, 32:NCH], uT3[:, m, 256:T][:, ::-1].rearrange("p (j s) -> p s j", s=8), eng="pool")
        U = A_.alloc(32 * NCH, BF16); U3 = U.v("p (g j) -> p g j", g=32)
        for d_ in range(2):
            for g in range(16):
                m, gl = g // 8, g % 8
                fw.dma(SCRU[d_][g].rearrange("s c j -> c s j"), s4(stg[d_])[16 * gl:16 * gl + 16, m])
        yield 20.0
        for d_ in range(2):
            for g in range(16):
                fw.dma(U3[:, d_ * 16 + g, :], SCRU[d_][g].rearrange("s c j -> (s c) j"))
            yield 15.0
        two = lambda n, dt=F32: [A_.alloc(n, dt) for _ in range(2)]
        cosb = two(NCH); sinb = two(NCH)
        Sre = two(NCH); Sim = two(NCH)
        t1 = two(NCH); t2 = two(NCH); t3 = two(NCH); t4 = two(NCH)
        wr_in = two(NCH); wi_in = two(NCH); wr = two(NCH); wi = two(NCH)
        Hre = two(NCH, BF16); Him = two(NCH, BF16)
        Yg = [A_.alloc(NCH, BF16) for _ in range(4)]

        def s5_front(it, d_, q):
            p = it % 2
            cb = cosb[p]; sb = sinb[p]; hr = Hre[p]; hi = Him[p]
            bS = (0, 1)
            fw.dma(cb.ap(), S5COS[l][d_][:, q, :])
            fw.dma(sb.ap(), S5SIN[l][d_][:, q, :])
            for gi in range(2):
                g = d_ * 16 + 2 * q + gi
                sl = slice(64 * gi, 64 * gi + 64)
                fw.matmul(Bk[bS[0]][sl, 0:NCH], B4[:, g, 0, :], U3[:, g, :])
                fw.matmul(Bk[bS[1]][sl, 0:NCH], B4[:, g, 1, :], U3[:, g, :])
            fw.copy(Sre[p].ap(), Bk[bS[0]][:, 0:NCH], eng="act")
            fw.copy(Sim[p].ap(), Bk[bS[1]][:, 0:NCH], eng="act")
            fw.tt(t1[p].ap(), Sre[p].ap(), cb.ap(), ALU.mult)
            fw.tt(t2[p].ap(), Sim[p].ap(), sb.ap(), ALU.mult, eng="pool")
            fw.tt(wr_in[p].ap(), t1[p].ap(), t2[p].ap(), ALU.add)
            fw.tt(t3[p].ap(), Sim[p].ap(), cb.ap(), ALU.mult, eng="pool")
            fw.tt(t4[p].ap(), Sre[p].ap(), sb.ap(), ALU.mult)
            fw.tt(wi_in[p].ap(), t3[p].ap(), t4[p].ap(), ALU.subtract, eng="pool")
            dec = DEC3[:, d_, q:q + 1].bc([128, NCH])
            fw.scan(wr[p].ap(), dec, wr_in[p].ap(), 0.0)
            fw.scan(wi[p].ap(), dec, wi_in[p].ap(), 0.0)
            fw.tt(t1[p].ap(), wr[p].ap(), cb.ap(), ALU.mult)
            fw.tt(t2[p].ap(), wi[p].ap(), sb.ap(), ALU.mult, eng="pool")
            fw.tt(hr.ap(), t1[p].ap(), t2[p].ap(), ALU.subtract)
            fw.tt(t3[p].ap(), wr[p].ap(), sb.ap(), ALU.mult, eng="pool")
            fw.tt(t4[p].ap(), wi[p].ap(), cb.ap(), ALU.mult)
            fw.tt(hi.ap(), t3[p].ap(), t4[p].ap(), ALU.add, eng="pool")

        def s5_back(it, d_, q):
            p = it % 2
            hr = Hre[p]; hi = Him[p]
            bY = (2, 3)
            for gi in range(2):
                g16 = 2 * q + gi
                g = d_ * 16 + g16
                sl = slice(64 * gi, 64 * gi + 64)
                psY = Bk[bY[gi]]
                fw.matmul(psY[:, 0:NCH], T3[:, g, :], U3[:, g, :], start=True, stop=False)
                fw.matmul(psY[:, 1:NCH], C4[sl, d_ * 8 + q, 0, :], hr[sl, 0:NCH - 1], start=False, stop=False)
                fw.matmul(psY[:, 1:NCH], C4[sl, d_ * 8 + q, 1, :], hi[sl, 0:NCH - 1], start=False, stop=True)
                yg = Yg[p * 2 + gi]
                fw.copy(yg.ap(), psY[:, 0:NCH], eng="act")
                fw.dma(SCRY[d_][g16].rearrange("t c j -> (t c) j"), yg.ap())

        its = [(d_, q) for d_ in range(2) for q in range(8)]
        for it, (d_, q) in enumerate(its):
            s5_front(it, d_, q)
            if it >= 1:
                s5_back(it - 1, *its[it - 1])
            yield 8.0
        s5_back(len(its) - 1, *its[-1])
        ys = stg
        for d_ in range(2):
            for g in range(16):
                m, gl = g // 8, g % 8
                fw.dma(s4(ys[d_])[16 * gl:16 * gl + 16, m], SCRY[d_][g].rearrange("t c j -> c t j"))
        yield 20.0
        y = A_.alloc(2 * T); y3 = y.v("p (m n) -> p m n", m=2)
        gb = A_.alloc(2 * T, BF16); gb3 = gb.v("p (m n) -> p m n", m=2)
        for m in range(2):
            f3 = s4(ys[0])[:, m]
            r3 = s4(ys[1])[:, m]
            eng = "dve" if m == 0 else "pool"
            fw.tt(y3[:, m, 0:256].rearrange("p (j t) -> p j t", t=8), f3[:, :, 0:32].rearrange("p t j -> p j t"),
                  r3[:, :, 0:32].rearrange("p s n -> p n s")[:, ::-1, ::-1], ALU.add, eng=eng)
            fw.tt(y3[:, m, 256:T].rearrange("p (j t) -> p j t", t=8), f3[:, :, 32:NCH].rearrange("p t j -> p j t"),
                  r3[:, :, 32:NCH].rearrange("p s n -> p n s")[:, ::-1, ::-1], ALU.add, eng=eng)
            fw.stt(y3[:, m, :], uT3[:, m, :], Dcol[:, m:m + 1], y3[:, m, :], ALU.mult, ALU.add, eng=eng)
        yield 30.0
        fw.act(y.ap(), y.ap(), AF.Gelu_apprx_tanh)
        fw.copy(gb.ap(), y.ap(), eng="pool")
        osT = A_.alloc(2 * T, BF16); osT3 = osT.v("p (m n) -> p m n", m=2)
        gate = [A_.alloc(512) for _ in range(2)]
        bi = 0
        for mo in range(2):
            for (c0, nn) in [(0, 512), (512, 512), (1024, 512), (1536, 512), (2048, 256)]:
                ps = Bk[bi % 2]
                gt = gate[bi % 2]
                bi += 1
                for m in range(2):
                    fw.matmul(ps[:, 0:nn], wgb3[:, m, mo * 128:(mo + 1) * 128], gb3[:, m, c0:c0 + nn], start=(m == 0), stop=(m == 1))
                fw.act(gt[:, 0:nn], ps[:, 0:nn], AF.Sigmoid, bias=bgl[:, mo:mo + 1])
                fw.tt(osT3[:, mo, c0:c0 + nn], y3[:, mo, c0:c0 + nn], gt[:, 0:nn], ALU.mult, eng=("dve" if bi % 2 == 0 else "pool"))
                yield 4.0
        otb = [A_.alloc(256, BF16) for _ in range(2)]
        for i in range(NT):
            if l == 1 and i < 2:
                continue
            pT = View(Bk[2 + i % 2], Bk[2 + i % 2].t.bitcast(BF16))
            for mo in range(2):
                fw.transpose(pT[:, mo * 128:(mo + 1) * 128], osT3[:, mo, i * 128:(i + 1) * 128], idb.ap())
            ot = otb[i % 2]
            fw.copy(ot.ap(), pT[:, 0:256], eng="act")
            fw.dma(OO[b][i * 128:(i + 1) * 128, 768:1024], ot.ap())
            yield 2.0

    def run(gen):
        for _ in gen:
            pass

    def corun(gens):
        acc = [0.0] * len(gens)
        live = list(range(len(gens)))
        while live:
            i_ = min(live, key=lambda k_: acc[k_])
            try:
                c_ = next(gens[i_])
                acc[i_] += (c_ or 1.0)
            except StopIteration:
                live.remove(i_)

    full = all(st_ in stages for st_ in all_stages) and tuple(layers) == (0, 1) and tuple(batches) == (0, 1)
    PRO_W = 17408

    def prologue(l, A_, bk):
        yield from stage_M(l, A_, bk)
        A_.reset()
        yield from stage_NAB(l, A_, bk)
        A_.reset()
        yield from stage_FW(l, A_, bk)
        A_.reset()
        yield from stage_S5P(l, A_, bk)

    if full:
        ar.reset()
        corun([stage_W(), prologue(0, ar.sub(0, PRO_W), (0, 1, 2, 3))])
    elif "W" in stages:
        run(stage_W())
    for l in layers:
        if not full:
            if "M" in stages:
                run(stage_M(l))
            if "NAB" in stages:
                run(stage_NAB(l))
            if "FW" in stages:
                run(stage_FW(l))
            if "S5P" in stages:
                run(stage_S5P(l))
        for b in batches:
            if "A" in stages:
                stage_A(b, l)
            if all(st_ in stages for st_ in ("GQA", "NA", "F", "S5")):
                ar.reset()
                corun([stage_S5(b, l, ar.sub(0, 38400), (0, 1, 2, 3)), stage_GQA(b, l, ar.sub(38400, 9728), (4, 5, 6, 7))])
                ar.reset()
                if full and l == 0 and b == 1:
                    gens = [stage_NA(b, l, ar.sub(0, 19200), (0, 1, 2, 2)), stage_F(b, l, ar.sub(19200, 8192), (4, 5, 6, 7)),
                            prologue(1, ar.sub(27392, PRO_W), (3, 3, 3, 3))]
                else:
                    gens = [stage_NA(b, l, ar.sub(0, 19200), (0, 1, 2, 3)), stage_F(b, l, ar.sub(19200, 8192), (4, 5, 6, 7))]
                corun(gens)
            else:
                if "GQA" in stages:
                    run(stage_GQA(b, l))
                if "NA" in stages:
                    run(stage_NA(b, l))
                if "F" in stages:
                    run(stage_F(b, l))
                if "S5" in stages:
                    run(stage_S5(b, l))
            if "C" in stages:
                stage_C(b, l)
    fw.barrier()
    fw.emit()
    return nc


_RESHAPE = {
    "c_ctx": (1, D),
    "na_rel_bias": (2, 60, 31),
}


def make_in_maps(inputs, ncores=NCORES):
    consts = host_constants()
    shared = {}
    for name, arr in inputs.items():
        if name in ("x", "ctx", "c"):
            continue
        a = np.ascontiguousarray(arr)
        if name in _RESHAPE:
            a = a.reshape(_RESHAPE[name])
        shared[name] = a
    shared.update(consts)
    maps = []
    for i in range(ncores):
        m = dict(shared)
        m["x"] = np.ascontiguousarray(inputs["x"][i * NB:(i + 1) * NB])
        m["ctx"] = np.ascontiguousarray(inputs["ctx"][i * NB:(i + 1) * NB])
        m["c"] = np.ascontiguousarray(inputs["c"][i * NB:(i + 1) * NB])
        maps.append(m)
    return maps


def kernel(**inputs):
    nc = build()
    maps = make_in_maps(inputs)
    res = run_bass_kernel_spmd(nc, maps, core_ids=list(range(NCORES)))
    outs = [np.asarray(r["out"]) for r in res.results]
    return np.concatenate(outs, axis=0).astype(np.float32)
```

```python
import contextlib
import math
import numpy as np
import ml_dtypes
import concourse.bass as bass
import concourse.mybir as mybir
from concourse.bass_utils import run_bass_kernel_spmd

F32 = mybir.dt.float32
BF16 = mybir.dt.bfloat16
AF = mybir.ActivationFunctionType
ALU = mybir.AluOpType
AX = mybir.AxisListType

NCORES = 8
NB = 2
D = 1024
SEQ = 2048
CTXL = 256
T = SEQ + CTXL
NT = T // 128
DFF = 2816
NF = DFF // 128
EPS = 1e-6
NCH = T // 8
BIG = -30000.0
DMA_K = 16
COMPUTE = ("pe", "act", "dve", "pool")
SAME_ENGINE_SYNC = True


class Buf:
    __slots__ = ("t", "name", "writers", "readers")

    def __init__(self, ap, name=""):
        self.t = ap
        self.name = name
        self.writers = {}
        self.readers = {}

    def __getitem__(self, idx):
        return View(self, self.t[idx])

    def ap(self):
        return View(self, self.t)

    def v(self, pattern=None, **kw):
        if pattern is None:
            return View(self, self.t)
        return View(self, self.t.rearrange(pattern, **kw))


class View:
    __slots__ = ("buf", "ap")

    def __init__(self, buf, ap):
        self.buf = buf
        self.ap = ap

    def __getitem__(self, idx):
        return View(self.buf, self.ap[idx])

    def rearrange(self, *a, **k):
        return View(self.buf, self.ap.rearrange(*a, **k))

    def bitcast(self, dt):
        return View(self.buf, self.ap.bitcast(dt))

    def bc(self, shape):
        return View(self.buf, self.ap.to_broadcast(list(shape)))

    def unsq(self, i):
        return View(self.buf, self.ap.unsqueeze(i))


class FW:
    def __init__(self, nc):
        self.nc = nc
        self.stack = contextlib.ExitStack()
        self.streams = {e: [] for e in ("pe", "act", "dve", "pool", "sp")}
        self.seq = {e: 0 for e in COMPUTE}
        self.sems = {}
        self.waited = {e: {} for e in self.streams}
        self.dma_count = {"sp": 0, "pool": 0, "act": 0}
        self.dma_last = {}
        self.n_inst = 0

    def sem(self, key):
        if key not in self.sems:
            name = "s_" + ("_".join(str(k) for k in key) if isinstance(key, tuple) else str(key))
            self.sems[key] = self.stack.enter_context(self.nc.semaphore(name))
        return self.sems[key]

    def sbuf(self, name, shape, dtype):
        t = self.stack.enter_context(self.nc.sbuf_tensor(name, list(shape), dtype))
        return Buf(t[:], name)

    def psum(self, name, shape, dtype=F32):
        t = self.stack.enter_context(self.nc.psum_tensor(name, list(shape), dtype))
        return Buf(t[:], name)

    def dram(self, name, shape, dtype, kind="Internal"):
        t = self.nc.dram_tensor(name, list(shape), dtype, kind=kind)
        return Buf(t.ap(), name)

    def _need(self, eng, waits, key, val):
        if self.waited[eng].get(key, 0) >= val:
            return
        if val > waits.get(key, 0):
            waits[key] = val

    def _deps(self, eng, reads, writes):
        waits = {}
        for v in reads:
            for key, val in v.buf.writers.items():
                if key == eng and eng == "pe":
                    continue
                self._need(eng, waits, key, val)
        for v in writes:
            for key, val in v.buf.writers.items():
                if key == eng and (eng == "pe" or not SAME_ENGINE_SYNC):
                    continue
                self._need(eng, waits, key, val)
            for key, val in v.buf.readers.items():
                if key == eng and (eng == "pe" or not SAME_ENGINE_SYNC):
                    continue
                self._need(eng, waits, key, val)
        for key, val in waits.items():
            self.waited[eng][key] = val
        return list(waits.items())

    def _mark(self, token, reads, writes):
        key, val = token
        for v in reads:
            v.buf.readers[key] = val
        for v in writes:
            b = v.buf
            b.writers = {key: val}
            b.readers = {}

    def op(self, eng, fn, reads=(), writes=()):
        reads = [r for r in reads if isinstance(r, View)]
        writes = [w for w in writes if isinstance(w, View)]
        waits = self._deps(eng, reads, writes)
        self.seq[eng] += 1
        token = (eng, self.seq[eng])
        self._mark(token, reads, writes)
        self.streams[eng].append((waits, fn, (eng, 1)))
        self.n_inst += 1
        return token

    def dma(self, out, in_, q="sp", **kw):
        i = self.dma_count[q]
        self.dma_count[q] += 1
        r, rnd = i % DMA_K, i // DMA_K
        key = ("dma", q, r)
        waits = {}
        if rnd > 0:
            self._need(q, waits, key, 16 * rnd)
        for k2, v2 in waits.items():
            self.waited[q][k2] = v2
        ob = out.buf
        merge = (not ob.readers) and bool(ob.writers) and all(isinstance(k_, tuple) for k_ in ob.writers)
        w2 = self._deps(q, [in_], [] if merge else [out])
        allw = list(waits.items()) + w2
        token = (key, 16 * (rnd + 1))
        self.dma_last[key] = 16 * (rnd + 1)
        if merge:
            in_.buf.readers[key] = token[1]
            ob.writers[key] = max(ob.writers.get(key, 0), token[1])
        else:
            self._mark(token, [in_], [out])
        oa, ia = out.ap, in_.ap
        self.streams[q].append((allw, lambda e, oa=oa, ia=ia, kw=kw: e.dma_start(out=oa, in_=ia, **kw), (key, 16)))
        self.n_inst += 1
        return token

    def barrier(self):
        toks = [(e, self.seq[e]) for e in COMPUTE if self.seq[e] > 0]
        toks += list(self.dma_last.items())
        for eng in self.streams:
            waits = {}
            for key, val in toks:
                if key == eng:
                    continue
                self._need(eng, waits, key, val)
            for k2, v2 in waits.items():
                self.waited[eng][k2] = v2
            if waits:
                self.streams[eng].append((list(waits.items()), None, None))

    def emit(self):
        nc = self.nc
        for e in COMPUTE:
            self.sem(e)
        for q in self.dma_count:
            for r in range(DMA_K):
                self.sem(("dma", q, r))
        engmap = {"pe": "tensor", "act": "scalar", "dve": "vector", "pool": "gpsimd", "sp": "sync"}
        with nc.Block() as block:
            for e, attr in engmap.items():
                stream = self.streams[e]

                def body(engine, stream=stream):
                    for waits, fn, inc in stream:
                        for key, val in waits:
                            engine.wait_ge(self.sems[key], val)
                        if fn is None:
                            continue
                        ins = fn(engine)
                        if inc is not None:
                            ins.then_inc(self.sems[inc[0]], inc[1])

                getattr(block, attr)(body)
        self.stack.close()

    def matmul(self, out, lhsT, rhs, start=True, stop=True, **kw):
        oa, la, ra = out.ap, lhsT.ap, rhs.ap
        return self.op("pe", lambda e: e.matmul(oa, la, ra, start=start, stop=stop, **kw),
                       reads=[lhsT, rhs], writes=[out])

    def transpose(self, out, in_, ident):
        oa, ia, da = out.ap, in_.ap, ident.ap
        return self.op("pe", lambda e: e.transpose(oa, ia, da), reads=[in_, ident], writes=[out])

    def act(self, out, in_, func, bias=None, scale=None, accum_out=None):
        oa, ia = out.ap, in_.ap
        kw = {}
        reads = [in_]
        writes = [out]
        if bias is not None:
            if isinstance(bias, View):
                kw["bias"] = bias.ap
                reads.append(bias)
            else:
                kw["bias"] = bias
        if scale is not None:
            if isinstance(scale, View):
                kw["scale"] = scale.ap
                reads.append(scale)
            else:
                kw["scale"] = scale
        if accum_out is not None:
            kw["accum_out"] = accum_out.ap
            writes.append(accum_out)
        return self.op("act", lambda e: e.activation(oa, ia, func, **kw), reads=reads, writes=writes)

    def tt(self, out, in0, in1, op, eng="dve"):
        oa, a, b = out.ap, in0.ap, in1.ap
        return self.op(eng, lambda e: e.tensor_tensor(oa, a, b, op), reads=[in0, in1], writes=[out])

    def ts(self, out, in0, s1, s2, op0, op1=None, eng="dve"):
        oa, a = out.ap, in0.ap
        reads = [in0]
        s1a = s1.ap if isinstance(s1, View) else s1
        s2a = s2.ap if isinstance(s2, View) else s2
        if isinstance(s1, View):
            reads.append(s1)
        if isinstance(s2, View):
            reads.append(s2)
        kw = {}
        if op1 is not None:
            kw["op1"] = op1
        return self.op(eng, lambda e: e.tensor_scalar(oa, a, s1a, s2a, op0, **kw), reads=reads, writes=[out])

    def stt(self, out, in0, scalar, in1, op0, op1, eng="dve"):
        oa, a, b = out.ap, in0.ap, in1.ap
        reads = [in0, in1]
        sa = scalar.ap if isinstance(scalar, View) else scalar
        if isinstance(scalar, View):
            reads.append(scalar)
        eng = "dve"
        return self.op(eng, lambda e: e.scalar_tensor_tensor(oa, a, sa, b, op0, op1), reads=reads, writes=[out])

    def copy(self, out, in_, eng="dve"):
        oa, ia = out.ap, in_.ap
        if eng == "act":
            return self.op(eng, lambda e: e.copy(oa, ia), reads=[in_], writes=[out])
        return self.op(eng, lambda e: e.tensor_copy(oa, ia), reads=[in_], writes=[out])

    def memset(self, out, val, eng="pool"):
        oa = out.ap
        return self.op(eng, lambda e: e.memset(oa, val), reads=[], writes=[out])

    def reduce(self, out, in_, op, axis=AX.X, eng="dve"):
        oa, ia = out.ap, in_.ap
        return self.op(eng, lambda e: e.tensor_reduce(oa, ia, axis, op), reads=[in_], writes=[out])

    def recip(self, out, in_):
        oa, ia = out.ap, in_.ap
        return self.op("dve", lambda e: e.reciprocal(oa, ia), reads=[in_], writes=[out])

    def scan(self, out, d0, d1, initial, op0=ALU.mult, op1=ALU.add):
        oa, a, b = out.ap, d0.ap, d1.ap
        reads = [d0, d1]
        ia = initial.ap if isinstance(initial, View) else initial
        if isinstance(initial, View):
            reads.append(initial)
        return self.op("dve", lambda e: e.tensor_tensor_scan(oa, a, b, ia, op0, op1), reads=reads, writes=[out])


class Arena:
    def __init__(self, fw, words):
        self.fw = fw
        self.words = words
        self.base = fw.sbuf("arena", [128, words], F32)
        self.off = 0
        self.n = 0

    def alloc(self, nelem, dtype=F32, name=None):
        if dtype == BF16:
            w = (nelem + 1) // 2
        else:
            w = nelem
        w = (w + 15) // 16 * 16
        assert self.off + w <= self.words, "arena overflow %d + %d > %d" % (self.off, w, self.words)
        ap = self.base.t[:, self.off:self.off + w]
        if dtype == BF16:
            ap = ap.bitcast(BF16)[:, 0:nelem]
        else:
            ap = ap[:, 0:nelem]
        self.off += w
        self.n += 1
        return Buf(ap, name or ("a%d" % self.n))

    def reset(self):
        self.fw.barrier()
        self.off = 0

    def sub(self, start, size):
        return SubArena(self, start, size)


class SubArena:
    def __init__(self, parent, start, size):
        self.parent = parent
        self.start = start
        self.limit = start + size
        self.off = start
        assert self.limit <= parent.words

    def alloc(self, nelem, dtype=F32, name=None):
        w = (nelem + 1) // 2 if dtype == BF16 else nelem
        w = (w + 15) // 16 * 16
        assert self.off + w <= self.limit, "sub-arena overflow %d + %d > %d" % (self.off, w, self.limit)
        ap = self.parent.base.t[:, self.off:self.off + w]
        if dtype == BF16:
            ap = ap.bitcast(BF16)[:, 0:nelem]
        else:
            ap = ap[:, 0:nelem]
        self.off += w
        return Buf(ap, name or ("s%d" % self.off))

    def reset(self):
        self.parent.fw.barrier()
        self.off = self.start


def _bf16(a):
    return np.asarray(a, dtype=np.float32).astype(ml_dtypes.bfloat16)


def host_constants():
    k = {}
    k["k_ident_bf"] = _bf16(np.eye(128))
    k["k_ident_f"] = np.eye(128, dtype=np.float32)
    t = np.arange(SEQ)
    rows = (t // 64).astype(np.float32)
    cols = (t % 64).astype(np.float32)
    inv = (np.float32(10000.0) ** (-np.arange(0, 32, 2, dtype=np.float32) / np.float32(32))).astype(np.float32)
    ar = (rows[:, None] * inv[None, :]).astype(np.float32)
    ac = (cols[:, None] * inv[None, :]).astype(np.float32)
    cr, sr, cc, sc = np.cos(ar), np.sin(ar), np.cos(ac), np.sin(ac)
    cos_t = np.concatenate([cr, cr, cc, cc], axis=1).astype(np.float32)
    sin_t = np.concatenate([-sr, sr, -sc, sc], axis=1).astype(np.float32)
    k["k_rope_cos"] = cos_t.reshape(16, 128, 64)
    k["k_rope_sin"] = sin_t.reshape(16, 128, 64)
    n = np.arange(SEQ, dtype=np.int64)
    nk = (n[:, None] * n[None, :]) % SEQ
    ang = nk.astype(np.float64) * (2.0 * np.pi / SEQ)
    k["k_dft"] = np.stack([_bf16(np.cos(ang)), _bf16(np.sin(ang))])
    m = np.arange(64, dtype=np.int64)
    a64 = ((m[:, None] * m[None, :]) % 64).astype(np.float64) * (2.0 * np.pi / 64)
    cb = np.zeros((256, 256), np.float32)
    sb = np.zeros((256, 256), np.float32)
    for h in range(4):
        cb[h * 64:(h + 1) * 64, h * 64:(h + 1) * 64] = np.cos(a64)
        sb[h * 64:(h + 1) * 64, h * 64:(h + 1) * 64] = np.sin(a64)
    k["k_cblk"] = cb
    k["k_sblk"] = sb
    s_idx = np.arange(128) // 16
    k["k_s5mask"] = (s_idx[None, :] >= s_idx[:, None]).astype(np.float32)
    k["k_tau"] = np.tile(np.arange(-7, 9, dtype=np.float32)[None, :], (128, 1))
    k["k_nidx"] = np.tile(np.arange(NCH, dtype=np.float32)[None, :], (128, 1))
    qc = np.arange(64)
    cs = np.clip(qc - 8, 0, 48)
    kc = np.arange(64)
    ok = (kc[:, None] >= cs[None, :]) & (kc[:, None] < cs[None, :] + 16)
    blk = np.where(ok, 0.0, BIG).astype(np.float32)
    k["k_colmask"] = np.tile(blk, (2, 8))
    return k


def na_plan():
    pats = {}
    plan = []
    for t in range(16):
        lst = []
        for u in range(16):
            pat = []
            anyv = False
            for krl in range(2):
                for qrl in range(2):
                    kr, qr = 2 * u + krl, 2 * t + qrl
                    rs = min(max(qr - 4, 0), 24)
                    if rs <= kr < rs + 8:
                        pat.append(kr - qr + 7)
                        anyv = True
                    else:
                        pat.append(None)
            if anyv:
                pat = tuple(pat)
                if pat not in pats:
                    pats[pat] = len(pats)
                lst.append((u, pats[pat]))
        plan.append(lst)
    plist = [None] * len(pats)
    for p, i in pats.items():
        plist[i] = p
    return plan, plist


def build(stages=None, dbg=(), layers=(0, 1), batches=(0, 1)):
    nc = bass.Bass("TRN2", target_bir_lowering=False)
    fw = FW(nc)
    dbg = set(dbg)

    def ein(name, shape, dt=F32):
        return fw.dram(name, shape, dt, kind="ExternalInput")

    def scratch(name, shape, dt):
        return fw.dram(name, shape, dt, kind=("ExternalOutput" if name in dbg else "Internal"))

    x = ein("x", [NB, SEQ, D])
    ctx = ein("ctx", [NB, CTXL, D])
    c = ein("c", [NB, D])
    c_ctx = ein("c_ctx", [1, D])
    w_mod = ein("w_mod", [2, D, 6 * D])
    b_mod = ein("b_mod", [2, 6 * D])
    g_norm1 = ein("g_norm1", [2, D])
    w_in = ein("w_in", [2, D, 1792])
    att_q_gain = ein("att_q_gain", [2, 64])
    att_k_gain = ein("att_k_gain", [2, 64])
    na_q_gain = ein("na_q_gain", [2, 64])
    na_k_gain = ein("na_k_gain", [2, 64])
    na_rel_bias = ein("na_rel_bias", [2, 60, 31])
    w_fourier = ein("w_fourier", [2, 256, 256])
    ssm_lam_re = ein("ssm_lam_re", [2, 2, 16, 64])
    ssm_lam_im = ein("ssm_lam_im", [2, 2, 16, 64])
    ssm_log_dt = ein("ssm_log_dt", [2, 2, 16])
    ssm_b_re = ein("ssm_b_re", [2, 2, 16, 64, 16])
    ssm_b_im = ein("ssm_b_im", [2, 2, 16, 64, 16])
    ssm_c_re = ein("ssm_c_re", [2, 2, 16, 16, 64])
    ssm_c_im = ein("ssm_c_im", [2, 2, 16, 16, 64])
    ssm_d = ein("ssm_d", [2, 256])
    w_glu = ein("w_glu", [2, 256, 256])
    b_glu = ein("b_glu", [2, 256])
    g_group = ein("g_group", [2, D])
    w_out = ein("w_out", [2, D, D])
    g_norm2 = ein("g_norm2", [2, D])
    w_ff1 = ein("w_ff1", [2, D, DFF])
    w_ff3 = ein("w_ff3", [2, D, DFF])
    w_ff2 = ein("w_ff2", [2, DFF, D])
    k_ident_bf = ein("k_ident_bf", [128, 128], BF16)
    k_ident_f = ein("k_ident_f", [128, 128])
    k_rope_cos = ein("k_rope_cos", [16, 128, 64])
    k_rope_sin = ein("k_rope_sin", [16, 128, 64])
    k_dft = ein("k_dft", [2, SEQ, SEQ], BF16)
    k_cblk = ein("k_cblk", [256, 256])
    k_sblk = ein("k_sblk", [256, 256])
    k_s5mask = ein("k_s5mask", [128, 128])
    k_tau = ein("k_tau", [128, 16])
    k_nidx = ein("k_nidx", [128, NCH])
    k_colmask = ein("k_colmask", [128, 512])

    out = fw.dram("out", [NB, SEQ, D], F32, kind="ExternalOutput")

    XR = scratch("XR", [NB, T, D], F32)
    MODV = scratch("MODV", [2, 3, 6 * D], F32)
    WIN = scratch("WIN", [2, 128, 8, 1792], BF16)
    WOUT = scratch("WOUT", [2, 128, 8, D], BF16)
    W1 = scratch("W1", [2, NF, 128, 8, 128], BF16)
    W3 = scratch("W3", [2, NF, 128, 8, 128], BF16)
    W2 = scratch("W2", [2, DFF, D], BF16)
    QK = scratch("QK", [NB, 896, T], BF16)
    VV = scratch("VV", [NB, T, 390], BF16)
    FIN = scratch("FIN", [NB, T, 256], BF16)
    UT = scratch("UT", [NB, 256, T], BF16)
    OO = scratch("OO", [NB, T, D], BF16)
    PADT = scratch("PADT", [2, 61, 8192], F32)
    BIAS = scratch("BIAS", [2, 32, 128, 512], BF16)
    WCS = scratch("WCS", [2, 2, 256, 256], BF16)
    SCRU = scratch("SCRU", [2, 16, 8, 16, NCH], BF16)
    SCRY = scratch("SCRY", [2, 16, 8, 16, NCH], BF16)
    S5T = scratch("S5T", [2, 2, 16, 128, 128], BF16)
    S5B = scratch("S5B", [2, 2, 16, 128, 2, 64], BF16)
    S5C = scratch("S5C", [2, 2, 8, 128, 2, 128], BF16)
    S5D = scratch("S5D", [2, 2, 128, 8], F32)
    S5COS = scratch("S5COS", [2, 2, 128, 8, NCH], F32)
    S5SIN = scratch("S5SIN", [2, 2, 128, 8, NCH], F32)

    ar = Arena(fw, 47 * 1024)
    banks = [fw.psum("bank%d" % i, [128, 512], F32) for i in range(8)]

    all_stages = ["W", "M", "A", "GQA", "NAB", "NA", "FW", "F", "S5P", "S5", "C"]
    if stages is None:
        stages = all_stages
    stages = set(stages)

    def bank_bf(i):
        return View(banks[i], banks[i].t.bitcast(BF16))

    def rstd_from(ss, n, rs, tmp):
        fw.act(tmp, ss, AF.Sqrt, scale=1.0 / n, bias=EPS)
        fw.recip(rs, tmp)

    def stage_W():
        for l in range(2):
            for half in range(2):
                sl = slice(half * 896, (half + 1) * 896)
                fw.dma(WIN[l][:, :, sl], w_in[l].rearrange("(kc p) n -> p kc n", p=128)[:, :, sl], q="pool")
                yield 4.0
            fw.dma(WOUT[l], w_out[l].rearrange("(kc p) n -> p kc n", p=128), q="pool")
            yield 4.0
            for f in range(NF):
                fw.dma(W1[l][f], w_ff1[l][:, f * 128:(f + 1) * 128].rearrange("(kc p) n -> p kc n", p=128), q="pool")
                yield 4.0
                fw.dma(W3[l][f], w_ff3[l][:, f * 128:(f + 1) * 128].rearrange("(kc p) n -> p kc n", p=128), q="pool")
                yield 4.0
            for f0 in range(0, DFF, 704):
                fw.dma(W2[l][f0:f0 + 704, :], w_ff2[l][f0:f0 + 704, :], q="pool")
                yield 4.0

    def stage_M(l, A_=None, bk=(0, 1, 2, 3)):
        if A_ is None:
            ar.reset()
            A_ = ar
        Bk = [banks[i_] for i_ in bk]
        cTr = [A_.alloc(8) for _ in range(3)]
        for r in range(2):
            fw.dma(cTr[r].ap(), c[r].rearrange("(kc p) -> p kc", p=128), allow_slow_non_contiguous=True)
        fw.dma(cTr[2].ap(), c_ctx[0].rearrange("(kc p) -> p kc", p=128), allow_slow_non_contiguous=True)
        cS = A_.alloc(24)
        cS3 = cS.v("p (k r) -> p k r", r=3)
        for r in range(3):
            fw.act(cS3[:, :, r], cTr[r].ap(), AF.Silu)
        bmb = [A_.alloc(256) for _ in range(2)]
        g1b = A_.alloc(D)
        g2b = A_.alloc(D)
        fw.dma(g1b[0:3, :], g_norm1[l:l + 1, :].bc([3, D]))
        fw.dma(g2b[0:3, :], g_norm2[l:l + 1, :].bc([3, D]))
        mv = A_.alloc(6 * D)
        wbuf = [A_.alloc(8 * 256) for _ in range(2)]
        for nb in range(24):
            sl = slice(nb * 256, (nb + 1) * 256)
            wt = wbuf[nb % 2].v("p (k n) -> p k n", k=8)
            bm = bmb[nb % 2]
            fw.dma(wt, w_mod[l][:, sl].rearrange("(kc p) n -> p kc n", p=128))
            fw.dma(bm[0:3, :], b_mod[l:l + 1, sl].bc([3, 256]))
            ps = Bk[nb % 2]
            for kc in range(8):
                fw.matmul(ps[0:3, 0:256], cS3[:, kc, :], wt[:, kc, :], start=(kc == 0), stop=(kc == 7))
            fw.tt(mv[0:3, sl], ps[0:3, 0:256], bm[0:3, :], ALU.add)
            yield 5.0
        fw.stt(mv[0:3, D:2 * D], mv[0:3, D:2 * D], 1.0, g1b[0:3, :], ALU.add, ALU.mult)
        fw.stt(mv[0:3, 4 * D:5 * D], mv[0:3, 4 * D:5 * D], 1.0, g2b[0:3, :], ALU.add, ALU.mult)
        fw.dma(MODV[l], mv[0:3, :])

    def modrow(l, r, idx):
        return MODV[l][r:r + 1, idx * D:(idx + 1) * D].bc([128, D])

    def load_consts_ident(A_=None):
        idb = (A_ or ar).alloc(128, BF16)
        fw.dma(idb.ap(), k_ident_bf.ap())
        return idb

    def stage_A(b, l):
        ar.reset()
        idb = load_consts_ident()
        win = ar.alloc(8 * 1792, BF16)
        win3 = win.v("p (k n) -> p k n", k=8)
        fw.dma(win3, WIN[l])
        A1 = ar.alloc(D); B1 = ar.alloc(D); A1c = ar.alloc(D); B1c = ar.alloc(D)
        fw.dma(A1.ap(), modrow(l, b, 1)); fw.dma(B1.ap(), modrow(l, b, 0))
        fw.dma(A1c.ap(), modrow(l, 2, 1)); fw.dma(B1c.ap(), modrow(l, 2, 0))
        G = ar.alloc(14 * 64)
        G3 = G.v("p (h d) -> p h d", d=64)
        gsrc = [att_q_gain] * 4 + [att_k_gain] * 2 + [na_q_gain] * 4 + [na_k_gain] * 4
        for h in range(14):
            fw.dma(G3[:, h, :], gsrc[h][l:l + 1, :].bc([128, 64]))
        COS = ar.alloc(16 * 64); SIN = ar.alloc(16 * 64)
        COS3 = COS.v("p (t d) -> p t d", d=64); SIN3 = SIN.v("p (t d) -> p t d", d=64)
        fw.dma(COS3, k_rope_cos.v("t p d -> p t d"))
        fw.dma(SIN3, k_rope_sin.v("t p d -> p t d"))
        two = lambda n, dt=F32: [ar.alloc(n, dt) for _ in range(2)]
        xs = two(D); sqj = ar.alloc(D, BF16)
        ss = two(4); sd = two(4); rstd = two(4)
        tt_ = two(D); hb = two(D, BF16); hT = two(D, BF16)
        qks = two(896); sq2 = two(896); ssh = two(16); sdh = two(16); rsh = two(16)
        qkn = two(896); vsw = two(384); m1 = two(384); m2 = two(384)
        qkb = two(896, BF16); qkT = two(896, BF16)
        vaug = two(6 * 65, BF16); finb = two(256, BF16); uTb = two(256, BF16)
        for vb in vaug:
            fw.memset(vb.v("p (h e) -> p h e", e=65)[:, :, 64:65], 1.0)

        def load(i):
            if i >= NT:
                return
            if l == 0:
                src = ctx[b][i * 128:(i + 1) * 128, :] if i < 2 else x[b][(i - 2) * 128:(i - 1) * 128, :]
            else:
                src = XR[b][i * 128:(i + 1) * 128, :]
            fw.dma(xs[i % 2].ap(), src)

        def frontA(i):
            j = i % 2
            isctx = i < 2
            fw.act(sqj.ap(), xs[j].ap(), AF.Square, accum_out=ss[j][:, 0:1])
            rstd_from(ss[j][:, 0:1], D, rstd[j][:, 0:1], sd[j][:, 0:1])
            fw.stt(tt_[j].ap(), xs[j].ap(), rstd[j][:, 0:1], (A1c if isctx else A1).ap(), ALU.mult, ALU.mult)
            fw.tt(hb[j].ap(), tt_[j].ap(), (B1c if isctx else B1).ap(), ALU.add, eng="pool")

        def frontB(i):
            j = i % 2
            pT = bank_bf(4 + 2 * j)
            for k in range(8):
                fw.transpose(pT[:, k * 128:(k + 1) * 128], hb[j][:, k * 128:(k + 1) * 128], idb.ap())
            fw.copy(hT[j].ap(), pT[:, 0:D], eng="act")
            hT3 = hT[j].v("p (k n) -> p k n", k=8)
            for nb in range(3):
                for k in range(8):
                    fw.matmul(banks[nb].ap(), hT3[:, k, :], win3[:, k, nb * 512:(nb + 1) * 512], start=(k == 0), stop=(k == 7))
            for m in range(2):
                for k in range(8):
                    fw.matmul(banks[3][:, m * 128:(m + 1) * 128], win3[:, k, 1536 + m * 128:1536 + (m + 1) * 128], hT3[:, k, :],
                              start=(k == 0), stop=(k == 7))

        def mid(i):
            j = i % 2
            va3 = vaug[j].v("p (h e) -> p h e", e=65)
            fw.copy(qks[j][:, 0:384], banks[0][:, 0:384], eng="act")
            fw.copy(va3[:, 0:2, 0:64], banks[0][:, 384:512].rearrange("p (h d) -> p h d", d=64), eng="act")
            fw.copy(qks[j][:, 384:896], banks[1][:, 0:512], eng="dve")
            fw.copy(va3[:, 2:6, 0:64], banks[2][:, 0:256].rearrange("p (h d) -> p h d", d=64), eng="act")
            fw.copy(finb[j].ap(), banks[2][:, 256:512], eng="act")
            fw.copy(uTb[j].ap(), banks[3][:, 0:256], eng="dve")
            fw.dma(VV[b][i * 128:(i + 1) * 128, :], vaug[j].ap())
            fw.dma(FIN[b][i * 128:(i + 1) * 128, :], finb[j].ap())
            fw.dma(UT[b].rearrange("(m p) n -> p m n", p=128)[:, :, i * 128:(i + 1) * 128],
                   uTb[j].v("p (m n) -> p m n", m=2))

        def backE(i):
            j = i % 2
            isctx = i < 2
            fw.act(sq2[j].ap(), qks[j].ap(), AF.Square)
            fw.reduce(ssh[j][:, 0:14], sq2[j].v("p (h d) -> p h d", d=64), ALU.add)
            rstd_from(ssh[j][:, 0:14], 64, rsh[j][:, 0:14], sdh[j][:, 0:14])
            qkn3 = qkn[j].v("p (h d) -> p h d", d=64)
            fw.tt(qkn3, qks[j].v("p (h d) -> p h d", d=64), rsh[j][:, 0:14].unsq(2).bc([128, 14, 64]), ALU.mult)
            fw.tt(qkn[j].ap(), qkn[j].ap(), G.ap(), ALU.mult, eng="pool")
            if isctx:
                fw.copy(qkb[j][:, 0:384], qkn[j][:, 0:384], eng="pool")
            else:
                ti = i - 2
                fw.copy(vsw[j].v("p (a two x) -> p a two x", two=2, x=16),
                        qkn[j][:, 0:384].rearrange("p (a two x) -> p a two x", two=2, x=16)[:, :, ::-1, :], eng="pool")
                fw.tt(m1[j].v("p (h d) -> p h d", d=64), qkn3[:, 0:6, :], COS3[:, ti, :].unsq(1).bc([128, 6, 64]), ALU.mult)
                fw.tt(m2[j].v("p (h d) -> p h d", d=64), vsw[j].v("p (h d) -> p h d", d=64),
                      SIN3[:, ti, :].unsq(1).bc([128, 6, 64]), ALU.mult, eng="pool")
                fw.tt(qkb[j][:, 0:384], m1[j].ap(), m2[j].ap(), ALU.add)
            fw.copy(qkb[j][:, 384:896], qkn[j][:, 384:896], eng="pool")

        def backP(i):
            j = i % 2
            pq = bank_bf(5 + 2 * j)
            for jj in range(7):
                fw.transpose(pq[:, jj * 128:(jj + 1) * 128], qkb[j][:, jj * 128:(jj + 1) * 128], idb.ap())
            fw.copy(qkT[j].ap(), pq[:, 0:896], eng="act")
            fw.dma(QK[b].rearrange("(j p) n -> p j n", p=128)[:, :, i * 128:(i + 1) * 128],
                   qkT[j].v("p (j n) -> p j n", j=7))

        load(0)
        load(1)
        for i in range(NT):
            frontA(i)
            if i >= 1:
                backE(i - 1)
            frontB(i)
            load(i + 2)
            if i >= 1:
                backP(i - 1)
            mid(i)
        backE(NT - 1)
        backP(NT - 1)

    def attn_block(QT3, KT3, V3, qhead, khead, vcol, qcol0, nq, ktiles, PT, psS, psO, st):
        nsub = nq // 128
        nk = len(ktiles)

        def pv(ki, kt, pt):
            for sub in range(nsub):
                fw.matmul(psO[:, sub * 65:(sub + 1) * 65], pt[:, sub * 128:(sub + 1) * 128], V3[:, kt, vcol:vcol + 65],
                          start=(ki == 0 and sub == 0), stop=(ki == nk - 1 and sub == nsub - 1))

        pend = None
        for ki, kt in enumerate(ktiles):
            pS = psS[st["s"] % len(psS)]
            st["s"] += 1
            pt = PT[st["p"] % len(PT)]
            st["p"] += 1
            fw.matmul(pS[:, 0:nq], KT3[0:64, khead, kt * 128:(kt + 1) * 128], QT3[0:64, qhead, qcol0:qcol0 + nq])
            fw.act(pt[:, 0:nq], pS[:, 0:nq], AF.Exp, scale=0.125)
            if pend is not None:
                pv(*pend)
            pend = (ki, kt, pt)
            yield 1.0
        pv(*pend)

    def attn_finish(psO, nsub, rden, ob):
        po3 = psO[:, 0:nsub * 65].rearrange("p (s e) -> p s e", e=65)
        fw.recip(rden[:, 0:nsub], po3[:, :, 64:65].rearrange("p s e -> p (s e)"))
        fw.tt(ob.v("p (s d) -> p s d", d=64)[:, 0:nsub, :], po3[:, :, 0:64],
              rden[:, 0:nsub].unsq(2).bc([128, nsub, 64]), ALU.mult)

    def stage_GQA(b, l, A_=None, bk=(0, 1, 2, 3)):
        if A_ is None:
            ar.reset()
            A_ = ar
        Bk = [banks[i_] for i_ in bk]
        QT = A_.alloc(4 * T, BF16); KT = A_.alloc(2 * T, BF16); VA = A_.alloc(NT * 130, BF16)
        QT3 = QT.v("p (h n) -> p h n", h=4); KT3 = KT.v("p (h n) -> p h n", h=2)
        VA3 = VA.v("p (t c) -> p t c", c=130)
        fw.dma(QT3[0:64], QK[b][0:256, :].rearrange("(h d) n -> d h n", d=64))
        fw.dma(KT3[0:64], QK[b][256:384, :].rearrange("(h d) n -> d h n", d=64))
        fw.dma(VA3, VV[b].rearrange("(t p) c -> p t c", p=128)[:, :, 0:130])
        PT = [A_.alloc(512, BF16) for _ in range(3)]
        rden = A_.alloc(4)
        obs = [A_.alloc(256, BF16) for _ in range(2)]
        st = {"s": 0, "p": 0}
        psS = [Bk[0], Bk[1]]
        cnt = 0
        O3 = OO[b].rearrange("(t p) c -> p t c", p=128)
        blocks = [(256 + qb * 512, 512, list(range(NT)), 2 + qb * 4) for qb in range(4)]
        if l == 0:
            blocks.append((0, 256, [0, 1], 0))
        for h in range(4):
            kv = h // 2
            for (qc0, nq, kts, t0) in blocks:
                psO = Bk[2 + cnt % 2]
                ob = obs[cnt % 2]
                cnt += 1
                yield from attn_block(QT3, KT3, VA3, h, kv, kv * 65, qc0, nq, kts, PT, psS, psO, st)
                nsub = nq // 128
                attn_finish(psO, nsub, rden, ob)
                fw.dma(O3[:, t0:t0 + nsub, h * 64:(h + 1) * 64], ob.v("p (s d) -> p s d", d=64)[:, 0:nsub, :])

    NA_PLAN, NA_PATS = na_plan()

    def stage_NAB(l, A_=None, bk=(0, 1, 2, 3)):
        if A_ is None:
            ar.reset()
            A_ = ar
        Bk = [banks[i_] for i_ in bk]
        rb = A_.alloc(31)
        fw.dma(rb[0:60, :], na_rel_bias[l])
        pad = A_.alloc(128)
        fw.memset(pad.ap(), BIG)
        fw.ts(pad[0:60, 48:79], rb[0:60, ::-1], 8.0, None, ALU.mult)
        padt = PADT[l].ap.tensor
        base = PADT[l].ap.offset
        fw.dma(View(PADT, bass.AP(tensor=padt, offset=base, ap=[[8192, 61], [127, 64], [1, 127]])),
               pad[0:61, 0:127].unsq(1).bc([61, 64, 127]))
        cm = A_.alloc(512)
        fw.dma(cm.ap(), k_colmask.ap())
        stg = [A_.alloc(512) for _ in range(2)]
        bt = [A_.alloc(512, BF16) for _ in range(2)]
        for bid, pat in enumerate(NA_PATS):
            s_ = stg[bid % 2]
            for krl in range(2):
                for qrl in range(2):
                    dri = pat[krl * 2 + qrl]
                    for h in range(4):
                        row = 60 if dri is None else h * 15 + dri
                        src = View(PADT, bass.AP(tensor=padt, offset=base + row * 8192 + 63, ap=[[126, 64], [1, 64]]))
                        fw.dma(s_[krl * 64:(krl + 1) * 64, h * 128 + qrl * 64:h * 128 + (qrl + 1) * 64], src)
            fw.tt(bt[bid % 2].ap(), s_.ap(), cm.ap(), ALU.add)
            fw.dma(BIAS[l][bid], bt[bid % 2].ap())
            yield 9.0

    def stage_NA(b, l, A_=None, bk=(0, 1, 2, 3)):
        if A_ is None:
            ar.reset()
            A_ = ar
        Bk = [banks[i_] for i_ in bk]
        idb = load_consts_ident(A_)
        QT = A_.alloc(4 * T, BF16); KT = A_.alloc(4 * T, BF16); VD = A_.alloc(NT * 260, BF16)
        QT3 = QT.v("p (h n) -> p h n", h=4); KT3 = KT.v("p (h n) -> p h n", h=4)
        VD3 = VD.v("p (t c) -> p t c", c=260)
        fw.dma(QT3[0:64], QK[b][384:640, :].rearrange("(h d) n -> d h n", d=64))
        fw.dma(KT3[0:64], QK[b][640:896, :].rearrange("(h d) n -> d h n", d=64))
        fw.dma(VD3, VV[b].rearrange("(t p) c -> p t c", p=128)[:, :, 130:390])
        nbias = len(NA_PATS)
        BT = A_.alloc(nbias * 512, BF16)
        BT3 = BT.v("p (i n) -> p i n", n=512)
        fw.dma(BT3, BIAS[l][0:nbias].rearrange("i p n -> p i n"))
        PT = [A_.alloc(512, BF16) for _ in range(3)]
        rden = A_.alloc(4)
        obs = [A_.alloc(256, BF16) for _ in range(2)]
        O3 = OO[b].rearrange("(t p) c -> p t c", p=128)
        si = 0
        for t in range(16):
            qc0 = 256 + t * 128
            klist = [(2 + u, bid) for (u, bid) in NA_PLAN[t]] + [(0, None), (1, None)]
            psO = Bk[2 + t % 2]
            ob = obs[t % 2]
            nk = len(klist)
            def pv(ki, kt, pt, psO=psO, nk=nk):
                for h in range(4):
                    fw.matmul(psO[:, h * 65:(h + 1) * 65], pt[:, h * 128:(h + 1) * 128], VD3[:, kt, h * 65:(h + 1) * 65],
                              start=(ki == 0 and h == 0), stop=(ki == nk - 1 and h == 3))

            pend = None
            for ki, (kt, bid) in enumerate(klist):
                pS = Bk[si % 2]
                pt = PT[si % 3]
                si += 1
                for h in range(4):
                    fw.matmul(pS[:, h * 128:(h + 1) * 128], KT3[0:64, h, kt * 128:(kt + 1) * 128], QT3[0:64, h, qc0:qc0 + 128],
                              start=(h == 0), stop=(h == 3 and bid is None))
                if bid is not None:
                    fw.matmul(pS.ap(), idb.ap(), BT3[:, bid, :], start=False, stop=True)
                fw.act(pt.ap(), pS.ap(), AF.Exp, scale=0.125)
                if pend is not None:
                    pv(*pend)
                pend = (ki, kt, pt)
                yield 3.0
            pv(*pend)
            attn_finish(psO, 4, rden, ob)
            fw.dma(OO[b][(2 + t) * 128:(3 + t) * 128, 256:512], ob.ap())
        if l == 0:
            st = {"s": 0, "p": 0}
            for h in range(4):
                psO = Bk[2 + h % 2]
                ob = obs[h % 2]
                yield from attn_block(QT3, KT3, VD3, h, h, h * 65, 0, 256, [0, 1], PT, [Bk[0], Bk[1]], psO, st)
                attn_finish(psO, 2, rden, ob)
                fw.dma(O3[:, 0:2, 256 + h * 64:256 + (h + 1) * 64], ob.v("p (s d) -> p s d", d=64)[:, 0:2, :])

    def stage_FW(l, A_=None, bk=(0, 1, 2, 3)):
        if A_ is None:
            ar.reset()
            A_ = ar
        Bk = [banks[i_] for i_ in bk]
        wf = A_.alloc(2 * 256)
        wf3 = wf.v("p (k n) -> p k n", k=2)
        fw.dma(wf3, w_fourier[l].rearrange("(k p) n -> p k n", p=128))
        for ci, ktab in enumerate((k_cblk, k_sblk)):
            cb = A_.alloc(2 * 256)
            cb3 = cb.v("p (k n) -> p k n", k=2)
            fw.dma(cb3, ktab.v("(k p) n -> p k n", p=128))
            wo = A_.alloc(2 * 256, BF16)
            wo3 = wo.v("p (k n) -> p k n", k=2)
            for fo in range(2):
                ps = Bk[(ci * 2 + fo) % 4]
                for k in range(2):
                    fw.matmul(ps[:, 0:256], cb3[:, k, fo * 128:(fo + 1) * 128], wf3[:, k, :], start=(k == 0), stop=(k == 1))
                fw.act(wo3[:, fo, :], ps[:, 0:256], AF.Copy, scale=(1.0 if ci == 0 else -1.0))
            fw.dma(WCS[l][ci].rearrange("(k p) n -> p k n", p=128), wo3)
            yield 15.0

    def stage_F(b, l, A_=None, bk=(0, 1, 2, 3)):
        if A_ is None:
            ar.reset()
            A_ = ar
        Bk = [banks[i_] for i_ in bk]
        fin = A_.alloc(NT * 256, BF16)
        fin3 = fin.v("p (t f) -> p t f", f=256)
        fw.dma(fin3, FIN[b].rearrange("(t p) f -> p t f", p=128))
        wcs = A_.alloc(2 * 2 * 256, BF16)
        wcs4 = wcs.v("p (c k n) -> p c k n", c=2, k=2)
        for ci in range(2):
            fw.dma(wcs4[:, ci], WCS[l][ci].rearrange("(k p) n -> p k n", p=128))
        tabs = [A_.alloc(512, BF16) for _ in range(6)]
        G = [A_.alloc(4 * 512, BF16) for _ in range(2)]
        ofb = [A_.alloc(256, BF16) for _ in range(2)]
        ti = 0
        oi = 0

        def second_stage(Gv, nk, tile0, scale):
            nonlocal oi
            for kt in range(nk // 128):
                ps = Bk[2 + oi % 2]
                n_ = 0
                for ci in range(2):
                    for fc in range(2):
                        fw.matmul(ps[:, 0:256], Gv[:, ci, fc, kt * 128:(kt + 1) * 128], wcs4[:, ci, fc, :],
                                  start=(n_ == 0), stop=(n_ == 3))
                        n_ += 1
                o_ = ofb[oi % 2]
                oi += 1
                fw.act(o_.ap(), ps[:, 0:256], AF.Copy, scale=scale)
                fw.dma(OO[b][(tile0 + kt) * 128:(tile0 + kt + 1) * 128, 512:768], o_.ap())

        for kb in range(4):
            for ncix in range(16):
                for ci in range(2):
                    tb = tabs[ti % 6]
                    ti += 1
                    fw.dma(tb.ap(), k_dft[ci][ncix * 128:(ncix + 1) * 128, kb * 512:(kb + 1) * 512])
                    for fc in range(2):
                        fw.matmul(Bk[ci * 2 + fc].ap(), fin3[:, 2 + ncix, fc * 128:(fc + 1) * 128], tb.ap(),
                                  start=(ncix == 0), stop=(ncix == 15))
                yield 2.0
            Gb = G[kb % 2]
            Gv = Gb.v("p (c f n) -> p c f n", c=2, f=2)
            for ci in range(2):
                for fc in range(2):
                    if (ci + fc) % 2 == 0:
                        fw.copy(Gv[:, ci, fc, :], Bk[ci * 2 + fc].ap(), eng="act")
                    else:
                        fw.copy(Gv[:, ci, fc, :], Bk[ci * 2 + fc].ap(), eng="dve")
            second_stage(Gv, 512, 2 + kb * 4, 1.0 / math.sqrt(SEQ * 64.0))
        if l == 0:
            for ncix in range(2):
                for ci in range(2):
                    tb = tabs[ti % 6]
                    ti += 1
                    fw.dma(tb[:, 0:256], k_dft[ci][ncix * 1024:(ncix + 1) * 1024:8, 0:256])
                    for fc in range(2):
                        fw.matmul(Bk[ci * 2 + fc][:, 0:256], fin3[:, ncix, fc * 128:(fc + 1) * 128], tb[:, 0:256],
                                  start=(ncix == 0), stop=(ncix == 1))
            Gb = G[0]
            Gv = Gb.v("p (c f n) -> p c f n", c=2, f=2)
            for ci in range(2):
                for fc in range(2):
                    fw.copy(Gv[:, ci, fc, 0:256], Bk[ci * 2 + fc][:, 0:256], eng=("act" if (ci + fc) % 2 == 0 else "dve"))
            second_stage(Gv, 256, 0, 1.0 / math.sqrt(CTXL * 64.0))

    def stage_C(b, l):
        ar.reset()
        idb = load_consts_ident()
        wo = ar.alloc(8 * D, BF16)
        wo3 = wo.v("p (k n) -> p k n", k=8)
        fw.dma(wo3, WOUT[l])
        gg = ar.alloc(D)
        fw.dma(gg.ap(), g_group[l:l + 1, :].bc([128, D]))
        GA1 = ar.alloc(D); A2 = ar.alloc(D); B2 = ar.alloc(D); GA2 = ar.alloc(D)
        hid = ar.alloc(NF * 512, BF16)
        hid3 = hid.v("p (f n) -> p f n", f=NF)
        two = lambda n, dt=F32: [ar.alloc(n, dt) for _ in range(2)]
        h2T = two(8 * 512, BF16)
        x1 = two(4 * D)
        wt1 = [ar.alloc(8 * 128, BF16) for _ in range(3)]
        wt3 = [ar.alloc(8 * 128, BF16) for _ in range(3)]
        w2t = [ar.alloc(D, BF16) for _ in range(3)]
        xs = two(D); ob = two(D, BF16)
        sq = ar.alloc(D)
        ss4 = two(4); sd4 = two(4); rs4 = two(4)
        ss = two(4); sd = two(4); rstd = two(4)
        onb = two(D, BF16); onT = two(D, BF16)
        tmpA = ar.alloc(D); tmpB = ar.alloc(D); h2b = two(D, BF16)
        sqj = ar.alloc(D, BF16)
        tmpC = ar.alloc(D)
        sa = two(512)
        xo = two(D)
        groups = [[2 + g * 4 + s_ for s_ in range(4)] for g in range(4)]
        if l == 0:
            groups = [[0, 1]] + groups
        rows = [(2 if grp[0] < 2 else b) for grp in groups]
        st = {"wi": 0, "w2i": 0, "oi": 0, "pre": 0}

        def h2T3(gp):
            return h2T[gp].v("p (k n) -> p k n", k=8)

        def x13(gp):
            return x1[gp].v("p (s n) -> p s n", s=4)

        def P1a(i, j):
            if l == 0:
                src = ctx[b][i * 128:(i + 1) * 128, :] if i < 2 else x[b][(i - 2) * 128:(i - 1) * 128, :]
            else:
                src = XR[b][i * 128:(i + 1) * 128, :]
            fw.dma(xs[j].ap(), src)
            fw.dma(ob[j].ap(), OO[b][i * 128:(i + 1) * 128, :])
            fw.act(sq.ap(), ob[j].ap(), AF.Square)
            fw.reduce(ss4[j][:, 0:4], sq.v("p (g d) -> p g d", g=4), ALU.add)
            rstd_from(ss4[j][:, 0:4], 256, rs4[j][:, 0:4], sd4[j][:, 0:4])
            for g_ in range(4):
                sl = slice(g_ * 256, (g_ + 1) * 256)
                fw.stt(onb[j][:, sl], ob[j][:, sl], rs4[j][:, g_:g_ + 1], gg[:, sl], ALU.mult, ALU.mult)

        def P1b(i, j):
            pT = bank_bf(6)
            for k in range(8):
                fw.transpose(pT[:, k * 128:(k + 1) * 128], onb[j][:, k * 128:(k + 1) * 128], idb.ap())
            fw.copy(onT[j].ap(), pT[:, 0:D], eng="act")
            onT3 = onT[j].v("p (k n) -> p k n", k=8)
            for nb in range(2):
                for k in range(8):
                    fw.matmul(banks[4 + nb].ap(), onT3[:, k, :], wo3[:, k, nb * 512:(nb + 1) * 512], start=(k == 0), stop=(k == 7))

        def P2a(i, j, gp, s_):
            for nb in range(2):
                sl = slice(nb * 512, (nb + 1) * 512)
                fw.tt(tmpA[:, sl], banks[4 + nb].ap(), GA1[:, sl], ALU.mult)
            fw.tt(x13(gp)[:, s_, :], tmpA.ap(), xs[j].ap(), ALU.add, eng="pool")
            fw.act(sqj.ap(), x13(gp)[:, s_, :], AF.Square, accum_out=ss[j][:, 0:1])
            rstd_from(ss[j][:, 0:1], D, rstd[j][:, 0:1], sd[j][:, 0:1])
            fw.stt(tmpB.ap(), x13(gp)[:, s_, :], rstd[j][:, 0:1], A2.ap(), ALU.mult, ALU.mult)
            fw.tt(h2b[j].ap(), tmpB.ap(), B2.ap(), ALU.add, eng="pool")

        def P2b(i, j, gp, s_):
            pT2 = bank_bf(7)
            for k in range(8):
                fw.transpose(pT2[:, k * 128:(k + 1) * 128], h2b[j][:, k * 128:(k + 1) * 128], idb.ap())
            fw.copy(h2T3(gp)[:, :, s_ * 128:(s_ + 1) * 128], pT2[:, 0:D].rearrange("p (k n) -> p k n", k=8), eng="act")

        def pre_slots(grp, gp):
            n = len(grp)
            slots = []
            for k in range(n + 2):
                def slot(k=k):
                    if 0 <= k < n:
                        P1a(grp[k], k % 2)
                    if 0 <= k - 2 < n:
                        P2b(grp[k - 2], (k - 2) % 2, gp, k - 2)
                    if 0 <= k - 1 < n:
                        P1b(grp[k - 1], (k - 1) % 2)
                        P2a(grp[k - 1], (k - 1) % 2, gp, k - 1)
                slots.append(slot)
            return slots

        def load_mods(row):
            fw.dma(GA1.ap(), modrow(l, row, 2)); fw.dma(A2.ap(), modrow(l, row, 4))
            fw.dma(B2.ap(), modrow(l, row, 3)); fw.dma(GA2.ap(), modrow(l, row, 5))

        def issue_w13(f):
            a_ = wt1[st["wi"] % 3]; b_ = wt3[st["wi"] % 3]
            st["wi"] += 1
            fw.dma(a_.ap(), W1[l][f].rearrange("p k n -> p (k n)"))
            fw.dma(b_.ap(), W3[l][f].rearrange("p k n -> p (k n)"))
            return a_, b_

        pre_done = False
        preissued = []
        for gi_, grp in enumerate(groups):
            gp = gi_ % 2
            nt_ = len(grp)
            ntk = nt_ * 128
            if not pre_done:
                load_mods(rows[gi_])
                for slot in pre_slots(grp, gp):
                    slot()
            nxt = gi_ + 1
            can_pipe = nxt < len(groups) and rows[nxt] == rows[gi_]
            nslots = pre_slots(groups[nxt], nxt % 2) if can_pipe else []
            for f in range(NF):
                if preissued:
                    a_, b_ = preissued.pop(0)
                else:
                    a_, b_ = issue_w13(f)
                a3 = a_.v("p (k n) -> p k n", k=8); b3_ = b_.v("p (k n) -> p k n", k=8)
                pa = banks[(f % 2) * 2]; pb = banks[1 + (f % 2) * 2]
                for k in range(8):
                    fw.matmul(pa[:, 0:ntk], a3[:, k, :], h2T3(gp)[:, k, 0:ntk], start=(k == 0), stop=(k == 7))
                for k in range(8):
                    fw.matmul(pb[:, 0:ntk], b3_[:, k, :], h2T3(gp)[:, k, 0:ntk], start=(k == 0), stop=(k == 7))
                sj = sa[f % 2]
                fw.act(sj[:, 0:ntk], pa[:, 0:ntk], AF.Silu)
                fw.tt(hid3[:, f, 0:ntk], sj[:, 0:ntk], pb[:, 0:ntk], ALU.mult)
                if nslots and f % 3 == 1:
                    nslots.pop(0)()
            while nslots:
                nslots.pop(0)()
            pre_done = can_pipe
            for f in range(NF):
                w_ = w2t[st["w2i"] % 3]
                st["w2i"] += 1
                fw.dma(w_.ap(), W2[l][f * 128:(f + 1) * 128, :])
                for s_ in range(nt_):
                    for nb in range(2):
                        fw.matmul(banks[s_ * 2 + nb].ap(), hid3[:, f, s_ * 128:(s_ + 1) * 128], w_[:, nb * 512:(nb + 1) * 512],
                                  start=(f == 0), stop=(f == NF - 1))
            if nxt < len(groups):
                preissued = [issue_w13(f) for f in range(3)]
            for s_, i in enumerate(grp):
                xoj = xo[st["oi"] % 2]
                st["oi"] += 1
                for nb in range(2):
                    sl = slice(nb * 512, (nb + 1) * 512)
                    fw.tt(tmpC[:, sl], banks[s_ * 2 + nb].ap(), GA2[:, sl], ALU.mult)
                fw.tt(xoj.ap(), tmpC.ap(), x13(gp)[:, s_, :], ALU.add, eng="pool")
                if l == 0:
                    fw.dma(XR[b][i * 128:(i + 1) * 128, :], xoj.ap())
                else:
                    fw.dma(out[b][(i - 2) * 128:(i - 1) * 128, :], xoj.ap())

    TWO_PI = 2.0 * math.pi
    CW1 = 6.28125
    CW2 = TWO_PI - 6.28125
    MAGIC = 12582912.0
    PI_SAFE = 3.141592

    def range_reduce(out, in_, kbuf, shift=0.0, eng="dve"):
        if shift != 0.0:
            fw.ts(out, in_, shift, None, ALU.add, eng=eng)
            src = out
        else:
            src = in_
        fw.ts(kbuf, src, 1.0 / TWO_PI, MAGIC, ALU.mult, ALU.add, eng=eng)
        fw.ts(kbuf, kbuf, -MAGIC, None, ALU.add, eng=eng)
        fw.stt(out, kbuf, -CW1, src, ALU.mult, ALU.add, eng=eng)
        fw.stt(out, kbuf, -CW2, out, ALU.mult, ALU.add, eng=eng)
        fw.ts(out, out, -PI_SAFE, PI_SAFE, ALU.max, ALU.min, eng=eng)

    def stage_S5P(l, A_=None, bk=(0, 1, 2, 3)):
        if A_ is None:
            ar.reset()
            A_ = ar
        Bk = [banks[i_] for i_ in bk]
        idf = A_.alloc(128)
        fw.dma(idf.ap(), k_ident_f.ap())
        tau = A_.alloc(16)
        fw.dma(tau.ap(), k_tau.ap())
        nidx = A_.alloc(NCH)
        fw.dma(nidx.ap(), k_nidx.ap())
        msk = A_.alloc(128)
        fw.dma(msk.ap(), k_s5mask.ap())
        mark = A_.off
        for d_ in range(2):
            A_.off = mark
            if d_ == 1:
                fw.barrier()
            lamre = A_.alloc(8); lamim = A_.alloc(8); dt = A_.alloc(8)
            for gi in range(2):
                sl = slice(64 * gi, 64 * gi + 64)
                fw.dma(lamre[sl, :], ssm_lam_re[l][d_].rearrange("(q two) p -> two p q", two=2)[gi], allow_slow_non_contiguous=True)
                fw.dma(lamim[sl, :], ssm_lam_im[l][d_].rearrange("(q two) p -> two p q", two=2)[gi], allow_slow_non_contiguous=True)
                fw.dma(dt[sl, :], ssm_log_dt[l][d_:d_ + 1, :].rearrange("o (q two) -> two o q", two=2)[gi].bc([64, 8]),
                       allow_slow_non_contiguous=True)
            fw.act(dt.ap(), dt.ap(), AF.Exp)
            zr = A_.alloc(8); zi = A_.alloc(8)
            fw.tt(zr.ap(), lamre.ap(), dt.ap(), ALU.mult)
            fw.tt(zi.ap(), lamim.ap(), dt.ap(), ALU.mult)
            PZ = A_.alloc(128); PEx = A_.alloc(128); KB = A_.alloc(128); RS = A_.alloc(128); RC = A_.alloc(128)
            Are = A_.alloc(128); Aim = A_.alloc(128)
            v3 = lambda bf: bf.v("p (t q) -> p t q", q=8)
            fw.tt(v3(PZ), tau.ap().unsq(2).bc([128, 16, 8]), zr.ap().unsq(1).bc([128, 16, 8]), ALU.mult)
            fw.act(PEx.ap(), PZ.ap(), AF.Exp)
            fw.tt(v3(PZ), tau.ap().unsq(2).bc([128, 16, 8]), zi.ap().unsq(1).bc([128, 16, 8]), ALU.mult)
            range_reduce(RS.ap(), PZ.ap(), KB.ap())
            range_reduce(RC.ap(), PZ.ap(), KB.ap(), shift=0.5 * math.pi)
            fw.act(RS.ap(), RS.ap(), AF.Sin)
            fw.act(RC.ap(), RC.ap(), AF.Sin)
            fw.tt(Are.ap(), PEx.ap(), RC.ap(), ALU.mult)
            fw.tt(Aim.ap(), PEx.ap(), RS.ap(), ALU.mult)
            Are3 = v3(Are); Aim3 = v3(Aim)
            fw.dma(S5D[l][d_], v3(PEx)[:, 15, :])
            phi = A_.alloc(8); kb8 = A_.alloc(8)
            fw.ts(phi.ap(), zi.ap(), 8.0, None, ALU.mult)
            range_reduce(phi.ap(), phi.ap(), kb8.ap())
            ANG = A_.alloc(8 * NCH); KB2 = A_.alloc(8 * NCH); R2 = A_.alloc(8 * NCH)
            a3 = lambda bf: bf.v("p (q n) -> p q n", q=8)
            fw.tt(a3(ANG), nidx.ap().unsq(1).bc([128, 8, NCH]), phi.ap().unsq(2).bc([128, 8, NCH]), ALU.mult)
            range_reduce(R2.ap(), ANG.ap(), KB2.ap())
            fw.act(R2.ap(), R2.ap(), AF.Sin)
            fw.dma(S5SIN[l][d_], a3(R2))
            R3 = R2
            range_reduce(R3.ap(), ANG.ap(), KB2.ap(), shift=0.5 * math.pi)
            fw.act(R3.ap(), R3.ap(), AF.Sin)
            fw.dma(S5COS[l][d_], a3(R3))
            yield 40.0
            nr = A_.alloc(8); den = A_.alloc(8); t8a = A_.alloc(8); t8b = A_.alloc(8); cr = A_.alloc(8); ci = A_.alloc(8)
            fw.ts(nr.ap(), Are3[:, 8, :], -1.0, None, ALU.add)
            fw.tt(den.ap(), lamre.ap(), lamre.ap(), ALU.mult)
            fw.tt(t8a.ap(), lamim.ap(), lamim.ap(), ALU.mult)
            fw.tt(den.ap(), den.ap(), t8a.ap(), ALU.add)
            fw.recip(den.ap(), den.ap())
            fw.tt(t8a.ap(), nr.ap(), lamre.ap(), ALU.mult)
            fw.tt(t8b.ap(), Aim3[:, 8, :], lamim.ap(), ALU.mult)
            fw.tt(t8a.ap(), t8a.ap(), t8b.ap(), ALU.add)
            fw.tt(cr.ap(), t8a.ap(), den.ap(), ALU.mult)
            fw.tt(t8a.ap(), Aim3[:, 8, :], lamre.ap(), ALU.mult)
            fw.tt(t8b.ap(), nr.ap(), lamim.ap(), ALU.mult)
            fw.tt(t8a.ap(), t8a.ap(), t8b.ap(), ALU.subtract)
            fw.tt(ci.ap(), t8a.ap(), den.ap(), ALU.mult)
            bre = A_.alloc(128); bim = A_.alloc(128); Bre = A_.alloc(128); Bim = A_.alloc(128); tb1 = A_.alloc(128); tb2 = A_.alloc(128)
            b3 = lambda bf: bf.v("p (q c) -> p q c", q=8)
            for gi in range(2):
                sl = slice(64 * gi, 64 * gi + 64)
                fw.dma(b3(bre)[sl], ssm_b_re[l][d_].rearrange("(q two) p c -> two p q c", two=2)[gi])
                fw.dma(b3(bim)[sl], ssm_b_im[l][d_].rearrange("(q two) p c -> two p q c", two=2)[gi])
            crb = cr.ap().unsq(2).bc([128, 8, 16]); cib = ci.ap().unsq(2).bc([128, 8, 16])
            fw.tt(b3(tb1), b3(bre), crb, ALU.mult); fw.tt(b3(tb2), b3(bim), cib, ALU.mult)
            fw.tt(Bre.ap(), tb1.ap(), tb2.ap(), ALU.subtract)
            fw.tt(b3(tb1), b3(bim), crb, ALU.mult); fw.tt(b3(tb2), b3(bre), cib, ALU.mult)
            fw.tt(Bim.ap(), tb1.ap(), tb2.ap(), ALU.add)
            yield 25.0
            Cre = A_.alloc(128); Cim = A_.alloc(128)
            for (Cdst, csrc, bkx) in ((Cre, ssm_c_re, 0), (Cim, ssm_c_im, 1)):
                X = A_.alloc(128)
                for q in range(8):
                    for gi in range(2):
                        fw.dma(X[16 * q:16 * q + 16, 64 * gi:64 * gi + 64], csrc[l][d_][2 * q + gi])
                fw.transpose(Bk[bkx][:, 0:128], X.ap(), idf.ap())
                fw.copy(Cdst.ap(), Bk[bkx][:, 0:128], eng="act")
            BcR = A_.alloc(8 * 128); BcI = A_.alloc(8 * 128)
            CpR = A_.alloc(8 * 128); CpI = A_.alloc(8 * 128)
            CcR = A_.alloc(8 * 128, BF16); CcI = A_.alloc(8 * 128, BF16)
            u1 = A_.alloc(1024); u2 = A_.alloc(1024)
            f4 = lambda bf: bf.v("p (q s c) -> p q s c", q=8, s=8)
            pw = lambda A3, sl: A3[:, sl, :].rearrange("p s q -> p q s").unsq(3).bc([128, 8, 8, 16])
            qc = lambda bf: b3(bf).unsq(2).bc([128, 8, 8, 16])

            def cmul(dst_re, dst_im, Ar, Ai, Xr, Xi, neg_im=False):
                fw.tt(f4(u1), Ar, Xr, ALU.mult); fw.tt(f4(u2), Ai, Xi, ALU.mult, eng="pool")
                fw.tt(f4(dst_re), f4(u1), f4(u2), ALU.subtract)
                fw.tt(f4(u1), Ar, Xi, ALU.mult); fw.tt(f4(u2), Ai, Xr, ALU.mult, eng="pool")
                if neg_im:
                    fw.stt(f4(dst_im), f4(u1), -1.0, f4(u2), ALU.mult, ALU.subtract)
                else:
                    fw.tt(f4(dst_im), f4(u1), f4(u2), ALU.add)

            cmul(BcR, BcI, pw(Are3, slice(14, 6, -1)), pw(Aim3, slice(14, 6, -1)), qc(Bre), qc(Bim))
            cmul(CpR, CpI, pw(Are3, slice(0, 8)), pw(Aim3, slice(0, 8)), qc(Cre), qc(Cim), neg_im=True)
            cmul(CcR, CcI, pw(Are3, slice(8, 16)), pw(Aim3, slice(8, 16)), qc(Cre), qc(Cim), neg_im=True)
            fw.dma(S5C[l][d_].rearrange("q p r n -> p q r n")[:, :, 0, :], CcR.v("p (q n) -> p q n", q=8))
            fw.dma(S5C[l][d_].rearrange("q p r n -> p q r n")[:, :, 1, :], CcI.v("p (q n) -> p q n", q=8))
            Tb = [A_.alloc(128, BF16) for _ in range(2)]
            BT = [A_.alloc(256, BF16) for _ in range(2)]
            qv = lambda bf, q: bf.v("p (q n) -> p q n", q=8)[:, q, :]
            for q in range(8):
                for gi in range(2):
                    g = 2 * q + gi
                    sl = slice(64 * gi, 64 * gi + 64)
                    ps = Bk[gi]
                    fw.matmul(ps[:, 0:128], qv(BcR, q)[sl], qv(CpR, q)[sl], start=True, stop=False)
                    fw.matmul(ps[:, 0:128], qv(BcI, q)[sl], qv(CpI, q)[sl], start=False, stop=True)
                    tb_ = Tb[g % 2]
                    fw.tt(tb_.ap(), ps[:, 0:128], msk.ap(), ALU.mult)
                    fw.dma(S5T[l][d_][g], tb_.ap())
                bt_ = BT[q % 2]
                for ri, src in enumerate((BcR, BcI)):
                    ps = Bk[2 + ri]
                    fw.transpose(ps[:, 0:128], qv(src, q), idf.ap())
                    fw.copy(bt_.v("p (g r n) -> p g r n", g=2, r=2)[:, :, ri, :], ps[:, 0:128].rearrange("p (g n) -> p g n", g=2), eng="act")
                fw.dma(S5B[l][d_][2 * q:2 * q + 2].rearrange("g sc r n -> sc g (r n)"), bt_.v("p (g x) -> p g x", g=2))
                yield 8.0

    def stage_S5(b, l, A_=None, bk=(0, 1, 2, 3)):
        if A_ is None:
            ar.reset()
            A_ = ar
        Bk = [banks[i_] for i_ in bk]
        idb = load_consts_ident(A_)
        uT = A_.alloc(2 * T, BF16)
        uT3 = uT.v("p (m n) -> p m n", m=2)
        fw.dma(uT3, UT[b].rearrange("(m p) n -> p m n", p=128))
        Dcol = A_.alloc(2)
        fw.dma(Dcol.ap(), ssm_d[l].rearrange("(m p) -> p m", p=128), allow_slow_non_contiguous=True)
        bgl = A_.alloc(2)
        fw.dma(bgl.ap(), b_glu[l].rearrange("(m p) -> p m", p=128), allow_slow_non_contiguous=True)
        wgf = A_.alloc(512)
        fw.dma(wgf.v("p (k n) -> p k n", k=2), w_glu[l].rearrange("(k p) n -> p k n", p=128))
        wgb = A_.alloc(512, BF16)
        fw.copy(wgb.ap(), wgf.ap())
        wgb3 = wgb.v("p (k n) -> p k n", k=2)
        Tm = A_.alloc(32 * 128, BF16); T3 = Tm.v("p (g n) -> p g n", g=32)
        fw.dma(T3, S5T[l].rearrange("d g sc n -> sc (d g) n"))
        Bm = A_.alloc(32 * 128, BF16); B4 = Bm.v("p (g r n) -> p g r n", g=32, r=2)
        fw.dma(Bm.v("p (g x) -> p g x", g=32), S5B[l].rearrange("d g sc r n -> sc (d g) (r n)"))
        Cm = A_.alloc(16 * 256, BF16); C4 = Cm.v("p (g r n) -> p g r n", g=16, r=2)
        fw.dma(Cm.v("p (g x) -> p g x", g=16), S5C[l].rearrange("d q p r n -> p (d q) (r n)"))
        DEC = A_.alloc(16); DEC3 = DEC.v("p (d q) -> p d q", d=2)
        fw.dma(DEC3, S5D[l].rearrange("d p q -> p d q"))
        stg = [A_.alloc(2 * 8 * NCH, BF16) for _ in range(2)]
        s4 = lambda bf: bf.v("p (m s j) -> p m s j", m=2, s=8)
        yield 10.0
        for m in range(2):
            fw.copy(s4(stg[0])[:, m], uT3[:, m, :].rearrange("p (j s) -> p s j", s=8), eng=("dve" if m == 0 else "pool"))
            fw.copy(s4(stg[1])[:, m, :, 0:32], uT3[:, m, 0:256][:, ::-1].rearrange("p (j s) -> p s j", s=8), eng="dve")
            fw.copy(s4(stg[1])[:, m, :, 32:NCH], uT3[:, m, 256:T][:, ::-1].rearrange("p (j s) -> p s j", s=8), eng="pool")
        U = A_.alloc(32 * NCH, BF16); U3 = U.v("p (g j) -> p g j", g=32)
        for d_ in range(2):
            for g in range(16):
                m, gl = g // 8, g % 8
                fw.dma(SCRU[d_][g].rearrange("s c j -> c s j"), s4(stg[d_])[16 * gl:16 * gl + 16, m])
        yield 20.0
        for d_ in range(2):
            for g in range(16):
                fw.dma(U3[:, d_ * 16 + g, :], SCRU[d_][g].rearrange("s c j -> (s c) j"))
            yield 15.0
        two = lambda n, dt=F32: [A_.alloc(n, dt) for _ in range(2)]
        cosb = two(NCH); sinb = two(NCH)
        Sre = two(NCH); Sim = two(NCH)
        t1 = two(NCH); t2 = two(NCH); t3 = two(NCH); t4 = two(NCH)
        wr_in = two(NCH); wi_in = two(NCH); wr = two(NCH); wi = two(NCH)
        Hre = two(NCH, BF16); Him = two(NCH, BF16)
        Yg = [A_.alloc(NCH, BF16) for _ in range(4)]

        def s5_front(it, d_, q):
            p = it % 2
            cb = cosb[p]; sb = sinb[p]; hr = Hre[p]; hi = Him[p]
            bS = (0, 1)
            fw.dma(cb.ap(), S5COS[l][d_][:, q, :])
            fw.dma(sb.ap(), S5SIN[l][d_][:, q, :])
            for gi in range(2):
                g = d_ * 16 + 2 * q + gi
                sl = slice(64 * gi, 64 * gi + 64)
                fw.matmul(Bk[bS[0]][sl, 0:NCH], B4[:, g, 0, :], U3[:, g, :])
                fw.matmul(Bk[bS[1]][sl, 0:NCH], B4[:, g, 1, :], U3[:, g, :])
            fw.copy(Sre[p].ap(), Bk[bS[0]][:, 0:NCH], eng="act")
            fw.copy(Sim[p].ap(), Bk[bS[1]][:, 0:NCH], eng="act")
            fw.tt(t1[p].ap(), Sre[p].ap(), cb.ap(), ALU.mult)
            fw.tt(t2[p].ap(), Sim[p].ap(), sb.ap(), ALU.mult, eng="pool")
            fw.tt(wr_in[p].ap(), t1[p].ap(), t2[p].ap(), ALU.add)
            fw.tt(t3[p].ap(), Sim[p].ap(), cb.ap(), ALU.mult, eng="pool")
            fw.tt(t4[p].ap(), Sre[p].ap(), sb.ap(), ALU.mult)
            fw.tt(wi_in[p].ap(), t3[p].ap(), t4[p].ap(), ALU.subtract, eng="pool")
            dec = DEC3[:, d_, q:q + 1].bc([128, NCH])
            fw.scan(wr[p].ap(), dec, wr_in[p].ap(), 0.0)
            fw.scan(wi[p].ap(), dec, wi_in[p].ap(), 0.0)
            fw.tt(t1[p].ap(), wr[p].ap(), cb.ap(), ALU.mult)
            fw.tt(t2[p].ap(), wi[p].ap(), sb.ap(), ALU.mult, eng="pool")
            fw.tt(hr.ap(), t1[p].ap(), t2[p].ap(), ALU.subtract)
            fw.tt(t3[p].ap(), wr[p].ap(), sb.ap(), ALU.mult, eng="pool")
            fw.tt(t4[p].ap(), wi[p].ap(), cb.ap(), ALU.mult)
            fw.tt(hi.ap(), t3[p].ap(), t4[p].ap(), ALU.add, eng="pool")

        def s5_back(it, d_, q):
            p = it % 2
            hr = Hre[p]; hi = Him[p]
            bY = (2, 3)
            for gi in range(2):
                g16 = 2 * q + gi
                g = d_ * 16 + g16
                sl = slice(64 * gi, 64 * gi + 64)
                psY = Bk[bY[gi]]
                fw.matmul(psY[:, 0:NCH], T3[:, g, :], U3[:, g, :], start=True, stop=False)
                fw.matmul(psY[:, 1:NCH], C4[sl, d_ * 8 + q, 0, :], hr[sl, 0:NCH - 1], start=False, stop=False)
                fw.matmul(psY[:, 1:NCH], C4[sl, d_ * 8 + q, 1, :], hi[sl, 0:NCH - 1], start=False, stop=True)
                yg = Yg[p * 2 + gi]
                fw.copy(yg.ap(), psY[:, 0:NCH], eng="act")
                fw.dma(SCRY[d_][g16].rearrange("t c j -> (t c) j"), yg.ap())

        its = [(d_, q) for d_ in range(2) for q in range(8)]
        for it, (d_, q) in enumerate(its):
            s5_front(it, d_, q)
            if it >= 1:
                s5_back(it - 1, *its[it - 1])
            yield 8.0
        s5_back(len(its) - 1, *its[-1])
        ys = stg
        for d_ in range(2):
            for g in range(16):
                m, gl = g // 8, g % 8
                fw.dma(s4(ys[d_])[16 * gl:16 * gl + 16, m], SCRY[d_][g].rearrange("t c j -> c t j"))
        yield 20.0
        y = A_.alloc(2 * T); y3 = y.v("p (m n) -> p m n", m=2)
        gb = A_.alloc(2 * T, BF16); gb3 = gb.v("p (m n) -> p m n", m=2)
        for m in range(2):
            f3 = s4(ys[0])[:, m]
            r3 = s4(ys[1])[:, m]
            eng = "dve" if m == 0 else "pool"
            fw.tt(y3[:, m, 0:256].rearrange("p (j t) -> p j t", t=8), f3[:, :, 0:32].rearrange("p t j -> p j t"),
                  r3[:, :, 0:32].rearrange("p s n -> p n s")[:, ::-1, ::-1], ALU.add, eng=eng)
            fw.tt(y3[:, m, 256:T].rearrange("p (j t) -> p j t", t=8), f3[:, :, 32:NCH].rearrange("p t j -> p j t"),
                  r3[:, :, 32:NCH].rearrange("p s n -> p n s")[:, ::-1, ::-1], ALU.add, eng=eng)
            fw.stt(y3[:, m, :], uT3[:, m, :], Dcol[:, m:m + 1], y3[:, m, :], ALU.mult, ALU.add, eng=eng)
        yield 30.0
        fw.act(y.ap(), y.ap(), AF.Gelu_apprx_tanh)
        fw.copy(gb.ap(), y.ap(), eng="pool")
        osT = A_.alloc(2 * T, BF16); osT3 = osT.v("p (m n) -> p m n", m=2)
        gate = [A_.alloc(512) for _ in range(2)]
        bi = 0
        for mo in range(2):
            for (c0, nn) in [(0, 512), (512, 512), (1024, 512), (1536, 512), (2048, 256)]:
                ps = Bk[bi % 2]
                gt = gate[bi % 2]
                bi += 1
                for m in range(2):
                    fw.matmul(ps[:, 0:nn], wgb3[:, m, mo * 128:(mo + 1) * 128], gb3[:, m, c0:c0 + nn], start=(m == 0), stop=(m == 1))
                fw.act(gt[:, 0:nn], ps[:, 0:nn], AF.Sigmoid, bias=bgl[:, mo:mo + 1])
                fw.tt(osT3[:, mo, c0:c0 + nn], y3[:, mo, c0:c0 + nn], gt[:, 0:nn], ALU.mult, eng=("dve" if bi % 2 == 0 else "pool"))
                yield 4.0
        otb = [A_.alloc(256, BF16) for _ in range(2)]
        for i in range(NT):
            if l == 1 and i < 2:
                continue
            pT = View(Bk[2 + i % 2], Bk[2 + i % 2].t.bitcast(BF16))
            for mo in range(2):
                fw.transpose(pT[:, mo * 128:(mo + 1) * 128], osT3[:, mo, i * 128:(i + 1) * 128], idb.ap())
            ot = otb[i % 2]
            fw.copy(ot.ap(), pT[:, 0:256], eng="act")
            fw.dma(OO[b][i * 128:(i + 1) * 128, 768:1024], ot.ap())
            yield 2.0

    def run(gen):
        for _ in gen:
            pass

    def corun(gens):
        acc = [0.0] * len(gens)
        live = list(range(len(gens)))
        while live:
            i_ = min(live, key=lambda k_: acc[k_])
            try:
                c_ = next(gens[i_])
                acc[i_] += (c_ or 1.0)
            except StopIteration:
                live.remove(i_)

    full = all(st_ in stages for st_ in all_stages) and tuple(layers) == (0, 1) and tuple(batches) == (0, 1)
    PRO_W = 17408

    def prologue(l, A_, bk):
        yield from stage_M(l, A_, bk)
        A_.reset()
        yield from stage_NAB(l, A_, bk)
        A_.reset()
        yield from stage_FW(l, A_, bk)
        A_.reset()
        yield from stage_S5P(l, A_, bk)

    if full:
        ar.reset()
        corun([stage_W(), prologue(0, ar.sub(0, PRO_W), (0, 1, 2, 3))])
    elif "W" in stages:
        run(stage_W())
    for l in layers:
        if not full:
            if "M" in stages:
                run(stage_M(l))
            if "NAB" in stages:
                run(stage_NAB(l))
            if "FW" in stages:
                run(stage_FW(l))
            if "S5P" in stages:
                run(stage_S5P(l))
        for b in batches:
            if "A" in stages:
                stage_A(b, l)
            if all(st_ in stages for st_ in ("GQA", "NA", "F", "S5")):
                ar.reset()
                corun([stage_S5(b, l, ar.sub(0, 38400), (0, 1, 2, 3)), stage_GQA(b, l, ar.sub(38400, 9728), (4, 5, 6, 7))])
                ar.reset()
                if full and l == 0 and b == 1:
                    gens = [stage_NA(b, l, ar.sub(0, 19200), (0, 1, 2, 2)), stage_F(b, l, ar.sub(19200, 8192), (4, 5, 6, 7)),
                            prologue(1, ar.sub(27392, PRO_W), (3, 3, 3, 3))]
                else:
                    gens = [stage_NA(b, l, ar.sub(0, 19200), (0, 1, 2, 3)), stage_F(b, l, ar.sub(19200, 8192), (4, 5, 6, 7))]
                corun(gens)
            else:
                if "GQA" in stages:
                    run(stage_GQA(b, l))
                if "NA" in stages:
                    run(stage_NA(b, l))
                if "F" in stages:
                    run(stage_F(b, l))
                if "S5" in stages:
                    run(stage_S5(b, l))
            if "C" in stages:
                stage_C(b, l)
    fw.barrier()
    fw.emit()
    return nc


_RESHAPE = {
    "c_ctx": (1, D),
    "na_rel_bias": (2, 60, 31),
}


def make_in_maps(inputs, ncores=NCORES):
    consts = host_constants()
    shared = {}
    for name, arr in inputs.items():
        if name in ("x", "ctx", "c"):
            continue
        a = np.ascontiguousarray(arr)
        if name in _RESHAPE:
            a = a.reshape(_RESHAPE[name])
        shared[name] = a
    shared.update(consts)
    maps = []
    for i in range(ncores):
        m = dict(shared)
        m["x"] = np.ascontiguousarray(inputs["x"][i * NB:(i + 1) * NB])
        m["ctx"] = np.ascontiguousarray(inputs["ctx"][i * NB:(i + 1) * NB])
        m["c"] = np.ascontiguousarray(inputs["c"][i * NB:(i + 1) * NB])
        maps.append(m)
    return maps


def kernel(**inputs):
    nc = build()
    maps = make_in_maps(inputs)
    res = run_bass_kernel_spmd(nc, maps, core_ids=list(range(NCORES)))
    outs = [np.asarray(r["out"]) for r in res.results]
    return np.concatenate(outs, axis=0).astype(np.float32)
```

```python
import contextlib
import math
import numpy as np
import ml_dtypes
import concourse.bass as bass
import concourse.mybir as mybir
from concourse.bass_utils import run_bass_kernel_spmd

F32 = mybir.dt.float32
BF16 = mybir.dt.bfloat16
AF = mybir.ActivationFunctionType
ALU = mybir.AluOpType
AX = mybir.AxisListType

NCORES = 8
NB = 2
D = 1024
SEQ = 2048
CTXL = 256
T = SEQ + CTXL
NT = T // 128
DFF = 2816
NF = DFF // 128
EPS = 1e-6
NCH = T // 8
BIG = -30000.0
DMA_K = 16
COMPUTE = ("pe", "act", "dve", "pool")
SAME_ENGINE_SYNC = True


class Buf:
    __slots__ = ("t", "name", "writers", "readers")

    def __init__(self, ap, name=""):
        self.t = ap
        self.name = name
        self.writers = {}
        self.readers = {}

    def __getitem__(self, idx):
        return View(self, self.t[idx])

    def ap(self):
        return View(self, self.t)

    def v(self, pattern=None, **kw):
        if pattern is None:
            return View(self, self.t)
        return View(self, self.t.rearrange(pattern, **kw))


class View:
    __slots__ = ("buf", "ap")

    def __init__(self, buf, ap):
        self.buf = buf
        self.ap = ap

    def __getitem__(self, idx):
        return View(self.buf, self.ap[idx])

    def rearrange(self, *a, **k):
        return View(self.buf, self.ap.rearrange(*a, **k))

    def bitcast(self, dt):
        return View(self.buf, self.ap.bitcast(dt))

    def bc(self, shape):
        return View(self.buf, self.ap.to_broadcast(list(shape)))

    def unsq(self, i):
        return View(self.buf, self.ap.unsqueeze(i))


class FW:
    def __init__(self, nc):
        self.nc = nc
        self.stack = contextlib.ExitStack()
        self.streams = {e: [] for e in ("pe", "act", "dve", "pool", "sp")}
        self.seq = {e: 0 for e in COMPUTE}
        self.sems = {}
        self.waited = {e: {} for e in self.streams}
        self.dma_count = {"sp": 0, "pool": 0, "act": 0}
        self.dma_last = {}
        self.n_inst = 0

    def sem(self, key):
        if key not in self.sems:
            name = "s_" + ("_".join(str(k) for k in key) if isinstance(key, tuple) else str(key))
            self.sems[key] = self.stack.enter_context(self.nc.semaphore(name))
        return self.sems[key]

    def sbuf(self, name, shape, dtype):
        t = self.stack.enter_context(self.nc.sbuf_tensor(name, list(shape), dtype))
        return Buf(t[:], name)

    def psum(self, name, shape, dtype=F32):
        t = self.stack.enter_context(self.nc.psum_tensor(name, list(shape), dtype))
        return Buf(t[:], name)

    def dram(self, name, shape, dtype, kind="Internal"):
        t = self.nc.dram_tensor(name, list(shape), dtype, kind=kind)
        return Buf(t.ap(), name)

    def _need(self, eng, waits, key, val):
        if self.waited[eng].get(key, 0) >= val:
            return
        if val > waits.get(key, 0):
            waits[key] = val

    def _deps(self, eng, reads, writes):
        waits = {}
        for v in reads:
            for key, val in v.buf.writers.items():
                if key == eng and eng == "pe":
                    continue
                self._need(eng, waits, key, val)
        for v in writes:
            for key, val in v.buf.writers.items():
                if key == eng and (eng == "pe" or not SAME_ENGINE_SYNC):
                    continue
                self._need(eng, waits, key, val)
            for key, val in v.buf.readers.items():
                if key == eng and (eng == "pe" or not SAME_ENGINE_SYNC):
                    continue
                self._need(eng, waits, key, val)
        for key, val in waits.items():
            self.waited[eng][key] = val
        return list(waits.items())

    def _mark(self, token, reads, writes):
        key, val = token
        for v in reads:
            v.buf.readers[key] = val
        for v in writes:
            b = v.buf
            b.writers = {key: val}
            b.readers = {}

    def op(self, eng, fn, reads=(), writes=()):
        reads = [r for r in reads if isinstance(r, View)]
        writes = [w for w in writes if isinstance(w, View)]
        waits = self._deps(eng, reads, writes)
        self.seq[eng] += 1
        token = (eng, self.seq[eng])
        self._mark(token, reads, writes)
        self.streams[eng].append((waits, fn, (eng, 1)))
        self.n_inst += 1
        return token

    def dma(self, out, in_, q="sp", **kw):
        i = self.dma_count[q]
        self.dma_count[q] += 1
        r, rnd = i % DMA_K, i // DMA_K
        key = ("dma", q, r)
        waits = {}
        if rnd > 0:
            self._need(q, waits, key, 16 * rnd)
        for k2, v2 in waits.items():
            self.waited[q][k2] = v2
        ob = out.buf
        merge = (not ob.readers) and bool(ob.writers) and all(isinstance(k_, tuple) for k_ in ob.writers)
        w2 = self._deps(q, [in_], [] if merge else [out])
        allw = list(waits.items()) + w2
        token = (key, 16 * (rnd + 1))
        self.dma_last[key] = 16 * (rnd + 1)
        if merge:
            in_.buf.readers[key] = token[1]
            ob.writers[key] = max(ob.writers.get(key, 0), token[1])
        else:
            self._mark(token, [in_], [out])
        oa, ia = out.ap, in_.ap
        self.streams[q].append((allw, lambda e, oa=oa, ia=ia, kw=kw: e.dma_start(out=oa, in_=ia, **kw), (key, 16)))
        self.n_inst += 1
        return token

    def barrier(self):
        toks = [(e, self.seq[e]) for e in COMPUTE if self.seq[e] > 0]
        toks += list(self.dma_last.items())
        for eng in self.streams:
            waits = {}
            for key, val in toks:
                if key == eng:
                    continue
                self._need(eng, waits, key, val)
            for k2, v2 in waits.items():
                self.waited[eng][k2] = v2
            if waits:
                self.streams[eng].append((list(waits.items()), None, None))

    def emit(self):
        nc = self.nc
        for e in COMPUTE:
            self.sem(e)
        for q in self.dma_count:
            for r in range(DMA_K):
                self.sem(("dma", q, r))
        engmap = {"pe": "tensor", "act": "scalar", "dve": "vector", "pool": "gpsimd", "sp": "sync"}
        with nc.Block() as block:
            for e, attr in engmap.items():
                stream = self.streams[e]

                def body(engine, stream=stream):
                    for waits, fn, inc in stream:
                        for key, val in waits:
                            engine.wait_ge(self.sems[key], val)
                        if fn is None:
                            continue
                        ins = fn(engine)
                        if inc is not None:
                            ins.then_inc(self.sems[inc[0]], inc[1])

                getattr(block, attr)(body)
        self.stack.close()

    def matmul(self, out, lhsT, rhs, start=True, stop=True, **kw):
        oa, la, ra = out.ap, lhsT.ap, rhs.ap
        return self.op("pe", lambda e: e.matmul(oa, la, ra, start=start, stop=stop, **kw),
                       reads=[lhsT, rhs], writes=[out])

    def transpose(self, out, in_, ident):
        oa, ia, da = out.ap, in_.ap, ident.ap
        return self.op("pe", lambda e: e.transpose(oa, ia, da), reads=[in_, ident], writes=[out])

    def act(self, out, in_, func, bias=None, scale=None, accum_out=None):
        oa, ia = out.ap, in_.ap
        kw = {}
        reads = [in_]
        writes = [out]
        if bias is not None:
            if isinstance(bias, View):
                kw["bias"] = bias.ap
                reads.append(bias)
            else:
                kw["bias"] = bias
        if scale is not None:
            if isinstance(scale, View):
                kw["scale"] = scale.ap
                reads.append(scale)
            else:
                kw["scale"] = scale
        if accum_out is not None:
            kw["accum_out"] = accum_out.ap
            writes.append(accum_out)
        return self.op("act", lambda e: e.activation(oa, ia, func, **kw), reads=reads, writes=writes)

    def tt(self, out, in0, in1, op, eng="dve"):
        oa, a, b = out.ap, in0.ap, in1.ap
        return self.op(eng, lambda e: e.tensor_tensor(oa, a, b, op), reads=[in0, in1], writes=[out])

    def ts(self, out, in0, s1, s2, op0, op1=None, eng="dve"):
        oa, a = out.ap, in0.ap
        reads = [in0]
        s1a = s1.ap if isinstance(s1, View) else s1
        s2a = s2.ap if isinstance(s2, View) else s2
        if isinstance(s1, View):
            reads.append(s1)
        if isinstance(s2, View):
            reads.append(s2)
        kw = {}
        if op1 is not None:
            kw["op1"] = op1
        return self.op(eng, lambda e: e.tensor_scalar(oa, a, s1a, s2a, op0, **kw), reads=reads, writes=[out])

    def stt(self, out, in0, scalar, in1, op0, op1, eng="dve"):
        oa, a, b = out.ap, in0.ap, in1.ap
        reads = [in0, in1]
        sa = scalar.ap if isinstance(scalar, View) else scalar
        if isinstance(scalar, View):
            reads.append(scalar)
        eng = "dve"
        return self.op(eng, lambda e: e.scalar_tensor_tensor(oa, a, sa, b, op0, op1), reads=reads, writes=[out])

    def copy(self, out, in_, eng="dve"):
        oa, ia = out.ap, in_.ap
        if eng == "act":
            return self.op(eng, lambda e: e.copy(oa, ia), reads=[in_], writes=[out])
        return self.op(eng, lambda e: e.tensor_copy(oa, ia), reads=[in_], writes=[out])

    def memset(self, out, val, eng="pool"):
        oa = out.ap
        return self.op(eng, lambda e: e.memset(oa, val), reads=[], writes=[out])

    def reduce(self, out, in_, op, axis=AX.X, eng="dve"):
        oa, ia = out.ap, in_.ap
        return self.op(eng, lambda e: e.tensor_reduce(oa, ia, axis, op), reads=[in_], writes=[out])

    def recip(self, out, in_):
        oa, ia = out.ap, in_.ap
        return self.op("dve", lambda e: e.reciprocal(oa, ia), reads=[in_], writes=[out])

    def scan(self, out, d0, d1, initial, op0=ALU.mult, op1=ALU.add):
        oa, a, b = out.ap, d0.ap, d1.ap
        reads = [d0, d1]
        ia = initial.ap if isinstance(initial, View) else initial
        if isinstance(initial, View):
            reads.append(initial)
        return self.op("dve", lambda e: e.tensor_tensor_scan(oa, a, b, ia, op0, op1), reads=reads, writes=[out])


class Arena:
    def __init__(self, fw, words):
        self.fw = fw
        self.words = words
        self.base = fw.sbuf("arena", [128, words], F32)
        self.off = 0
        self.n = 0

    def alloc(self, nelem, dtype=F32, name=None):
        if dtype == BF16:
            w = (nelem + 1) // 2
        else:
            w = nelem
        w = (w + 15) // 16 * 16
        assert self.off + w <= self.words, "arena overflow %d + %d > %d" % (self.off, w, self.words)
        ap = self.base.t[:, self.off:self.off + w]
        if dtype == BF16:
            ap = ap.bitcast(BF16)[:, 0:nelem]
        else:
            ap = ap[:, 0:nelem]
        self.off += w
        self.n += 1
        return Buf(ap, name or ("a%d" % self.n))

    def reset(self):
        self.fw.barrier()
        self.off = 0

    def sub(self, start, size):
        return SubArena(self, start, size)


class SubArena:
    def __init__(self, parent, start, size):
        self.parent = parent
        self.start = start
        self.limit = start + size
        self.off = start
        assert self.limit <= parent.words

    def alloc(self, nelem, dtype=F32, name=None):
        w = (nelem + 1) // 2 if dtype == BF16 else nelem
        w = (w + 15) // 16 * 16
        assert self.off + w <= self.limit, "sub-arena overflow %d + %d > %d" % (self.off, w, self.limit)
        ap = self.parent.base.t[:, self.off:self.off + w]
        if dtype == BF16:
            ap = ap.bitcast(BF16)[:, 0:nelem]
        else:
            ap = ap[:, 0:nelem]
        self.off += w
        return Buf(ap, name or ("s%d" % self.off))

    def reset(self):
        self.parent.fw.barrier()
        self.off = self.start


def _bf16(a):
    return np.asarray(a, dtype=np.float32).astype(ml_dtypes.bfloat16)


def host_constants():
    k = {}
    k["k_ident_bf"] = _bf16(np.eye(128))
    k["k_ident_f"] = np.eye(128, dtype=np.float32)
    t = np.arange(SEQ)
    rows = (t // 64).astype(np.float32)
    cols = (t % 64).astype(np.float32)
    inv = (np.float32(10000.0) ** (-np.arange(0, 32, 2, dtype=np.float32) / np.float32(32))).astype(np.float32)
    ar = (rows[:, None] * inv[None, :]).astype(np.float32)
    ac = (cols[:, None] * inv[None, :]).astype(np.float32)
    cr, sr, cc, sc = np.cos(ar), np.sin(ar), np.cos(ac), np.sin(ac)
    cos_t = np.concatenate([cr, cr, cc, cc], axis=1).astype(np.float32)
    sin_t = np.concatenate([-sr, sr, -sc, sc], axis=1).astype(np.float32)
    k["k_rope_cos"] = cos_t.reshape(16, 128, 64)
    k["k_rope_sin"] = sin_t.reshape(16, 128, 64)
    n = np.arange(SEQ, dtype=np.int64)
    nk = (n[:, None] * n[None, :]) % SEQ
    ang = nk.astype(np.float64) * (2.0 * np.pi / SEQ)
    k["k_dft"] = np.stack([_bf16(np.cos(ang)), _bf16(np.sin(ang))])
    m = np.arange(64, dtype=np.int64)
    a64 = ((m[:, None] * m[None, :]) % 64).astype(np.float64) * (2.0 * np.pi / 64)
    cb = np.zeros((256, 256), np.float32)
    sb = np.zeros((256, 256), np.float32)
    for h in range(4):
        cb[h * 64:(h + 1) * 64, h * 64:(h + 1) * 64] = np.cos(a64)
        sb[h * 64:(h + 1) * 64, h * 64:(h + 1) * 64] = np.sin(a64)
    k["k_cblk"] = cb
    k["k_sblk"] = sb
    s_idx = np.arange(128) // 16
    k["k_s5mask"] = (s_idx[None, :] >= s_idx[:, None]).astype(np.float32)
    k["k_tau"] = np.tile(np.arange(-7, 9, dtype=np.float32)[None, :], (128, 1))
    k["k_nidx"] = np.tile(np.arange(NCH, dtype=np.float32)[None, :], (128, 1))
    qc = np.arange(64)
    cs = np.clip(qc - 8, 0, 48)
    kc = np.arange(64)
    ok = (kc[:, None] >= cs[None, :]) & (kc[:, None] < cs[None, :] + 16)
    blk = np.where(ok, 0.0, BIG).astype(np.float32)
    k["k_colmask"] = np.tile(blk, (2, 8))
    return k


def na_plan():
    pats = {}
    plan = []
    for t in range(16):
        lst = []
        for u in range(16):
            pat = []
            anyv = False
            for krl in range(2):
                for qrl in range(2):
                    kr, qr = 2 * u + krl, 2 * t + qrl
                    rs = min(max(qr - 4, 0), 24)
                    if rs <= kr < rs + 8:
                        pat.append(kr - qr + 7)
                        anyv = True
                    else:
                        pat.append(None)
            if anyv:
                pat = tuple(pat)
                if pat not in pats:
                    pats[pat] = len(pats)
                lst.append((u, pats[pat]))
        plan.append(lst)
    plist = [None] * len(pats)
    for p, i in pats.items():
        plist[i] = p
    return plan, plist


def build(stages=None, dbg=(), layers=(0, 1), batches=(0, 1)):
    nc = bass.Bass("TRN2", target_bir_lowering=False)
    fw = FW(nc)
    dbg = set(dbg)

    def ein(name, shape, dt=F32):
        return fw.dram(name, shape, dt, kind="ExternalInput")

    def scratch(name, shape, dt):
        return fw.dram(name, shape, dt, kind=("ExternalOutput" if name in dbg else "Internal"))

    x = ein("x", [NB, SEQ, D])
    ctx = ein("ctx", [NB, CTXL, D])
    c = ein("c", [NB, D])
    c_ctx = ein("c_ctx", [1, D])
    w_mod = ein("w_mod", [2, D, 6 * D])
    b_mod = ein("b_mod", [2, 6 * D])
    g_norm1 = ein("g_norm1", [2, D])
    w_in = ein("w_in", [2, D, 1792])
    att_q_gain = ein("att_q_gain", [2, 64])
    att_k_gain = ein("att_k_gain", [2, 64])
    na_q_gain = ein("na_q_gain", [2, 64])
    na_k_gain = ein("na_k_gain", [2, 64])
    na_rel_bias = ein("na_rel_bias", [2, 60, 31])
    w_fourier = ein("w_fourier", [2, 256, 256])
    ssm_lam_re = ein("ssm_lam_re", [2, 2, 16, 64])
    ssm_lam_im = ein("ssm_lam_im", [2, 2, 16, 64])
    ssm_log_dt = ein("ssm_log_dt", [2, 2, 16])
    ssm_b_re = ein("ssm_b_re", [2, 2, 16, 64, 16])
    ssm_b_im = ein("ssm_b_im", [2, 2, 16, 64, 16])
    ssm_c_re = ein("ssm_c_re", [2, 2, 16, 16, 64])
    ssm_c_im = ein("ssm_c_im", [2, 2, 16, 16, 64])
    ssm_d = ein("ssm_d", [2, 256])
    w_glu = ein("w_glu", [2, 256, 256])
    b_glu = ein("b_glu", [2, 256])
    g_group = ein("g_group", [2, D])
    w_out = ein("w_out", [2, D, D])
    g_norm2 = ein("g_norm2", [2, D])
    w_ff1 = ein("w_ff1", [2, D, DFF])
    w_ff3 = ein("w_ff3", [2, D, DFF])
    w_ff2 = ein("w_ff2", [2, DFF, D])
    k_ident_bf = ein("k_ident_bf", [128, 128], BF16)
    k_ident_f = ein("k_ident_f", [128, 128])
    k_rope_cos = ein("k_rope_cos", [16, 128, 64])
    k_rope_sin = ein("k_rope_sin", [16, 128, 64])
    k_dft = ein("k_dft", [2, SEQ, SEQ], BF16)
    k_cblk = ein("k_cblk", [256, 256])
    k_sblk = ein("k_sblk", [256, 256])
    k_s5mask = ein("k_s5mask", [128, 128])
    k_tau = ein("k_tau", [128, 16])
    k_nidx = ein("k_nidx", [128, NCH])
    k_colmask = ein("k_colmask", [128, 512])

    out = fw.dram("out", [NB, SEQ, D], F32, kind="ExternalOutput")

    XR = scratch("XR", [NB, T, D], F32)
    MODV = scratch("MODV", [2, 3, 6 * D], F32)
    WIN = scratch("WIN", [2, 128, 8, 1792], BF16)
    WOUT = scratch("WOUT", [2, 128, 8, D], BF16)
    W1 = scratch("W1", [2, NF, 128, 8, 128], BF16)
    W3 = scratch("W3", [2, NF, 128, 8, 128], BF16)
    W2 = scratch("W2", [2, DFF, D], BF16)
    QK = scratch("QK", [NB, 896, T], BF16)
    VV = scratch("VV", [NB, T, 390], BF16)
    FIN = scratch("FIN", [NB, T, 256], BF16)
    UT = scratch("UT", [NB, 256, T], BF16)
    OO = scratch("OO", [NB, T, D], BF16)
    PADT = scratch("PADT", [2, 61, 8192], F32)
    BIAS = scratch("BIAS", [2, 32, 128, 512], BF16)
    WCS = scratch("WCS", [2, 2, 256, 256], BF16)
    SCRU = scratch("SCRU", [2, 16, 8, 16, NCH], BF16)
    SCRY = scratch("SCRY", [2, 16, 8, 16, NCH], BF16)
    S5T = scratch("S5T", [2, 2, 16, 128, 128], BF16)
    S5B = scratch("S5B", [2, 2, 16, 128, 2, 64], BF16)
    S5C = scratch("S5C", [2, 2, 8, 128, 2, 128], BF16)
    S5D = scratch("S5D", [2, 2, 128, 8], F32)
    S5COS = scratch("S5COS", [2, 2, 128, 8, NCH], F32)
    S5SIN = scratch("S5SIN", [2, 2, 128, 8, NCH], F32)

    ar = Arena(fw, 47 * 1024)
    banks = [fw.psum("bank%d" % i, [128, 512], F32) for i in range(8)]

    all_stages = ["W", "M", "A", "GQA", "NAB", "NA", "FW", "F", "S5P", "S5", "C"]
    if stages is None:
        stages = all_stages
    stages = set(stages)

    def bank_bf(i):
        return View(banks[i], banks[i].t.bitcast(BF16))

    def rstd_from(ss, n, rs, tmp):
        fw.act(tmp, ss, AF.Sqrt, scale=1.0 / n, bias=EPS)
        fw.recip(rs, tmp)

    def stage_W():
        for l in range(2):
            for half in range(2):
                sl = slice(half * 896, (half + 1) * 896)
                fw.dma(WIN[l][:, :, sl], w_in[l].rearrange("(kc p) n -> p kc n", p=128)[:, :, sl], q="pool")
                yield 4.0
            fw.dma(WOUT[l], w_out[l].rearrange("(kc p) n -> p kc n", p=128), q="pool")
            yield 4.0
            for f in range(NF):
                fw.dma(W1[l][f], w_ff1[l][:, f * 128:(f + 1) * 128].rearrange("(kc p) n -> p kc n", p=128), q="pool")
                yield 4.0
                fw.dma(W3[l][f], w_ff3[l][:, f * 128:(f + 1) * 128].rearrange("(kc p) n -> p kc n", p=128), q="pool")
                yield 4.0
            for f0 in range(0, DFF, 704):
                fw.dma(W2[l][f0:f0 + 704, :], w_ff2[l][f0:f0 + 704, :], q="pool")
                yield 4.0

    def stage_M(l, A_=None, bk=(0, 1, 2, 3)):
        if A_ is None:
            ar.reset()
            A_ = ar
        Bk = [banks[i_] for i_ in bk]
        cTr = [A_.alloc(8) for _ in range(3)]
        for r in range(2):
            fw.dma(cTr[r].ap(), c[r].rearrange("(kc p) -> p kc", p=128), allow_slow_non_contiguous=True)
        fw.dma(cTr[2].ap(), c_ctx[0].rearrange("(kc p) -> p kc", p=128), allow_slow_non_contiguous=True)
        cS = A_.alloc(24)
        cS3 = cS.v("p (k r) -> p k r", r=3)
        for r in range(3):
            fw.act(cS3[:, :, r], cTr[r].ap(), AF.Silu)
        bmb = [A_.alloc(256) for _ in range(2)]
        g1b = A_.alloc(D)
        g2b = A_.alloc(D)
        fw.dma(g1b[0:3, :], g_norm1[l:l + 1, :].bc([3, D]))
        fw.dma(g2b[0:3, :], g_norm2[l:l + 1, :].bc([3, D]))
        mv = A_.alloc(6 * D)
        wbuf = [A_.alloc(8 * 256) for _ in range(2)]
        for nb in range(24):
            sl = slice(nb * 256, (nb + 1) * 256)
            wt = wbuf[nb % 2].v("p (k n) -> p k n", k=8)
            bm = bmb[nb % 2]
            fw.dma(wt, w_mod[l][:, sl].rearrange("(kc p) n -> p kc n", p=128))
            fw.dma(bm[0:3, :], b_mod[l:l + 1, sl].bc([3, 256]))
            ps = Bk[nb % 2]
            for kc in range(8):
                fw.matmul(ps[0:3, 0:256], cS3[:, kc, :], wt[:, kc, :], start=(kc == 0), stop=(kc == 7))
            fw.tt(mv[0:3, sl], ps[0:3, 0:256], bm[0:3, :], ALU.add)
            yield 5.0
        fw.stt(mv[0:3, D:2 * D], mv[0:3, D:2 * D], 1.0, g1b[0:3, :], ALU.add, ALU.mult)
        fw.stt(mv[0:3, 4 * D:5 * D], mv[0:3, 4 * D:5 * D], 1.0, g2b[0:3, :], ALU.add, ALU.mult)
        fw.dma(MODV[l], mv[0:3, :])

    def modrow(l, r, idx):
        return MODV[l][r:r + 1, idx * D:(idx + 1) * D].bc([128, D])

    def load_consts_ident(A_=None):
        idb = (A_ or ar).alloc(128, BF16)
        fw.dma(idb.ap(), k_ident_bf.ap())
        return idb

    def stage_A(b, l):
        ar.reset()
        idb = load_consts_ident()
        win = ar.alloc(8 * 1792, BF16)
        win3 = win.v("p (k n) -> p k n", k=8)
        fw.dma(win3, WIN[l])
        A1 = ar.alloc(D); B1 = ar.alloc(D); A1c = ar.alloc(D); B1c = ar.alloc(D)
        fw.dma(A1.ap(), modrow(l, b, 1)); fw.dma(B1.ap(), modrow(l, b, 0))
        fw.dma(A1c.ap(), modrow(l, 2, 1)); fw.dma(B1c.ap(), modrow(l, 2, 0))
        G = ar.alloc(14 * 64)
        G3 = G.v("p (h d) -> p h d", d=64)
        gsrc = [att_q_gain] * 4 + [att_k_gain] * 2 + [na_q_gain] * 4 + [na_k_gain] * 4
        for h in range(14):
            fw.dma(G3[:, h, :], gsrc[h][l:l + 1, :].bc([128, 64]))
        COS = ar.alloc(16 * 64); SIN = ar.alloc(16 * 64)
        COS3 = COS.v("p (t d) -> p t d", d=64); SIN3 = SIN.v("p (t d) -> p t d", d=64)
        fw.dma(COS3, k_rope_cos.v("t p d -> p t d"))
        fw.dma(SIN3, k_rope_sin.v("t p d -> p t d"))
        two = lambda n, dt=F32: [ar.alloc(n, dt) for _ in range(2)]
        xs = two(D); sqj = ar.alloc(D, BF16)
        ss = two(4); sd = two(4); rstd = two(4)
        tt_ = two(D); hb = two(D, BF16); hT = two(D, BF16)
        qks = two(896); sq2 = two(896); ssh = two(16); sdh = two(16); rsh = two(16)
        qkn = two(896); vsw = two(384); m1 = two(384); m2 = two(384)
        qkb = two(896, BF16); qkT = two(896, BF16)
        vaug = two(6 * 65, BF16); finb = two(256, BF16); uTb = two(256, BF16)
        for vb in vaug:
            fw.memset(vb.v("p (h e) -> p h e", e=65)[:, :, 64:65], 1.0)

        def load(i):
            if i >= NT:
                return
            if l == 0:
                src = ctx[b][i * 128:(i + 1) * 128, :] if i < 2 else x[b][(i - 2) * 128:(i - 1) * 128, :]
            else:
                src = XR[b][i * 128:(i + 1) * 128, :]
            fw.dma(xs[i % 2].ap(), src)

        def frontA(i):
            j = i % 2
            isctx = i < 2
            fw.act(sqj.ap(), xs[j].ap(), AF.Square, accum_out=ss[j][:, 0:1])
            rstd_from(ss[j][:, 0:1], D, rstd[j][:, 0:1], sd[j][:, 0:1])
            fw.stt(tt_[j].ap(), xs[j].ap(), rstd[j][:, 0:1], (A1c if isctx else A1).ap(), ALU.mult, ALU.mult)
            fw.tt(hb[j].ap(), tt_[j].ap(), (B1c if isctx else B1).ap(), ALU.add, eng="pool")

        def frontB(i):
            j = i % 2
            pT = bank_bf(4 + 2 * j)
            for k in range(8):
                fw.transpose(pT[:, k * 128:(k + 1) * 128], hb[j][:, k * 128:(k + 1) * 128], idb.ap())
            fw.copy(hT[j].ap(), pT[:, 0:D], eng="act")
            hT3 = hT[j].v("p (k n) -> p k n", k=8)
            for nb in range(3):
                for k in range(8):
                    fw.matmul(banks[nb].ap(), hT3[:, k, :], win3[:, k, nb * 512:(nb + 1) * 512], start=(k == 0), stop=(k == 7))
            for m in range(2):
                for k in range(8):
                    fw.matmul(banks[3][:, m * 128:(m + 1) * 128], win3[:, k, 1536 + m * 128:1536 + (m + 1) * 128], hT3[:, k, :],
                              start=(k == 0), stop=(k == 7))

        def mid(i):
            j = i % 2
            va3 = vaug[j].v("p (h e) -> p h e", e=65)
            fw.copy(qks[j][:, 0:384], banks[0][:, 0:384], eng="act")
            fw.copy(va3[:, 0:2, 0:64], banks[0][:, 384:512].rearrange("p (h d) -> p h d", d=64), eng="act")
            fw.copy(qks[j][:, 384:896], banks[1][:, 0:512], eng="dve")
            fw.copy(va3[:, 2:6, 0:64], banks[2][:, 0:256].rearrange("p (h d) -> p h d", d=64), eng="act")
            fw.copy(finb[j].ap(), banks[2][:, 256:512], eng="act")
            fw.copy(uTb[j].ap(), banks[3][:, 0:256], eng="dve")
            fw.dma(VV[b][i * 128:(i + 1) * 128, :], vaug[j].ap())
            fw.dma(FIN[b][i * 128:(i + 1) * 128, :], finb[j].ap())
            fw.dma(UT[b].rearrange("(m p) n -> p m n", p=128)[:, :, i * 128:(i + 1) * 128],
                   uTb[j].v("p (m n) -> p m n", m=2))

        def backE(i):
            j = i % 2
            isctx = i < 2
            fw.act(sq2[j].ap(), qks[j].ap(), AF.Square)
            fw.reduce(ssh[j][:, 0:14], sq2[j].v("p (h d) -> p h d", d=64), ALU.add)
            rstd_from(ssh[j][:, 0:14], 64, rsh[j][:, 0:14], sdh[j][:, 0:14])
            qkn3 = qkn[j].v("p (h d) -> p h d", d=64)
            fw.tt(qkn3, qks[j].v("p (h d) -> p h d", d=64), rsh[j][:, 0:14].unsq(2).bc([128, 14, 64]), ALU.mult)
            fw.tt(qkn[j].ap(), qkn[j].ap(), G.ap(), ALU.mult, eng="pool")
            if isctx:
                fw.copy(qkb[j][:, 0:384], qkn[j][:, 0:384], eng="pool")
            else:
                ti = i - 2
                fw.copy(vsw[j].v("p (a two x) -> p a two x", two=2, x=16),
                        qkn[j][:, 0:384].rearrange("p (a two x) -> p a two x", two=2, x=16)[:, :, ::-1, :], eng="pool")
                fw.tt(m1[j].v("p (h d) -> p h d", d=64), qkn3[:, 0:6, :], COS3[:, ti, :].unsq(1).bc([128, 6, 64]), ALU.mult)
                fw.tt(m2[j].v("p (h d) -> p h d", d=64), vsw[j].v("p (h d) -> p h d", d=64),
                      SIN3[:, ti, :].unsq(1).bc([128, 6, 64]), ALU.mult, eng="pool")
                fw.tt(qkb[j][:, 0:384], m1[j].ap(), m2[j].ap(), ALU.add)
            fw.copy(qkb[j][:, 384:896], qkn[j][:, 384:896], eng="pool")

        def backP(i):
            j = i % 2
            pq = bank_bf(5 + 2 * j)
            for jj in range(7):
                fw.transpose(pq[:, jj * 128:(jj + 1) * 128], qkb[j][:, jj * 128:(jj + 1) * 128], idb.ap())
            fw.copy(qkT[j].ap(), pq[:, 0:896], eng="act")
            fw.dma(QK[b].rearrange("(j p) n -> p j n", p=128)[:, :, i * 128:(i + 1) * 128],
                   qkT[j].v("p (j n) -> p j n", j=7))

        load(0)
        load(1)
        frontA(0)
        for i in range(NT):
            if i >= 1:
                backE(i - 1)
            frontB(i)
            load(i + 2)
            if i + 1 < NT:
                frontA(i + 1)
            if i >= 1:
                backP(i - 1)
            mid(i)
        backE(NT - 1)
        backP(NT - 1)

    def attn_block(QT3, KT3, V3, qhead, khead, vcol, qcol0, nq, ktiles, PT, psS, psO, st):
        nsub = nq // 128
        nk = len(ktiles)

        def pv(ki, kt, pt):
            for sub in range(nsub):
                fw.matmul(psO[:, sub * 65:(sub + 1) * 65], pt[:, sub * 128:(sub + 1) * 128], V3[:, kt, vcol:vcol + 65],
                          start=(ki == 0 and sub == 0), stop=(ki == nk - 1 and sub == nsub - 1))

        pend = None
        for ki, kt in enumerate(ktiles):
            pS = psS[st["s"] % len(psS)]
            st["s"] += 1
            pt = PT[st["p"] % len(PT)]
            st["p"] += 1
            fw.matmul(pS[:, 0:nq], KT3[0:64, khead, kt * 128:(kt + 1) * 128], QT3[0:64, qhead, qcol0:qcol0 + nq])
            fw.act(pt[:, 0:nq], pS[:, 0:nq], AF.Exp, scale=0.125)
            if pend is not None:
                pv(*pend)
            pend = (ki, kt, pt)
            yield 1.0
        pv(*pend)

    def attn_finish(psO, nsub, rden, ob):
        po3 = psO[:, 0:nsub * 65].rearrange("p (s e) -> p s e", e=65)
        fw.recip(rden[:, 0:nsub], po3[:, :, 64:65].rearrange("p s e -> p (s e)"))
        fw.tt(ob.v("p (s d) -> p s d", d=64)[:, 0:nsub, :], po3[:, :, 0:64],
              rden[:, 0:nsub].unsq(2).bc([128, nsub, 64]), ALU.mult)

    def stage_GQA(b, l, A_=None, bk=(0, 1, 2, 3)):
        if A_ is None:
            ar.reset()
            A_ = ar
        Bk = [banks[i_] for i_ in bk]
        QT = A_.alloc(4 * T, BF16); KT = A_.alloc(2 * T, BF16); VA = A_.alloc(NT * 130, BF16)
        QT3 = QT.v("p (h n) -> p h n", h=4); KT3 = KT.v("p (h n) -> p h n", h=2)
        VA3 = VA.v("p (t c) -> p t c", c=130)
        fw.dma(QT3[0:64], QK[b][0:256, :].rearrange("(h d) n -> d h n", d=64))
        fw.dma(KT3[0:64], QK[b][256:384, :].rearrange("(h d) n -> d h n", d=64))
        fw.dma(VA3, VV[b].rearrange("(t p) c -> p t c", p=128)[:, :, 0:130])
        PT = [A_.alloc(512, BF16) for _ in range(3)]
        rden = A_.alloc(4)
        obs = [A_.alloc(256, BF16) for _ in range(2)]
        st = {"s": 0, "p": 0}
        psS = [Bk[0], Bk[1]]
        cnt = 0
        O3 = OO[b].rearrange("(t p) c -> p t c", p=128)
        blocks = [(256 + qb * 512, 512, list(range(NT)), 2 + qb * 4) for qb in range(4)]
        if l == 0:
            blocks.append((0, 256, [0, 1], 0))
        for h in range(4):
            kv = h // 2
            for (qc0, nq, kts, t0) in blocks:
                psO = Bk[2 + cnt % 2]
                ob = obs[cnt % 2]
                cnt += 1
                yield from attn_block(QT3, KT3, VA3, h, kv, kv * 65, qc0, nq, kts, PT, psS, psO, st)
                nsub = nq // 128
                attn_finish(psO, nsub, rden, ob)
                fw.dma(O3[:, t0:t0 + nsub, h * 64:(h + 1) * 64], ob.v("p (s d) -> p s d", d=64)[:, 0:nsub, :])

    NA_PLAN, NA_PATS = na_plan()

    def stage_NAB(l, A_=None, bk=(0, 1, 2, 3)):
        if A_ is None:
            ar.reset()
            A_ = ar
        Bk = [banks[i_] for i_ in bk]
        rb = A_.alloc(31)
        fw.dma(rb[0:60, :], na_rel_bias[l])
        pad = A_.alloc(128)
        fw.memset(pad.ap(), BIG)
        fw.ts(pad[0:60, 48:79], rb[0:60, ::-1], 8.0, None, ALU.mult)
        padt = PADT[l].ap.tensor
        base = PADT[l].ap.offset
        fw.dma(View(PADT, bass.AP(tensor=padt, offset=base, ap=[[8192, 61], [127, 64], [1, 127]])),
               pad[0:61, 0:127].unsq(1).bc([61, 64, 127]))
        cm = A_.alloc(512)
        fw.dma(cm.ap(), k_colmask.ap())
        stg = [A_.alloc(512) for _ in range(2)]
        bt = [A_.alloc(512, BF16) for _ in range(2)]
        for bid, pat in enumerate(NA_PATS):
            s_ = stg[bid % 2]
            for krl in range(2):
                for qrl in range(2):
                    dri = pat[krl * 2 + qrl]
                    for h in range(4):
                        row = 60 if dri is None else h * 15 + dri
                        src = View(PADT, bass.AP(tensor=padt, offset=base + row * 8192 + 63, ap=[[126, 64], [1, 64]]))
                        fw.dma(s_[krl * 64:(krl + 1) * 64, h * 128 + qrl * 64:h * 128 + (qrl + 1) * 64], src)
            fw.tt(bt[bid % 2].ap(), s_.ap(), cm.ap(), ALU.add)
            fw.dma(BIAS[l][bid], bt[bid % 2].ap())
            yield 9.0

    def stage_NA(b, l, A_=None, bk=(0, 1, 2, 3)):
        if A_ is None:
            ar.reset()
            A_ = ar
        Bk = [banks[i_] for i_ in bk]
        idb = load_consts_ident(A_)
        QT = A_.alloc(4 * T, BF16); KT = A_.alloc(4 * T, BF16); VD = A_.alloc(NT * 260, BF16)
        QT3 = QT.v("p (h n) -> p h n", h=4); KT3 = KT.v("p (h n) -> p h n", h=4)
        VD3 = VD.v("p (t c) -> p t c", c=260)
        fw.dma(QT3[0:64], QK[b][384:640, :].rearrange("(h d) n -> d h n", d=64))
        fw.dma(KT3[0:64], QK[b][640:896, :].rearrange("(h d) n -> d h n", d=64))
        fw.dma(VD3, VV[b].rearrange("(t p) c -> p t c", p=128)[:, :, 130:390])
        nbias = len(NA_PATS)
        BT = A_.alloc(nbias * 512, BF16)
        BT3 = BT.v("p (i n) -> p i n", n=512)
        fw.dma(BT3, BIAS[l][0:nbias].rearrange("i p n -> p i n"))
        PT = [A_.alloc(512, BF16) for _ in range(3)]
        rden = A_.alloc(4)
        obs = [A_.alloc(256, BF16) for _ in range(2)]
        O3 = OO[b].rearrange("(t p) c -> p t c", p=128)
        si = 0
        for t in range(16):
            qc0 = 256 + t * 128
            klist = [(2 + u, bid) for (u, bid) in NA_PLAN[t]] + [(0, None), (1, None)]
            psO = Bk[2 + t % 2]
            ob = obs[t % 2]
            nk = len(klist)
            def pv(ki, kt, pt, psO=psO, nk=nk):
                for h in range(4):
                    fw.matmul(psO[:, h * 65:(h + 1) * 65], pt[:, h * 128:(h + 1) * 128], VD3[:, kt, h * 65:(h + 1) * 65],
                              start=(ki == 0 and h == 0), stop=(ki == nk - 1 and h == 3))

            pend = None
            for ki, (kt, bid) in enumerate(klist):
                pS = Bk[si % 2]
                pt = PT[si % 3]
                si += 1
                for h in range(4):
                    fw.matmul(pS[:, h * 128:(h + 1) * 128], KT3[0:64, h, kt * 128:(kt + 1) * 128], QT3[0:64, h, qc0:qc0 + 128],
                              start=(h == 0), stop=(h == 3 and bid is None))
                if bid is not None:
                    fw.matmul(pS.ap(), idb.ap(), BT3[:, bid, :], start=False, stop=True)
                fw.act(pt.ap(), pS.ap(), AF.Exp, scale=0.125)
                if pend is not None:
                    pv(*pend)
                pend = (ki, kt, pt)
                yield 3.0
            pv(*pend)
            attn_finish(psO, 4, rden, ob)
            fw.dma(OO[b][(2 + t) * 128:(3 + t) * 128, 256:512], ob.ap())
        if l == 0:
            st = {"s": 0, "p": 0}
            for h in range(4):
                psO = Bk[2 + h % 2]
                ob = obs[h % 2]
                yield from attn_block(QT3, KT3, VD3, h, h, h * 65, 0, 256, [0, 1], PT, [Bk[0], Bk[1]], psO, st)
                attn_finish(psO, 2, rden, ob)
                fw.dma(O3[:, 0:2, 256 + h * 64:256 + (h + 1) * 64], ob.v("p (s d) -> p s d", d=64)[:, 0:2, :])

    def stage_FW(l, A_=None, bk=(0, 1, 2, 3)):
        if A_ is None:
            ar.reset()
            A_ = ar
        Bk = [banks[i_] for i_ in bk]
        wf = A_.alloc(2 * 256)
        wf3 = wf.v("p (k n) -> p k n", k=2)
        fw.dma(wf3, w_fourier[l].rearrange("(k p) n -> p k n", p=128))
        for ci, ktab in enumerate((k_cblk, k_sblk)):
            cb = A_.alloc(2 * 256)
            cb3 = cb.v("p (k n) -> p k n", k=2)
            fw.dma(cb3, ktab.v("(k p) n -> p k n", p=128))
            wo = A_.alloc(2 * 256, BF16)
            wo3 = wo.v("p (k n) -> p k n", k=2)
            for fo in range(2):
                ps = Bk[(ci * 2 + fo) % 4]
                for k in range(2):
                    fw.matmul(ps[:, 0:256], cb3[:, k, fo * 128:(fo + 1) * 128], wf3[:, k, :], start=(k == 0), stop=(k == 1))
                fw.act(wo3[:, fo, :], ps[:, 0:256], AF.Copy, scale=(1.0 if ci == 0 else -1.0))
            fw.dma(WCS[l][ci].rearrange("(k p) n -> p k n", p=128), wo3)
            yield 15.0

    def stage_F(b, l, A_=None, bk=(0, 1, 2, 3)):
        if A_ is None:
            ar.reset()
            A_ = ar
        Bk = [banks[i_] for i_ in bk]
        fin = A_.alloc(NT * 256, BF16)
        fin3 = fin.v("p (t f) -> p t f", f=256)
        fw.dma(fin3, FIN[b].rearrange("(t p) f -> p t f", p=128))
        wcs = A_.alloc(2 * 2 * 256, BF16)
        wcs4 = wcs.v("p (c k n) -> p c k n", c=2, k=2)
        for ci in range(2):
            fw.dma(wcs4[:, ci], WCS[l][ci].rearrange("(k p) n -> p k n", p=128))
        tabs = [A_.alloc(512, BF16) for _ in range(6)]
        G = [A_.alloc(4 * 512, BF16) for _ in range(2)]
        ofb = [A_.alloc(256, BF16) for _ in range(2)]
        ti = 0
        oi = 0

        def second_stage(Gv, nk, tile0, scale):
            nonlocal oi
            for kt in range(nk // 128):
                ps = Bk[2 + oi % 2]
                n_ = 0
                for ci in range(2):
                    for fc in range(2):
                        fw.matmul(ps[:, 0:256], Gv[:, ci, fc, kt * 128:(kt + 1) * 128], wcs4[:, ci, fc, :],
                                  start=(n_ == 0), stop=(n_ == 3))
                        n_ += 1
                o_ = ofb[oi % 2]
                oi += 1
                fw.act(o_.ap(), ps[:, 0:256], AF.Copy, scale=scale)
                fw.dma(OO[b][(tile0 + kt) * 128:(tile0 + kt + 1) * 128, 512:768], o_.ap())

        for kb in range(4):
            for ncix in range(16):
                for ci in range(2):
                    tb = tabs[ti % 6]
                    ti += 1
                    fw.dma(tb.ap(), k_dft[ci][ncix * 128:(ncix + 1) * 128, kb * 512:(kb + 1) * 512])
                    for fc in range(2):
                        fw.matmul(Bk[ci * 2 + fc].ap(), fin3[:, 2 + ncix, fc * 128:(fc + 1) * 128], tb.ap(),
                                  start=(ncix == 0), stop=(ncix == 15))
                yield 2.0
            Gb = G[kb % 2]
            Gv = Gb.v("p (c f n) -> p c f n", c=2, f=2)
            for ci in range(2):
                for fc in range(2):
                    if (ci + fc) % 2 == 0:
                        fw.copy(Gv[:, ci, fc, :], Bk[ci * 2 + fc].ap(), eng="act")
                    else:
                        fw.copy(Gv[:, ci, fc, :], Bk[ci * 2 + fc].ap(), eng="dve")
            second_stage(Gv, 512, 2 + kb * 4, 1.0 / math.sqrt(SEQ * 64.0))
        if l == 0:
            for ncix in range(2):
                for ci in range(2):
                    tb = tabs[ti % 6]
                    ti += 1
                    fw.dma(tb[:, 0:256], k_dft[ci][ncix * 1024:(ncix + 1) * 1024:8, 0:256])
                    for fc in range(2):
                        fw.matmul(Bk[ci * 2 + fc][:, 0:256], fin3[:, ncix, fc * 128:(fc + 1) * 128], tb[:, 0:256],
                                  start=(ncix == 0), stop=(ncix == 1))
            Gb = G[0]
            Gv = Gb.v("p (c f n) -> p c f n", c=2, f=2)
            for ci in range(2):
                for fc in range(2):
                    fw.copy(Gv[:, ci, fc, 0:256], Bk[ci * 2 + fc][:, 0:256], eng=("act" if (ci + fc) % 2 == 0 else "dve"))
            second_stage(Gv, 256, 0, 1.0 / math.sqrt(CTXL * 64.0))

    def stage_C(b, l):
        ar.reset()
        idb = load_consts_ident()
        wo = ar.alloc(8 * D, BF16)
        wo3 = wo.v("p (k n) -> p k n", k=8)
        fw.dma(wo3, WOUT[l])
        gg = ar.alloc(D)
        fw.dma(gg.ap(), g_group[l:l + 1, :].bc([128, D]))
        GA1 = ar.alloc(D); A2 = ar.alloc(D); B2 = ar.alloc(D); GA2 = ar.alloc(D)
        hid = ar.alloc(NF * 512, BF16)
        hid3 = hid.v("p (f n) -> p f n", f=NF)
        two = lambda n, dt=F32: [ar.alloc(n, dt) for _ in range(2)]
        h2T = two(8 * 512, BF16)
        x1 = two(4 * D)
        wt1 = [ar.alloc(8 * 128, BF16) for _ in range(3)]
        wt3 = [ar.alloc(8 * 128, BF16) for _ in range(3)]
        w2t = [ar.alloc(D, BF16) for _ in range(3)]
        xs = two(D); ob = two(D, BF16)
        sq = ar.alloc(D)
        ss4 = two(4); sd4 = two(4); rs4 = two(4)
        ss = two(4); sd = two(4); rstd = two(4)
        onb = two(D, BF16); onT = two(D, BF16)
        tmpA = ar.alloc(D); tmpB = ar.alloc(D); h2b = two(D, BF16)
        sqj = ar.alloc(D, BF16)
        tmpC = ar.alloc(D)
        sa = two(512)
        xo = two(D)
        groups = [[2 + g * 4 + s_ for s_ in range(4)] for g in range(4)]
        if l == 0:
            groups = [[0, 1]] + groups
        rows = [(2 if grp[0] < 2 else b) for grp in groups]
        st = {"wi": 0, "w2i": 0, "oi": 0, "pre": 0}

        def h2T3(gp):
            return h2T[gp].v("p (k n) -> p k n", k=8)

        def x13(gp):
            return x1[gp].v("p (s n) -> p s n", s=4)

        def P1a(i, j):
            if l == 0:
                src = ctx[b][i * 128:(i + 1) * 128, :] if i < 2 else x[b][(i - 2) * 128:(i - 1) * 128, :]
            else:
                src = XR[b][i * 128:(i + 1) * 128, :]
            fw.dma(xs[j].ap(), src)
            fw.dma(ob[j].ap(), OO[b][i * 128:(i + 1) * 128, :])
            fw.act(sq.ap(), ob[j].ap(), AF.Square)
            fw.reduce(ss4[j][:, 0:4], sq.v("p (g d) -> p g d", g=4), ALU.add)
            rstd_from(ss4[j][:, 0:4], 256, rs4[j][:, 0:4], sd4[j][:, 0:4])
            for g_ in range(4):
                sl = slice(g_ * 256, (g_ + 1) * 256)
                fw.stt(onb[j][:, sl], ob[j][:, sl], rs4[j][:, g_:g_ + 1], gg[:, sl], ALU.mult, ALU.mult)

        def P1b(i, j):
            pT = bank_bf(6)
            for k in range(8):
                fw.transpose(pT[:, k * 128:(k + 1) * 128], onb[j][:, k * 128:(k + 1) * 128], idb.ap())
            fw.copy(onT[j].ap(), pT[:, 0:D], eng="act")
            onT3 = onT[j].v("p (k n) -> p k n", k=8)
            for nb in range(2):
                for k in range(8):
                    fw.matmul(banks[4 + nb].ap(), onT3[:, k, :], wo3[:, k, nb * 512:(nb + 1) * 512], start=(k == 0), stop=(k == 7))

        def P2a(i, j, gp, s_):
            for nb in range(2):
                sl = slice(nb * 512, (nb + 1) * 512)
                fw.tt(tmpA[:, sl], banks[4 + nb].ap(), GA1[:, sl], ALU.mult)
            fw.tt(x13(gp)[:, s_, :], tmpA.ap(), xs[j].ap(), ALU.add, eng="pool")
            fw.act(sqj.ap(), x13(gp)[:, s_, :], AF.Square, accum_out=ss[j][:, 0:1])
            rstd_from(ss[j][:, 0:1], D, rstd[j][:, 0:1], sd[j][:, 0:1])
            fw.stt(tmpB.ap(), x13(gp)[:, s_, :], rstd[j][:, 0:1], A2.ap(), ALU.mult, ALU.mult)
            fw.tt(h2b[j].ap(), tmpB.ap(), B2.ap(), ALU.add, eng="pool")

        def P2b(i, j, gp, s_):
            pT2 = bank_bf(7)
            for k in range(8):
                fw.transpose(pT2[:, k * 128:(k + 1) * 128], h2b[j][:, k * 128:(k + 1) * 128], idb.ap())
            fw.copy(h2T3(gp)[:, :, s_ * 128:(s_ + 1) * 128], pT2[:, 0:D].rearrange("p (k n) -> p k n", k=8), eng="act")

        def pre_slots(grp, gp):
            n = len(grp)
            slots = []
            for k in range(n + 2):
                def slot(k=k):
                    if 0 <= k < n:
                        P1a(grp[k], k % 2)
                    if 0 <= k - 2 < n:
                        P2b(grp[k - 2], (k - 2) % 2, gp, k - 2)
                    if 0 <= k - 1 < n:
                        P1b(grp[k - 1], (k - 1) % 2)
                        P2a(grp[k - 1], (k - 1) % 2, gp, k - 1)
                slots.append(slot)
            return slots

        def load_mods(row):
            fw.dma(GA1.ap(), modrow(l, row, 2)); fw.dma(A2.ap(), modrow(l, row, 4))
            fw.dma(B2.ap(), modrow(l, row, 3)); fw.dma(GA2.ap(), modrow(l, row, 5))

        def issue_w13(f):
            a_ = wt1[st["wi"] % 3]; b_ = wt3[st["wi"] % 3]
            st["wi"] += 1
            fw.dma(a_.ap(), W1[l][f].rearrange("p k n -> p (k n)"))
            fw.dma(b_.ap(), W3[l][f].rearrange("p k n -> p (k n)"))
            return a_, b_

        pre_done = False
        preissued = []
        for gi_, grp in enumerate(groups):
            gp = gi_ % 2
            nt_ = len(grp)
            ntk = nt_ * 128
            if not pre_done:
                load_mods(rows[gi_])
                for slot in pre_slots(grp, gp):
                    slot()
            nxt = gi_ + 1
            can_pipe = nxt < len(groups) and rows[nxt] == rows[gi_]
            nslots = pre_slots(groups[nxt], nxt % 2) if can_pipe else []
            for f in range(NF):
                if preissued:
                    a_, b_ = preissued.pop(0)
                else:
                    a_, b_ = issue_w13(f)
                a3 = a_.v("p (k n) -> p k n", k=8); b3_ = b_.v("p (k n) -> p k n", k=8)
                pa = banks[(f % 2) * 2]; pb = banks[1 + (f % 2) * 2]
                for k in range(8):
                    fw.matmul(pa[:, 0:ntk], a3[:, k, :], h2T3(gp)[:, k, 0:ntk], start=(k == 0), stop=(k == 7))
                for k in range(8):
                    fw.matmul(pb[:, 0:ntk], b3_[:, k, :], h2T3(gp)[:, k, 0:ntk], start=(k == 0), stop=(k == 7))
                sj = sa[f % 2]
                fw.act(sj[:, 0:ntk], pa[:, 0:ntk], AF.Silu)
                fw.tt(hid3[:, f, 0:ntk], sj[:, 0:ntk], pb[:, 0:ntk], ALU.mult)
                if nslots and f % 3 == 1:
                    nslots.pop(0)()
            while nslots:
                nslots.pop(0)()
            pre_done = can_pipe
            for f in range(NF):
                w_ = w2t[st["w2i"] % 3]
                st["w2i"] += 1
                fw.dma(w_.ap(), W2[l][f * 128:(f + 1) * 128, :])
                for s_ in range(nt_):
                    for nb in range(2):
                        fw.matmul(banks[s_ * 2 + nb].ap(), hid3[:, f, s_ * 128:(s_ + 1) * 128], w_[:, nb * 512:(nb + 1) * 512],
                                  start=(f == 0), stop=(f == NF - 1))
            if nxt < len(groups):
                preissued = [issue_w13(f) for f in range(3)]
            for s_, i in enumerate(grp):
                xoj = xo[st["oi"] % 2]
                st["oi"] += 1
                for nb in range(2):
                    sl = slice(nb * 512, (nb + 1) * 512)
                    fw.tt(tmpC[:, sl], banks[s_ * 2 + nb].ap(), GA2[:, sl], ALU.mult)
                fw.tt(xoj.ap(), tmpC.ap(), x13(gp)[:, s_, :], ALU.add, eng="pool")
                if l == 0:
                    fw.dma(XR[b][i * 128:(i + 1) * 128, :], xoj.ap())
                else:
                    fw.dma(out[b][(i - 2) * 128:(i - 1) * 128, :], xoj.ap())

    TWO_PI = 2.0 * math.pi
    CW1 = 6.28125
    CW2 = TWO_PI - 6.28125
    MAGIC = 12582912.0
    PI_SAFE = 3.141592

    def range_reduce(out, in_, kbuf, shift=0.0, eng="dve"):
        if shift != 0.0:
            fw.ts(out, in_, shift, None, ALU.add, eng=eng)
            src = out
        else:
            src = in_
        fw.ts(kbuf, src, 1.0 / TWO_PI, MAGIC, ALU.mult, ALU.add, eng=eng)
        fw.ts(kbuf, kbuf, -MAGIC, None, ALU.add, eng=eng)
        fw.stt(out, kbuf, -CW1, src, ALU.mult, ALU.add, eng=eng)
        fw.stt(out, kbuf, -CW2, out, ALU.mult, ALU.add, eng=eng)
        fw.ts(out, out, -PI_SAFE, PI_SAFE, ALU.max, ALU.min, eng=eng)

    def stage_S5P(l, A_=None, bk=(0, 1, 2, 3)):
        if A_ is None:
            ar.reset()
            A_ = ar
        Bk = [banks[i_] for i_ in bk]
        idf = A_.alloc(128)
        fw.dma(idf.ap(), k_ident_f.ap())
        tau = A_.alloc(16)
        fw.dma(tau.ap(), k_tau.ap())
        nidx = A_.alloc(NCH)
        fw.dma(nidx.ap(), k_nidx.ap())
        msk = A_.alloc(128)
        fw.dma(msk.ap(), k_s5mask.ap())
        mark = A_.off
        for d_ in range(2):
            A_.off = mark
            if d_ == 1:
                fw.barrier()
            lamre = A_.alloc(8); lamim = A_.alloc(8); dt = A_.alloc(8)
            for gi in range(2):
                sl = slice(64 * gi, 64 * gi + 64)
                fw.dma(lamre[sl, :], ssm_lam_re[l][d_].rearrange("(q two) p -> two p q", two=2)[gi], allow_slow_non_contiguous=True)
                fw.dma(lamim[sl, :], ssm_lam_im[l][d_].rearrange("(q two) p -> two p q", two=2)[gi], allow_slow_non_contiguous=True)
                fw.dma(dt[sl, :], ssm_log_dt[l][d_:d_ + 1, :].rearrange("o (q two) -> two o q", two=2)[gi].bc([64, 8]),
                       allow_slow_non_contiguous=True)
            fw.act(dt.ap(), dt.ap(), AF.Exp)
            zr = A_.alloc(8); zi = A_.alloc(8)
            fw.tt(zr.ap(), lamre.ap(), dt.ap(), ALU.mult)
            fw.tt(zi.ap(), lamim.ap(), dt.ap(), ALU.mult)
            PZ = A_.alloc(128); PEx = A_.alloc(128); KB = A_.alloc(128); RS = A_.alloc(128); RC = A_.alloc(128)
            Are = A_.alloc(128); Aim = A_.alloc(128)
            v3 = lambda bf: bf.v("p (t q) -> p t q", q=8)
            fw.tt(v3(PZ), tau.ap().unsq(2).bc([128, 16, 8]), zr.ap().unsq(1).bc([128, 16, 8]), ALU.mult)
            fw.act(PEx.ap(), PZ.ap(), AF.Exp)
            fw.tt(v3(PZ), tau.ap().unsq(2).bc([128, 16, 8]), zi.ap().unsq(1).bc([128, 16, 8]), ALU.mult)
            range_reduce(RS.ap(), PZ.ap(), KB.ap())
            range_reduce(RC.ap(), PZ.ap(), KB.ap(), shift=0.5 * math.pi)
            fw.act(RS.ap(), RS.ap(), AF.Sin)
            fw.act(RC.ap(), RC.ap(), AF.Sin)
            fw.tt(Are.ap(), PEx.ap(), RC.ap(), ALU.mult)
            fw.tt(Aim.ap(), PEx.ap(), RS.ap(), ALU.mult)
            Are3 = v3(Are); Aim3 = v3(Aim)
            fw.dma(S5D[l][d_], v3(PEx)[:, 15, :])
            phi = A_.alloc(8); kb8 = A_.alloc(8)
            fw.ts(phi.ap(), zi.ap(), 8.0, None, ALU.mult)
            range_reduce(phi.ap(), phi.ap(), kb8.ap())
            ANG = A_.alloc(8 * NCH); KB2 = A_.alloc(8 * NCH); R2 = A_.alloc(8 * NCH)
            a3 = lambda bf: bf.v("p (q n) -> p q n", q=8)
            fw.tt(a3(ANG), nidx.ap().unsq(1).bc([128, 8, NCH]), phi.ap().unsq(2).bc([128, 8, NCH]), ALU.mult)
            range_reduce(R2.ap(), ANG.ap(), KB2.ap())
            fw.act(R2.ap(), R2.ap(), AF.Sin)
            fw.dma(S5SIN[l][d_], a3(R2))
            R3 = R2
            range_reduce(R3.ap(), ANG.ap(), KB2.ap(), shift=0.5 * math.pi)
            fw.act(R3.ap(), R3.ap(), AF.Sin)
            fw.dma(S5COS[l][d_], a3(R3))
            yield 40.0
            nr = A_.alloc(8); den = A_.alloc(8); t8a = A_.alloc(8); t8b = A_.alloc(8); cr = A_.alloc(8); ci = A_.alloc(8)
            fw.ts(nr.ap(), Are3[:, 8, :], -1.0, None, ALU.add)
            fw.tt(den.ap(), lamre.ap(), lamre.ap(), ALU.mult)
            fw.tt(t8a.ap(), lamim.ap(), lamim.ap(), ALU.mult)
            fw.tt(den.ap(), den.ap(), t8a.ap(), ALU.add)
            fw.recip(den.ap(), den.ap())
            fw.tt(t8a.ap(), nr.ap(), lamre.ap(), ALU.mult)
            fw.tt(t8b.ap(), Aim3[:, 8, :], lamim.ap(), ALU.mult)
            fw.tt(t8a.ap(), t8a.ap(), t8b.ap(), ALU.add)
            fw.tt(cr.ap(), t8a.ap(), den.ap(), ALU.mult)
            fw.tt(t8a.ap(), Aim3[:, 8, :], lamre.ap(), ALU.mult)
            fw.tt(t8b.ap(), nr.ap(), lamim.ap(), ALU.mult)
            fw.tt(t8a.ap(), t8a.ap(), t8b.ap(), ALU.subtract)
            fw.tt(ci.ap(), t8a.ap(), den.ap(), ALU.mult)
            bre = A_.alloc(128); bim = A_.alloc(128); Bre = A_.alloc(128); Bim = A_.alloc(128); tb1 = A_.alloc(128); tb2 = A_.alloc(128)
            b3 = lambda bf: bf.v("p (q c) -> p q c", q=8)
            for gi in range(2):
                sl = slice(64 * gi, 64 * gi + 64)
                fw.dma(b3(bre)[sl], ssm_b_re[l][d_].rearrange("(q two) p c -> two p q c", two=2)[gi])
                fw.dma(b3(bim)[sl], ssm_b_im[l][d_].rearrange("(q two) p c -> two p q c", two=2)[gi])
            crb = cr.ap().unsq(2).bc([128, 8, 16]); cib = ci.ap().unsq(2).bc([128, 8, 16])
            fw.tt(b3(tb1), b3(bre), crb, ALU.mult); fw.tt(b3(tb2), b3(bim), cib, ALU.mult)
            fw.tt(Bre.ap(), tb1.ap(), tb2.ap(), ALU.subtract)
            fw.tt(b3(tb1), b3(bim), crb, ALU.mult); fw.tt(b3(tb2), b3(bre), cib, ALU.mult)
            fw.tt(Bim.ap(), tb1.ap(), tb2.ap(), ALU.add)
            yield 25.0
            Cre = A_.alloc(128); Cim = A_.alloc(128)
            for (Cdst, csrc, bkx) in ((Cre, ssm_c_re, 0), (Cim, ssm_c_im, 1)):
                X = A_.alloc(128)
                for q in range(8):
                    for gi in range(2):
                        fw.dma(X[16 * q:16 * q + 16, 64 * gi:64 * gi + 64], csrc[l][d_][2 * q + gi])
                fw.transpose(Bk[bkx][:, 0:128], X.ap(), idf.ap())
                fw.copy(Cdst.ap(), Bk[bkx][:, 0:128], eng="act")
            BcR = A_.alloc(8 * 128); BcI = A_.alloc(8 * 128)
            CpR = A_.alloc(8 * 128); CpI = A_.alloc(8 * 128)
            CcR = A_.alloc(8 * 128, BF16); CcI = A_.alloc(8 * 128, BF16)
            u1 = A_.alloc(1024); u2 = A_.alloc(1024)
            f4 = lambda bf: bf.v("p (q s c) -> p q s c", q=8, s=8)
            pw = lambda A3, sl: A3[:, sl, :].rearrange("p s q -> p q s").unsq(3).bc([128, 8, 8, 16])
            qc = lambda bf: b3(bf).unsq(2).bc([128, 8, 8, 16])

            def cmul(dst_re, dst_im, Ar, Ai, Xr, Xi, neg_im=False):
                fw.tt(f4(u1), Ar, Xr, ALU.mult); fw.tt(f4(u2), Ai, Xi, ALU.mult, eng="pool")
                fw.tt(f4(dst_re), f4(u1), f4(u2), ALU.subtract)
                fw.tt(f4(u1), Ar, Xi, ALU.mult); fw.tt(f4(u2), Ai, Xr, ALU.mult, eng="pool")
                if neg_im:
                    fw.stt(f4(dst_im), f4(u1), -1.0, f4(u2), ALU.mult, ALU.subtract)
                else:
                    fw.tt(f4(dst_im), f4(u1), f4(u2), ALU.add)

            cmul(BcR, BcI, pw(Are3, slice(14, 6, -1)), pw(Aim3, slice(14, 6, -1)), qc(Bre), qc(Bim))
            cmul(CpR, CpI, pw(Are3, slice(0, 8)), pw(Aim3, slice(0, 8)), qc(Cre), qc(Cim), neg_im=True)
            cmul(CcR, CcI, pw(Are3, slice(8, 16)), pw(Aim3, slice(8, 16)), qc(Cre), qc(Cim), neg_im=True)
            fw.dma(S5C[l][d_].rearrange("q p r n -> p q r n")[:, :, 0, :], CcR.v("p (q n) -> p q n", q=8))
            fw.dma(S5C[l][d_].rearrange("q p r n -> p q r n")[:, :, 1, :], CcI.v("p (q n) -> p q n", q=8))
            Tb = [A_.alloc(128, BF16) for _ in range(2)]
            BT = [A_.alloc(256, BF16) for _ in range(2)]
            qv = lambda bf, q: bf.v("p (q n) -> p q n", q=8)[:, q, :]
            for q in range(8):
                for gi in range(2):
                    g = 2 * q + gi
                    sl = slice(64 * gi, 64 * gi + 64)
                    ps = Bk[gi]
                    fw.matmul(ps[:, 0:128], qv(BcR, q)[sl], qv(CpR, q)[sl], start=True, stop=False)
                    fw.matmul(ps[:, 0:128], qv(BcI, q)[sl], qv(CpI, q)[sl], start=False, stop=True)
                    tb_ = Tb[g % 2]
                    fw.tt(tb_.ap(), ps[:, 0:128], msk.ap(), ALU.mult)
                    fw.dma(S5T[l][d_][g], tb_.ap())
                bt_ = BT[q % 2]
                for ri, src in enumerate((BcR, BcI)):
                    ps = Bk[2 + ri]
                    fw.transpose(ps[:, 0:128], qv(src, q), idf.ap())
                    fw.copy(bt_.v("p (g r n) -> p g r n", g=2, r=2)[:, :, ri, :], ps[:, 0:128].rearrange("p (g n) -> p g n", g=2), eng="act")
                fw.dma(S5B[l][d_][2 * q:2 * q + 2].rearrange("g sc r n -> sc g (r n)"), bt_.v("p (g x) -> p g x", g=2))
                yield 8.0

    def stage_S5(b, l, A_=None, bk=(0, 1, 2, 3)):
        if A_ is None:
            ar.reset()
            A_ = ar
        Bk = [banks[i_] for i_ in bk]
        idb = load_consts_ident(A_)
        uT = A_.alloc(2 * T, BF16)
        uT3 = uT.v("p (m n) -> p m n", m=2)
        fw.dma(uT3, UT[b].rearrange("(m p) n -> p m n", p=128))
        Dcol = A_.alloc(2)
        fw.dma(Dcol.ap(), ssm_d[l].rearrange("(m p) -> p m", p=128), allow_slow_non_contiguous=True)
        bgl = A_.alloc(2)
        fw.dma(bgl.ap(), b_glu[l].rearrange("(m p) -> p m", p=128), allow_slow_non_contiguous=True)
        wgf = A_.alloc(512)
        fw.dma(wgf.v("p (k n) -> p k n", k=2), w_glu[l].rearrange("(k p) n -> p k n", p=128))
        wgb = A_.alloc(512, BF16)
        fw.copy(wgb.ap(), wgf.ap())
        wgb3 = wgb.v("p (k n) -> p k n", k=2)
        Tm = A_.alloc(32 * 128, BF16); T3 = Tm.v("p (g n) -> p g n", g=32)
        fw.dma(T3, S5T[l].rearrange("d g sc n -> sc (d g) n"))
        Bm = A_.alloc(32 * 128, BF16); B4 = Bm.v("p (g r n) -> p g r n", g=32, r=2)
        fw.dma(Bm.v("p (g x) -> p g x", g=32), S5B[l].rearrange("d g sc r n -> sc (d g) (r n)"))
        Cm = A_.alloc(16 * 256, BF16); C4 = Cm.v("p (g r n) -> p g r n", g=16, r=2)
        fw.dma(Cm.v("p (g x) -> p g x", g=16), S5C[l].rearrange("d q p r n -> p (d q) (r n)"))
        DEC = A_.alloc(16); DEC3 = DEC.v("p (d q) -> p d q", d=2)
        fw.dma(DEC3, S5D[l].rearrange("d p q -> p d q"))
        stg = [A_.alloc(2 * 8 * NCH, BF16) for _ in range(2)]
        s4 = lambda bf: bf.v("p (m s j) -> p m s j", m=2, s=8)
        yield 10.0
        for m in range(2):
            fw.copy(s4(stg[0])[:, m], uT3[:, m, :].rearrange("p (j s) -> p s j", s=8), eng=("dve" if m == 0 else "pool"))
            fw.copy(s4(stg[1])[:, m, :, 0:32], uT3[:, m, 0:256][:, ::-1].rearrange("p (j s) -> p s j", s=8), eng="dve")
            fw.copy(s4(stg[1])[:, m, :, 32:NCH], uT3[:, m, 256:T][:, ::-1].rearrange("p (j s) -> p s j", s=8), eng="pool")
        U = A_.alloc(32 * NCH, BF16); U3 = U.v("p (g j) -> p g j", g=32)
        for d_ in range(2):
            for g in range(16):
                m, gl = g // 8, g % 8
                fw.dma(SCRU[d_][g].rearrange("s c j -> c s j"), s4(stg[d_])[16 * gl:16 * gl + 16, m])
        yield 20.0
        for d_ in range(2):
            for g in range(16):
                fw.dma(U3[:, d_ * 16 + g, :], SCRU[d_][g].rearrange("s c j -> (s c) j"))
            yield 15.0
        two = lambda n, dt=F32: [A_.alloc(n, dt) for _ in range(2)]
        cosb = two(NCH); sinb = two(NCH)
        Sre = two(NCH); Sim = two(NCH)
        t1 = two(NCH); t2 = two(NCH); t3 = two(NCH); t4 = two(NCH)
        wr_in = two(NCH); wi_in = two(NCH); wr = two(NCH); wi = two(NCH)
        Hre = two(NCH, BF16); Him = two(NCH, BF16)
        Yg = [A_.alloc(NCH, BF16) for _ in range(4)]

        def s5_front(it, d_, q):
            p = it % 2
            cb = cosb[p]; sb = sinb[p]; hr = Hre[p]; hi = Him[p]
            bS = (0, 1)
            fw.dma(cb.ap(), S5COS[l][d_][:, q, :])
            fw.dma(sb.ap(), S5SIN[l][d_][:, q, :])
            for gi in range(2):
                g = d_ * 16 + 2 * q + gi
                sl = slice(64 * gi, 64 * gi + 64)
                fw.matmul(Bk[bS[0]][sl, 0:NCH], B4[:, g, 0, :], U3[:, g, :])
                fw.matmul(Bk[bS[1]][sl, 0:NCH], B4[:, g, 1, :], U3[:, g, :])
            fw.copy(Sre[p].ap(), Bk[bS[0]][:, 0:NCH], eng="act")
            fw.copy(Sim[p].ap(), Bk[bS[1]][:, 0:NCH], eng="act")
            fw.tt(t1[p].ap(), Sre[p].ap(), cb.ap(), ALU.mult)
            fw.tt(t2[p].ap(), Sim[p].ap(), sb.ap(), ALU.mult, eng="pool")
            fw.tt(wr_in[p].ap(), t1[p].ap(), t2[p].ap(), ALU.add)
            fw.tt(t3[p].ap(), Sim[p].ap(), cb.ap(), ALU.mult, eng="pool")
            fw.tt(t4[p].ap(), Sre[p].ap(), sb.ap(), ALU.mult)
            fw.tt(wi_in[p].ap(), t3[p].ap(), t4[p].ap(), ALU.subtract, eng="pool")
            dec = DEC3[:, d_, q:q + 1].bc([128, NCH])
            fw.scan(wr[p].ap(), dec, wr_in[p].ap(), 0.0)
            fw.scan(wi[p].ap(), dec, wi_in[p].ap(), 0.0)
            fw.tt(t1[p].ap(), wr[p].ap(), cb.ap(), ALU.mult)
            fw.tt(t2[p].ap(), wi[p].ap(), sb.ap(), ALU.mult, eng="pool")
            fw.tt(hr.ap(), t1[p].ap(), t2[p].ap(), ALU.subtract)
            fw.tt(t3[p].ap(), wr[p].ap(), sb.ap(), ALU.mult, eng="pool")
            fw.tt(t4[p].ap(), wi[p].ap(), cb.ap(), ALU.mult)
            fw.tt(hi.ap(), t3[p].ap(), t4[p].ap(), ALU.add, eng="pool")

        def s5_back(it, d_, q):
            p = it % 2
            hr = Hre[p]; hi = Him[p]
            bY = (2, 3)
            for gi in range(2):
                g16 = 2 * q + gi
                g = d_ * 16 + g16
                sl = slice(64 * gi, 64 * gi + 64)
                psY = Bk[bY[gi]]
                fw.matmul(psY[:, 0:NCH], T3[:, g, :], U3[:, g, :], start=True, stop=False)
                fw.matmul(psY[:, 1:NCH], C4[sl, d_ * 8 + q, 0, :], hr[sl, 0:NCH - 1], start=False, stop=False)
                fw.matmul(psY[:, 1:NCH], C4[sl, d_ * 8 + q, 1, :], hi[sl, 0:NCH - 1], start=False, stop=True)
                yg = Yg[p * 2 + gi]
                fw.copy(yg.ap(), psY[:, 0:NCH], eng="act")
                fw.dma(SCRY[d_][g16].rearrange("t c j -> (t c) j"), yg.ap())

        its = [(d_, q) for d_ in range(2) for q in range(8)]
        for it, (d_, q) in enumerate(its):
            s5_front(it, d_, q)
            if it >= 1:
                s5_back(it - 1, *its[it - 1])
            yield 8.0
        s5_back(len(its) - 1, *its[-1])
        ys = stg
        for d_ in range(2):
            for g in range(16):
                m, gl = g // 8, g % 8
                fw.dma(s4(ys[d_])[16 * gl:16 * gl + 16, m], SCRY[d_][g].rearrange("t c j -> c t j"))
        yield 20.0
        y = A_.alloc(2 * T); y3 = y.v("p (m n) -> p m n", m=2)
        gb = A_.alloc(2 * T, BF16); gb3 = gb.v("p (m n) -> p m n", m=2)
        for m in range(2):
            f3 = s4(ys[0])[:, m]
            r3 = s4(ys[1])[:, m]
            eng = "dve" if m == 0 else "pool"
            fw.tt(y3[:, m, 0:256].rearrange("p (j t) -> p j t", t=8), f3[:, :, 0:32].rearrange("p t j -> p j t"),
                  r3[:, :, 0:32].rearrange("p s n -> p n s")[:, ::-1, ::-1], ALU.add, eng=eng)
            fw.tt(y3[:, m, 256:T].rearrange("p (j t) -> p j t", t=8), f3[:, :, 32:NCH].rearrange("p t j -> p j t"),
                  r3[:, :, 32:NCH].rearrange("p s n -> p n s")[:, ::-1, ::-1], ALU.add, eng=eng)
            fw.stt(y3[:, m, :], uT3[:, m, :], Dcol[:, m:m + 1], y3[:, m, :], ALU.mult, ALU.add, eng=eng)
        yield 30.0
        fw.act(y.ap(), y.ap(), AF.Gelu_apprx_tanh)
        fw.copy(gb.ap(), y.ap(), eng="pool")
        osT = A_.alloc(2 * T, BF16); osT3 = osT.v("p (m n) -> p m n", m=2)
        gate = [A_.alloc(512) for _ in range(2)]
        bi = 0
        for mo in range(2):
            for (c0, nn) in [(0, 512), (512, 512), (1024, 512), (1536, 512), (2048, 256)]:
                ps = Bk[bi % 2]
                gt = gate[bi % 2]
                bi += 1
                for m in range(2):
                    fw.matmul(ps[:, 0:nn], wgb3[:, m, mo * 128:(mo + 1) * 128], gb3[:, m, c0:c0 + nn], start=(m == 0), stop=(m == 1))
                fw.act(gt[:, 0:nn], ps[:, 0:nn], AF.Sigmoid, bias=bgl[:, mo:mo + 1])
                fw.tt(osT3[:, mo, c0:c0 + nn], y3[:, mo, c0:c0 + nn], gt[:, 0:nn], ALU.mult, eng=("dve" if bi % 2 == 0 else "pool"))
                yield 4.0
        otb = [A_.alloc(256, BF16) for _ in range(2)]
        for i in range(NT):
            if l == 1 and i < 2:
                continue
            pT = View(Bk[2 + i % 2], Bk[2 + i % 2].t.bitcast(BF16))
            for mo in range(2):
                fw.transpose(pT[:, mo * 128:(mo + 1) * 128], osT3[:, mo, i * 128:(i + 1) * 128], idb.ap())
            ot = otb[i % 2]
            fw.copy(ot.ap(), pT[:, 0:256], eng="act")
            fw.dma(OO[b][i * 128:(i + 1) * 128, 768:1024], ot.ap())
            yield 2.0

    def run(gen):
        for _ in gen:
            pass

    def corun(gens):
        acc = [0.0] * len(gens)
        live = list(range(len(gens)))
        while live:
            i_ = min(live, key=lambda k_: acc[k_])
            try:
                c_ = next(gens[i_])
                acc[i_] += (c_ or 1.0)
            except StopIteration:
                live.remove(i_)

    full = all(st_ in stages for st_ in all_stages) and tuple(layers) == (0, 1) and tuple(batches) == (0, 1)
    PRO_W = 17408

    def prologue(l, A_, bk):
        yield from stage_M(l, A_, bk)
        A_.reset()
        yield from stage_NAB(l, A_, bk)
        A_.reset()
        yield from stage_FW(l, A_, bk)
        A_.reset()
        yield from stage_S5P(l, A_, bk)

    if full:
        ar.reset()
        corun([stage_W(), prologue(0, ar.sub(0, PRO_W), (0, 1, 2, 3))])
    elif "W" in stages:
        run(stage_W())
    for l in layers:
        if not full:
            if "M" in stages:
                run(stage_M(l))
            if "NAB" in stages:
                run(stage_NAB(l))
            if "FW" in stages:
                run(stage_FW(l))
            if "S5P" in stages:
                run(stage_S5P(l))
        for b in batches:
            if "A" in stages:
                stage_A(b, l)
            if all(st_ in stages for st_ in ("GQA", "NA", "F", "S5")):
                ar.reset()
                corun([stage_S5(b, l, ar.sub(0, 38400), (0, 1, 2, 3)), stage_GQA(b, l, ar.sub(38400, 9728), (4, 5, 6, 7))])
                ar.reset()
                if full and l == 0 and b == 1:
                    gens = [stage_NA(b, l, ar.sub(0, 19200), (0, 1, 2, 2)), stage_F(b, l, ar.sub(19200, 8192), (4, 5, 6, 7)),
                            prologue(1, ar.sub(27392, PRO_W), (3, 3, 3, 3))]
                else:
                    gens = [stage_NA(b, l, ar.sub(0, 19200), (0, 1, 2, 3)), stage_F(b, l, ar.sub(19200, 8192), (4, 5, 6, 7))]
                corun(gens)
            else:
                if "GQA" in stages:
                    run(stage_GQA(b, l))
                if "NA" in stages:
                    run(stage_NA(b, l))
                if "F" in stages:
                    run(stage_F(b, l))
                if "S5" in stages:
                    run(stage_S5(b, l))
            if "C" in stages:
                stage_C(b, l)
    fw.barrier()
    fw.emit()
    return nc


_RESHAPE = {
    "c_ctx": (1, D),
    "na_rel_bias": (2, 60, 31),
}


def make_in_maps(inputs, ncores=NCORES):
    consts = host_constants()
    shared = {}
    for name, arr in inputs.items():
        if name in ("x", "ctx", "c"):
            continue
        a = np.ascontiguousarray(arr)
        if name in _RESHAPE:
            a = a.reshape(_RESHAPE[name])
        shared[name] = a
    shared.update(consts)
    maps = []
    for i in range(ncores):
        m = dict(shared)
        m["x"] = np.ascontiguousarray(inputs["x"][i * NB:(i + 1) * NB])
        m["ctx"] = np.ascontiguousarray(inputs["ctx"][i * NB:(i + 1) * NB])
        m["c"] = np.ascontiguousarray(inputs["c"][i * NB:(i + 1) * NB])
        maps.append(m)
    return maps


def kernel(**inputs):
    nc = build()
    maps = make_in_maps(inputs)
    res = run_bass_kernel_spmd(nc, maps, core_ids=list(range(NCORES)))
    outs = [np.asarray(r["out"]) for r in res.results]
    return np.concatenate(outs, axis=0).astype(np.float32)
```

```python
import contextlib
import math
import numpy as np
import ml_dtypes
import concourse.bass as bass
import concourse.mybir as mybir
from concourse.bass_utils import run_bass_kernel_spmd

F32 = mybir.dt.float32
BF16 = mybir.dt.bfloat16
AF = mybir.ActivationFunctionType
ALU = mybir.AluOpType
AX = mybir.AxisListType

NCORES = 8
NB = 2
D = 1024
SEQ = 2048
CTXL = 256
T = SEQ + CTXL
NT = T // 128
DFF = 2816
NF = DFF // 128
EPS = 1e-6
NCH = T // 8
BIG = -30000.0
DMA_K = 16
COMPUTE = ("pe", "act", "dve", "pool")
SAME_ENGINE_SYNC = True


class Buf:
    __slots__ = ("t", "name", "writers", "readers")

    def __init__(self, ap, name=""):
        self.t = ap
        self.name = name
        self.writers = {}
        self.readers = {}

    def __getitem__(self, idx):
        return View(self, self.t[idx])

    def ap(self):
        return View(self, self.t)

    def v(self, pattern=None, **kw):
        if pattern is None:
            return View(self, self.t)
        return View(self, self.t.rearrange(pattern, **kw))


class View:
    __slots__ = ("buf", "ap")

    def __init__(self, buf, ap):
        self.buf = buf
        self.ap = ap

    def __getitem__(self, idx):
        return View(self.buf, self.ap[idx])

    def rearrange(self, *a, **k):
        return View(self.buf, self.ap.rearrange(*a, **k))

    def bitcast(self, dt):
        return View(self.buf, self.ap.bitcast(dt))

    def bc(self, shape):
        return View(self.buf, self.ap.to_broadcast(list(shape)))

    def unsq(self, i):
        return View(self.buf, self.ap.unsqueeze(i))


class FW:
    def __init__(self, nc):
        self.nc = nc
        self.stack = contextlib.ExitStack()
        self.streams = {e: [] for e in ("pe", "act", "dve", "pool", "sp")}
        self.seq = {e: 0 for e in COMPUTE}
        self.sems = {}
        self.waited = {e: {} for e in self.streams}
        self.dma_count = {"sp": 0, "pool": 0, "act": 0}
        self.dma_last = {}
        self.n_inst = 0

    def sem(self, key):
        if key not in self.sems:
            name = "s_" + ("_".join(str(k) for k in key) if isinstance(key, tuple) else str(key))
            self.sems[key] = self.stack.enter_context(self.nc.semaphore(name))
        return self.sems[key]

    def sbuf(self, name, shape, dtype):
        t = self.stack.enter_context(self.nc.sbuf_tensor(name, list(shape), dtype))
        return Buf(t[:], name)

    def psum(self, name, shape, dtype=F32):
        t = self.stack.enter_context(self.nc.psum_tensor(name, list(shape), dtype))
        return Buf(t[:], name)

    def dram(self, name, shape, dtype, kind="Internal"):
        t = self.nc.dram_tensor(name, list(shape), dtype, kind=kind)
        return Buf(t.ap(), name)

    def _need(self, eng, waits, key, val):
        if self.waited[eng].get(key, 0) >= val:
            return
        if val > waits.get(key, 0):
            waits[key] = val

    def _deps(self, eng, reads, writes):
        waits = {}
        for v in reads:
            for key, val in v.buf.writers.items():
                if key == eng and eng == "pe":
                    continue
                self._need(eng, waits, key, val)
        for v in writes:
            for key, val in v.buf.writers.items():
                if key == eng and (eng == "pe" or not SAME_ENGINE_SYNC):
                    continue
                self._need(eng, waits, key, val)
            for key, val in v.buf.readers.items():
                if key == eng and (eng == "pe" or not SAME_ENGINE_SYNC):
                    continue
                self._need(eng, waits, key, val)
        for key, val in waits.items():
            self.waited[eng][key] = val
        return list(waits.items())

    def _mark(self, token, reads, writes):
        key, val = token
        for v in reads:
            v.buf.readers[key] = val
        for v in writes:
            b = v.buf
            b.writers = {key: val}
            b.readers = {}

    def op(self, eng, fn, reads=(), writes=()):
        reads = [r for r in reads if isinstance(r, View)]
        writes = [w for w in writes if isinstance(w, View)]
        waits = self._deps(eng, reads, writes)
        self.seq[eng] += 1
        token = (eng, self.seq[eng])
        self._mark(token, reads, writes)
        self.streams[eng].append((waits, fn, (eng, 1)))
        self.n_inst += 1
        return token

    def dma(self, out, in_, q="sp", **kw):
        i = self.dma_count[q]
        self.dma_count[q] += 1
        r, rnd = i % DMA_K, i // DMA_K
        key = ("dma", q, r)
        waits = {}
        if rnd > 0:
            self._need(q, waits, key, 16 * rnd)
        for k2, v2 in waits.items():
            self.waited[q][k2] = v2
        ob = out.buf
        merge = (not ob.readers) and bool(ob.writers) and all(isinstance(k_, tuple) for k_ in ob.writers)
        w2 = self._deps(q, [in_], [] if merge else [out])
        allw = list(waits.items()) + w2
        token = (key, 16 * (rnd + 1))
        self.dma_last[key] = 16 * (rnd + 1)
        if merge:
            in_.buf.readers[key] = token[1]
            ob.writers[key] = max(ob.writers.get(key, 0), token[1])
        else:
            self._mark(token, [in_], [out])
        oa, ia = out.ap, in_.ap
        self.streams[q].append((allw, lambda e, oa=oa, ia=ia, kw=kw: e.dma_start(out=oa, in_=ia, **kw), (key, 16)))
        self.n_inst += 1
        return token

    def barrier(self):
        toks = [(e, self.seq[e]) for e in COMPUTE if self.seq[e] > 0]
        toks += list(self.dma_last.items())
        for eng in self.streams:
            waits = {}
            for key, val in toks:
                if key == eng:
                    continue
                self._need(eng, waits, key, val)
            for k2, v2 in waits.items():
                self.waited[eng][k2] = v2
            if waits:
                self.streams[eng].append((list(waits.items()), None, None))

    def emit(self):
        nc = self.nc
        for e in COMPUTE:
            self.sem(e)
        for q in self.dma_count:
            for r in range(DMA_K):
                self.sem(("dma", q, r))
        engmap = {"pe": "tensor", "act": "scalar", "dve": "vector", "pool": "gpsimd", "sp": "sync"}
        with nc.Block() as block:
            for e, attr in engmap.items():
                stream = self.streams[e]

                def body(engine, stream=stream):
                    for waits, fn, inc in stream:
                        for key, val in waits:
                            engine.wait_ge(self.sems[key], val)
                        if fn is None:
                            continue
                        ins = fn(engine)
                        if inc is not None:
                            ins.then_inc(self.sems[inc[0]], inc[1])

                getattr(block, attr)(body)
        self.stack.close()

    def matmul(self, out, lhsT, rhs, start=True, stop=True, **kw):
        oa, la, ra = out.ap, lhsT.ap, rhs.ap
        return self.op("pe", lambda e: e.matmul(oa, la, ra, start=start, stop=stop, **kw),
                       reads=[lhsT, rhs], writes=[out])

    def transpose(self, out, in_, ident):
        oa, ia, da = out.ap, in_.ap, ident.ap
        return self.op("pe", lambda e: e.transpose(oa, ia, da), reads=[in_, ident], writes=[out])

    def act(self, out, in_, func, bias=None, scale=None, accum_out=None):
        oa, ia = out.ap, in_.ap
        kw = {}
        reads = [in_]
        writes = [out]
        if bias is not None:
            if isinstance(bias, View):
                kw["bias"] = bias.ap
                reads.append(bias)
            else:
                kw["bias"] = bias
        if scale is not None:
            if isinstance(scale, View):
                kw["scale"] = scale.ap
                reads.append(scale)
            else:
                kw["scale"] = scale
        if accum_out is not None:
            kw["accum_out"] = accum_out.ap
            writes.append(accum_out)
        return self.op("act", lambda e: e.activation(oa, ia, func, **kw), reads=reads, writes=writes)

    def tt(self, out, in0, in1, op, eng="dve"):
        oa, a, b = out.ap, in0.ap, in1.ap
        return self.op(eng, lambda e: e.tensor_tensor(oa, a, b, op), reads=[in0, in1], writes=[out])

    def ts(self, out, in0, s1, s2, op0, op1=None, eng="dve"):
        oa, a = out.ap, in0.ap
        reads = [in0]
        s1a = s1.ap if isinstance(s1, View) else s1
        s2a = s2.ap if isinstance(s2, View) else s2
        if isinstance(s1, View):
            reads.append(s1)
        if isinstance(s2, View):
            reads.append(s2)
        kw = {}
        if op1 is not None:
            kw["op1"] = op1
        return self.op(eng, lambda e: e.tensor_scalar(oa, a, s1a, s2a, op0, **kw), reads=reads, writes=[out])

    def stt(self, out, in0, scalar, in1, op0, op1, eng="dve"):
        oa, a, b = out.ap, in0.ap, in1.ap
        reads = [in0, in1]
        sa = scalar.ap if isinstance(scalar, View) else scalar
        if isinstance(scalar, View):
            reads.append(scalar)
        eng = "dve"
        return self.op(eng, lambda e: e.scalar_tensor_tensor(oa, a, sa, b, op0, op1), reads=reads, writes=[out])

    def copy(self, out, in_, eng="dve"):
        oa, ia = out.ap, in_.ap
        if eng == "act":
            return self.op(eng, lambda e: e.copy(oa, ia), reads=[in_], writes=[out])
        return self.op(eng, lambda e: e.tensor_copy(oa, ia), reads=[in_], writes=[out])

    def memset(self, out, val, eng="pool"):
        oa = out.ap
        return self.op(eng, lambda e: e.memset(oa, val), reads=[], writes=[out])

    def reduce(self, out, in_, op, axis=AX.X, eng="dve"):
        oa, ia = out.ap, in_.ap
        return self.op(eng, lambda e: e.tensor_reduce(oa, ia, axis, op), reads=[in_], writes=[out])

    def recip(self, out, in_):
        oa, ia = out.ap, in_.ap
        return self.op("dve", lambda e: e.reciprocal(oa, ia), reads=[in_], writes=[out])

    def scan(self, out, d0, d1, initial, op0=ALU.mult, op1=ALU.add):
        oa, a, b = out.ap, d0.ap, d1.ap
        reads = [d0, d1]
        ia = initial.ap if isinstance(initial, View) else initial
        if isinstance(initial, View):
            reads.append(initial)
        return self.op("dve", lambda e: e.tensor_tensor_scan(oa, a, b, ia, op0, op1), reads=reads, writes=[out])


class Arena:
    def __init__(self, fw, words):
        self.fw = fw
        self.words = words
        self.base = fw.sbuf("arena", [128, words], F32)
        self.off = 0
        self.n = 0

    def alloc(self, nelem, dtype=F32, name=None):
        if dtype == BF16:
            w = (nelem + 1) // 2
        else:
            w = nelem
        w = (w + 15) // 16 * 16
        assert self.off + w <= self.words, "arena overflow %d + %d > %d" % (self.off, w, self.words)
        ap = self.base.t[:, self.off:self.off + w]
        if dtype == BF16:
            ap = ap.bitcast(BF16)[:, 0:nelem]
        else:
            ap = ap[:, 0:nelem]
        self.off += w
        self.n += 1
        return Buf(ap, name or ("a%d" % self.n))

    def reset(self):
        self.fw.barrier()
        self.off = 0

    def sub(self, start, size):
        return SubArena(self, start, size)


class SubArena:
    def __init__(self, parent, start, size):
        self.parent = parent
        self.start = start
        self.limit = start + size
        self.off = start
        assert self.limit <= parent.words

    def alloc(self, nelem, dtype=F32, name=None):
        w = (nelem + 1) // 2 if dtype == BF16 else nelem
        w = (w + 15) // 16 * 16
        assert self.off + w <= self.limit, "sub-arena overflow %d + %d > %d" % (self.off, w, self.limit)
        ap = self.parent.base.t[:, self.off:self.off + w]
        if dtype == BF16:
            ap = ap.bitcast(BF16)[:, 0:nelem]
        else:
            ap = ap[:, 0:nelem]
        self.off += w
        return Buf(ap, name or ("s%d" % self.off))

    def reset(self):
        self.parent.fw.barrier()
        self.off = self.start


def _bf16(a):
    return np.asarray(a, dtype=np.float32).astype(ml_dtypes.bfloat16)


def host_constants():
    k = {}
    k["k_ident_bf"] = _bf16(np.eye(128))
    k["k_ident_f"] = np.eye(128, dtype=np.float32)
    t = np.arange(SEQ)
    rows = (t // 64).astype(np.float32)
    cols = (t % 64).astype(np.float32)
    inv = (np.float32(10000.0) ** (-np.arange(0, 32, 2, dtype=np.float32) / np.float32(32))).astype(np.float32)
    ar = (rows[:, None] * inv[None, :]).astype(np.float32)
    ac = (cols[:, None] * inv[None, :]).astype(np.float32)
    cr, sr, cc, sc = np.cos(ar), np.sin(ar), np.cos(ac), np.sin(ac)
    cos_t = np.concatenate([cr, cr, cc, cc], axis=1).astype(np.float32)
    sin_t = np.concatenate([-sr, sr, -sc, sc], axis=1).astype(np.float32)
    k["k_rope_cos"] = cos_t.reshape(16, 128, 64)
    k["k_rope_sin"] = sin_t.reshape(16, 128, 64)
    n = np.arange(SEQ, dtype=np.int64)
    nk = (n[:, None] * n[None, :]) % SEQ
    ang = nk.astype(np.float64) * (2.0 * np.pi / SEQ)
    k["k_dft"] = np.stack([_bf16(np.cos(ang)), _bf16(np.sin(ang))])
    m = np.arange(64, dtype=np.int64)
    a64 = ((m[:, None] * m[None, :]) % 64).astype(np.float64) * (2.0 * np.pi / 64)
    cb = np.zeros((256, 256), np.float32)
    sb = np.zeros((256, 256), np.float32)
    for h in range(4):
        cb[h * 64:(h + 1) * 64, h * 64:(h + 1) * 64] = np.cos(a64)
        sb[h * 64:(h + 1) * 64, h * 64:(h + 1) * 64] = np.sin(a64)
    k["k_cblk"] = cb
    k["k_sblk"] = sb
    s_idx = np.arange(128) // 16
    k["k_s5mask"] = (s_idx[None, :] >= s_idx[:, None]).astype(np.float32)
    k["k_tau"] = np.tile(np.arange(-7, 9, dtype=np.float32)[None, :], (128, 1))
    k["k_nidx"] = np.tile(np.arange(NCH, dtype=np.float32)[None, :], (128, 1))
    qc = np.arange(64)
    cs = np.clip(qc - 8, 0, 48)
    kc = np.arange(64)
    ok = (kc[:, None] >= cs[None, :]) & (kc[:, None] < cs[None, :] + 16)
    blk = np.where(ok, 0.0, BIG).astype(np.float32)
    k["k_colmask"] = np.tile(blk, (2, 8))
    return k


def na_plan():
    pats = {}
    plan = []
    for t in range(16):
        lst = []
        for u in range(16):
            pat = []
            anyv = False
            for krl in range(2):
                for qrl in range(2):
                    kr, qr = 2 * u + krl, 2 * t + qrl
                    rs = min(max(qr - 4, 0), 24)
                    if rs <= kr < rs + 8:
                        pat.append(kr - qr + 7)
                        anyv = True
                    else:
                        pat.append(None)
            if anyv:
                pat = tuple(pat)
                if pat not in pats:
                    pats[pat] = len(pats)
                lst.append((u, pats[pat]))
        plan.append(lst)
    plist = [None] * len(pats)
    for p, i in pats.items():
        plist[i] = p
    return plan, plist


def build(stages=None, dbg=(), layers=(0, 1), batches=(0, 1)):
    nc = bass.Bass("TRN2", target_bir_lowering=False)
    fw = FW(nc)
    dbg = set(dbg)

    def ein(name, shape, dt=F32):
        return fw.dram(name, shape, dt, kind="ExternalInput")

    def scratch(name, shape, dt):
        return fw.dram(name, shape, dt, kind=("ExternalOutput" if name in dbg else "Internal"))

    x = ein("x", [NB, SEQ, D])
    ctx = ein("ctx", [NB, CTXL, D])
    c = ein("c", [NB, D])
    c_ctx = ein("c_ctx", [1, D])
    w_mod = ein("w_mod", [2, D, 6 * D])
    b_mod = ein("b_mod", [2, 6 * D])
    g_norm1 = ein("g_norm1", [2, D])
    w_in = ein("w_in", [2, D, 1792])
    att_q_gain = ein("att_q_gain", [2, 64])
    att_k_gain = ein("att_k_gain", [2, 64])
    na_q_gain = ein("na_q_gain", [2, 64])
    na_k_gain = ein("na_k_gain", [2, 64])
    na_rel_bias = ein("na_rel_bias", [2, 60, 31])
    w_fourier = ein("w_fourier", [2, 256, 256])
    ssm_lam_re = ein("ssm_lam_re", [2, 2, 16, 64])
    ssm_lam_im = ein("ssm_lam_im", [2, 2, 16, 64])
    ssm_log_dt = ein("ssm_log_dt", [2, 2, 16])
    ssm_b_re = ein("ssm_b_re", [2, 2, 16, 64, 16])
    ssm_b_im = ein("ssm_b_im", [2, 2, 16, 64, 16])
    ssm_c_re = ein("ssm_c_re", [2, 2, 16, 16, 64])
    ssm_c_im = ein("ssm_c_im", [2, 2, 16, 16, 64])
    ssm_d = ein("ssm_d", [2, 256])
    w_glu = ein("w_glu", [2, 256, 256])
    b_glu = ein("b_glu", [2, 256])
    g_group = ein("g_group", [2, D])
    w_out = ein("w_out", [2, D, D])
    g_norm2 = ein("g_norm2", [2, D])
    w_ff1 = ein("w_ff1", [2, D, DFF])
    w_ff3 = ein("w_ff3", [2, D, DFF])
    w_ff2 = ein("w_ff2", [2, DFF, D])
    k_ident_bf = ein("k_ident_bf", [128, 128], BF16)
    k_ident_f = ein("k_ident_f", [128, 128])
    k_rope_cos = ein("k_rope_cos", [16, 128, 64])
    k_rope_sin = ein("k_rope_sin", [16, 128, 64])
    k_dft = ein("k_dft", [2, SEQ, SEQ], BF16)
    k_cblk = ein("k_cblk", [256, 256])
    k_sblk = ein("k_sblk", [256, 256])
    k_s5mask = ein("k_s5mask", [128, 128])
    k_tau = ein("k_tau", [128, 16])
    k_nidx = ein("k_nidx", [128, NCH])
    k_colmask = ein("k_colmask", [128, 512])

    out = fw.dram("out", [NB, SEQ, D], F32, kind="ExternalOutput")

    XR = scratch("XR", [NB, T, D], F32)
    MODV = scratch("MODV", [2, 3, 6 * D], F32)
    WIN = scratch("WIN", [2, 128, 8, 1792], BF16)
    WOUT = scratch("WOUT", [2, 128, 8, D], BF16)
    W1 = scratch("W1", [2, NF, 128, 8, 128], BF16)
    W3 = scratch("W3", [2, NF, 128, 8, 128], BF16)
    W2 = scratch("W2", [2, DFF, D], BF16)
    QK = scratch("QK", [NB, 896, T], BF16)
    VV = scratch("VV", [NB, T, 390], BF16)
    FIN = scratch("FIN", [NB, T, 256], BF16)
    UT = scratch("UT", [NB, 256, T], BF16)
    OO = scratch("OO", [NB, T, D], BF16)
    PADT = scratch("PADT", [2, 61, 8192], F32)
    BIAS = scratch("BIAS", [2, 32, 128, 512], BF16)
    WCS = scratch("WCS", [2, 2, 256, 256], BF16)
    SCRU = scratch("SCRU", [2, 16, 8, 16, NCH], BF16)
    SCRY = scratch("SCRY", [2, 16, 8, 16, NCH], BF16)
    S5T = scratch("S5T", [2, 2, 16, 128, 128], BF16)
    S5B = scratch("S5B", [2, 2, 16, 128, 2, 64], BF16)
    S5C = scratch("S5C", [2, 2, 8, 128, 2, 128], BF16)
    S5D = scratch("S5D", [2, 2, 128, 8], F32)
    S5COS = scratch("S5COS", [2, 2, 128, 8, NCH], F32)
    S5SIN = scratch("S5SIN", [2, 2, 128, 8, NCH], F32)

    ar = Arena(fw, 47 * 1024)
    banks = [fw.psum("bank%d" % i, [128, 512], F32) for i in range(8)]

    all_stages = ["W", "M", "A", "GQA", "NAB", "NA", "FW", "F", "S5P", "S5", "C"]
    if stages is None:
        stages = all_stages
    stages = set(stages)

    def bank_bf(i):
        return View(banks[i], banks[i].t.bitcast(BF16))

    def rstd_from(ss, n, rs, tmp):
        fw.act(tmp, ss, AF.Sqrt, scale=1.0 / n, bias=EPS)
        fw.recip(rs, tmp)

    def stage_W():
        for l in range(2):
            for half in range(2):
                sl = slice(half * 896, (half + 1) * 896)
                fw.dma(WIN[l][:, :, sl], w_in[l].rearrange("(kc p) n -> p kc n", p=128)[:, :, sl], q="pool")
                yield 4.0
            fw.dma(WOUT[l], w_out[l].rearrange("(kc p) n -> p kc n", p=128), q="pool")
            yield 4.0
            for f in range(NF):
                fw.dma(W1[l][f], w_ff1[l][:, f * 128:(f + 1) * 128].rearrange("(kc p) n -> p kc n", p=128), q="pool")
                yield 4.0
                fw.dma(W3[l][f], w_ff3[l][:, f * 128:(f + 1) * 128].rearrange("(kc p) n -> p kc n", p=128), q="pool")
                yield 4.0
            for f0 in range(0, DFF, 704):
                fw.dma(W2[l][f0:f0 + 704, :], w_ff2[l][f0:f0 + 704, :], q="pool")
                yield 4.0

    def stage_M(l, A_=None, bk=(0, 1, 2, 3)):
        if A_ is None:
            ar.reset()
            A_ = ar
        Bk = [banks[i_] for i_ in bk]
        cTr = [A_.alloc(8) for _ in range(3)]
        for r in range(2):
            fw.dma(cTr[r].ap(), c[r].rearrange("(kc p) -> p kc", p=128), allow_slow_non_contiguous=True)
        fw.dma(cTr[2].ap(), c_ctx[0].rearrange("(kc p) -> p kc", p=128), allow_slow_non_contiguous=True)
        cS = A_.alloc(24)
        cS3 = cS.v("p (k r) -> p k r", r=3)
        for r in range(3):
            fw.act(cS3[:, :, r], cTr[r].ap(), AF.Silu)
        bmb = [A_.alloc(256) for _ in range(2)]
        g1b = A_.alloc(D)
        g2b = A_.alloc(D)
        fw.dma(g1b[0:3, :], g_norm1[l:l + 1, :].bc([3, D]))
        fw.dma(g2b[0:3, :], g_norm2[l:l + 1, :].bc([3, D]))
        mv = A_.alloc(6 * D)
        wbuf = [A_.alloc(8 * 256) for _ in range(2)]
        for nb in range(24):
            sl = slice(nb * 256, (nb + 1) * 256)
            wt = wbuf[nb % 2].v("p (k n) -> p k n", k=8)
            bm = bmb[nb % 2]
            fw.dma(wt, w_mod[l][:, sl].rearrange("(kc p) n -> p kc n", p=128))
            fw.dma(bm[0:3, :], b_mod[l:l + 1, sl].bc([3, 256]))
            ps = Bk[nb % 2]
            for kc in range(8):
                fw.matmul(ps[0:3, 0:256], cS3[:, kc, :], wt[:, kc, :], start=(kc == 0), stop=(kc == 7))
            fw.tt(mv[0:3, sl], ps[0:3, 0:256], bm[0:3, :], ALU.add)
            yield 5.0
        fw.stt(mv[0:3, D:2 * D], mv[0:3, D:2 * D], 1.0, g1b[0:3, :], ALU.add, ALU.mult)
        fw.stt(mv[0:3, 4 * D:5 * D], mv[0:3, 4 * D:5 * D], 1.0, g2b[0:3, :], ALU.add, ALU.mult)
        fw.dma(MODV[l], mv[0:3, :])

    def modrow(l, r, idx):
        return MODV[l][r:r + 1, idx * D:(idx + 1) * D].bc([128, D])

    def load_consts_ident(A_=None):
        idb = (A_ or ar).alloc(128, BF16)
        fw.dma(idb.ap(), k_ident_bf.ap())
        return idb

    def stage_A(b, l):
        ar.reset()
        idb = load_consts_ident()
        win = ar.alloc(8 * 1792, BF16)
        win3 = win.v("p (k n) -> p k n", k=8)
        fw.dma(win3, WIN[l])
        A1 = ar.alloc(D); B1 = ar.alloc(D); A1c = ar.alloc(D); B1c = ar.alloc(D)
        fw.dma(A1.ap(), modrow(l, b, 1)); fw.dma(B1.ap(), modrow(l, b, 0))
        fw.dma(A1c.ap(), modrow(l, 2, 1)); fw.dma(B1c.ap(), modrow(l, 2, 0))
        G = ar.alloc(14 * 64)
        G3 = G.v("p (h d) -> p h d", d=64)
        gsrc = [att_q_gain] * 4 + [att_k_gain] * 2 + [na_q_gain] * 4 + [na_k_gain] * 4
        for h in range(14):
            fw.dma(G3[:, h, :], gsrc[h][l:l + 1, :].bc([128, 64]))
        COS = ar.alloc(16 * 64); SIN = ar.alloc(16 * 64)
        COS3 = COS.v("p (t d) -> p t d", d=64); SIN3 = SIN.v("p (t d) -> p t d", d=64)
        fw.dma(COS3, k_rope_cos.v("t p d -> p t d"))
        fw.dma(SIN3, k_rope_sin.v("t p d -> p t d"))
        two = lambda n, dt=F32: [ar.alloc(n, dt) for _ in range(2)]
        xs = two(D); sqj = ar.alloc(D, BF16)
        ss = two(4); sd = two(4); rstd = two(4)
        tt_ = two(D); hb = two(D, BF16); hT = two(D, BF16)
        qks = two(896); sq2 = two(896); ssh = two(16); sdh = two(16); rsh = two(16)
        qkn = two(896); vsw = two(384); m1 = two(384); m2 = two(384)
        qkb = two(896, BF16); qkT = two(896, BF16)
        vaug = two(6 * 65, BF16); finb = two(256, BF16); uTb = two(256, BF16)
        for vb in vaug:
            fw.memset(vb.v("p (h e) -> p h e", e=65)[:, :, 64:65], 1.0)

        def load(i):
            if i >= NT:
                return
            if l == 0:
                src = ctx[b][i * 128:(i + 1) * 128, :] if i < 2 else x[b][(i - 2) * 128:(i - 1) * 128, :]
            else:
                src = XR[b][i * 128:(i + 1) * 128, :]
            fw.dma(xs[i % 2].ap(), src)

        def frontA(i):
            j = i % 2
            isctx = i < 2
            fw.act(sqj.ap(), xs[j].ap(), AF.Square, accum_out=ss[j][:, 0:1])
            rstd_from(ss[j][:, 0:1], D, rstd[j][:, 0:1], sd[j][:, 0:1])
            fw.stt(tt_[j].ap(), xs[j].ap(), rstd[j][:, 0:1], (A1c if isctx else A1).ap(), ALU.mult, ALU.mult)
            fw.tt(hb[j].ap(), tt_[j].ap(), (B1c if isctx else B1).ap(), ALU.add, eng="pool")

        def frontB(i):
            j = i % 2
            pT = bank_bf(4 + 2 * j)
            for k in range(8):
                fw.transpose(pT[:, k * 128:(k + 1) * 128], hb[j][:, k * 128:(k + 1) * 128], idb.ap())
            fw.copy(hT[j].ap(), pT[:, 0:D], eng="act")
            hT3 = hT[j].v("p (k n) -> p k n", k=8)
            for nb in range(3):
                for k in range(8):
                    fw.matmul(banks[nb].ap(), hT3[:, k, :], win3[:, k, nb * 512:(nb + 1) * 512], start=(k == 0), stop=(k == 7))
            for m in range(2):
                for k in range(8):
                    fw.matmul(banks[3][:, m * 128:(m + 1) * 128], win3[:, k, 1536 + m * 128:1536 + (m + 1) * 128], hT3[:, k, :],
                              start=(k == 0), stop=(k == 7))

        def mid(i):
            j = i % 2
            va3 = vaug[j].v("p (h e) -> p h e", e=65)
            fw.copy(qks[j][:, 0:384], banks[0][:, 0:384], eng="act")
            fw.copy(va3[:, 0:2, 0:64], banks[0][:, 384:512].rearrange("p (h d) -> p h d", d=64), eng="act")
            fw.copy(qks[j][:, 384:896], banks[1][:, 0:512], eng="dve")
            fw.copy(va3[:, 2:6, 0:64], banks[2][:, 0:256].rearrange("p (h d) -> p h d", d=64), eng="act")
            fw.copy(finb[j].ap(), banks[2][:, 256:512], eng="act")
            fw.copy(uTb[j].ap(), banks[3][:, 0:256], eng="dve")
            fw.dma(VV[b][i * 128:(i + 1) * 128, :], vaug[j].ap())
            fw.dma(FIN[b][i * 128:(i + 1) * 128, :], finb[j].ap())
            fw.dma(UT[b].rearrange("(m p) n -> p m n", p=128)[:, :, i * 128:(i + 1) * 128],
                   uTb[j].v("p (m n) -> p m n", m=2))

        def backE(i):
            j = i % 2
            isctx = i < 2
            fw.act(sq2[j].ap(), qks[j].ap(), AF.Square)
            fw.reduce(ssh[j][:, 0:14], sq2[j].v("p (h d) -> p h d", d=64), ALU.add)
            rstd_from(ssh[j][:, 0:14], 64, rsh[j][:, 0:14], sdh[j][:, 0:14])
            qkn3 = qkn[j].v("p (h d) -> p h d", d=64)
            fw.tt(qkn3, qks[j].v("p (h d) -> p h d", d=64), rsh[j][:, 0:14].unsq(2).bc([128, 14, 64]), ALU.mult)
            fw.tt(qkn[j].ap(), qkn[j].ap(), G.ap(), ALU.mult, eng="pool")
            if isctx:
                fw.copy(qkb[j][:, 0:384], qkn[j][:, 0:384], eng="pool")
            else:
                ti = i - 2
                fw.copy(vsw[j].v("p (a two x) -> p a two x", two=2, x=16),
                        qkn[j][:, 0:384].rearrange("p (a two x) -> p a two x", two=2, x=16)[:, :, ::-1, :], eng="pool")
                fw.tt(m1[j].v("p (h d) -> p h d", d=64), qkn3[:, 0:6, :], COS3[:, ti, :].unsq(1).bc([128, 6, 64]), ALU.mult)
                fw.tt(m2[j].v("p (h d) -> p h d", d=64), vsw[j].v("p (h d) -> p h d", d=64),
                      SIN3[:, ti, :].unsq(1).bc([128, 6, 64]), ALU.mult, eng="pool")
                fw.tt(qkb[j][:, 0:384], m1[j].ap(), m2[j].ap(), ALU.add)
            fw.copy(qkb[j][:, 384:896], qkn[j][:, 384:896], eng="pool")

        def backP(i):
            j = i % 2
            pq = bank_bf(5 + 2 * j)
            for jj in range(7):
                fw.transpose(pq[:, jj * 128:(jj + 1) * 128], qkb[j][:, jj * 128:(jj + 1) * 128], idb.ap())
            fw.copy(qkT[j].ap(), pq[:, 0:896], eng="act")
            fw.dma(QK[b].rearrange("(j p) n -> p j n", p=128)[:, :, i * 128:(i + 1) * 128],
                   qkT[j].v("p (j n) -> p j n", j=7))

        load(0)
        load(1)
        frontA(0)
        for i in range(NT):
            if i >= 1:
                backE(i - 1)
            frontB(i)
            load(i + 2)
            if i + 1 < NT:
                frontA(i + 1)
            if i >= 1:
                backP(i - 1)
            mid(i)
        backE(NT - 1)
        backP(NT - 1)

    def attn_block(QT3, KT3, V3, qhead, khead, vcol, qcol0, nq, ktiles, PT, psS, psO, st):
        nsub = nq // 128
        nk = len(ktiles)

        def pv(ki, kt, pt):
            for sub in range(nsub):
                fw.matmul(psO[:, sub * 65:(sub + 1) * 65], pt[:, sub * 128:(sub + 1) * 128], V3[:, kt, vcol:vcol + 65],
                          start=(ki == 0 and sub == 0), stop=(ki == nk - 1 and sub == nsub - 1))

        pend = None
        for ki, kt in enumerate(ktiles):
            pS = psS[st["s"] % len(psS)]
            st["s"] += 1
            pt = PT[st["p"] % len(PT)]
            st["p"] += 1
            fw.matmul(pS[:, 0:nq], KT3[0:64, khead, kt * 128:(kt + 1) * 128], QT3[0:64, qhead, qcol0:qcol0 + nq])
            fw.act(pt[:, 0:nq], pS[:, 0:nq], AF.Exp, scale=0.125)
            if pend is not None:
                pv(*pend)
            pend = (ki, kt, pt)
            yield 1.0
        pv(*pend)

    def attn_finish(psO, nsub, rden, ob):
        po3 = psO[:, 0:nsub * 65].rearrange("p (s e) -> p s e", e=65)
        fw.recip(rden[:, 0:nsub], po3[:, :, 64:65].rearrange("p s e -> p (s e)"))
        fw.tt(ob.v("p (s d) -> p s d", d=64)[:, 0:nsub, :], po3[:, :, 0:64],
              rden[:, 0:nsub].unsq(2).bc([128, nsub, 64]), ALU.mult)

    def stage_GQA(b, l, A_=None, bk=(0, 1, 2, 3)):
        if A_ is None:
            ar.reset()
            A_ = ar
        Bk = [banks[i_] for i_ in bk]
        QT = A_.alloc(4 * T, BF16); KT = A_.alloc(2 * T, BF16); VA = A_.alloc(NT * 130, BF16)
        QT3 = QT.v("p (h n) -> p h n", h=4); KT3 = KT.v("p (h n) -> p h n", h=2)
        VA3 = VA.v("p (t c) -> p t c", c=130)
        fw.dma(QT3[0:64], QK[b][0:256, :].rearrange("(h d) n -> d h n", d=64))
        fw.dma(KT3[0:64], QK[b][256:384, :].rearrange("(h d) n -> d h n", d=64))
        fw.dma(VA3, VV[b].rearrange("(t p) c -> p t c", p=128)[:, :, 0:130])
        PT = [A_.alloc(512, BF16) for _ in range(3)]
        rden = A_.alloc(4)
        obs = [A_.alloc(256, BF16) for _ in range(2)]
        st = {"s": 0, "p": 0}
        psS = [Bk[0], Bk[1]]
        cnt = 0
        O3 = OO[b].rearrange("(t p) c -> p t c", p=128)
        blocks = [(256 + qb * 512, 512, list(range(NT)), 2 + qb * 4) for qb in range(4)]
        if l == 0:
            blocks.append((0, 256, [0, 1], 0))
        for h in range(4):
            kv = h // 2
            for (qc0, nq, kts, t0) in blocks:
                psO = Bk[2 + cnt % 2]
                ob = obs[cnt % 2]
                cnt += 1
                yield from attn_block(QT3, KT3, VA3, h, kv, kv * 65, qc0, nq, kts, PT, psS, psO, st)
                nsub = nq // 128
                attn_finish(psO, nsub, rden, ob)
                fw.dma(O3[:, t0:t0 + nsub, h * 64:(h + 1) * 64], ob.v("p (s d) -> p s d", d=64)[:, 0:nsub, :])

    NA_PLAN, NA_PATS = na_plan()

    def stage_NAB(l, A_=None, bk=(0, 1, 2, 3)):
        if A_ is None:
            ar.reset()
            A_ = ar
        Bk = [banks[i_] for i_ in bk]
        rb = A_.alloc(31)
        fw.dma(rb[0:60, :], na_rel_bias[l])
        pad = A_.alloc(128)
        fw.memset(pad.ap(), BIG)
        fw.ts(pad[0:60, 48:79], rb[0:60, ::-1], 8.0, None, ALU.mult)
        padt = PADT[l].ap.tensor
        base = PADT[l].ap.offset
        fw.dma(View(PADT, bass.AP(tensor=padt, offset=base, ap=[[8192, 61], [127, 64], [1, 127]])),
               pad[0:61, 0:127].unsq(1).bc([61, 64, 127]))
        cm = A_.alloc(512)
        fw.dma(cm.ap(), k_colmask.ap())
        stg = [A_.alloc(512) for _ in range(2)]
        bt = [A_.alloc(512, BF16) for _ in range(2)]
        for bid, pat in enumerate(NA_PATS):
            s_ = stg[bid % 2]
            for krl in range(2):
                for qrl in range(2):
                    dri = pat[krl * 2 + qrl]
                    for h in range(4):
                        row = 60 if dri is None else h * 15 + dri
                        src = View(PADT, bass.AP(tensor=padt, offset=base + row * 8192 + 63, ap=[[126, 64], [1, 64]]))
                        fw.dma(s_[krl * 64:(krl + 1) * 64, h * 128 + qrl * 64:h * 128 + (qrl + 1) * 64], src)
            fw.tt(bt[bid % 2].ap(), s_.ap(), cm.ap(), ALU.add)
            fw.dma(BIAS[l][bid], bt[bid % 2].ap())
            yield 9.0

    def stage_NA(b, l, A_=None, bk=(0, 1, 2, 3)):
        if A_ is None:
            ar.reset()
            A_ = ar
        Bk = [banks[i_] for i_ in bk]
        idb = load_consts_ident(A_)
        QT = A_.alloc(4 * T, BF16); KT = A_.alloc(4 * T, BF16); VD = A_.alloc(NT * 260, BF16)
        QT3 = QT.v("p (h n) -> p h n", h=4); KT3 = KT.v("p (h n) -> p h n", h=4)
        VD3 = VD.v("p (t c) -> p t c", c=260)
        fw.dma(QT3[0:64], QK[b][384:640, :].rearrange("(h d) n -> d h n", d=64))
        fw.dma(KT3[0:64], QK[b][640:896, :].rearrange("(h d) n -> d h n", d=64))
        fw.dma(VD3, VV[b].rearrange("(t p) c -> p t c", p=128)[:, :, 130:390])
        nbias = len(NA_PATS)
        BT = A_.alloc(nbias * 512, BF16)
        BT3 = BT.v("p (i n) -> p i n", n=512)
        fw.dma(BT3, BIAS[l][0:nbias].rearrange("i p n -> p i n"))
        PT = [A_.alloc(512, BF16) for _ in range(3)]
        rden = A_.alloc(4)
        obs = [A_.alloc(256, BF16) for _ in range(2)]
        O3 = OO[b].rearrange("(t p) c -> p t c", p=128)
        si = 0
        for t in range(16):
            qc0 = 256 + t * 128
            klist = [(2 + u, bid) for (u, bid) in NA_PLAN[t]] + [(0, None), (1, None)]
            psO = Bk[2 + t % 2]
            ob = obs[t % 2]
            nk = len(klist)
            def pv(ki, kt, pt, psO=psO, nk=nk):
                for h in range(4):
                    fw.matmul(psO[:, h * 65:(h + 1) * 65], pt[:, h * 128:(h + 1) * 128], VD3[:, kt, h * 65:(h + 1) * 65],
                              start=(ki == 0 and h == 0), stop=(ki == nk - 1 and h == 3))

            pend = None
            for ki, (kt, bid) in enumerate(klist):
                pS = Bk[si % 2]
                pt = PT[si % 3]
                si += 1
                for h in range(4):
                    fw.matmul(pS[:, h * 128:(h + 1) * 128], KT3[0:64, h, kt * 128:(kt + 1) * 128], QT3[0:64, h, qc0:qc0 + 128],
                              start=(h == 0), stop=(h == 3 and bid is None))
                if bid is not None:
                    fw.matmul(pS.ap(), idb.ap(), BT3[:, bid, :], start=False, stop=True)
                fw.act(pt.ap(), pS.ap(), AF.Exp, scale=0.125)
                if pend is not None:
                    pv(*pend)
                pend = (ki, kt, pt)
                yield 3.0
            pv(*pend)
            attn_finish(psO, 4, rden, ob)
            fw.dma(OO[b][(2 + t) * 128:(3 + t) * 128, 256:512], ob.ap())
        if l == 0:
            st = {"s": 0, "p": 0}
            for h in range(4):
                psO = Bk[2 + h % 2]
                ob = obs[h % 2]
                yield from attn_block(QT3, KT3, VD3, h, h, h * 65, 0, 256, [0, 1], PT, [Bk[0], Bk[1]], psO, st)
                attn_finish(psO, 2, rden, ob)
                fw.dma(O3[:, 0:2, 256 + h * 64:256 + (h + 1) * 64], ob.v("p (s d) -> p s d", d=64)[:, 0:2, :])

    def stage_FW(l, A_=None, bk=(0, 1, 2, 3)):
        if A_ is None:
            ar.reset()
            A_ = ar
        Bk = [banks[i_] for i_ in bk]
        wf = A_.alloc(2 * 256)
        wf3 = wf.v("p (k n) -> p k n", k=2)
        fw.dma(wf3, w_fourier[l].rearrange("(k p) n -> p k n", p=128))
        for ci, ktab in enumerate((k_cblk, k_sblk)):
            cb = A_.alloc(2 * 256)
            cb3 = cb.v("p (k n) -> p k n", k=2)
            fw.dma(cb3, ktab.v("(k p) n -> p k n", p=128))
            wo = A_.alloc(2 * 256, BF16)
            wo3 = wo.v("p (k n) -> p k n", k=2)
            for fo in range(2):
                ps = Bk[(ci * 2 + fo) % 4]
                for k in range(2):
                    fw.matmul(ps[:, 0:256], cb3[:, k, fo * 128:(fo + 1) * 128], wf3[:, k, :], start=(k == 0), stop=(k == 1))
                fw.act(wo3[:, fo, :], ps[:, 0:256], AF.Copy, scale=(1.0 if ci == 0 else -1.0))
            fw.dma(WCS[l][ci].rearrange("(k p) n -> p k n", p=128), wo3)
            yield 15.0

    def stage_F(b, l, A_=None, bk=(0, 1, 2, 3)):
        if A_ is None:
            ar.reset()
            A_ = ar
        Bk = [banks[i_] for i_ in bk]
        fin = A_.alloc(NT * 256, BF16)
        fin3 = fin.v("p (t f) -> p t f", f=256)
        fw.dma(fin3, FIN[b].rearrange("(t p) f -> p t f", p=128))
        wcs = A_.alloc(2 * 2 * 256, BF16)
        wcs4 = wcs.v("p (c k n) -> p c k n", c=2, k=2)
        for ci in range(2):
            fw.dma(wcs4[:, ci], WCS[l][ci].rearrange("(k p) n -> p k n", p=128))
        tabs = [A_.alloc(512, BF16) for _ in range(6)]
        G = [A_.alloc(4 * 512, BF16) for _ in range(2)]
        ofb = [A_.alloc(256, BF16) for _ in range(2)]
        ti = 0
        oi = 0

        def second_stage(Gv, nk, tile0, scale):
            nonlocal oi
            for kt in range(nk // 128):
                ps = Bk[2 + oi % 2]
                n_ = 0
                for ci in range(2):
                    for fc in range(2):
                        fw.matmul(ps[:, 0:256], Gv[:, ci, fc, kt * 128:(kt + 1) * 128], wcs4[:, ci, fc, :],
                                  start=(n_ == 0), stop=(n_ == 3))
                        n_ += 1
                o_ = ofb[oi % 2]
                oi += 1
                fw.act(o_.ap(), ps[:, 0:256], AF.Copy, scale=scale)
                fw.dma(OO[b][(tile0 + kt) * 128:(tile0 + kt + 1) * 128, 512:768], o_.ap())

        for kb in range(4):
            for ncix in range(16):
                for ci in range(2):
                    tb = tabs[ti % 6]
                    ti += 1
                    fw.dma(tb.ap(), k_dft[ci][ncix * 128:(ncix + 1) * 128, kb * 512:(kb + 1) * 512])
                    for fc in range(2):
                        fw.matmul(Bk[ci * 2 + fc].ap(), fin3[:, 2 + ncix, fc * 128:(fc + 1) * 128], tb.ap(),
                                  start=(ncix == 0), stop=(ncix == 15))
                yield 2.0
            Gb = G[kb % 2]
            Gv = Gb.v("p (c f n) -> p c f n", c=2, f=2)
            for ci in range(2):
                for fc in range(2):
                    if (ci + fc) % 2 == 0:
                        fw.copy(Gv[:, ci, fc, :], Bk[ci * 2 + fc].ap(), eng="act")
                    else:
                        fw.copy(Gv[:, ci, fc, :], Bk[ci * 2 + fc].ap(), eng="dve")
            second_stage(Gv, 512, 2 + kb * 4, 1.0 / math.sqrt(SEQ * 64.0))
        if l == 0:
            for ncix in range(2):
                for ci in range(2):
                    tb = tabs[ti % 6]
                    ti += 1
                    fw.dma(tb[:, 0:256], k_dft[ci][ncix * 1024:(ncix + 1) * 1024:8, 0:256])
                    for fc in range(2):
                        fw.matmul(Bk[ci * 2 + fc][:, 0:256], fin3[:, ncix, fc * 128:(fc + 1) * 128], tb[:, 0:256],
                                  start=(ncix == 0), stop=(ncix == 1))
            Gb = G[0]
            Gv = Gb.v("p (c f n) -> p c f n", c=2, f=2)
            for ci in range(2):
                for fc in range(2):
                    fw.copy(Gv[:, ci, fc, 0:256], Bk[ci * 2 + fc][:, 0:256], eng=("act" if (ci + fc) % 2 == 0 else "dve"))
            second_stage(Gv, 256, 0, 1.0 / math.sqrt(CTXL * 64.0))

    def stage_C(b, l):
        ar.reset()
        idb = load_consts_ident()
        wo = ar.alloc(8 * D, BF16)
        wo3 = wo.v("p (k n) -> p k n", k=8)
        fw.dma(wo3, WOUT[l])
        gg = ar.alloc(D)
        fw.dma(gg.ap(), g_group[l:l + 1, :].bc([128, D]))
        GA1 = ar.alloc(D); A2 = ar.alloc(D); B2 = ar.alloc(D); GA2 = ar.alloc(D)
        hid = ar.alloc(NF * 512, BF16)
        hid3 = hid.v("p (f n) -> p f n", f=NF)
        two = lambda n, dt=F32: [ar.alloc(n, dt) for _ in range(2)]
        h2T = two(8 * 512, BF16)
        x1 = two(4 * D)
        wt1 = [ar.alloc(8 * 128, BF16) for _ in range(3)]
        wt3 = [ar.alloc(8 * 128, BF16) for _ in range(3)]
        w2t = [ar.alloc(D, BF16) for _ in range(3)]
        xs = two(D); ob = two(D, BF16)
        sq = ar.alloc(D)
        ss4 = two(4); sd4 = two(4); rs4 = two(4)
        ss = two(4); sd = two(4); rstd = two(4)
        onb = two(D, BF16); onT = two(D, BF16)
        tmpA = ar.alloc(D); tmpB = ar.alloc(D); h2b = two(D, BF16)
        sqj = ar.alloc(D, BF16)
        tmpC = ar.alloc(D)
        sa = two(512)
        xo = two(D)
        groups = [[2 + g * 4 + s_ for s_ in range(4)] for g in range(4)]
        if l == 0:
            groups = [[0, 1]] + groups
        rows = [(2 if grp[0] < 2 else b) for grp in groups]
        st = {"wi": 0, "w2i": 0, "oi": 0, "pre": 0}

        def h2T3(gp):
            return h2T[gp].v("p (k n) -> p k n", k=8)

        def x13(gp):
            return x1[gp].v("p (s n) -> p s n", s=4)

        def P1a(i, j):
            if l == 0:
                src = ctx[b][i * 128:(i + 1) * 128, :] if i < 2 else x[b][(i - 2) * 128:(i - 1) * 128, :]
            else:
                src = XR[b][i * 128:(i + 1) * 128, :]
            fw.dma(xs[j].ap(), src)
            fw.dma(ob[j].ap(), OO[b][i * 128:(i + 1) * 128, :])
            fw.act(sq.ap(), ob[j].ap(), AF.Square)
            fw.reduce(ss4[j][:, 0:4], sq.v("p (g d) -> p g d", g=4), ALU.add)
            rstd_from(ss4[j][:, 0:4], 256, rs4[j][:, 0:4], sd4[j][:, 0:4])
            for g_ in range(4):
                sl = slice(g_ * 256, (g_ + 1) * 256)
                fw.stt(onb[j][:, sl], ob[j][:, sl], rs4[j][:, g_:g_ + 1], gg[:, sl], ALU.mult, ALU.mult)

        def P1b(i, j):
            pT = bank_bf(6)
            for k in range(8):
                fw.transpose(pT[:, k * 128:(k + 1) * 128], onb[j][:, k * 128:(k + 1) * 128], idb.ap())
            fw.copy(onT[j].ap(), pT[:, 0:D], eng="act")
            onT3 = onT[j].v("p (k n) -> p k n", k=8)
            for nb in range(2):
                for k in range(8):
                    fw.matmul(banks[4 + nb].ap(), onT3[:, k, :], wo3[:, k, nb * 512:(nb + 1) * 512], start=(k == 0), stop=(k == 7))

        def P2a(i, j, gp, s_):
            for nb in range(2):
                sl = slice(nb * 512, (nb + 1) * 512)
                fw.tt(tmpA[:, sl], banks[4 + nb].ap(), GA1[:, sl], ALU.mult)
            fw.tt(x13(gp)[:, s_, :], tmpA.ap(), xs[j].ap(), ALU.add, eng="pool")
            fw.act(sqj.ap(), x13(gp)[:, s_, :], AF.Square, accum_out=ss[j][:, 0:1])
            rstd_from(ss[j][:, 0:1], D, rstd[j][:, 0:1], sd[j][:, 0:1])
            fw.stt(tmpB.ap(), x13(gp)[:, s_, :], rstd[j][:, 0:1], A2.ap(), ALU.mult, ALU.mult)
            fw.tt(h2b[j].ap(), tmpB.ap(), B2.ap(), ALU.add, eng="pool")

        def P2b(i, j, gp, s_):
            pT2 = bank_bf(7)
            for k in range(8):
                fw.transpose(pT2[:, k * 128:(k + 1) * 128], h2b[j][:, k * 128:(k + 1) * 128], idb.ap())
            fw.copy(h2T3(gp)[:, :, s_ * 128:(s_ + 1) * 128], pT2[:, 0:D].rearrange("p (k n) -> p k n", k=8), eng="act")

        def pre_slots(grp, gp):
            n = len(grp)
            slots = []
            for k in range(n + 2):
                def slot(k=k):
                    if 0 <= k < n:
                        P1a(grp[k], k % 2)
                    if 0 <= k - 2 < n:
                        P2b(grp[k - 2], (k - 2) % 2, gp, k - 2)
                    if 0 <= k - 1 < n:
                        P1b(grp[k - 1], (k - 1) % 2)
                        P2a(grp[k - 1], (k - 1) % 2, gp, k - 1)
                slots.append(slot)
            return slots

        def load_mods(row):
            fw.dma(GA1.ap(), modrow(l, row, 2)); fw.dma(A2.ap(), modrow(l, row, 4))
            fw.dma(B2.ap(), modrow(l, row, 3)); fw.dma(GA2.ap(), modrow(l, row, 5))

        def issue_w13(f):
            a_ = wt1[st["wi"] % 3]; b_ = wt3[st["wi"] % 3]
            st["wi"] += 1
            fw.dma(a_.ap(), W1[l][f].rearrange("p k n -> p (k n)"))
            fw.dma(b_.ap(), W3[l][f].rearrange("p k n -> p (k n)"))
            return a_, b_

        pre_done = False
        preissued = []
        for gi_, grp in enumerate(groups):
            gp = gi_ % 2
            nt_ = len(grp)
            ntk = nt_ * 128
            if not pre_done:
                load_mods(rows[gi_])
                for slot in pre_slots(grp, gp):
                    slot()
            nxt = gi_ + 1
            can_pipe = nxt < len(groups) and rows[nxt] == rows[gi_]
            nslots = pre_slots(groups[nxt], nxt % 2) if can_pipe else []
            for f in range(NF):
                if preissued:
                    a_, b_ = preissued.pop(0)
                else:
                    a_, b_ = issue_w13(f)
                a3 = a_.v("p (k n) -> p k n", k=8); b3_ = b_.v("p (k n) -> p k n", k=8)
                pa = banks[(f % 2) * 2]; pb = banks[1 + (f % 2) * 2]
                for k in range(8):
                    fw.matmul(pa[:, 0:ntk], a3[:, k, :], h2T3(gp)[:, k, 0:ntk], start=(k == 0), stop=(k == 7))
                for k in range(8):
                    fw.matmul(pb[:, 0:ntk], b3_[:, k, :], h2T3(gp)[:, k, 0:ntk], start=(k == 0), stop=(k == 7))
                sj = sa[f % 2]
                fw.act(sj[:, 0:ntk], pa[:, 0:ntk], AF.Silu)
                fw.tt(hid3[:, f, 0:ntk], sj[:, 0:ntk], pb[:, 0:ntk], ALU.mult)
                if nslots and f % 3 == 1:
                    nslots.pop(0)()
            while nslots:
                nslots.pop(0)()
            pre_done = can_pipe
            for f in range(NF):
                w_ = w2t[st["w2i"] % 3]
                st["w2i"] += 1
                fw.dma(w_.ap(), W2[l][f * 128:(f + 1) * 128, :])
                for s_ in range(nt_):
                    for nb in range(2):
                        fw.matmul(banks[s_ * 2 + nb].ap(), hid3[:, f, s_ * 128:(s_ + 1) * 128], w_[:, nb * 512:(nb + 1) * 512],
                                  start=(f == 0), stop=(f == NF - 1))
            if nxt < len(groups):
                preissued = [issue_w13(f) for f in range(3)]
            for s_, i in enumerate(grp):
                xoj = xo[st["oi"] % 2]
                st["oi"] += 1
                for nb in range(2):
                    sl = slice(nb * 512, (nb + 1) * 512)
                    fw.tt(tmpC[:, sl], banks[s_ * 2 + nb].ap(), GA2[:, sl], ALU.mult)
                fw.tt(xoj.ap(), tmpC.ap(), x13(gp)[:, s_, :], ALU.add, eng="pool")
                if l == 0:
                    fw.dma(XR[b][i * 128:(i + 1) * 128, :], xoj.ap())
                else:
                    fw.dma(out[b][(i - 2) * 128:(i - 1) * 128, :], xoj.ap())

    TWO_PI = 2.0 * math.pi
    CW1 = 6.28125
    CW2 = TWO_PI - 6.28125
    MAGIC = 12582912.0
    PI_SAFE = 3.141592

    def range_reduce(out, in_, kbuf, shift=0.0, eng="dve"):
        if shift != 0.0:
            fw.ts(out, in_, shift, None, ALU.add, eng=eng)
            src = out
        else:
            src = in_
        fw.ts(kbuf, src, 1.0 / TWO_PI, MAGIC, ALU.mult, ALU.add, eng=eng)
        fw.ts(kbuf, kbuf, -MAGIC, None, ALU.add, eng=eng)
        fw.stt(out, kbuf, -CW1, src, ALU.mult, ALU.add, eng=eng)
        fw.stt(out, kbuf, -CW2, out, ALU.mult, ALU.add, eng=eng)
        fw.ts(out, out, -PI_SAFE, PI_SAFE, ALU.max, ALU.min, eng=eng)

    def stage_S5P(l, A_=None, bk=(0, 1, 2, 3)):
        if A_ is None:
            ar.reset()
            A_ = ar
        Bk = [banks[i_] for i_ in bk]
        idf = A_.alloc(128)
        fw.dma(idf.ap(), k_ident_f.ap())
        tau = A_.alloc(16)
        fw.dma(tau.ap(), k_tau.ap())
        nidx = A_.alloc(NCH)
        fw.dma(nidx.ap(), k_nidx.ap())
        msk = A_.alloc(128)
        fw.dma(msk.ap(), k_s5mask.ap())
        mark = A_.off
        for d_ in range(2):
            A_.off = mark
            if d_ == 1:
                fw.barrier()
            lamre = A_.alloc(8); lamim = A_.alloc(8); dt = A_.alloc(8)
            for gi in range(2):
                sl = slice(64 * gi, 64 * gi + 64)
                fw.dma(lamre[sl, :], ssm_lam_re[l][d_].rearrange("(q two) p -> two p q", two=2)[gi], allow_slow_non_contiguous=True)
                fw.dma(lamim[sl, :], ssm_lam_im[l][d_].rearrange("(q two) p -> two p q", two=2)[gi], allow_slow_non_contiguous=True)
                fw.dma(dt[sl, :], ssm_log_dt[l][d_:d_ + 1, :].rearrange("o (q two) -> two o q", two=2)[gi].bc([64, 8]),
                       allow_slow_non_contiguous=True)
            fw.act(dt.ap(), dt.ap(), AF.Exp)
            zr = A_.alloc(8); zi = A_.alloc(8)
            fw.tt(zr.ap(), lamre.ap(), dt.ap(), ALU.mult)
            fw.tt(zi.ap(), lamim.ap(), dt.ap(), ALU.mult)
            PZ = A_.alloc(128); PEx = A_.alloc(128); KB = A_.alloc(128); RS = A_.alloc(128); RC = A_.alloc(128)
            Are = A_.alloc(128); Aim = A_.alloc(128)
            v3 = lambda bf: bf.v("p (t q) -> p t q", q=8)
            fw.tt(v3(PZ), tau.ap().unsq(2).bc([128, 16, 8]), zr.ap().unsq(1).bc([128, 16, 8]), ALU.mult)
            fw.act(PEx.ap(), PZ.ap(), AF.Exp)
            fw.tt(v3(PZ), tau.ap().unsq(2).bc([128, 16, 8]), zi.ap().unsq(1).bc([128, 16, 8]), ALU.mult)
            range_reduce(RS.ap(), PZ.ap(), KB.ap())
            range_reduce(RC.ap(), PZ.ap(), KB.ap(), shift=0.5 * math.pi)
            fw.act(RS.ap(), RS.ap(), AF.Sin)
            fw.act(RC.ap(), RC.ap(), AF.Sin)
            fw.tt(Are.ap(), PEx.ap(), RC.ap(), ALU.mult)
            fw.tt(Aim.ap(), PEx.ap(), RS.ap(), ALU.mult)
            Are3 = v3(Are); Aim3 = v3(Aim)
            fw.dma(S5D[l][d_], v3(PEx)[:, 15, :])
            phi = A_.alloc(8); kb8 = A_.alloc(8)
            fw.ts(phi.ap(), zi.ap(), 8.0, None, ALU.mult)
            range_reduce(phi.ap(), phi.ap(), kb8.ap())
            ANG = A_.alloc(8 * NCH); KB2 = A_.alloc(8 * NCH); R2 = A_.alloc(8 * NCH)
            a3 = lambda bf: bf.v("p (q n) -> p q n", q=8)
            fw.tt(a3(ANG), nidx.ap().unsq(1).bc([128, 8, NCH]), phi.ap().unsq(2).bc([128, 8, NCH]), ALU.mult)
            range_reduce(R2.ap(), ANG.ap(), KB2.ap())
            fw.act(R2.ap(), R2.ap(), AF.Sin)
            fw.dma(S5SIN[l][d_], a3(R2))
            R3 = R2
            range_reduce(R3.ap(), ANG.ap(), KB2.ap(), shift=0.5 * math.pi)
            fw.act(R3.ap(), R3.ap(), AF.Sin)
            fw.dma(S5COS[l][d_], a3(R3))
            yield 40.0
            nr = A_.alloc(8); den = A_.alloc(8); t8a = A_.alloc(8); t8b = A_.alloc(8); cr = A_.alloc(8); ci = A_.alloc(8)
            fw.ts(nr.ap(), Are3[:, 8, :], -1.0, None, ALU.add)
            fw.tt(den.ap(), lamre.ap(), lamre.ap(), ALU.mult)
            fw.tt(t8a.ap(), lamim.ap(), lamim.ap(), ALU.mult)
            fw.tt(den.ap(), den.ap(), t8a.ap(), ALU.add)
            fw.recip(den.ap(), den.ap())
            fw.tt(t8a.ap(), nr.ap(), lamre.ap(), ALU.mult)
            fw.tt(t8b.ap(), Aim3[:, 8, :], lamim.ap(), ALU.mult)
            fw.tt(t8a.ap(), t8a.ap(), t8b.ap(), ALU.add)
            fw.tt(cr.ap(), t8a.ap(), den.ap(), ALU.mult)
            fw.tt(t8a.ap(), Aim3[:, 8, :], lamre.ap(), ALU.mult)
            fw.tt(t8b.ap(), nr.ap(), lamim.ap(), ALU.mult)
            fw.tt(t8a.ap(), t8a.ap(), t8b.ap(), ALU.subtract)
            fw.tt(ci.ap(), t8a.ap(), den.ap(), ALU.mult)
            bre = A_.alloc(128); bim = A_.alloc(128); Bre = A_.alloc(128); Bim = A_.alloc(128); tb1 = A_.alloc(128); tb2 = A_.alloc(128)
            b3 = lambda bf: bf.v("p (q c) -> p q c", q=8)
            for gi in range(2):
                sl = slice(64 * gi, 64 * gi + 64)
                fw.dma(b3(bre)[sl], ssm_b_re[l][d_].rearrange("(q two) p c -> two p q c", two=2)[gi])
                fw.dma(b3(bim)[sl], ssm_b_im[l][d_].rearrange("(q two) p c -> two p q c", two=2)[gi])
            crb = cr.ap().unsq(2).bc([128, 8, 16]); cib = ci.ap().unsq(2).bc([128, 8, 16])
            fw.tt(b3(tb1), b3(bre), crb, ALU.mult); fw.tt(b3(tb2), b3(bim), cib, ALU.mult)
            fw.tt(Bre.ap(), tb1.ap(), tb2.ap(), ALU.subtract)
            fw.tt(b3(tb1), b3(bim), crb, ALU.mult); fw.tt(b3(tb2), b3(bre), cib, ALU.mult)
            fw.tt(Bim.ap(), tb1.ap(), tb2.ap(), ALU.add)
            yield 25.0
            Cre = A_.alloc(128); Cim = A_.alloc(128)
            for (Cdst, csrc, bkx) in ((Cre, ssm_c_re, 0), (Cim, ssm_c_im, 1)):
                X = A_.alloc(128)
                for q in range(8):
                    for gi in range(2):
                        fw.dma(X[16 * q:16 * q + 16, 64 * gi:64 * gi + 64], csrc[l][d_][2 * q + gi])
                fw.transpose(Bk[bkx][:, 0:128], X.ap(), idf.ap())
                fw.copy(Cdst.ap(), Bk[bkx][:, 0:128], eng="act")
            BcR = A_.alloc(8 * 128); BcI = A_.alloc(8 * 128)
            CpR = A_.alloc(8 * 128); CpI = A_.alloc(8 * 128)
            CcR = A_.alloc(8 * 128, BF16); CcI = A_.alloc(8 * 128, BF16)
            u1 = A_.alloc(1024); u2 = A_.alloc(1024)
            f4 = lambda bf: bf.v("p (q s c) -> p q s c", q=8, s=8)
            pw = lambda A3, sl: A3[:, sl, :].rearrange("p s q -> p q s").unsq(3).bc([128, 8, 8, 16])
            qc = lambda bf: b3(bf).unsq(2).bc([128, 8, 8, 16])

            def cmul(dst_re, dst_im, Ar, Ai, Xr, Xi, neg_im=False):
                fw.tt(f4(u1), Ar, Xr, ALU.mult); fw.tt(f4(u2), Ai, Xi, ALU.mult, eng="pool")
                fw.tt(f4(dst_re), f4(u1), f4(u2), ALU.subtract)
                fw.tt(f4(u1), Ar, Xi, ALU.mult); fw.tt(f4(u2), Ai, Xr, ALU.mult, eng="pool")
                if neg_im:
                    fw.stt(f4(dst_im), f4(u1), -1.0, f4(u2), ALU.mult, ALU.subtract)
                else:
                    fw.tt(f4(dst_im), f4(u1), f4(u2), ALU.add)

            cmul(BcR, BcI, pw(Are3, slice(14, 6, -1)), pw(Aim3, slice(14, 6, -1)), qc(Bre), qc(Bim))
            cmul(CpR, CpI, pw(Are3, slice(0, 8)), pw(Aim3, slice(0, 8)), qc(Cre), qc(Cim), neg_im=True)
            cmul(CcR, CcI, pw(Are3, slice(8, 16)), pw(Aim3, slice(8, 16)), qc(Cre), qc(Cim), neg_im=True)
            fw.dma(S5C[l][d_].rearrange("q p r n -> p q r n")[:, :, 0, :], CcR.v("p (q n) -> p q n", q=8))
            fw.dma(S5C[l][d_].rearrange("q p r n -> p q r n")[:, :, 1, :], CcI.v("p (q n) -> p q n", q=8))
            Tb = [A_.alloc(128, BF16) for _ in range(2)]
            BT = [A_.alloc(256, BF16) for _ in range(2)]
            qv = lambda bf, q: bf.v("p (q n) -> p q n", q=8)[:, q, :]
            for q in range(8):
                for gi in range(2):
                    g = 2 * q + gi
                    sl = slice(64 * gi, 64 * gi + 64)
                    ps = Bk[gi]
                    fw.matmul(ps[:, 0:128], qv(BcR, q)[sl], qv(CpR, q)[sl], start=True, stop=False)
                    fw.matmul(ps[:, 0:128], qv(BcI, q)[sl], qv(CpI, q)[sl], start=False, stop=True)
                    tb_ = Tb[g % 2]
                    fw.tt(tb_.ap(), ps[:, 0:128], msk.ap(), ALU.mult)
                    fw.dma(S5T[l][d_][g], tb_.ap())
                bt_ = BT[q % 2]
                for ri, src in enumerate((BcR, BcI)):
                    ps = Bk[2 + ri]
                    fw.transpose(ps[:, 0:128], qv(src, q), idf.ap())
                    fw.copy(bt_.v("p (g r n) -> p g r n", g=2, r=2)[:, :, ri, :], ps[:, 0:128].rearrange("p (g n) -> p g n", g=2), eng="act")
                fw.dma(S5B[l][d_][2 * q:2 * q + 2].rearrange("g sc r n -> sc g (r n)"), bt_.v("p (g x) -> p g x", g=2))
                yield 8.0

    def stage_S5(b, l, A_=None, bk=(0, 1, 2, 3)):
        if A_ is None:
            ar.reset()
            A_ = ar
        Bk = [banks[i_] for i_ in bk]
        idb = load_consts_ident(A_)
        uT = A_.alloc(2 * T, BF16)
        uT3 = uT.v("p (m n) -> p m n", m=2)
        fw.dma(uT3, UT[b].rearrange("(m p) n -> p m n", p=128))
        Dcol = A_.alloc(2)
        fw.dma(Dcol.ap(), ssm_d[l].rearrange("(m p) -> p m", p=128), allow_slow_non_contiguous=True)
        bgl = A_.alloc(2)
        fw.dma(bgl.ap(), b_glu[l].rearrange("(m p) -> p m", p=128), allow_slow_non_contiguous=True)
        wgf = A_.alloc(512)
        fw.dma(wgf.v("p (k n) -> p k n", k=2), w_glu[l].rearrange("(k p) n -> p k n", p=128))
        wgb = A_.alloc(512, BF16)
        fw.copy(wgb.ap(), wgf.ap())
        wgb3 = wgb.v("p (k n) -> p k n", k=2)
        Tm = A_.alloc(32 * 128, BF16); T3 = Tm.v("p (g n) -> p g n", g=32)
        fw.dma(T3, S5T[l].rearrange("d g sc n -> sc (d g) n"))
        Bm = A_.alloc(32 * 128, BF16); B4 = Bm.v("p (g r n) -> p g r n", g=32, r=2)
        fw.dma(Bm.v("p (g x) -> p g x", g=32), S5B[l].rearrange("d g sc r n -> sc (d g) (r n)"))
        Cm = A_.alloc(16 * 256, BF16); C4 = Cm.v("p (g r n) -> p g r n", g=16, r=2)
        fw.dma(Cm.v("p (g x) -> p g x", g=16), S5C[l].rearrange("d q p r n -> p (d q) (r n)"))
        DEC = A_.alloc(16); DEC3 = DEC.v("p (d q) -> p d q", d=2)
        fw.dma(DEC3, S5D[l].rearrange("d p q -> p d q"))
        stg = [A_.alloc(2 * 8 * NCH, BF16) for _ in range(2)]
        s4 = lambda bf: bf.v("p (m s j) -> p m s j", m=2, s=8)
        yield 10.0
        for m in range(2):
            fw.copy(s4(stg[0])[:, m], uT3[:, m, :].rearrange("p (j s) -> p s j", s=8), eng="dve")
            fw.copy(s4(stg[1])[:, m, :, 0:32], uT3[:, m, 0:256][:, ::-1].rearrange("p (j s) -> p s j", s=8), eng="dve")
            fw.copy(s4(stg[1])[:, m, :, 32:NCH], uT3[:, m, 256:T][:, ::-1].rearrange("p (j s) -> p s j", s=8), eng="dve")
        U = A_.alloc(32 * NCH, BF16); U3 = U.v("p (g j) -> p g j", g=32)
        for d_ in range(2):
            for g in range(16):
                m, gl = g // 8, g % 8
                fw.dma(SCRU[d_][g].rearrange("s c j -> c s j"), s4(stg[d_])[16 * gl:16 * gl + 16, m])
        yield 20.0
        for d_ in range(2):
            for g in range(16):
                fw.dma(U3[:, d_ * 16 + g, :], SCRU[d_][g].rearrange("s c j -> (s c) j"))
            yield 15.0
        two = lambda n, dt=F32: [A_.alloc(n, dt) for _ in range(2)]
        cosb = two(NCH); sinb = two(NCH)
        Sre = two(NCH); Sim = two(NCH)
        t1 = two(NCH); t2 = two(NCH); t3 = two(NCH); t4 = two(NCH)
        wr_in = two(NCH); wi_in = two(NCH); wr = two(NCH); wi = two(NCH)
        Hre = two(NCH, BF16); Him = two(NCH, BF16)
        Yg = [A_.alloc(NCH, BF16) for _ in range(4)]

        def s5_front(it, d_, q):
            p = it % 2
            cb = cosb[p]; sb = sinb[p]; hr = Hre[p]; hi = Him[p]
            bS = (0, 1)
            fw.dma(cb.ap(), S5COS[l][d_][:, q, :])
            fw.dma(sb.ap(), S5SIN[l][d_][:, q, :])
            for gi in range(2):
                g = d_ * 16 + 2 * q + gi
                sl = slice(64 * gi, 64 * gi + 64)
                fw.matmul(Bk[bS[0]][sl, 0:NCH], B4[:, g, 0, :], U3[:, g, :])
                fw.matmul(Bk[bS[1]][sl, 0:NCH], B4[:, g, 1, :], U3[:, g, :])
            fw.copy(Sre[p].ap(), Bk[bS[0]][:, 0:NCH], eng="act")
            fw.copy(Sim[p].ap(), Bk[bS[1]][:, 0:NCH], eng="act")
            fw.tt(t1[p].ap(), Sre[p].ap(), cb.ap(), ALU.mult)
            fw.tt(t2[p].ap(), Sim[p].ap(), sb.ap(), ALU.mult, eng="pool")
            fw.tt(wr_in[p].ap(), t1[p].ap(), t2[p].ap(), ALU.add)
            fw.tt(t3[p].ap(), Sim[p].ap(), cb.ap(), ALU.mult, eng="pool")
            fw.tt(t4[p].ap(), Sre[p].ap(), sb.ap(), ALU.mult)
            fw.tt(wi_in[p].ap(), t3[p].ap(), t4[p].ap(), ALU.subtract, eng="pool")
            dec = DEC3[:, d_, q:q + 1].bc([128, NCH])
            fw.scan(wr[p].ap(), dec, wr_in[p].ap(), 0.0)
            fw.scan(wi[p].ap(), dec, wi_in[p].ap(), 0.0)
            fw.tt(t1[p].ap(), wr[p].ap(), cb.ap(), ALU.mult)
            fw.tt(t2[p].ap(), wi[p].ap(), sb.ap(), ALU.mult, eng="pool")
            fw.tt(hr.ap(), t1[p].ap(), t2[p].ap(), ALU.subtract)
            fw.tt(t3[p].ap(), wr[p].ap(), sb.ap(), ALU.mult, eng="pool")
            fw.tt(t4[p].ap(), wi[p].ap(), cb.ap(), ALU.mult)
            fw.tt(hi.ap(), t3[p].ap(), t4[p].ap(), ALU.add, eng="pool")

        def s5_back(it, d_, q):
            p = it % 2
            hr = Hre[p]; hi = Him[p]
            bY = (2, 3)
            for gi in range(2):
                g16 = 2 * q + gi
                g = d_ * 16 + g16
                sl = slice(64 * gi, 64 * gi + 64)
                psY = Bk[bY[gi]]
                fw.matmul(psY[:, 0:NCH], T3[:, g, :], U3[:, g, :], start=True, stop=False)
                fw.matmul(psY[:, 1:NCH], C4[sl, d_ * 8 + q, 0, :], hr[sl, 0:NCH - 1], start=False, stop=False)
                fw.matmul(psY[:, 1:NCH], C4[sl, d_ * 8 + q, 1, :], hi[sl, 0:NCH - 1], start=False, stop=True)
                yg = Yg[p * 2 + gi]
                fw.copy(yg.ap(), psY[:, 0:NCH], eng="act")
                fw.dma(SCRY[d_][g16].rearrange("t c j -> (t c) j"), yg.ap())

        its = [(d_, q) for d_ in range(2) for q in range(8)]
        for it, (d_, q) in enumerate(its):
            s5_front(it, d_, q)
            if it >= 1:
                s5_back(it - 1, *its[it - 1])
            yield 8.0
        s5_back(len(its) - 1, *its[-1])
        ys = stg
        for d_ in range(2):
            for g in range(16):
                m, gl = g // 8, g % 8
                fw.dma(s4(ys[d_])[16 * gl:16 * gl + 16, m], SCRY[d_][g].rearrange("t c j -> c t j"))
        yield 20.0
        y = A_.alloc(2 * T); y3 = y.v("p (m n) -> p m n", m=2)
        gb = A_.alloc(2 * T, BF16); gb3 = gb.v("p (m n) -> p m n", m=2)
        for m in range(2):
            f3 = s4(ys[0])[:, m]
            r3 = s4(ys[1])[:, m]
            eng = "dve"
            fw.tt(y3[:, m, 0:256].rearrange("p (j t) -> p j t", t=8), f3[:, :, 0:32].rearrange("p t j -> p j t"),
                  r3[:, :, 0:32].rearrange("p s n -> p n s")[:, ::-1, ::-1], ALU.add, eng=eng)
            fw.tt(y3[:, m, 256:T].rearrange("p (j t) -> p j t", t=8), f3[:, :, 32:NCH].rearrange("p t j -> p j t"),
                  r3[:, :, 32:NCH].rearrange("p s n -> p n s")[:, ::-1, ::-1], ALU.add, eng=eng)
            fw.stt(y3[:, m, :], uT3[:, m, :], Dcol[:, m:m + 1], y3[:, m, :], ALU.mult, ALU.add, eng=eng)
        yield 30.0
        fw.act(y.ap(), y.ap(), AF.Gelu_apprx_tanh)
        fw.copy(gb.ap(), y.ap(), eng="dve")
        osT = A_.alloc(2 * T, BF16); osT3 = osT.v("p (m n) -> p m n", m=2)
        gate = [A_.alloc(512) for _ in range(2)]
        bi = 0
        for mo in range(2):
            for (c0, nn) in [(0, 512), (512, 512), (1024, 512), (1536, 512), (2048, 256)]:
                ps = Bk[bi % 2]
                gt = gate[bi % 2]
                bi += 1
                for m in range(2):
                    fw.matmul(ps[:, 0:nn], wgb3[:, m, mo * 128:(mo + 1) * 128], gb3[:, m, c0:c0 + nn], start=(m == 0), stop=(m == 1))
                fw.act(gt[:, 0:nn], ps[:, 0:nn], AF.Sigmoid, bias=bgl[:, mo:mo + 1])
                fw.tt(osT3[:, mo, c0:c0 + nn], y3[:, mo, c0:c0 + nn], gt[:, 0:nn], ALU.mult, eng=("dve" if bi % 2 == 0 else "pool"))
                yield 4.0
        otb = [A_.alloc(256, BF16) for _ in range(2)]
        for i in range(NT):
            if l == 1 and i < 2:
                continue
            pT = View(Bk[2 + i % 2], Bk[2 + i % 2].t.bitcast(BF16))
            for mo in range(2):
                fw.transpose(pT[:, mo * 128:(mo + 1) * 128], osT3[:, mo, i * 128:(i + 1) * 128], idb.ap())
            ot = otb[i % 2]
            fw.copy(ot.ap(), pT[:, 0:256], eng="act")
            fw.dma(OO[b][i * 128:(i + 1) * 128, 768:1024], ot.ap())
            yield 2.0

    def run(gen):
        for _ in gen:
            pass

    def corun(gens):
        acc = [0.0] * len(gens)
        live = list(range(len(gens)))
        while live:
            i_ = min(live, key=lambda k_: acc[k_])
            try:
                c_ = next(gens[i_])
                acc[i_] += (c_ or 1.0)
            except StopIteration:
                live.remove(i_)

    full = all(st_ in stages for st_ in all_stages) and tuple(layers) == (0, 1) and tuple(batches) == (0, 1)
    PRO_W = 17408

    def prologue(l, A_, bk):
        yield from stage_M(l, A_, bk)
        A_.reset()
        yield from stage_NAB(l, A_, bk)
        A_.reset()
        yield from stage_FW(l, A_, bk)
        A_.reset()
        yield from stage_S5P(l, A_, bk)

    if full:
        ar.reset()
        corun([stage_W(), prologue(0, ar.sub(0, PRO_W), (0, 1, 2, 3))])
    elif "W" in stages:
        run(stage_W())
    for l in layers:
        if not full:
            if "M" in stages:
                run(stage_M(l))
            if "NAB" in stages:
                run(stage_NAB(l))
            if "FW" in stages:
                run(stage_FW(l))
            if "S5P" in stages:
                run(stage_S5P(l))
        for b in batches:
            if "A" in stages:
                stage_A(b, l)
            if all(st_ in stages for st_ in ("GQA", "NA", "F", "S5")):
                ar.reset()
                corun([stage_S5(b, l, ar.sub(0, 38400), (0, 1, 2, 3)), stage_GQA(b, l, ar.sub(38400, 9728), (4, 5, 6, 7))])
                ar.reset()
                if full and l == 0 and b == 1:
                    gens = [stage_NA(b, l, ar.sub(0, 19200), (0, 1, 2, 2)), stage_F(b, l, ar.sub(19200, 8192), (4, 5, 6, 7)),
                            prologue(1, ar.sub(27392, PRO_W), (3, 3, 3, 3))]
                else:
                    gens = [stage_NA(b, l, ar.sub(0, 19200), (0, 1, 2, 3)), stage_F(b, l, ar.sub(19200, 8192), (4, 5, 6, 7))]
                corun(gens)
            else:
                if "GQA" in stages:
                    run(stage_GQA(b, l))
                if "NA" in stages:
                    run(stage_NA(b, l))
                if "F" in stages:
                    run(stage_F(b, l))
                if "S5" in stages:
                    run(stage_S5(b, l))
            if "C" in stages:
                stage_C(b, l)
    fw.barrier()
    fw.emit()
    return nc


_RESHAPE = {
    "c_ctx": (1, D),
    "na_rel_bias": (2, 60, 31),
}


def make_in_maps(inputs, ncores=NCORES):
    consts = host_constants()
    shared = {}
    for name, arr in inputs.items():
        if name in ("x", "ctx", "c"):
            continue
        a = np.ascontiguousarray(arr)
        if name in _RESHAPE:
            a = a.reshape(_RESHAPE[name])
        shared[name] = a
    shared.update(consts)
    maps = []
    for i in range(ncores):
        m = dict(shared)
        m["x"] = np.ascontiguousarray(inputs["x"][i * NB:(i + 1) * NB])
        m["ctx"] = np.ascontiguousarray(inputs["ctx"][i * NB:(i + 1) * NB])
        m["c"] = np.ascontiguousarray(inputs["c"][i * NB:(i + 1) * NB])
        maps.append(m)
    return maps


def kernel(**inputs):
    nc = build()
    maps = make_in_maps(inputs)
    res = run_bass_kernel_spmd(nc, maps, core_ids=list(range(NCORES)))
    outs = [np.asarray(r["out"]) for r in res.results]
    return np.concatenate(outs, axis=0).astype(np.float32)
```
